# Optimizing a Trainium2 kernel written in Bass

```python
import math
import jax, jax.numpy as jnp
from jax import lax
import numpy as np

D_MODEL = 1024
BATCH = 32
SEQ = 256
DEPTH = 1
DEC_BATCH = 2
DEC_SEQ = 1024
PAST_LEN = 512

GRID_W = 64
HEAD_DIM_A = 64
WIDTH_A = D_MODEL
N_HEADS_A = WIDTH_A // (2 * HEAD_DIM_A)
HEAD_DIM_B = 64
WIDTH_B = D_MODEL
N_HEADS_B = WIDTH_B // HEAD_DIM_B
LORA_R = 64
D_MIX = WIDTH_A + WIDTH_B
D_SHIFT = 3 * WIDTH_B + 4 * LORA_R
D_IN = 4 * WIDTH_A + D_SHIFT + WIDTH_B
Q_BLOCK = 128
ROPE_BASE = 10000.0
NORM_EPS = 1e-6
SUBLN_EPS = 1e-5
LNX_EPS = 64e-5

kernel_name = 'hymba_diffattn_rwkv7_prefix_dit'

F32 = jnp.float32


def rmsnorm(x, g, eps):
    xf = x.astype(F32)
    y = xf * lax.rsqrt(jnp.mean(xf * xf, axis=-1, keepdims=True) + eps)
    return (y * g.astype(F32)).astype(x.dtype)


def axial_rope(x, rows):
    T = rows * GRID_W
    row = jnp.repeat(jnp.arange(rows), GRID_W)
    col = jnp.arange(T) % GRID_W
    n_freq = HEAD_DIM_A // 4
    inv = ROPE_BASE ** (-jnp.arange(n_freq, dtype=F32) / n_freq)

    def rot(seg, pos):
        ang = pos.astype(F32)[:, None] * inv[None, :]
        cos = jnp.cos(ang)[None, :, None, None, :].astype(x.dtype)
        sin = jnp.sin(ang)[None, :, None, None, :].astype(x.dtype)
        x1, x2 = jnp.split(seg, 2, axis=-1)
        return jnp.concatenate([x1 * cos - x2 * sin, x2 * cos + x1 * sin], axis=-1)

    half = HEAD_DIM_A // 2
    return jnp.concatenate([rot(x[..., :half], row), rot(x[..., half:], col)], axis=-1)


def diff_attention(q, k, v, lam, lam_init, subln_g):
    B, Tq = q.shape[0], q.shape[1]
    nq = Tq // Q_BLOCK
    qb = jnp.moveaxis(q.reshape(B, nq, Q_BLOCK, N_HEADS_A, 2, HEAD_DIM_A), 1, 0)
    scale = HEAD_DIM_A ** -0.5

    def block(qblk):
        s = jnp.einsum('bqhmd,bkhmd->bhmqk', qblk, k).astype(F32) * scale
        p = jax.nn.softmax(s, axis=-1)
        a = p[:, :, 0] - lam * p[:, :, 1]
        return jnp.einsum('bhqk,bkhv->bqhv', a.astype(v.dtype), v)

    o = lax.map(block, qb)
    o = jnp.moveaxis(o, 0, 1).reshape(B, Tq, N_HEADS_A, 2 * HEAD_DIM_A)
    o = rmsnorm(o, subln_g, SUBLN_EPS) * (1.0 - lam_init)
    return o.reshape(B, Tq, WIDTH_A)


def token_shift(u, mu_prev, mu_next):
    zeros = jnp.zeros_like(u[:, :1])
    prev = jnp.concatenate([zeros, u[:, :-1]], axis=1)
    nxt = jnp.concatenate([u[:, 1:], zeros], axis=1)
    return u + mu_prev * (prev - u) + mu_next * (nxt - u)


def rwkv_scan(r, w, k, v, kk, b, s0, reverse):
    xs = tuple(jnp.moveaxis(t.astype(F32), 1, 0) for t in (r, w, k, v, kk, b))

    def step(S, inp):
        r_t, w_t, k_t, v_t, kk_t, b_t = inp
        sa = jnp.einsum('bhvk,bhk->bhv', S, -kk_t)
        S = S * w_t[:, :, None, :] + sa[..., None] * b_t[:, :, None, :] + v_t[..., None] * k_t[:, :, None, :]
        return S, jnp.einsum('bhvk,bhk->bhv', S, r_t)

    s_final, y = lax.scan(step, s0.astype(F32), xs, reverse=reverse)
    return jnp.moveaxis(y, 0, 1), s_final


def rwkv_branch(u_b, g_b, s0, lp):
    B, T, _ = u_b.shape
    dt = u_b.dtype
    us = token_shift(u_b, lp['shift_mu'][0], lp['shift_mu'][1])
    r, k, v, lora = jnp.split(us, [WIDTH_B, 2 * WIDTH_B, 3 * WIDTH_B], axis=-1)
    lora = lora.reshape(B, T, 4, LORA_R).astype(F32)

    def heads(t):
        return t.reshape(B, T, N_HEADS_B, HEAD_DIM_B)

    rf, kf, vf = r.astype(F32), k.astype(F32), v.astype(F32)
    kk = heads(kf * lp['k_k'])
    kk = kk * lax.rsqrt(jnp.sum(kk * kk, axis=-1, keepdims=True) + 1e-12)
    ys, finals = [], []
    for d in range(2):
        w_log = -jax.nn.softplus(-(lp['decay_w0'][d] + jnp.tanh(lora[:, :, d]) @ lp['decay_w2'][d])) - 0.5
        decay = jnp.exp(-jnp.exp(w_log))
        a = jax.nn.sigmoid(lp['iclr_a0'][d] + lora[:, :, 2 + d] @ lp['iclr_a2'][d])
        k_d = kf * (1.0 + (a - 1.0) * lp['k_a'])
        y_d, s_d = rwkv_scan(heads(rf), heads(decay), heads(k_d), heads(vf), kk, kk * heads(a),
                             s0[:, d], reverse=(d == 1))
        ys.append(y_d)
        finals.append(s_d)
    y = ys[0] + ys[1]
    mu = jnp.mean(y, axis=-1, keepdims=True)
    var = jnp.mean(jnp.square(y - mu), axis=-1, keepdims=True)
    yn = ((y - mu) * lax.rsqrt(var + LNX_EPS)).reshape(B, T, WIDTH_B) * lp['lnx_g'] + lp['lnx_b']
    bonus = jnp.sum(heads(rf) * heads(kf) * lp['r_k'], axis=-1, keepdims=True) * heads(vf)
    out = (yn + bonus.reshape(B, T, WIDTH_B)).astype(dt) * jax.nn.silu(g_b)
    return out, jnp.stack(finals, axis=1)


def mixer_layer(x, mod, rows, ctx_k, ctx_v, s0, lp, lam_init):
    B, T, _ = x.shape
    shift, scale, gate = jnp.split(mod, 3, axis=-1)
    h = rmsnorm(x, lp['norm_g'], NORM_EPS) * (1.0 + scale[:, None, :]) + shift[:, None, :]
    u = h @ lp['w_in']
    q, k, v, g_a, u_b, g_b = jnp.split(
        u, [WIDTH_A, 2 * WIDTH_A, 3 * WIDTH_A, 4 * WIDTH_A, 4 * WIDTH_A + D_SHIFT], axis=-1)
    q = q.reshape(B, T, N_HEADS_A, 2, HEAD_DIM_A)
    k = k.reshape(B, T, N_HEADS_A, 2, HEAD_DIM_A)
    v = v.reshape(B, T, N_HEADS_A, 2 * HEAD_DIM_A)
    if rows is None:
        k_all, v_all = k, v
    else:
        q = axial_rope(q, rows)
        k = axial_rope(k, rows)
        k_all = jnp.concatenate([ctx_k.astype(k.dtype), k], axis=1)
        v_all = jnp.concatenate([ctx_v.astype(v.dtype), v], axis=1)
    lam = (jnp.exp(jnp.sum(lp['lam_q1'].astype(F32) * lp['lam_k1'].astype(F32)))
           - jnp.exp(jnp.sum(lp['lam_q2'].astype(F32) * lp['lam_k2'].astype(F32))) + lam_init)
    o_a = diff_attention(q, k_all, v_all, lam, lam_init, lp['subln_g']) * jax.nn.silu(g_a)
    o_b, s_final = rwkv_branch(u_b, g_b, s0, lp)
    x = x + gate[:, None, :] * (jnp.concatenate([o_a, o_b], axis=-1) @ lp['w_out'])
    return x, k, v, s_final


def setup_inputs(seed: int = 0) -> dict:
    key = jax.random.key(seed)
    ks = jax.random.split(key, 32)

    def nrm(k, shape, s):
        return jax.random.normal(k, shape, F32) * s

    return {
        'x_prompt': nrm(ks[0], (BATCH, SEQ, D_MODEL), 1.0),
        'x_sample': nrm(ks[1], (DEC_BATCH, DEC_SEQ, D_MODEL), 1.0),
        'cache_k': nrm(ks[2], (DEC_BATCH, DEPTH, PAST_LEN, N_HEADS_A, 2, HEAD_DIM_A), 1.0),
        'cache_v': nrm(ks[3], (DEC_BATCH, DEPTH, PAST_LEN, N_HEADS_A, 2 * HEAD_DIM_A), 1.0),
        'state_rwkv': nrm(ks[4], (DEC_BATCH, DEPTH, 2, N_HEADS_B, HEAD_DIM_B, HEAD_DIM_B), 0.5),
        'c': nrm(ks[5], (DEC_BATCH, D_MODEL), 1.0),
        'c_ctx': nrm(ks[6], (D_MODEL,), 1.0),
        'norm_g': 1.0 + nrm(ks[7], (DEPTH, D_MODEL), 0.02),
        'w_ada': nrm(ks[8], (DEPTH, D_MODEL, 3 * D_MODEL), 0.5 * D_MODEL ** -0.5),
        'b_ada': nrm(ks[9], (DEPTH, 3 * D_MODEL), 0.02),
        'w_in': nrm(ks[10], (DEPTH, D_MODEL, D_IN), D_MODEL ** -0.5),
        'lam_q1': nrm(ks[11], (DEPTH, HEAD_DIM_A), 0.1),
        'lam_k1': nrm(ks[12], (DEPTH, HEAD_DIM_A), 0.1),
        'lam_q2': nrm(ks[13], (DEPTH, HEAD_DIM_A), 0.1),
        'lam_k2': nrm(ks[14], (DEPTH, HEAD_DIM_A), 0.1),
        'subln_g': 1.0 + nrm(ks[15], (DEPTH, 2 * HEAD_DIM_A), 0.02),
        'shift_mu': jax.random.uniform(ks[16], (DEPTH, 2, D_SHIFT), F32, 0.0, 0.5),
        'decay_w0': jax.random.uniform(ks[17], (DEPTH, 2, WIDTH_B), F32, -5.0, 1.0),
        'decay_w2': nrm(ks[18], (DEPTH, 2, LORA_R, WIDTH_B), 0.1 * LORA_R ** -0.5),
        'iclr_a0': nrm(ks[19], (DEPTH, 2, WIDTH_B), 0.1),
        'iclr_a2': nrm(ks[20], (DEPTH, 2, LORA_R, WIDTH_B), 0.1 * LORA_R ** -0.5),
        'k_k': 0.85 + nrm(ks[21], (DEPTH, WIDTH_B), 0.02),
        'k_a': 1.0 + nrm(ks[22], (DEPTH, WIDTH_B), 0.02),
        'r_k': nrm(ks[23], (DEPTH, N_HEADS_B, HEAD_DIM_B), 0.1),
        'lnx_g': 1.0 + nrm(ks[24], (DEPTH, WIDTH_B), 0.02),
        'lnx_b': nrm(ks[25], (DEPTH, WIDTH_B), 0.02),
        'w_out': nrm(ks[26], (DEPTH, D_MIX, D_MODEL), D_MIX ** -0.5),
        'final_g': 1.0 + nrm(ks[27], (D_MODEL,), 0.02),
    }


def reference(x_prompt, x_sample, cache_k, cache_v, state_rwkv, c, c_ctx, norm_g, w_ada, b_ada,
              w_in, lam_q1, lam_k1, lam_q2, lam_k2, subln_g, shift_mu, decay_w0, decay_w2,
              iclr_a0, iclr_a2, k_k, k_a, r_k, lnx_g, lnx_b, w_out, final_g):
    rows = x_sample.shape[1] // GRID_W
    xp, xs = x_prompt, x_sample
    s0_ctx = jnp.zeros((x_prompt.shape[0], 2, N_HEADS_B, HEAD_DIM_B, HEAD_DIM_B), F32)
    new_k, new_v, new_s = [], [], []
    for l in range(DEPTH):
        lp = {
            'norm_g': norm_g[l], 'w_in': w_in[l], 'lam_q1': lam_q1[l], 'lam_k1': lam_k1[l],
            'lam_q2': lam_q2[l], 'lam_k2': lam_k2[l], 'subln_g': subln_g[l], 'shift_mu': shift_mu[l],
            'decay_w0': decay_w0[l], 'decay_w2': decay_w2[l], 'iclr_a0': iclr_a0[l],
            'iclr_a2': iclr_a2[l], 'k_k': k_k[l], 'k_a': k_a[l], 'r_k': r_k[l],
            'lnx_g': lnx_g[l], 'lnx_b': lnx_b[l], 'w_out': w_out[l],
        }
        lam_init = 0.8 - 0.6 * math.exp(-0.3 * l)
        mod_ctx = (jax.nn.silu(c_ctx) @ w_ada[l] + b_ada[l])[None]
        mod_lat = jax.nn.silu(c) @ w_ada[l] + b_ada[l]
        xp, k_l, v_l, s_l = mixer_layer(xp, mod_ctx, None, None, None, s0_ctx, lp, lam_init)
        new_k.append(k_l)
        new_v.append(v_l)
        new_s.append(s_l.astype(x_prompt.dtype))
        xs, _, _, _ = mixer_layer(xs, mod_lat, rows, cache_k[:, l], cache_v[:, l],
                                  state_rwkv[:, l], lp, lam_init)
    y_prompt = rmsnorm(xp, final_g, NORM_EPS)
    y_sample = rmsnorm(xs, final_g, NORM_EPS)
    return (y_prompt, y_sample, jnp.stack(new_k, axis=1), jnp.stack(new_v, axis=1), jnp.stack(new_s, axis=1))
```

```python
import math
from contextlib import ExitStack
import numpy as np
import concourse.bass as bass
import concourse.mybir as mybir
from concourse.bass_utils import run_bass_kernel_spmd

F32 = mybir.dt.float32
F32R = mybir.dt.float32r
LEVEL_DT = F32
BF16 = mybir.dt.bfloat16
AF = mybir.ActivationFunctionType
ALU = mybir.AluOpType
AX = mybir.AxisListType

ENG = ['pe', 'dve', 'act', 'pool', 'sp']
NDMA = 40
DC = math.exp(-0.5)


class Sched:
    def __init__(s, nc, stack):
        s.nc = nc
        s.ops = {e: [] for e in ENG}
        s.cnt = {e: 0 for e in ENG}
        s.known = {e: {f: 0 for f in ENG} for e in ENG}
        s.snap = {e: [] for e in ENG}
        s.kdma = {e: {} for e in ENG}
        s.lastw = {}
        s.rd_e = {}
        s.rd_d = {}
        s.sems = {e: stack.enter_context(nc.semaphore('c_' + e)) for e in ENG}
        s.dsems = [stack.enter_context(nc.semaphore('d%d' % i)) for i in range(2 * NDMA)]
        s.dcnt = {'sp': 0, 'pool': 0, 'act': 0}
        s.dcount = 0
        s.dlast = {}
        s.out_events = []

    def op(s, eng, fn, r=(), w=(), dma=False, is_out=False, noinc=False):
        r = [(k[0], k[1]) if (isinstance(k, tuple) and k[0] == 'ps') else k for k in r]
        w = [(k[0], k[1]) if (isinstance(k, tuple) and k[0] == 'ps') else k for k in w]
        psr = [k for k in r if isinstance(k, tuple) and k[0] == 'ps']
        if psr:
            r = [k for k in r if not (isinstance(k, tuple) and k[0] == 'ps')]
            w = list(w) + [k for k in psr if k not in w]
        deps = []
        for k in r:
            ev = s.lastw.get(k)
            if ev is not None:
                deps.append((ev, True))
        for k in w:
            ev = s.lastw.get(k)
            if ev is not None:
                deps.append((ev, False))
            for f, c in s.rd_e.get(k, {}).items():
                deps.append((('E', f, c), False))
            for ev in s.rd_d.get(k, ()):
                deps.append((ev, False))
        waits = {}
        for ev, raw in deps:
            if ev[0] == 'E':
                _, f, c = ev
                if f == eng and not dma:
                    if (not raw) or eng == 'pe':
                        continue
                if s.known[eng][f] >= c:
                    continue
                waits[('E', f)] = max(waits.get(('E', f), 0), c)
            else:
                _, si, v = ev
                if s.kdma[eng].get(si, 0) >= v:
                    continue
                waits[('D', si)] = max(waits.get(('D', si), 0), v)
        if dma:
            qn = s.dcnt[eng]
            si = qn % NDMA + (NDMA if eng == 'pool' else 0)
            v = 16 * (qn // NDMA + 1)
            if qn >= NDMA and s.kdma[eng].get(si, 0) < v - 16:
                waits[('D', si)] = max(waits.get(('D', si), 0), v - 16)
            s.dcnt[eng] += 1
            s.dcount += 1
            s.dlast[si] = v
            ev = ('D', si, v)
        for (t, x), v in waits.items():
            if t == 'E':
                kn = s.known[eng]
                if kn[x] < v:
                    kn[x] = v
                sn = s.snap[x][v - 1]
                for f2, c2 in sn.items():
                    if kn[f2] < c2:
                        kn[f2] = c2
            else:
                s.kdma[eng][x] = v
        if not dma:
            if noinc:
                ev = ('E', eng, s.cnt[eng] + 1)
            else:
                s.cnt[eng] += 1
                ev = ('E', eng, s.cnt[eng])
                s.snap[eng].append(dict(s.known[eng]))
        s.ops[eng].append((list(waits.items()), fn, None if noinc else ev))
        for k in r:
            if ev[0] == 'E':
                s.rd_e.setdefault(k, {})[eng] = ev[2]
            else:
                s.rd_d.setdefault(k, []).append(ev)
        for k in w:
            s.lastw[k] = ev
            s.rd_e[k] = {}
            s.rd_d[k] = []
        if is_out:
            s.out_events.append(ev)
        return ev

    def barrier(s):
        waits = {}
        for f in ENG:
            if f != 'sp' and s.cnt[f] > 0:
                waits[('E', f)] = s.cnt[f]
        for si, v in s.dlast.items():
            waits[('D', si)] = v
        s.cnt['sp'] += 1
        c = s.cnt['sp']
        for f in ENG:
            if f != 'sp':
                s.known['sp'][f] = s.cnt[f]
        s.kdma['sp'] = dict(s.dlast)
        s.snap['sp'].append(dict(s.known['sp']))
        s.ops['sp'].append((list(waits.items()), (lambda e: e.nop()), ('E', 'sp', c)))
        for f in ENG:
            if f == 'sp':
                continue
            s.ops[f].append(([(('E', 'sp'), c)], None, None))
            for g in ENG:
                if g != f:
                    s.known[f][g] = max(s.known[f][g], s.cnt[g])
            s.kdma[f] = dict(s.dlast)
        s.lastw.clear()
        s.rd_e.clear()
        s.rd_d.clear()

    def finish(s):
        waits = {}
        for ev in s.out_events:
            waits[('D', ev[1])] = max(waits.get(('D', ev[1]), 0), ev[2])
        s.ops['sp'].append((list(waits.items()), None, None))

    def emit(s, block):
        def mk(engname):
            def body(e):
                for waits, fn, ev in s.ops[engname]:
                    for (t, x), v in waits:
                        sem = s.sems[x] if t == 'E' else s.dsems[x]
                        e.wait_ge(sem, v)
                    if fn is None:
                        continue
                    ins = fn(e)
                    if ev is None:
                        continue
                    if ev[0] == 'E':
                        ins.then_inc(s.sems[engname], 1)
                    else:
                        ins.then_inc(s.dsems[ev[1]], 16)
            return body
        block.tensor(mk('pe'))
        block.vector(mk('dve'))
        block.scalar(mk('act'))
        block.gpsimd(mk('pool'))
        block.sync(mk('sp'))


class Arena:
    def __init__(s, ar, nwords):
        s.ar = ar
        s.off = 0
        s.n = nwords
        s.peak = 0

    def alloc(s, shape, dt):
        n = 1
        for x in shape:
            n *= x
        words = n if dt == F32 else (n + 1) // 2
        words = (words + 7) // 8 * 8
        a = s.ar[:, s.off:s.off + words]
        s.off += words
        s.peak = max(s.peak, s.off)
        assert s.off <= s.n, ("arena overflow", s.off, s.n)
        if dt == BF16:
            a = a.bitcast(BF16)
        a = a[:, 0:n]
        if len(shape) == 2:
            a = a.rearrange("p (a b) -> p a b", b=shape[1])
        elif len(shape) == 3:
            a = a.rearrange("p (a b c) -> p a b c", b=shape[1], c=shape[2])
        elif len(shape) == 4:
            a = a.rearrange("p (a b c d) -> p a b c d", b=shape[1], c=shape[2], d=shape[3])
        return a


IN_SPECS = [
    ("xp", [1024, 1024]), ("xs", [1024, 1024]), ("xo", [256, 1024]),
    ("ck", [512, 1024]), ("cv", [512, 1024]), ("s0", [2, 16, 64, 64]),
    ("cT", [128, 16]), ("w_in", [1024, 8448]), ("w_ada", [1024, 3072]), ("w_out", [2048, 1024]),
    ("bada_fm", [128, 24]), ("bgate", [1, 1024]), ("normg_fm", [128, 8]), ("fg", [1, 1024]),
    ("lamv", [1, 256]), ("sublng", [1, 128]), ("mu_fm", [128, 52]), ("w0_fm", [128, 16]),
    ("a0_fm", [128, 16]), ("w2", [128, 1024]), ("a2", [128, 1024]), ("kk_fm", [128, 8]),
    ("ka_fm", [128, 8]), ("rk_fm", [128, 8]), ("lnxg", [1, 1024]), ("lnxb", [1, 1024]),
    ("ident", [128, 128]), ("maskNM", [2, 128, 512]), ("maskT", [2, 128, 128]),
    ("bones", [128, 128]), ("hind", [128, 2]), ("selT", [1024, 256]),
    ("ropeall", [1024, 256]), ("ropeown", [256, 256]),
]
OUT_SPECS = [
    ("yp", [1024, 1024]), ("ys", [256, 1024]), ("nk", [1024, 1024]), ("nv", [1024, 1024]),
    ("ns", [4, 2, 16, 64, 64]),
]

ARENA_WORDS = 52480 - 4096
LVL_WORDS = 4096
NHEAD_A = 8
NHP = 8
NSEQ_R = 5
ND = 2
DEBUG = False
A_MODE = 'all'
DUMPS = []
STOP = None


def build():
    nc = bass.Bass("TRN2", target_bir_lowering=False)
    I = {n: nc.dram_tensor(n, sh, F32, kind="ExternalInput").ap() for n, sh in IN_SPECS}
    O = {n: nc.dram_tensor(n, sh, F32, kind="ExternalOutput").ap() for n, sh in OUT_SPECS}
    with ExitStack() as stack:
        ar = stack.enter_context(nc.sbuf_tensor("arena", [128, ARENA_WORDS], F32))
        PS = stack.enter_context(nc.psum_tensor("ps", [128, 4096], F32))
        lvl = stack.enter_context(nc.sbuf_tensor("lvl", [128, LVL_WORDS], LEVEL_DT))
        S = Sched(nc, stack)
        A = Arena(ar, ARENA_WORDS)
        block = stack.enter_context(nc.Block())
        _program(nc, S, A, PS, I, O, lvl)
        S.finish()
        S.emit(block)
    return nc


def _program(nc, S, A, PS, I, O, lvl):
    def dma(eng, out, in_, r, w, is_out=False):
        S.op(eng, lambda e: e.dma_start(out=out, in_=in_), r, w, dma=True, is_out=is_out)

    def mm(out, lhsT, rhs, start, stop, r, w):
        S.op('pe', lambda e: e.matmul(out, lhsT, rhs, start=start, stop=stop), r, w, noinc=(not stop))

    def tr(out, in_, ident, r, w):
        S.op('pe', lambda e: e.transpose(out, in_, ident), r, w)

    def act(out, in_, func, r, w, bias=None, scale=None, accum=None):
        def f(e):
            kw = {}
            if bias is not None:
                kw['bias'] = bias
            if scale is not None:
                kw['scale'] = scale
            if accum is not None:
                kw['accum_out'] = accum
            return e.activation(out=out, in_=in_, func=func, **kw)
        S.op('act', f, r, w)

    def tt(eng, out, in0, in1, op, r, w):
        S.op(eng, lambda e: e.tensor_tensor(out=out, in0=in0, in1=in1, op=op), r, w)

    def ts(eng, out, in0, s1, s2, op0, op1, r, w):
        if s2 is None:
            S.op(eng, lambda e: e.tensor_scalar(out=out, in0=in0, scalar1=s1, scalar2=None, op0=op0), r, w)
        else:
            S.op(eng, lambda e: e.tensor_scalar(out=out, in0=in0, scalar1=s1, scalar2=s2, op0=op0, op1=op1), r, w)

    def stt(eng, out, in0, sc, in1, op0, op1, r, w):
        S.op(eng, lambda e: e.scalar_tensor_tensor(out=out, in0=in0, scalar=sc, in1=in1, op0=op0, op1=op1), r, w)

    def cp(eng, out, in_, r, w):
        if eng == 'act':
            act(out, in_, AF.Identity, r, w)
        else:
            S.op(eng, lambda e: e.tensor_copy(out=out, in_=in_), r, w)

    def red(out, in_, op, r, w):
        S.op('dve', lambda e: e.tensor_reduce(out=out, in_=in_, axis=AX.X, op=op), r, w)

    def recip(out, in_, r, w):
        S.op('dve', lambda e: e.reciprocal(out=out, in_=in_), r, w)

    def memset(eng, out, val, w):
        S.op(eng, lambda e: e.memset(out, val), (), w)

    def bank(b, c0=0, c1=512):
        return PS[:, b * 512 + c0: b * 512 + c1]

    def bankb(b):
        return PS[:, b * 512:(b + 1) * 512].bitcast(BF16)

    w_in_v = I['w_in'].rearrange("(kc p) n -> p kc n", p=128)

    def dump_all(bufs):
        S.barrier()
        for name, ap in bufs.items():
            sh = list(ap.shape)
            dt = nc.dram_tensor('dbg_' + name, sh, ap.dtype, kind="ExternalOutput").ap()
            DUMPS.append('dbg_' + name)
            dma('sp', dt, ap, (), (), is_out=True)

    identf = A.alloc([128], F32)
    identb = A.alloc([128], BF16)
    onesf = A.alloc([128], F32)
    maskNM = A.alloc([2, 512], BF16)
    maskT = A.alloc([2, 128], BF16)
    bonesb = A.alloc([128], BF16)
    hindb = A.alloc([2], BF16)
    selb = A.alloc([8, 256], BF16)
    cst = A.alloc([4], F32)
    hTp = A.alloc([8, 1024], BF16)
    hTs = A.alloc([8, 1024], BF16)
    hTo = A.alloc([8, 256], BF16)
    mixT = A.alloc([16, 1280], BF16)
    modfm = A.alloc([24, 2], F32)
    scale1 = A.alloc([8, 2], F32)
    neglam = A.alloc([1], F32)
    sgl = A.alloc([128], F32)
    mu = A.alloc([26, 2], F32)
    c0v = A.alloc([26], F32)
    w0v = A.alloc([16], F32)
    a0v = A.alloc([16], F32)
    kkv = A.alloc([8], F32)
    kav = A.alloc([8], F32)
    omka = A.alloc([8], F32)
    rkv_ = A.alloc([8], F32)
    W2b = A.alloc([1024], BF16)
    A2b = A.alloc([1024], BF16)
    twT = A.alloc([2048], BF16)
    laT = A.alloc([2048], BF16)

    dma('sp', identf, I['ident'], (), ['identf'])
    dma('pool', identb, I['ident'], (), ['identb'])
    dma('pool', maskNM, I['maskNM'].rearrange("d p n -> p d n"), (), ['maskNM'])
    dma('pool', maskT, I['maskT'].rearrange("d p n -> p d n"), (), ['maskT'])
    dma('pool', bonesb, I['bones'], (), ['bonesb'])
    dma('pool', hindb, I['hind'], (), ['hindb'])
    dma('pool', selb, I['selT'].rearrange("(j p) n -> p j n", p=128), (), ['selb'])
    dma('sp', mu, I['mu_fm'].rearrange("p (c j) -> p c j", j=2), (), ['mu'])
    dma('sp', w0v, I['w0_fm'], (), ['w0v'])
    dma('sp', a0v, I['a0_fm'], (), ['a0v'])
    dma('sp', kkv, I['kk_fm'], (), ['kkv'])
    dma('sp', kav, I['ka_fm'], (), ['kav'])
    dma('sp', rkv_, I['rk_fm'], (), ['rkv_'])
    dma('pool', W2b, I['w2'], (), ['W2b'])
    dma('pool', A2b, I['a2'], (), ['A2b'])
    memset('dve', onesf, 1.0, ['onesf'])
    memset('dve', cst[:, 0:1], 1e-12, ['cst'])
    tt('dve', c0v, mu[:, :, 0], mu[:, :, 1], ALU.add, ['mu'], ['c0v'])
    ts('dve', c0v, c0v, -1.0, 1.0, ALU.mult, ALU.add, ['c0v'], ['c0v'])
    ts('dve', omka, kav, -1.0, 1.0, ALU.mult, ALU.add, ['kav'], ['omka'])

    scp = A.alloc([8, 2], BF16)
    m0 = A.off
    cT = A.alloc([16], F32)
    sc = A.alloc([8, 2], F32)
    wadaf = [A.alloc([8, 512], F32) for _ in range(4)]
    bada = A.alloc([24], F32)
    normg = A.alloc([8], F32)
    lamt = A.alloc([4, 64], F32)
    lamp = A.alloc([2, 64], F32)
    lams = A.alloc([4], F32)

    dma('sp', cT, I['cT'], (), ['cT'])
    dma('sp', bada, I['bada_fm'], (), ['bada'])
    dma('sp', normg, I['normg_fm'], (), ['normg'])
    dma('sp', lamt.rearrange("p a b -> p (a b)"), I['lamv'][0:1, :].partition_broadcast(128), (), ['lamt'])
    dma('sp', sgl, I['sublng'][0:1, :].partition_broadcast(128), (), ['sgl'])
    wada_v = I['w_ada'].rearrange("(kc p) n -> p kc n", p=128)
    for n in range(4):
        dma('sp', wadaf[n], wada_v[:, :, n * 512:(n + 1) * 512], (), [('wada', n)])
    act(sc, cT.rearrange("p (c v) -> p c v", v=2), AF.Silu, ['cT'], ['sc'])
    cp('dve', scp, sc, ['sc'], ['scp'])
    for fc in range(16):
        for kc in range(8):
            mm(bank(0, fc * 2, fc * 2 + 2), wadaf[fc // 4][:, kc, (fc % 4) * 128:(fc % 4 + 1) * 128], sc[:, kc, :],
               kc == 0, kc == 7, ['sc', ('wada', fc // 4)], [('ps', 0)])
    tt('dve', modfm[:, 0:16, :], bank(0, 0, 32).rearrange("p (a b) -> p a b", b=2),
       bada[:, 0:16].unsqueeze(2).to_broadcast([128, 16, 2]), ALU.add, [('ps', 0), 'bada'], ['modfm'])
    ts('dve', scale1, modfm[:, 8:16, :], 1.0, None, ALU.add, None, ['modfm'], ['scale1'])
    tt('dve', scale1, scale1, normg.unsqueeze(2).to_broadcast([128, 8, 2]), ALU.mult, ['scale1', 'normg'], ['scale1'])
    tt('dve', lamp[:, 0, :], lamt[:, 0, :], lamt[:, 1, :], ALU.mult, ['lamt'], ['lamp'])
    tt('dve', lamp[:, 1, :], lamt[:, 2, :], lamt[:, 3, :], ALU.mult, ['lamt', 'lamp'], ['lamp'])
    red(lams[:, 0:2], lamp, ALU.add, ['lamp'], ['lams'])
    act(lams[:, 2:4], lams[:, 0:2], AF.Exp, ['lams'], ['lams2'])
    lam_init = 0.8 - 0.6 * math.exp(-0.3 * 0)
    tt('dve', neglam, lams[:, 3:4], lams[:, 2:3], ALU.subtract, ['lams2'], ['neglam'])
    ts('dve', neglam, neglam, -lam_init, None, ALU.add, None, ['neglam'], ['neglam'])
    ts('dve', sgl, sgl, 1.0 - lam_init, None, ALU.mult, None, ['sgl'], ['sgl'])

    if STOP == '0':
        return
    xt = [A.alloc([1024], F32) for _ in range(2)]
    xn = [A.alloc([1024], BF16) for _ in range(2)]
    junk = A.alloc([1024], BF16)
    st1 = [A.alloc([4], F32) for _ in range(2)]
    tiles = [('xp', g, hTp, g, 0) for g in range(8)] + [('xs', g, hTs, g, 1) for g in range(8)] + \
            [('xo', g, hTo, g, 1) for g in range(2)]
    for ti, (src, g, hT, tg, v) in enumerate(tiles):
        b = ti % 2
        dma('sp', xt[b], I[src][g * 128:(g + 1) * 128, :], (), [('xt', b)])
        act(junk, xt[b], AF.Square, [('xt', b)], ['junk', ('st1', b)], accum=st1[b][:, 0:1])
        ts('dve', st1[b][:, 1:2], st1[b][:, 0:1], 1.0 / 1024, 1e-6, ALU.mult, ALU.add, [('st1', b)], [('st1b', b)])
        act(st1[b][:, 2:3], st1[b][:, 1:2], AF.Sqrt, [('st1b', b)], [('st1c', b)])
        recip(st1[b][:, 3:4], st1[b][:, 2:3], [('st1c', b)], [('st1d', b)])
        ts('dve', xn[b], xt[b], st1[b][:, 3:4], None, ALU.mult, None, [('xt', b), ('st1d', b)], [('xn', b)])
        pb_ = bankb(3 + b)
        for kc in range(8):
            tr(pb_[:, kc * 128:(kc + 1) * 128], xn[b][:, kc * 128:(kc + 1) * 128], identb,
               [('xn', b), 'identb'], [('ps', 3 + b)])
        for kc in range(8):
            act(hT[:, kc, tg * 128:(tg + 1) * 128], pb_[:, kc * 128:(kc + 1) * 128], AF.Identity,
                [('ps', 3 + b), 'scale1', 'modfm'], [(src + 'h', tg)],
                bias=modfm[:, kc, v:v + 1], scale=scale1[:, kc, v:v + 1])
    if STOP == '1':
        return
    S.barrier()
    A.off = m0
    if STOP == '1b':
        return

    mA = A.off
    wA = [A.alloc([8, 512], BF16) for _ in range(2)]
    qkb = [A.alloc([256], BF16) for _ in range(2)]
    kvf = [A.alloc([256], F32) for _ in range(2)]
    qT2 = [A.alloc([256], BF16) for _ in range(2)]
    sg2 = [A.alloc([2, 128], F32) for _ in range(2)]
    kTp = [A.alloc([256], BF16) for _ in range(2)]
    vbp = [A.alloc([2, 128], BF16) for _ in range(2)]
    kTs = A.alloc([1536], BF16)
    vbs = A.alloc([12, 128], BF16)
    ckb = A.alloc([4, 128], BF16)
    ropa = A.alloc([8, 256], F32)
    ropo = A.alloc([2, 256], F32)
    xf = [A.alloc([128], F32) for _ in range(2)]
    rt1 = [A.alloc([128], F32) for _ in range(2)]
    rt2 = [A.alloc([128], F32) for _ in range(2)]
    xb16 = [A.alloc([128], BF16) for _ in range(2)]
    pbuf = [A.alloc([1536], BF16) for _ in range(2)]
    pT = [A.alloc([1536], BF16) for _ in range(2)]
    pbufp = [[A.alloc([256], BF16) for _ in range(2)] for _ in range(2)]
    pTp = [[A.alloc([256], BF16) for _ in range(2)] for _ in range(2)]
    ast = [A.alloc([16], F32) for _ in range(4)]
    of = [A.alloc([128], F32) for _ in range(4)]
    o1 = [A.alloc([128], F32) for _ in range(4)]
    on = [A.alloc([128], F32) for _ in range(4)]
    ob = [A.alloc([128], BF16) for _ in range(4)]
    ajunk = A.alloc([128], BF16)

    dma('sp', ropa, I['ropeall'].rearrange("(j p) n -> p j n", p=128), (), ['ropa'])
    dma('sp', ropo, I['ropeown'].rearrange("(j p) n -> p j n", p=128), (), ['ropo'])

    ctr = {'x': 0}

    def rope(src_ps, tab, dst16, rkeys, wkey):
        i = ctr['x'] % 2
        ctr['x'] += 1
        cp('act', xf[i], src_ps, rkeys, [('xf', i)])
        tt('dve', rt1[i], xf[i], tab[:, 0:128], ALU.mult, [('xf', i), 'ropa', 'ropo'], [('rt1', i)])
        xv = xf[i].rearrange("p (g h f) -> p g h f", h=2, f=16)
        sv = tab[:, 128:256].rearrange("p (g h f) -> p g h f", h=2, f=16)
        r2 = rt2[i].rearrange("p (g h f) -> p g h f", h=2, f=16)
        tt('pool', r2[:, :, 0, :], xv[:, :, 1, :], sv[:, :, 0, :], ALU.mult, [('xf', i), 'ropa', 'ropo'], [('rt2', i, 0)])
        tt('pool', r2[:, :, 1, :], xv[:, :, 0, :], sv[:, :, 1, :], ALU.mult, [('xf', i), 'ropa', 'ropo'], [('rt2', i, 1)])
        tt('dve', dst16, rt1[i], rt2[i], ALU.add, [('rt1', i), ('rt2', i, 0), ('rt2', i, 1)], [wkey])

    def drive2(gens):
        gens = [g for g in gens if g is not None]
        while gens:
            for g in list(gens):
                try:
                    next(g)
                except StopIteration:
                    gens.remove(g)

    def attn_j(h, par, j, kind, mixcol0):
        ai = par * 2 + j
        if kind == 'p':
            ntk, kT_, vb_, kkey, vkey = 2, kTp[par], vbp[par], ('kTp', par), ('vbp', par)
            sb0, pb0, ob_ = 2 + j, 4 + j, 6 + j
            pbs, pTs = pbufp[j], pTp[j]
            pkey = ('pp', j)
        else:
            ntk, kT_, vb_, kkey, vkey = 12, kTs, vbs, 'kTs', 'vbs'
            sb0, pb0, ob_ = 2, 5, 7
            pbs, pTs = pbuf, pT
            pkey = ('ps_', 0)
        Tk = ntk * 128
        qT_ = qT2[par]
        s0c = sb0 * 512
        p0c = pb0 * 512
        for m in range(2):
            for n0 in range(0, Tk, 512):
                w_ = min(512, Tk - n0)
                mm(PS[:, s0c + n0:s0c + n0 + w_], qT_[64 * m:64 * m + 64, j * 128:(j + 1) * 128],
                   kT_[64 * m:64 * m + 64, n0:n0 + w_], True, True,
                   [('qT', par, j), kkey], [('ps', sb0 + n0 // 512)])
            sck = [('ps', sb0 + b_) for b_ in range((Tk + 511) // 512)]
            red(ast[ai][:, m:m + 1], PS[:, s0c:s0c + Tk], ALU.max, sck, [('ast', ai, 'mx', m)])
            ts('dve', ast[ai][:, 2 + m:3 + m], ast[ai][:, m:m + 1], -0.125, None, ALU.mult, None,
               [('ast', ai, 'mx', m)], [('ast', ai, 'nb', m)])
            act(pbs[m][:, 0:Tk], PS[:, s0c:s0c + Tk], AF.Exp, sck + [('ast', ai, 'nb', m)],
                [('pbuf', pkey, m), ('ast', ai, 'sum', m)], bias=ast[ai][:, 2 + m:3 + m], scale=0.125,
                accum=ast[ai][:, 4 + m:5 + m])
            yield
            ptv = PS[:, p0c:p0c + 1024].bitcast(BF16)
            for t in range(ntk):
                tr(ptv[:, t * 128:(t + 1) * 128], pbs[m][:, t * 128:(t + 1) * 128], identb,
                   [('pbuf', pkey, m), 'identb'], [('ps', pb0 + t // 8)])
            if ntk <= 2:
                cp('dve', pTs[m][:, 0:Tk], ptv[:, 0:Tk], [('ps', pb0)], [('pT', pkey, m, 0)])
                ptk = [('pT', pkey, m, 0)]
            else:
                cp('dve', pTs[m][:, 0:768], ptv[:, 0:768], [('ps', pb0)], [('pT', pkey, m, 0)])
                cp('act', pTs[m][:, 768:1536], ptv[:, 768:1536], [('ps', pb0), ('ps', pb0 + 1)], [('pT', pkey, m, 1)])
                ptk = [('pT', pkey, m, 0), ('pT', pkey, m, 1)]
            for t in range(ntk):
                mm(bank(ob_, m * 128, (m + 1) * 128), pTs[m][:, t * 128:(t + 1) * 128], vb_[:, t, :],
                   t == 0, t == ntk - 1, ptk + [vkey], [('ps', ob_)])
            yield
        a_ = ast[ai]
        recip(a_[:, 6:8], a_[:, 4:6], [('ast', ai, 'sum', 0), ('ast', ai, 'sum', 1)], [('ast', ai, 'rs')])
        tt('dve', a_[:, 8:9], a_[:, 7:8], neglam, ALU.mult, [('ast', ai, 'rs'), 'neglam'], [('ast', ai, 'c2')])
        act(o1[ai], bank(ob_, 0, 128), AF.Identity, [('ps', ob_), ('ast', ai, 'rs')], [('o1', ai)], scale=a_[:, 6:7])
        stt('dve', of[ai], bank(ob_, 128, 256), a_[:, 8:9], o1[ai], ALU.mult, ALU.add,
            [('ps', ob_), ('ast', ai, 'c2'), ('o1', ai)], [('of', ai)])
        yield
        act(ajunk, of[ai], AF.Square, [('of', ai)], ['ajunk', ('ast', ai, 'ss')], accum=a_[:, 9:10])
        ts('dve', a_[:, 10:11], a_[:, 9:10], 1.0 / 128, 1e-5, ALU.mult, ALU.add, [('ast', ai, 'ss')], [('ast', ai, 'ms')])
        act(a_[:, 11:12], a_[:, 10:11], AF.Sqrt, [('ast', ai, 'ms')], [('ast', ai, 'sd')])
        recip(a_[:, 12:13], a_[:, 11:12], [('ast', ai, 'sd')], [('ast', ai, 'rstd')])
        yield
        stt('dve', on[ai], of[ai], a_[:, 12:13], sgl, ALU.mult, ALU.mult, [('of', ai), ('ast', ai, 'rstd'), 'sgl'], [('on', ai)])
        tt('pool', ob[ai], on[ai], sg2[par][:, j, :], ALU.mult, [('on', ai), ('sg', par, j)], [('ob', ai)])
        pso = bankb(ob_)
        tr(pso[:, 512:640], ob[ai], identb, [('ob', ai), 'identb'], [('ps', ob_)])
        cp('act', mixT[:, h, mixcol0 + j * 128: mixcol0 + (j + 1) * 128], pso[:, 512:640], [('ps', ob_)],
           [('mixT', h, (mixcol0 // 128) + j)])
        yield

    def attn_gen(h, par, kind, mixcol0):
        if kind == 'p':
            g0, g1 = attn_j(h, par, 0, kind, mixcol0), attn_j(h, par, 1, kind, mixcol0)
            gens = [g0, g1]
            while gens:
                for g in list(gens):
                    try:
                        next(g)
                    except StopIteration:
                        gens.remove(g)
                yield
        else:
            for j in range(2):
                for _ in attn_j(h, par, j, kind, mixcol0):
                    yield

    def proj_gen(h, par, kind, s_):
        wb = wA[h % 2]
        wk = [('wA', h % 2, j4) for j4 in range(4)]
        pst = bankb(1)
        if kind == 'p':
            for j in range(2):
                g = s_ * 2 + j
                bi = g % 2
                for kc in range(8):
                    mm(bank(0), hTp[:, kc, g * 128:(g + 1) * 128], wb[:, kc, :], kc == 0, kc == 7,
                       [('xph', g)] + wk, [('ps', 0)])
                cp('dve', qkb[bi], bank(0, 0, 256), [('ps', 0)], [('qkb', bi)])
                cp('act', kvf[bi], bank(0, 128, 384), [('ps', 0)], [('kvf', bi)])
                dma('sp', O['nk'][g * 128:(g + 1) * 128, h * 128:(h + 1) * 128], kvf[bi][:, 0:128], [('kvf', bi)], (), is_out=True)
                dma('sp', O['nv'][g * 128:(g + 1) * 128, h * 128:(h + 1) * 128], kvf[bi][:, 128:256], [('kvf', bi)], (), is_out=True)
                cp('dve', vbp[par][:, j, :], bank(0, 256, 384), [('ps', 0)], [('vbp', par)])
                act(sg2[par][:, j, :], bank(0, 384, 512), AF.Silu, [('ps', 0)], [('sg', par, j)])
                yield
                tr(pst[:, 0:128], qkb[bi][:, 0:128], identb, [('qkb', bi), 'identb'], [('ps', 1)])
                tr(pst[:, 128:256], qkb[bi][:, 128:256], identb, [('qkb', bi), 'identb'], [('ps', 1)])
                cp('act', qT2[par][:, j * 128:(j + 1) * 128], pst[:, 0:128], [('ps', 1)], [('qT', par, j)])
                cp('act', kTp[par][:, j * 128:(j + 1) * 128], pst[:, 128:256], [('ps', 1)], [('kTp', par)])
                yield
        else:
            dma('pool', ckb, I['ck'].rearrange("(j p) c -> p j c", p=128)[:, :, h * 128:(h + 1) * 128], (), ['ckb'])
            dma('pool', vbs[:, 0:4, :], I['cv'].rearrange("(j p) c -> p j c", p=128)[:, :, h * 128:(h + 1) * 128], (), ['vbs'])
            for t in range(4):
                tr(pst[:, 512 + t * 128:512 + (t + 1) * 128], ckb[:, t, :], identb, ['ckb', 'identb'], [('ps', 1)])
            cp('dve', kTs[:, 0:512], pst[:, 512:1024], [('ps', 1)], ['kTs'])
            yield
            for j in range(8):
                for kc in range(8):
                    mm(bank(0, 0, 256), hTs[:, kc, j * 128:(j + 1) * 128], wb[:, kc, 128:384], kc == 0, kc == 7,
                       [('xsh', j)] + wk, [('ps', 0)])
                bi = j % 2
                cp('dve', vbs[:, 4 + j, :], bank(0, 128, 256), [('ps', 0)], ['vbs'])
                rope(bank(0, 0, 128), ropa[:, j, :], xb16[bi], [('ps', 0)], ('xb16', bi))
                yield
                tr(pst[:, 0:128], xb16[bi], identb, [('xb16', bi), 'identb'], [('ps', 1)])
                cp('act', kTs[:, 512 + j * 128:512 + (j + 1) * 128], pst[:, 0:128], [('ps', 1)], ['kTs'])
                yield
            for j in range(2):
                for kc in range(8):
                    mm(bank(0), hTo[:, kc, j * 128:(j + 1) * 128], wb[:, kc, :], kc == 0, kc == 7,
                       [('xoh', j)] + wk, [('ps', 0)])
                bi = j % 2
                act(sg2[par][:, j, :], bank(0, 384, 512), AF.Silu, [('ps', 0)], [('sg', par, j)])
                rope(bank(0, 0, 128), ropo[:, j, :], xb16[bi], [('ps', 0)], ('xb16', bi))
                yield
                tr(pst[:, 128:256], xb16[bi], identb, [('xb16', bi), 'identb'], [('ps', 1)])
                cp('act', qT2[par][:, j * 128:(j + 1) * 128], pst[:, 128:256], [('ps', 1)], [('qT', par, j)])
                yield

    ajobs = []
    for h in range(NHEAD_A):
        for s_ in range(4):
            ajobs.append((h, 'p', s_))
        ajobs.append((h, 's', 0))
    loaded = set()

    def load_w(h):
        if h in loaded or h >= NHEAD_A:
            return
        loaded.add(h)
        for j4, base in enumerate([0, 1024, 2048, 3072]):
            dma('pool', wA[h % 2][:, :, j4 * 128:(j4 + 1) * 128], w_in_v[:, :, base + h * 128: base + (h + 1) * 128], (),
                [('wA', h % 2, j4)])
    if ajobs:
        load_w(0)
        drive2([proj_gen(ajobs[0][0], 0, ajobs[0][1], ajobs[0][2])])
        for n, (h, kind, s_) in enumerate(ajobs):
            par = n % 2
            ag = attn_gen(h, par, kind, s_ * 256 if kind == 'p' else 1024)
            pg = None
            if n + 1 < len(ajobs):
                h2, kind2, s2 = ajobs[n + 1]
                load_w(h2)
                pg = proj_gen(h2, (n + 1) % 2, kind2, s2)
            drive2([ag, pg])
    if STOP == 'A':
        return
    S.barrier()
    A.off = mA

    wR = [A.alloc([8, 512], BF16) for _ in range(2)]
    raw = A.alloc([1024], F32)
    rkv = A.alloc([3, 1024], F32)
    kk = A.alloc([1024], F32)
    t1 = A.alloc([1024], F32)
    prodb = A.alloc([1024], BF16)
    sqb = prodb
    vb16 = A.alloc([1024], BF16)
    VT = A.alloc([8, 128], BF16)
    sgb = A.alloc([8, 128], F32)
    yacc = A.alloc([8, 128], F32)
    wL = yacc.rearrange("p a b -> p (a b)").bitcast(BF16).rearrange("p (a b) -> p a b", b=256)
    bsum = A.alloc([16], F32)
    lnxg = [A.alloc([128], F32) for _ in range(2)]
    lnxb = [A.alloc([128], F32) for _ in range(2)]
    sgw = A.alloc([256], F32)
    Pp = A.alloc([256], F32)
    csb = A.alloc([256], F32)
    Winv = A.alloc([256], F32)
    av = A.alloc([256], F32)
    tmpa = A.alloc([256], F32)
    tmpb = A.alloc([256], F32)
    BW = A.alloc([256], BF16)
    KW = A.alloc([256], BF16)
    Wb2 = [A.alloc([2, 130], F32) for _ in range(2)]
    Kt2 = [A.alloc([256], BF16) for _ in range(2)]
    Bt2 = [A.alloc([256], BF16) for _ in range(2)]
    ARb2 = [A.alloc([2, 256], BF16) for _ in range(2)]
    BKT2 = [A.alloc([2, 256], BF16) for _ in range(2)]
    ATb2 = [A.alloc([2, 128], BF16) for _ in range(2)]
    NM = [A.alloc([512], BF16) for _ in range(4)]
    lo = [0]

    def lalloc(n):
        a_ = lvl[:, lo[0]:lo[0] + n].bitcast(F32)
        lo[0] += n
        assert lo[0] <= LVL_WORDS
        return a_
    X0 = [lalloc(128) for _ in range(4)]
    X0T = [lalloc(128) for _ in range(4)]
    XX = [[lalloc(256) for _ in range(2)] for _ in range(4)]
    Zb = [[lalloc(128) for _ in range(2)] for _ in range(4)]
    Zh = [A.alloc([128], BF16) for _ in range(4)]
    GT = A.alloc([8, 64], BF16)
    Hs = A.alloc([8, 64], F32)
    Qb = A.alloc([8, 128], BF16)
    Sf = [A.alloc([64], F32) for _ in range(2)]
    Sb = [A.alloc([64], BF16) for _ in range(2)]
    s0raw = A.alloc([128], F32)
    stg = [A.alloc([128], F32) for _ in range(2)]
    gst = A.alloc([8, 16], F32)
    ysq = t1.rearrange("p (a b) -> p a b", b=128)
    ybon = raw.rearrange("p (a b) -> p a b", b=128)
    obR = A.alloc([8, 128], BF16)

    dma('pool', wL, w_in_v[:, :, 4096 + 3072:4096 + 3328], (), ['wL'])
    for q in range(2):
        memset('dve', Wb2[q][:, :, 0:1], 1.0, [('Wbpad0', q)])
        memset('dve', Wb2[q][:, :, 129:130], 1.0, [('Wbpad1', q)])

    T = 1024
    units = [(hTp, 'xph', 0, 'p'), (hTs, 'xsh', 1024, 's')][:max(1, min(2, NSEQ_R))]

    def project_fm(w_ap, wkeys, hT, hkey, dst, dkey):
        for n0 in range(0, T, 512):
            bnk = (n0 // 512) % 2
            hk = [(hkey, n0 // 128 + q) for q in range(4)]
            for kc in range(8):
                mm(bank(bnk), w_ap[:, kc, :], hT[:, kc, n0:n0 + 512], kc == 0, kc == 7, hk + wkeys, [('ps', bnk)])
            cp('act', dst[:, n0:n0 + 512], bank(bnk), [('ps', bnk)], [dkey])

    def shift(ci, dst, dkey, kind):
        ts('dve', dst[:, 0:T], raw[:, 0:T], c0v[:, ci:ci + 1], None, ALU.mult, None, ['raw', 'c0v'], [dkey])
        blocks = [(0, T)] if kind == 's' else [(q * 256, (q + 1) * 256) for q in range(4)]
        for (s_, e_) in blocks:
            stt('dve', dst[:, s_ + 1:e_], raw[:, s_:e_ - 1], mu[:, ci, 0:1], dst[:, s_ + 1:e_], ALU.mult, ALU.add,
                ['raw', 'mu', dkey], [dkey])
            stt('dve', dst[:, s_:e_ - 1], raw[:, s_ + 1:e_], mu[:, ci, 1:2], dst[:, s_:e_ - 1], ALU.mult, ALU.add,
                ['raw', 'mu', dkey], [dkey])

    for (hT, hkey, lc0, kind) in units:
        for c in range(2):
            project_fm(wL[:, :, c * 128:(c + 1) * 128], ['wL'], hT, hkey, raw, 'raw')
            shift(24 + c, t1, 't1', kind)
            if c == 0:
                act(twT[:, lc0:lc0 + T], t1[:, 0:T], AF.Tanh, ['t1'], [('twT', kind)])
            else:
                cp('act', laT[:, lc0:lc0 + T], t1[:, 0:T], ['t1'], [('laT', kind)])
    S.barrier()

    def prep_gen(hp, kind, lc0, d, seg, par):
        Wb, Kt, Bt, ARb, BKT, ATb = Wb2[par], Kt2[par], Bt2[par], ARb2[par], BKT2[par], ATb2[par]
        kT_, rT_ = rkv[:, 1, :], rkv[:, 0, :]
        c0_ = seg * 256
        lc = lc0 + c0_
        mm(bank(0, 0, 256), W2b[64 * d:64 * d + 64, hp * 128:(hp + 1) * 128], twT[64 * d:64 * d + 64, lc:lc + 256],
           True, True, ['W2b', ('twT', kind)], [('ps', 0)])
        act(sgw, bank(0, 0, 256), AF.Sigmoid, [('ps', 0), 'w0v'], ['sgw'], bias=w0v[:, hp * 2 + d:hp * 2 + d + 1])
        mm(bank(1, 0, 256), A2b[64 * d:64 * d + 64, hp * 128:(hp + 1) * 128], laT[64 * d:64 * d + 64, lc:lc + 256],
           True, True, ['A2b', ('laT', kind)], [('ps', 1)])
        act(av, bank(1, 0, 256), AF.Sigmoid, [('ps', 1), 'a0v'], ['av'], bias=a0v[:, hp * 2 + d:hp * 2 + d + 1])
        yield
        for t in range(2):
            S.op('dve', (lambda t=t: (lambda e: e.tensor_tensor_scan(
                out=Pp[:, t * 128:(t + 1) * 128], data0=onesf, data1=sgw[:, t * 128:(t + 1) * 128],
                initial=0.0, op0=ALU.mult, op1=ALU.add)))(), ['sgw', 'onesf'], [('Pp', t)])
        if d == 0:
            cs = Pp
            csk = [('Pp', 0), ('Pp', 1)]
        else:
            for t in range(2):
                stt('dve', csb[:, t * 128:(t + 1) * 128], sgw[:, t * 128:(t + 1) * 128],
                    Pp[:, t * 128 + 127:t * 128 + 128], Pp[:, t * 128:(t + 1) * 128], ALU.add, ALU.subtract,
                    ['sgw', ('Pp', t)], [('csb', t)])
            cs = csb
            csk = [('csb', 0), ('csb', 1)]
        yield
        act(Wb[:, :, 1:129], cs.rearrange("p (a b) -> p a b", b=128), AF.Exp, csk, [('Wb', par)], scale=-DC)
        act(Winv, cs, AF.Exp, csk, ['Winv'], scale=DC)
        ts('dve', tmpa, av, kav[:, hp:hp + 1], omka[:, hp:hp + 1], ALU.mult, ALU.add, ['av', 'kav', 'omka'], ['tmpa'])
        tt('pool', tmpb, kk[:, c0_:c0_ + 256], av, ALU.mult, ['kk', 'av'], ['tmpb'])
        yield
        tt('dve', tmpa, tmpa, kT_[:, c0_:c0_ + 256], ALU.mult, ['tmpa', ('rkv', 1)], ['tmpa'])
        tt('dve', Kt, tmpa, Winv, ALU.mult, ['tmpa', 'Winv'], [('Kt', par)])
        tt('dve', Bt, tmpb, Winv, ALU.mult, ['tmpb', 'Winv'], [('Bt', par)])
        yield
        Wprev = Wb[:, :, 0:128] if d == 0 else Wb[:, :, 2:130]
        stt('dve', ARb[:, :, 0:128], kk[:, c0_:c0_ + 256].rearrange("p (a b) -> p a b", b=128), -1.0, Wprev,
            ALU.mult, ALU.mult, ['kk', ('Wb', par), ('Wbpad0', par), ('Wbpad1', par)], [('ARb', par, 'a')])
        tt('dve', ARb[:, :, 128:256], rT_[:, c0_:c0_ + 256].rearrange("p (a b) -> p a b", b=128), Wb[:, :, 1:129],
           ALU.mult, [('rkv', 0), ('Wb', par)], [('ARb', par, 'r')])
        yield
        pst2 = bankb(2)
        for t in range(2):
            wc = Wb[:, t, 128:129] if d == 0 else Wb[:, t, 1:2]
            ts('dve', BW[:, t * 128:(t + 1) * 128], Bt[:, t * 128:(t + 1) * 128], wc, None, ALU.mult, None,
               [('Bt', par), ('Wb', par)], [('BW', t)])
            ts('dve', KW[:, t * 128:(t + 1) * 128], Kt[:, t * 128:(t + 1) * 128], wc, None, ALU.mult, None,
               [('Kt', par), ('Wb', par)], [('KW', t)])
            yield
        for t in range(2):
            tr(pst2[:, t * 384:t * 384 + 128], BW[:, t * 128:(t + 1) * 128], identb, [('BW', t), 'identb'], [('ps', 2)])
            tr(pst2[:, t * 384 + 128:t * 384 + 256], KW[:, t * 128:(t + 1) * 128], identb, [('KW', t), 'identb'], [('ps', 2)])
            tr(pst2[:, t * 384 + 256:t * 384 + 384], ARb[:, t, 0:128], identb, [('ARb', par, 'a'), 'identb'], [('ps', 2)])
        for t in range(2):
            cp('act', BKT[:, t, :], pst2[:, t * 384:t * 384 + 256], [('ps', 2)], [('BKT', par, t)])
            cp('act', ATb[:, t, :], pst2[:, t * 384 + 256:t * 384 + 384], [('ps', 2)], [('ATb', par, t)])
        yield

    def rest_gen(hp, kind, d, seg, par, state_in):
        Wb, Kt, Bt, ARb, BKT, ATb = Wb2[par], Kt2[par], Bt2[par], ARb2[par], BKT2[par], ATb2[par]
        gt0 = seg * 2
        P = []
        for t in range(2):
            for e in range(2):
                zi = t * 2 + e
                P.append(dict(t=t, e=e, zi=zi, si=zi, pb=64 * e, bM=4 + zi, gt=gt0 + t))
        for p in P:
            t, e, pb, bM, si = p['t'], p['e'], p['pb'], p['bM'], p['si']
            mm(bank(bM, 0, 256), Bt[pb:pb + 64, t * 128:(t + 1) * 128], ARb[pb:pb + 64, t, :], True, True,
               [('Bt', par), ('ARb', par, 'a'), ('ARb', par, 'r')], [('ps', bM)])
            mm(bank(bM, 256, 512), Kt[pb:pb + 64, t * 128:(t + 1) * 128], ARb[pb:pb + 64, t, :], True, True,
               [('Kt', par), ('ARb', par, 'a'), ('ARb', par, 'r')], [('ps', bM)])
        for p in P:
            bM, si = p['bM'], p['si']
            tt('dve', NM[si], bank(bM), maskNM[:, d, :], ALU.mult, [('ps', bM), 'maskNM'], [('NM', si)])
            tt('dve', X0[si].bitcast(LEVEL_DT), bank(bM, 0, 128), maskNM[:, d, 0:128], ALU.mult, [('ps', bM), 'maskNM'], [('X0', si)])
        yield
        for p in P:
            t, e, pb, bM, si, gt = p['t'], p['e'], p['pb'], p['bM'], p['si'], p['gt']
            mm(bank(bM, 0, 128), ARb[pb:pb + 64, t, 0:128], Bt[pb:pb + 64, t * 128:(t + 1) * 128], True, True,
               [('Bt', par), ('ARb', par, 'a')], [('ps', bM)])
            mm(bank(bM, 384, 448), NM[si][:, 256:384], VT[:, gt, e * 64:(e + 1) * 64], True, True,
               [('NM', si), 'VT'], [('ps', bM)])
        for p in P:
            t, e, bM, si, zi = p['t'], p['e'], p['bM'], p['si'], p['zi']
            tt('dve', X0T[si].bitcast(LEVEL_DT), bank(bM, 0, 128), maskT[:, d, :], ALU.mult, [('ps', bM), 'maskT'], [('X0T', si)])
            cp('act', Zb[zi][0].bitcast(LEVEL_DT)[:, 64:128], bank(bM, 384, 448), [('ps', bM)], [('Zb', zi, 0, 'u')])
            cp('pool', Zb[zi][0].bitcast(LEVEL_DT)[:, 0:64], ATb[:, t, e * 64:(e + 1) * 64], [('ATb', par, t)], [('Zb', zi, 0, 'a')])
        yield
        for j in range(7):
            for p in P:
                bM, si, zi = p['bM'], p['si'], p['zi']
                Xj = X0[si] if j == 0 else XX[si][j % 2][:, 0:128]
                XjT = X0T[si] if j == 0 else XX[si][j % 2][:, 128:256]
                xk = [('X0', si)] if j == 0 else [('XX', si, j % 2, 0)]
                xtk = [('X0T', si)] if j == 0 else [('XX', si, j % 2, 1)]
                zc = Zb[zi][j % 2]
                zck = [('Zb', zi, j % 2, 'a'), ('Zb', zi, j % 2, 'u')]
                Xr, XTr, zr = Xj.bitcast(LEVEL_DT), XjT.bitcast(LEVEL_DT), zc.bitcast(LEVEL_DT)
                mm(bank(bM, 0, 128), Xr, zr, True, True, xk + zck, [('ps', bM)])
                if j < 6:
                    mm(bank(bM, 128, 256), XTr, Xr, True, True, xk + xtk, [('ps', bM)])
            for p in P:
                bM, si, zi = p['bM'], p['si'], p['zi']
                zc = Zb[zi][j % 2]
                zn = Zb[zi][(j + 1) % 2]
                zck = [('Zb', zi, j % 2, 'a'), ('Zb', zi, j % 2, 'u')]
                znk = [('Zb', zi, (j + 1) % 2, 'a'), ('Zb', zi, (j + 1) % 2, 'u')]
                if j < 6:
                    cp('act', XX[si][(j + 1) % 2].bitcast(LEVEL_DT)[:, 0:128], bank(bM, 128, 256), [('ps', bM)],
                       [('XX', si, (j + 1) % 2, 0)])
                tt('dve', zn.bitcast(LEVEL_DT), bank(bM, 0, 128), zc, ALU.add, [('ps', bM)] + zck, znk)
            if j < 6:
                for p in P:
                    bM, si = p['bM'], p['si']
                    tr(bank(bM, 256, 384), XX[si][(j + 1) % 2][:, 0:128], identf, [('XX', si, (j + 1) % 2, 0), 'identf'], [('ps', bM)])
                for p in P:
                    bM, si = p['bM'], p['si']
                    cp('act', XX[si][(j + 1) % 2].bitcast(LEVEL_DT)[:, 128:256], bank(bM, 256, 384), [('ps', bM)],
                       [('XX', si, (j + 1) % 2, 1)])
            yield
        for p in P:
            si, zi = p['si'], p['zi']
            cp('pool', Zh[si], Zb[zi][1], [('Zb', zi, 1, 'a'), ('Zb', zi, 1, 'u')], [('Zh', si)])
        for p in P:
            t, e, pb, bM, si, gt = p['t'], p['e'], p['pb'], p['bM'], p['si'], p['gt']
            Z = Zh[si]
            zk = [('Zh', si)]
            mm(bank(bM, 448, 512)[pb:pb + 64, :], Z[:, 0:64], BKT[:, t, e * 64:(e + 1) * 64], True, True,
               zk + [('BKT', par, t)], [('ps', bM)])
            mm(bank(bM, 384, 448)[pb:pb + 64, :], BKT[:, t, e * 64:(e + 1) * 64], Z[:, 64:128], True, False,
               zk + [('BKT', par, t)], [('ps', bM)])
            mm(bank(bM, 384, 448)[pb:pb + 64, :], BKT[:, t, 128 + e * 64:128 + (e + 1) * 64],
               VT[:, gt, e * 64:(e + 1) * 64], False, True, ['VT', ('BKT', par, t)], [('ps', bM)])
            mm(bank(bM, 0, 128)[pb:pb + 64, :], Z[:, 0:64], NM[si][:, 128:256], True, True,
               zk + [('NM', si)], [('ps', bM)])
            mm(bank(bM, 128, 192), NM[si][:, 128:256], Z[:, 64:128], True, False, zk + [('NM', si)], [('ps', bM)])
            mm(bank(bM, 128, 192), NM[si][:, 384:512], VT[:, gt, e * 64:(e + 1) * 64], False, True,
               ['VT', ('NM', si)], [('ps', bM)])
        yield
        for p in P:
            t, e, pb, bM, si, gt = p['t'], p['e'], p['pb'], p['bM'], p['si'], p['gt']
            wc = Wb[pb:pb + 64, t, 128:129] if d == 0 else Wb[pb:pb + 64, t, 1:2]
            stt('dve', GT[pb:pb + 64, gt, :], identf[pb:pb + 64, pb:pb + 64], wc, bank(bM, 448, 512)[pb:pb + 64, :],
                ALU.mult, ALU.add, [('ps', bM), 'identf', ('Wb', par)], [('GT', gt, e)])
            tt('dve', Qb[pb:pb + 64, gt, :], bank(bM, 0, 128)[pb:pb + 64, :], ARb[pb:pb + 64, t, 128:256], ALU.add,
               [('ps', bM), ('ARb', par, 'r')], [('Qb', gt, e)])
            cp('act', Hs[pb:pb + 64, gt, :], bank(bM, 384, 448)[pb:pb + 64, :], [('ps', bM)], [('Hs', gt, e)])
            if d == 0:
                cp('act', yacc[:, gt, e * 64:(e + 1) * 64], bank(bM, 128, 192), [('ps', bM)], [('yacc', gt, e)])
            else:
                tt('dve', yacc[:, gt, e * 64:(e + 1) * 64], bank(bM, 128, 192), yacc[:, gt, e * 64:(e + 1) * 64],
                   ALU.add, [('ps', bM), ('yacc', gt, e)], [('yacc', gt, e)])
        yield
        have_state = state_in
        order = [0, 1] if d == 0 else [1, 0]
        for t in order:
            gt = gt0 + t
            for e in range(2):
                pb = 64 * e
                if have_state:
                    mm(bank(3, e * 64, e * 64 + 64), Qb[pb:pb + 64, gt, :], Sb[d][pb:pb + 64, :], True, True,
                       [('Qb', gt, e), ('Sb', d, e)], [('ps', 3)])
                    mm(bank(3, 128 + e * 64, 192 + e * 64)[pb:pb + 64, :], GT[pb:pb + 64, gt, :], Sb[d][pb:pb + 64, :],
                       True, True, [('GT', gt, e), ('Sb', d, e)], [('ps', 3)])
            for e in range(2):
                pb = 64 * e
                if have_state:
                    tt('dve', yacc[:, gt, e * 64:(e + 1) * 64], bank(3, e * 64, e * 64 + 64),
                       yacc[:, gt, e * 64:(e + 1) * 64], ALU.add, [('ps', 3), ('yacc', gt, e)], [('yacc', gt, e)])
                    tt('dve', Sf[d][pb:pb + 64, :], bank(3, 128 + e * 64, 192 + e * 64)[pb:pb + 64, :], Hs[pb:pb + 64, gt, :],
                       ALU.add, [('ps', 3), ('Hs', gt, e)], [('Sf', d, e)])
                else:
                    cp('dve', Sf[d][pb:pb + 64, :], Hs[pb:pb + 64, gt, :], [('Hs', gt, e)], [('Sf', d, e)])
                cp('act', Sb[d][pb:pb + 64, :], Sf[d][pb:pb + 64, :], [('Sf', d, e)], [('Sb', d, e)])
            have_state = True
            yield
        if kind == 'p':
            q = (seg + d) % 2
            tr(bank(3, 256, 384)[0:64, :], Sf[d], identf, [('Sf', d, 0), ('Sf', d, 1), 'identf'], [('ps', 3)])
            cp('act', stg[q][0:64, :], bank(3, 256, 384)[0:64, :], [('ps', 3)], [('stg', q)])
            dma('sp', O['ns'][seg, d, 2 * hp:2 * hp + 2, :, :].rearrange("e v k -> v e k"),
                stg[q][0:64, :].rearrange("p (e k) -> p e k", e=2), [('stg', q)], (), is_out=True)
            yield

    def drive(a, b):
        gens = [g for g in (a, b) if g is not None]
        while gens:
            for g in list(gens):
                try:
                    next(g)
                except StopIteration:
                    gens.remove(g)

    jobno = [0]
    for hp in range(NHP):
        wr = wR[hp % 2]
        for a_, base in enumerate([4096, 4096 + 1024, 4096 + 2048, 4096 + 3328]):
            dma('pool', wr[:, :, a_ * 128:(a_ + 1) * 128], w_in_v[:, :, base + hp * 128: base + (hp + 1) * 128], (),
                [('wR', hp % 2, a_)])
        dma('sp', lnxg[hp % 2], I['lnxg'][0:1, hp * 128:(hp + 1) * 128].partition_broadcast(128), (), [('lnxg', hp % 2)])
        dma('sp', lnxb[hp % 2], I['lnxb'][0:1, hp * 128:(hp + 1) * 128].partition_broadcast(128), (), [('lnxb', hp % 2)])
        for (hT, hkey, lc0, kind) in units:
            nt = 8
            for a_ in range(3):
                project_fm(wr[:, :, a_ * 128:(a_ + 1) * 128], [('wR', hp % 2, a_)], hT, hkey, raw, 'raw')
                shift(a_ * 8 + hp, rkv[:, a_, :], ('rkv', a_), kind)
            rT_, kT_, vT_ = rkv[:, 0, :], rkv[:, 1, :], rkv[:, 2, :]
            for t in range(nt):
                bnk = 2 + (t // 4) % 2
                for kc in range(8):
                    mm(bank(bnk, (t % 4) * 128, (t % 4 + 1) * 128), hT[:, kc, t * 128:(t + 1) * 128],
                       wr[:, kc, 384:512], kc == 0, kc == 7, [(hkey, t), ('wR', hp % 2, 3)], [('ps', bnk)])
                if t % 4 == 3:
                    act(sgb[:, t - 3:t + 1, :], bank(bnk).rearrange("p (a b) -> p a b", b=128), AF.Silu, [('ps', bnk)],
                        [('sgb', q) for q in range(t - 3, t + 1)])
            ts('dve', t1[:, 0:T], kT_[:, 0:T], kkv[:, hp:hp + 1], None, ALU.mult, None, [('rkv', 1), 'kkv'], ['t1'])
            act(sqb[:, 0:T], t1[:, 0:T], AF.Square, ['t1'], ['prodb'])
            for n0 in range(0, T, 512):
                bnk = (n0 // 512) % 2
                mm(bank(bnk), bonesb, sqb[:, n0:n0 + 512], True, True, ['bonesb', 'prodb'], [('ps', bnk)])
                act(kk[:, n0:n0 + 512], bank(bnk), AF.Sqrt, [('ps', bnk), 'cst'], ['kk'], bias=cst[:, 0:1])
            recip(kk[:, 0:T], kk[:, 0:T], ['kk'], ['kk'])
            tt('dve', kk[:, 0:T], kk[:, 0:T], t1[:, 0:T], ALU.mult, ['kk', 't1'], ['kk'])
            stt('dve', prodb[:, 0:T], rT_[:, 0:T], rkv_[:, hp:hp + 1], kT_[:, 0:T], ALU.mult, ALU.mult,
                [('rkv', 0), ('rkv', 1), 'rkv_'], ['prodb'])
            for t in range(nt):
                mm(bank(1, t * 2, t * 2 + 2), prodb[:, t * 128:(t + 1) * 128], hindb, True, True, ['prodb', 'hindb'], [('ps', 1)])
            cp('act', bsum[:, 0:nt * 2], bank(1, 0, nt * 2), [('ps', 1)], ['bsum'])
            cp('act', vb16[:, 0:T], vT_[:, 0:T], [('rkv', 2)], ['vb16'])
            pst2 = bankb(2)
            for t in range(nt):
                tr(pst2[:, t * 128:(t + 1) * 128], vb16[:, t * 128:(t + 1) * 128], identb, ['vb16', 'identb'], [('ps', 2)])
            cp('dve', VT[:, 0:nt, :], pst2[:, 0:nt * 128].rearrange("p (a b) -> p a b", b=128), [('ps', 2)], ['VT'])
            jobs = []
            for d in range(ND):
                segs = list(range(4)) if d == 0 else list(range(3, -1, -1))
                for i_, seg in enumerate(segs):
                    st_in = (kind == 's')
                    jobs.append((d, seg, st_in, (kind == 's' and i_ == 0)))
            pars = []
            for _ in jobs:
                pars.append(jobno[0] % 2)
                jobno[0] += 1
            pg = prep_gen(hp, kind, lc0, jobs[0][0], jobs[0][1], pars[0])
            drive(pg, None)
            for n, (d, seg, st_in, load_s0) in enumerate(jobs):
                if load_s0:
                    dma('sp', s0raw[0:64, :].rearrange("p (e k) -> p e k", e=2),
                        I['s0'][d, 2 * hp:2 * hp + 2, :, :].rearrange("e v k -> v e k"), (), ['s0raw'])
                    tr(bank(3, 0, 64), s0raw[0:64, :], identf[0:64, 0:64], ['s0raw', 'identf'], [('ps', 3)])
                    for e in range(2):
                        pb = 64 * e
                        cp('dve', Sf[d][pb:pb + 64, :], bank(3, 0, 64)[pb:pb + 64, :], [('ps', 3)], [('Sf', d, e)])
                        cp('act', Sb[d][pb:pb + 64, :], bank(3, 0, 64)[pb:pb + 64, :], [('ps', 3)], [('Sb', d, e)])
                rg = rest_gen(hp, kind, d, seg, pars[n], st_in)
                ng = None
                if n + 1 < len(jobs):
                    ng = prep_gen(hp, kind, lc0, jobs[n + 1][0], jobs[n + 1][1], pars[n + 1])
                drive(rg, ng)
            n2 = nt * 2
            yk = [('yacc', t, e) for t in range(nt) for e in range(2)]
            yv = yacc[:, 0:nt, :].rearrange("p a (e f) -> p (a e) f", e=2)
            red(gst[:, 0, 0:n2], yv, ALU.add, yk, [('gst', 0)])
            act(ysq[:, 0:nt, :], yacc[:, 0:nt, :], AF.Square, yk, ['t1'])
            red(gst[:, 1, 0:n2], ysq[:, 0:nt, :].rearrange("p a (e f) -> p (a e) f", e=2), ALU.add, ['t1'], [('gst', 1)])
            ts('dve', gst[:, 2, 0:n2], gst[:, 0, 0:n2], 1.0 / 64, None, ALU.mult, None, [('gst', 0)], [('gst', 2)])
            tt('dve', gst[:, 3, 0:n2], gst[:, 2, 0:n2], gst[:, 2, 0:n2], ALU.mult, [('gst', 2)], [('gst', 3)])
            stt('dve', gst[:, 4, 0:n2], gst[:, 1, 0:n2], 1.0 / 64, gst[:, 3, 0:n2], ALU.mult, ALU.subtract,
                [('gst', 1), ('gst', 3)], [('gst', 4)])
            ts('dve', gst[:, 4, 0:n2], gst[:, 4, 0:n2], 64e-5, None, ALU.add, None, [('gst', 4)], [('gst', 4)])
            act(gst[:, 5, 0:n2], gst[:, 4, 0:n2], AF.Sqrt, [('gst', 4)], [('gst', 5)])
            recip(gst[:, 6, 0:n2], gst[:, 5, 0:n2], [('gst', 5)], [('gst', 6)])
            ysv = ysq[:, 0:nt, :].rearrange("p a (e f) -> p (a e) f", e=2)
            tt('dve', ysv, yv, gst[:, 2, 0:n2].unsqueeze(2).to_broadcast([128, n2, 64]), ALU.subtract, yk + [('gst', 2)], ['t1'])
            tt('dve', ysv, ysv, gst[:, 6, 0:n2].unsqueeze(2).to_broadcast([128, n2, 64]), ALU.mult, ['t1', ('gst', 6)], ['t1'])
            tt('dve', ysq[:, 0:nt, :], ysq[:, 0:nt, :], lnxg[hp % 2].unsqueeze(1).to_broadcast([128, nt, 128]),
               ALU.mult, ['t1', ('lnxg', hp % 2)], ['t1'])
            tt('dve', ysq[:, 0:nt, :], ysq[:, 0:nt, :], lnxb[hp % 2].unsqueeze(1).to_broadcast([128, nt, 128]),
               ALU.add, ['t1', ('lnxb', hp % 2)], ['t1'])
            tt('dve', ybon[:, 0:nt, :].rearrange("p a (e f) -> p (a e) f", e=2),
               VT[:, 0:nt, :].rearrange("p a (e f) -> p (a e) f", e=2),
               bsum[:, 0:n2].unsqueeze(2).to_broadcast([128, n2, 64]), ALU.mult, ['VT', 'bsum'], ['raw'])
            tt('dve', ysq[:, 0:nt, :], ysq[:, 0:nt, :], ybon[:, 0:nt, :], ALU.add, ['t1', 'raw'], ['t1'])
            tt('dve', obR[:, 0:nt, :], ysq[:, 0:nt, :], sgb[:, 0:nt, :], ALU.mult, ['t1'] + [('sgb', t) for t in range(nt)], ['obR'])
            if kind == 'p':
                pst2 = bankb(2)
                for t in range(nt):
                    tr(pst2[:, t * 128:(t + 1) * 128], obR[:, t, :], identb, ['obR', 'identb'], [('ps', 2)])
                cp('act', mixT[:, 8 + hp, 0:1024], pst2[:, 0:1024], [('ps', 2)], [('mixT', 8 + hp, g) for g in range(8)])
            else:
                for t in range(8):
                    mm(bank(0, 0, 256), obR[:, t, :], selb[:, t, :], t == 0, t == 7, ['obR', 'selb'], [('ps', 0)])
                cp('act', mixT[:, 8 + hp, 1024:1280], bank(0, 0, 256), [('ps', 0)], [('mixT', 8 + hp, 8), ('mixT', 8 + hp, 9)])
    if DEBUG:
        dump_all(dict(yacc=yacc, Sf0=Sf[0], GT=GT, Hs=Hs))
    if STOP == 'R':
        return
    S.barrier()
    A.off = mA

    wout = A.alloc([16, 1024], BF16)
    fgbc = A.alloc([1024], F32)
    gatebc = A.alloc([2, 1024], F32)
    bgbc = A.alloc([1024], F32)
    scbc = A.alloc([8, 2, 128], BF16)
    wadg = A.alloc([8, 1024], BF16)
    xr = [A.alloc([1024], F32) for _ in range(2)]
    yv_ = [A.alloc([1024], F32) for _ in range(2)]
    ojunk = A.alloc([1024], BF16)
    ost = [A.alloc([4], F32) for _ in range(2)]
    wout_v = I['w_out'].rearrange("(c p) n -> p c n", p=128)
    wada_v = I['w_ada'].rearrange("(kc p) n -> p kc n", p=128)
    for n in range(2):
        dma('pool', wadg[:, :, n * 512:(n + 1) * 512], wada_v[:, :, 2048 + n * 512:2048 + (n + 1) * 512], (), [('wadg', n)])
    for c4 in range(4):
        dma('pool', wout[:, c4 * 4:(c4 + 1) * 4, :], wout_v[:, c4 * 4:(c4 + 1) * 4, :], (), [('wout', c4)])
    dma('sp', fgbc, I['fg'][0:1, :].partition_broadcast(128), (), ['fgbc'])
    dma('sp', bgbc, I['bgate'][0:1, :].partition_broadcast(128), (), ['bgbc'])
    cp('dve', scbc, scp.unsqueeze(3).to_broadcast([128, 8, 2, 128]), ['scp'], ['scbc'])
    for v in range(2):
        for n in range(2):
            for kc in range(8):
                mm(bank(2 + n), scbc[:, kc, v, :], wadg[:, kc, n * 512:(n + 1) * 512],
                   kc == 0, kc == 7, ['scbc', ('wadg', n)], [('ps', 2 + n)])
            tt('dve', gatebc[:, v, n * 512:(n + 1) * 512], bank(2 + n), bgbc[:, n * 512:(n + 1) * 512], ALU.add,
               [('ps', 2 + n), 'bgbc'], [('gatebc', v, n)])
    otiles = [('xp', g, 'yp', 0) for g in range(8)] + [('xo', g, 'ys', 1) for g in range(2)]
    for ti, (src, g, dst, v) in enumerate(otiles):
        b = ti % 2
        mg = g if src == 'xp' else 8 + g
        dma('sp', xr[b], I[src][g * 128:(g + 1) * 128, :], (), [('xr', b)])
        for n in range(2):
            for c in range(16):
                mm(bank(n), mixT[:, c, mg * 128:(mg + 1) * 128], wout[:, c, n * 512:(n + 1) * 512], c == 0, c == 15,
                   [('mixT', c, mg), ('wout', c // 4)], [('ps', n)])
            tt('dve', yv_[b][:, n * 512:(n + 1) * 512], bank(n), gatebc[:, v, n * 512:(n + 1) * 512], ALU.mult,
               [('ps', n), ('gatebc', v, n)], [('yv', b, n)])
            tt('pool', yv_[b][:, n * 512:(n + 1) * 512], yv_[b][:, n * 512:(n + 1) * 512], xr[b][:, n * 512:(n + 1) * 512], ALU.add,
               [('yv', b, n), ('xr', b)], [('yv', b, n)])
        act(ojunk, yv_[b], AF.Square, [('yv', b, 0), ('yv', b, 1)], ['ojunk', ('ost', b)], accum=ost[b][:, 0:1])
        ts('dve', ost[b][:, 1:2], ost[b][:, 0:1], 1.0 / 1024, 1e-6, ALU.mult, ALU.add, [('ost', b)], [('ostb', b)])
        act(ost[b][:, 2:3], ost[b][:, 1:2], AF.Sqrt, [('ostb', b)], [('ostc', b)])
        recip(ost[b][:, 3:4], ost[b][:, 2:3], [('ostc', b)], [('ostd', b)])
        stt('dve', xr[b], yv_[b], ost[b][:, 3:4], fgbc, ALU.mult, ALU.mult, [('yv', b, 0), ('yv', b, 1), ('ostd', b), 'fgbc', ('xr', b)], [('xr', b)])
        dma('sp', O[dst][g * 128:(g + 1) * 128, :], xr[b], [('xr', b)], (), is_out=True)


_NC = None


def _rope_tab(pos_rows, pos_cols):
    n_freq = 16
    inv = (10000.0 ** (-np.arange(n_freq, dtype=np.float32) / n_freq)).astype(np.float32)
    T = len(pos_rows)
    tab = np.zeros((T, 256), np.float32)
    for s_, pos in enumerate([pos_rows, pos_cols]):
        ang = pos.astype(np.float32)[:, None] * inv[None, :]
        c, sn = np.cos(ang).astype(np.float32), np.sin(ang).astype(np.float32)
        for m in range(2):
            for hf in range(2):
                o = m * 64 + s_ * 32 + hf * 16
                tab[:, o:o + 16] = c
                tab[:, 128 + o:128 + o + 16] = -sn if hf == 0 else sn
    return tab


def kernel(x_prompt, x_sample, cache_k, cache_v, state_rwkv, c, c_ctx, norm_g, w_ada, b_ada,
           w_in, lam_q1, lam_k1, lam_q2, lam_k2, subln_g, shift_mu, decay_w0, decay_w2,
           iclr_a0, iclr_a2, k_k, k_a, r_k, lnx_g, lnx_b, w_out, final_g):
    global _NC
    f = lambda a: np.ascontiguousarray(np.asarray(a, dtype=np.float32))
    x_prompt, x_sample, cache_k, cache_v, state_rwkv = map(f, (x_prompt, x_sample, cache_k, cache_v, state_rwkv))
    c, c_ctx = f(c), f(c_ctx)
    if _NC is None:
        _NC = build()
    nc = _NC

    def fm(v, nch):
        return np.ascontiguousarray(f(v).reshape(nch, 128).T)

    i = np.arange(128)
    su = (i[:, None] < i[None, :]).astype(np.float32)
    ui = (i[:, None] <= i[None, :]).astype(np.float32)
    sl = (i[:, None] > i[None, :]).astype(np.float32)
    li = (i[:, None] >= i[None, :]).astype(np.float32)
    maskNM = np.stack([np.concatenate([su, ui, su, ui], 1), np.concatenate([sl, li, sl, li], 1)])
    maskT = np.stack([sl, su])
    bones = np.kron(np.eye(2, dtype=np.float32), np.ones((64, 64), np.float32))
    hind = np.kron(np.eye(2, dtype=np.float32), np.ones((64, 1), np.float32))
    tok = np.arange(1024)
    ropeall = _rope_tab(tok // 64, tok % 64)
    shared = {
        "w_in": f(w_in)[0], "w_ada": f(w_ada)[0], "w_out": f(w_out)[0],
        "bada_fm": fm(f(b_ada)[0], 24), "bgate": f(b_ada)[0:1, 2048:3072], "normg_fm": fm(f(norm_g)[0], 8),
        "fg": f(final_g)[None, :],
        "lamv": np.concatenate([f(lam_q1)[0], f(lam_k1)[0], f(lam_q2)[0], f(lam_k2)[0]])[None, :],
        "sublng": f(subln_g)[0:1],
        "mu_fm": np.ascontiguousarray(np.stack([fm(f(shift_mu)[0, 0], 26), fm(f(shift_mu)[0, 1], 26)], -1).reshape(128, 52)),
        "w0_fm": np.ascontiguousarray(np.stack([fm(f(decay_w0)[0, 0], 8), fm(f(decay_w0)[0, 1], 8)], -1).reshape(128, 16)),
        "a0_fm": np.ascontiguousarray(np.stack([fm(f(iclr_a0)[0, 0], 8), fm(f(iclr_a0)[0, 1], 8)], -1).reshape(128, 16)),
        "w2": f(decay_w2)[0].reshape(128, 1024), "a2": f(iclr_a2)[0].reshape(128, 1024),
        "kk_fm": fm(f(k_k)[0], 8), "ka_fm": fm(f(k_a)[0], 8), "rk_fm": fm(f(r_k)[0].reshape(-1), 8),
        "lnxg": f(lnx_g)[0:1], "lnxb": f(lnx_b)[0:1],
        "ident": np.eye(128, dtype=np.float32), "maskNM": maskNM, "maskT": maskT, "bones": bones, "hind": hind,
        "ropeall": ropeall,
    }
    in_maps = []
    for core in range(8):
        b, q = core // 4, core % 4
        sel = np.zeros((1024, 256), np.float32)
        sel[q * 256 + np.arange(256), np.arange(256)] = 1.0
        cT = np.stack([fm(c_ctx, 8), fm(c[b], 8)], -1).reshape(128, 16)
        m = dict(shared)
        m.update({
            "xp": x_prompt[core * 4:(core + 1) * 4].reshape(1024, 1024),
            "xs": x_sample[b], "xo": x_sample[b, q * 256:(q + 1) * 256],
            "ck": cache_k[b, 0].reshape(512, 1024), "cv": cache_v[b, 0].reshape(512, 1024),
            "s0": state_rwkv[b, 0], "cT": np.ascontiguousarray(cT), "selT": sel,
            "ropeown": np.ascontiguousarray(ropeall[q * 256:(q + 1) * 256]),
        })
        in_maps.append({k: np.ascontiguousarray(v, dtype=np.float32) for k, v in m.items()})
    res = run_bass_kernel_spmd(nc, in_maps, core_ids=list(range(8)))
    R = res.results
    y_prompt = np.concatenate([R[i]["yp"].reshape(4, 256, 1024) for i in range(8)], 0)
    y_sample = np.stack([np.concatenate([R[b * 4 + q]["ys"] for q in range(4)], 0) for b in range(2)], 0)
    new_k = np.concatenate([R[i]["nk"].reshape(4, 1, 256, 8, 2, 64) for i in range(8)], 0)
    new_v = np.concatenate([R[i]["nv"].reshape(4, 1, 256, 8, 128) for i in range(8)], 0)
    new_s = np.concatenate([R[i]["ns"].reshape(4, 1, 2, 16, 64, 64) for i in range(8)], 0)
    return (y_prompt.astype(np.float32), y_sample.astype(np.float32), new_k.astype(np.float32),
            new_v.astype(np.float32), new_s.astype(np.float32))
```

```python
import math
from contextlib import ExitStack
import numpy as np
import concourse.bass as bass
import concourse.mybir as mybir
from concourse.bass_utils import run_bass_kernel_spmd

F32 = mybir.dt.float32
F32R = mybir.dt.float32r
LEVEL_DT = F32
BF16 = mybir.dt.bfloat16
AF = mybir.ActivationFunctionType
ALU = mybir.AluOpType
AX = mybir.AxisListType

ENG = ['pe', 'dve', 'act', 'pool', 'sp']
NDMA = 40
DC = math.exp(-0.5)


class Sched:
    def __init__(s, nc, stack):
        s.nc = nc
        s.ops = {e: [] for e in ENG}
        s.cnt = {e: 0 for e in ENG}
        s.known = {e: {f: 0 for f in ENG} for e in ENG}
        s.snap = {e: [] for e in ENG}
        s.kdma = {e: {} for e in ENG}
        s.lastw = {}
        s.rd_e = {}
        s.rd_d = {}
        s.sems = {e: stack.enter_context(nc.semaphore('c_' + e)) for e in ENG}
        s.dsems = [stack.enter_context(nc.semaphore('d%d' % i)) for i in range(2 * NDMA)]
        s.dcnt = {'sp': 0, 'pool': 0, 'act': 0}
        s.dcount = 0
        s.dlast = {}
        s.out_events = []

    def op(s, eng, fn, r=(), w=(), dma=False, is_out=False, noinc=False):
        r = [(k[0], k[1]) if (isinstance(k, tuple) and k[0] == 'ps') else k for k in r]
        w = [(k[0], k[1]) if (isinstance(k, tuple) and k[0] == 'ps') else k for k in w]
        psr = [k for k in r if isinstance(k, tuple) and k[0] == 'ps']
        if psr:
            r = [k for k in r if not (isinstance(k, tuple) and k[0] == 'ps')]
            w = list(w) + [k for k in psr if k not in w]
        deps = []
        for k in r:
            ev = s.lastw.get(k)
            if ev is not None:
                deps.append((ev, True))
        for k in w:
            ev = s.lastw.get(k)
            if ev is not None:
                deps.append((ev, False))
            for f, c in s.rd_e.get(k, {}).items():
                deps.append((('E', f, c), True))
            for ev in s.rd_d.get(k, ()):
                deps.append((ev, False))
        waits = {}
        for ev, raw in deps:
            if ev[0] == 'E':
                _, f, c = ev
                if f == eng and not dma:
                    if (not raw) or eng == 'pe':
                        continue
                if s.known[eng][f] >= c:
                    continue
                waits[('E', f)] = max(waits.get(('E', f), 0), c)
            else:
                _, si, v = ev
                if s.kdma[eng].get(si, 0) >= v:
                    continue
                waits[('D', si)] = max(waits.get(('D', si), 0), v)
        if dma:
            qn = s.dcnt[eng]
            si = qn % NDMA + (NDMA if eng == 'pool' else 0)
            v = 16 * (qn // NDMA + 1)
            if qn >= NDMA and s.kdma[eng].get(si, 0) < v - 16:
                waits[('D', si)] = max(waits.get(('D', si), 0), v - 16)
            s.dcnt[eng] += 1
            s.dcount += 1
            s.dlast[si] = v
            ev = ('D', si, v)
        for (t, x), v in waits.items():
            if t == 'E':
                kn = s.known[eng]
                if kn[x] < v:
                    kn[x] = v
                sn = s.snap[x][v - 1]
                for f2, c2 in sn.items():
                    if kn[f2] < c2:
                        kn[f2] = c2
            else:
                s.kdma[eng][x] = v
        if not dma:
            if noinc:
                ev = ('E', eng, s.cnt[eng] + 1)
            else:
                s.cnt[eng] += 1
                ev = ('E', eng, s.cnt[eng])
                s.snap[eng].append(dict(s.known[eng]))
        s.ops[eng].append((list(waits.items()), fn, None if noinc else ev))
        for k in r:
            if ev[0] == 'E':
                s.rd_e.setdefault(k, {})[eng] = ev[2]
            else:
                s.rd_d.setdefault(k, []).append(ev)
        for k in w:
            s.lastw[k] = ev
            s.rd_e[k] = {}
            s.rd_d[k] = []
        if is_out:
            s.out_events.append(ev)
        return ev

    def barrier(s):
        waits = {}
        for f in ENG:
            if f != 'sp' and s.cnt[f] > 0:
                waits[('E', f)] = s.cnt[f]
        for si, v in s.dlast.items():
            waits[('D', si)] = v
        s.cnt['sp'] += 1
        c = s.cnt['sp']
        for f in ENG:
            if f != 'sp':
                s.known['sp'][f] = s.cnt[f]
        s.kdma['sp'] = dict(s.dlast)
        s.snap['sp'].append(dict(s.known['sp']))
        s.ops['sp'].append((list(waits.items()), (lambda e: e.nop()), ('E', 'sp', c)))
        for f in ENG:
            if f == 'sp':
                continue
            s.ops[f].append(([(('E', 'sp'), c)], None, None))
            for g in ENG:
                if g != f:
                    s.known[f][g] = max(s.known[f][g], s.cnt[g])
            s.kdma[f] = dict(s.dlast)
        s.lastw.clear()
        s.rd_e.clear()
        s.rd_d.clear()

    def finish(s):
        waits = {}
        for ev in s.out_events:
            waits[('D', ev[1])] = max(waits.get(('D', ev[1]), 0), ev[2])
        s.ops['sp'].append((list(waits.items()), None, None))

    def emit(s, block):
        def mk(engname):
            def body(e):
                for waits, fn, ev in s.ops[engname]:
                    for (t, x), v in waits:
                        sem = s.sems[x] if t == 'E' else s.dsems[x]
                        e.wait_ge(sem, v)
                    if fn is None:
                        continue
                    ins = fn(e)
                    if ev is None:
                        continue
                    if ev[0] == 'E':
                        ins.then_inc(s.sems[engname], 1)
                    else:
                        ins.then_inc(s.dsems[ev[1]], 16)
            return body
        block.tensor(mk('pe'))
        block.vector(mk('dve'))
        block.scalar(mk('act'))
        block.gpsimd(mk('pool'))
        block.sync(mk('sp'))


class Arena:
    def __init__(s, ar, nwords):
        s.ar = ar
        s.off = 0
        s.n = nwords
        s.peak = 0

    def alloc(s, shape, dt):
        n = 1
        for x in shape:
            n *= x
        words = n if dt == F32 else (n + 1) // 2
        words = (words + 7) // 8 * 8
        a = s.ar[:, s.off:s.off + words]
        s.off += words
        s.peak = max(s.peak, s.off)
        assert s.off <= s.n, ("arena overflow", s.off, s.n)
        if dt == BF16:
            a = a.bitcast(BF16)
        a = a[:, 0:n]
        if len(shape) == 2:
            a = a.rearrange("p (a b) -> p a b", b=shape[1])
        elif len(shape) == 3:
            a = a.rearrange("p (a b c) -> p a b c", b=shape[1], c=shape[2])
        elif len(shape) == 4:
            a = a.rearrange("p (a b c d) -> p a b c d", b=shape[1], c=shape[2], d=shape[3])
        return a


IN_SPECS = [
    ("xp", [1024, 1024]), ("xs", [1024, 1024]), ("xo", [256, 1024]),
    ("ck", [512, 1024]), ("cv", [512, 1024]), ("s0", [2, 16, 64, 64]),
    ("cT", [128, 16]), ("w_in", [1024, 8448]), ("w_ada", [1024, 3072]), ("w_out", [2048, 1024]),
    ("bada_fm", [128, 24]), ("bgate", [1, 1024]), ("normg_fm", [128, 8]), ("fg", [1, 1024]),
    ("lamv", [1, 256]), ("sublng", [1, 128]), ("mu_fm", [128, 52]), ("w0_fm", [128, 16]),
    ("a0_fm", [128, 16]), ("w2", [128, 1024]), ("a2", [128, 1024]), ("kk_fm", [128, 8]),
    ("ka_fm", [128, 8]), ("rk_fm", [128, 8]), ("lnxg", [1, 1024]), ("lnxb", [1, 1024]),
    ("ident", [128, 128]), ("maskNM", [2, 128, 512]), ("maskT", [2, 128, 128]),
    ("bones", [128, 128]), ("hind", [128, 2]), ("selT", [1024, 256]),
    ("ropeall", [1024, 256]), ("ropeown", [256, 256]),
]
OUT_SPECS = [
    ("yp", [1024, 1024]), ("ys", [256, 1024]), ("nk", [1024, 1024]), ("nv", [1024, 1024]),
    ("ns", [4, 2, 16, 64, 64]),
]

ARENA_WORDS = 52480 - 4096
LVL_WORDS = 4096
NHEAD_A = 8
NHP = 8
NSEQ_R = 5
ND = 2
DEBUG = False
A_MODE = 'all'
DUMPS = []
STOP = None


def build():
    nc = bass.Bass("TRN2", target_bir_lowering=False)
    I = {n: nc.dram_tensor(n, sh, F32, kind="ExternalInput").ap() for n, sh in IN_SPECS}
    O = {n: nc.dram_tensor(n, sh, F32, kind="ExternalOutput").ap() for n, sh in OUT_SPECS}
    with ExitStack() as stack:
        ar = stack.enter_context(nc.sbuf_tensor("arena", [128, ARENA_WORDS], F32))
        PS = stack.enter_context(nc.psum_tensor("ps", [128, 4096], F32))
        lvl = stack.enter_context(nc.sbuf_tensor("lvl", [128, LVL_WORDS], LEVEL_DT))
        S = Sched(nc, stack)
        A = Arena(ar, ARENA_WORDS)
        block = stack.enter_context(nc.Block())
        _program(nc, S, A, PS, I, O, lvl)
        S.finish()
        S.emit(block)
    return nc


def _program(nc, S, A, PS, I, O, lvl):
    def dma(eng, out, in_, r, w, is_out=False):
        S.op(eng, lambda e: e.dma_start(out=out, in_=in_), r, w, dma=True, is_out=is_out)

    def mm(out, lhsT, rhs, start, stop, r, w):
        S.op('pe', lambda e: e.matmul(out, lhsT, rhs, start=start, stop=stop), r, w, noinc=(not stop))

    def tr(out, in_, ident, r, w):
        S.op('pe', lambda e: e.transpose(out, in_, ident), r, w)

    def act(out, in_, func, r, w, bias=None, scale=None, accum=None):
        def f(e):
            kw = {}
            if bias is not None:
                kw['bias'] = bias
            if scale is not None:
                kw['scale'] = scale
            if accum is not None:
                kw['accum_out'] = accum
            return e.activation(out=out, in_=in_, func=func, **kw)
        S.op('act', f, r, w)

    def tt(eng, out, in0, in1, op, r, w):
        S.op(eng, lambda e: e.tensor_tensor(out=out, in0=in0, in1=in1, op=op), r, w)

    def ts(eng, out, in0, s1, s2, op0, op1, r, w):
        if s2 is None:
            S.op(eng, lambda e: e.tensor_scalar(out=out, in0=in0, scalar1=s1, scalar2=None, op0=op0), r, w)
        else:
            S.op(eng, lambda e: e.tensor_scalar(out=out, in0=in0, scalar1=s1, scalar2=s2, op0=op0, op1=op1), r, w)

    def stt(eng, out, in0, sc, in1, op0, op1, r, w):
        S.op(eng, lambda e: e.scalar_tensor_tensor(out=out, in0=in0, scalar=sc, in1=in1, op0=op0, op1=op1), r, w)

    def cp(eng, out, in_, r, w):
        if eng == 'act':
            act(out, in_, AF.Identity, r, w)
        else:
            S.op(eng, lambda e: e.tensor_copy(out=out, in_=in_), r, w)

    def red(out, in_, op, r, w):
        S.op('dve', lambda e: e.tensor_reduce(out=out, in_=in_, axis=AX.X, op=op), r, w)

    def recip(out, in_, r, w):
        S.op('dve', lambda e: e.reciprocal(out=out, in_=in_), r, w)

    def memset(eng, out, val, w):
        S.op(eng, lambda e: e.memset(out, val), (), w)

    def bank(b, c0=0, c1=512):
        return PS[:, b * 512 + c0: b * 512 + c1]

    def bankb(b):
        return PS[:, b * 512:(b + 1) * 512].bitcast(BF16)

    w_in_v = I['w_in'].rearrange("(kc p) n -> p kc n", p=128)

    def dump_all(bufs):
        S.barrier()
        for name, ap in bufs.items():
            sh = list(ap.shape)
            dt = nc.dram_tensor('dbg_' + name, sh, ap.dtype, kind="ExternalOutput").ap()
            DUMPS.append('dbg_' + name)
            dma('sp', dt, ap, (), (), is_out=True)

    identf = A.alloc([128], F32)
    identb = A.alloc([128], BF16)
    onesf = A.alloc([128], F32)
    maskNM = A.alloc([2, 512], BF16)
    maskT = A.alloc([2, 128], BF16)
    bonesb = A.alloc([128], BF16)
    hindb = A.alloc([2], BF16)
    selb = A.alloc([8, 256], BF16)
    cst = A.alloc([4], F32)
    hTp = A.alloc([8, 1024], BF16)
    hTs = A.alloc([8, 1024], BF16)
    hTo = A.alloc([8, 256], BF16)
    mixT = A.alloc([16, 1280], BF16)
    modfm = A.alloc([24, 2], F32)
    scale1 = A.alloc([8, 2], F32)
    neglam = A.alloc([1], F32)
    sgl = A.alloc([128], F32)
    mu = A.alloc([26, 2], F32)
    c0v = A.alloc([26], F32)
    w0v = A.alloc([16], F32)
    a0v = A.alloc([16], F32)
    kkv = A.alloc([8], F32)
    kav = A.alloc([8], F32)
    omka = A.alloc([8], F32)
    rkv_ = A.alloc([8], F32)
    W2b = A.alloc([1024], BF16)
    A2b = A.alloc([1024], BF16)
    twT = A.alloc([2048], BF16)
    laT = A.alloc([2048], BF16)

    dma('sp', identf, I['ident'], (), ['identf'])
    dma('pool', identb, I['ident'], (), ['identb'])
    dma('pool', maskNM, I['maskNM'].rearrange("d p n -> p d n"), (), ['maskNM'])
    dma('pool', maskT, I['maskT'].rearrange("d p n -> p d n"), (), ['maskT'])
    dma('pool', bonesb, I['bones'], (), ['bonesb'])
    dma('pool', hindb, I['hind'], (), ['hindb'])
    dma('pool', selb, I['selT'].rearrange("(j p) n -> p j n", p=128), (), ['selb'])
    dma('sp', mu, I['mu_fm'].rearrange("p (c j) -> p c j", j=2), (), ['mu'])
    dma('sp', w0v, I['w0_fm'], (), ['w0v'])
    dma('sp', a0v, I['a0_fm'], (), ['a0v'])
    dma('sp', kkv, I['kk_fm'], (), ['kkv'])
    dma('sp', kav, I['ka_fm'], (), ['kav'])
    dma('sp', rkv_, I['rk_fm'], (), ['rkv_'])
    dma('pool', W2b, I['w2'], (), ['W2b'])
    dma('pool', A2b, I['a2'], (), ['A2b'])
    memset('dve', onesf, 1.0, ['onesf'])
    memset('dve', cst[:, 0:1], 1e-12, ['cst'])
    tt('dve', c0v, mu[:, :, 0], mu[:, :, 1], ALU.add, ['mu'], ['c0v'])
    ts('dve', c0v, c0v, -1.0, 1.0, ALU.mult, ALU.add, ['c0v'], ['c0v'])
    ts('dve', omka, kav, -1.0, 1.0, ALU.mult, ALU.add, ['kav'], ['omka'])

    scp = A.alloc([8, 2], BF16)
    m0 = A.off
    cT = A.alloc([16], F32)
    sc = A.alloc([8, 2], F32)
    wadaf = [A.alloc([8, 512], F32) for _ in range(4)]
    bada = A.alloc([24], F32)
    normg = A.alloc([8], F32)
    lamt = A.alloc([4, 64], F32)
    lamp = A.alloc([2, 64], F32)
    lams = A.alloc([4], F32)

    dma('sp', cT, I['cT'], (), ['cT'])
    dma('sp', bada, I['bada_fm'], (), ['bada'])
    dma('sp', normg, I['normg_fm'], (), ['normg'])
    dma('sp', lamt.rearrange("p a b -> p (a b)"), I['lamv'][0:1, :].partition_broadcast(128), (), ['lamt'])
    dma('sp', sgl, I['sublng'][0:1, :].partition_broadcast(128), (), ['sgl'])
    wada_v = I['w_ada'].rearrange("(kc p) n -> p kc n", p=128)
    for n in range(4):
        dma('sp', wadaf[n], wada_v[:, :, n * 512:(n + 1) * 512], (), [('wada', n)])
    act(sc, cT.rearrange("p (c v) -> p c v", v=2), AF.Silu, ['cT'], ['sc'])
    cp('dve', scp, sc, ['sc'], ['scp'])
    for fc in range(16):
        for kc in range(8):
            mm(bank(0, fc * 2, fc * 2 + 2), wadaf[fc // 4][:, kc, (fc % 4) * 128:(fc % 4 + 1) * 128], sc[:, kc, :],
               kc == 0, kc == 7, ['sc', ('wada', fc // 4)], [('ps', 0)])
    tt('dve', modfm[:, 0:16, :], bank(0, 0, 32).rearrange("p (a b) -> p a b", b=2),
       bada[:, 0:16].unsqueeze(2).to_broadcast([128, 16, 2]), ALU.add, [('ps', 0), 'bada'], ['modfm'])
    ts('dve', scale1, modfm[:, 8:16, :], 1.0, None, ALU.add, None, ['modfm'], ['scale1'])
    tt('dve', scale1, scale1, normg.unsqueeze(2).to_broadcast([128, 8, 2]), ALU.mult, ['scale1', 'normg'], ['scale1'])
    tt('dve', lamp[:, 0, :], lamt[:, 0, :], lamt[:, 1, :], ALU.mult, ['lamt'], ['lamp'])
    tt('dve', lamp[:, 1, :], lamt[:, 2, :], lamt[:, 3, :], ALU.mult, ['lamt', 'lamp'], ['lamp'])
    red(lams[:, 0:2], lamp, ALU.add, ['lamp'], ['lams'])
    act(lams[:, 2:4], lams[:, 0:2], AF.Exp, ['lams'], ['lams2'])
    lam_init = 0.8 - 0.6 * math.exp(-0.3 * 0)
    tt('dve', neglam, lams[:, 3:4], lams[:, 2:3], ALU.subtract, ['lams2'], ['neglam'])
    ts('dve', neglam, neglam, -lam_init, None, ALU.add, None, ['neglam'], ['neglam'])
    ts('dve', sgl, sgl, 1.0 - lam_init, None, ALU.mult, None, ['sgl'], ['sgl'])

    if STOP == '0':
        return
    xt = [A.alloc([1024], F32) for _ in range(2)]
    xn = [A.alloc([1024], BF16) for _ in range(2)]
    junk = A.alloc([1024], BF16)
    st1 = [A.alloc([4], F32) for _ in range(2)]
    tiles = [('xp', g, hTp, g, 0) for g in range(8)] + [('xs', g, hTs, g, 1) for g in range(8)] + \
            [('xo', g, hTo, g, 1) for g in range(2)]
    for ti, (src, g, hT, tg, v) in enumerate(tiles):
        b = ti % 2
        dma('sp', xt[b], I[src][g * 128:(g + 1) * 128, :], (), [('xt', b)])
        act(junk, xt[b], AF.Square, [('xt', b)], ['junk', ('st1', b)], accum=st1[b][:, 0:1])
        ts('dve', st1[b][:, 1:2], st1[b][:, 0:1], 1.0 / 1024, 1e-6, ALU.mult, ALU.add, [('st1', b)], [('st1b', b)])
        act(st1[b][:, 2:3], st1[b][:, 1:2], AF.Sqrt, [('st1b', b)], [('st1c', b)])
        recip(st1[b][:, 3:4], st1[b][:, 2:3], [('st1c', b)], [('st1d', b)])
        ts('dve', xn[b], xt[b], st1[b][:, 3:4], None, ALU.mult, None, [('xt', b), ('st1d', b)], [('xn', b)])
        pb_ = bankb(3 + b)
        for kc in range(8):
            tr(pb_[:, kc * 128:(kc + 1) * 128], xn[b][:, kc * 128:(kc + 1) * 128], identb,
               [('xn', b), 'identb'], [('ps', 3 + b)])
        for kc in range(8):
            act(hT[:, kc, tg * 128:(tg + 1) * 128], pb_[:, kc * 128:(kc + 1) * 128], AF.Identity,
                [('ps', 3 + b), 'scale1', 'modfm'], [(src + 'h', tg)],
                bias=modfm[:, kc, v:v + 1], scale=scale1[:, kc, v:v + 1])
    if STOP == '1':
        return
    S.barrier()
    A.off = m0
    if STOP == '1b':
        return

    mA = A.off
    wA = [A.alloc([8, 512], BF16) for _ in range(2)]
    qkb = [A.alloc([256], BF16) for _ in range(2)]
    kvf = [A.alloc([256], F32) for _ in range(2)]
    qT2 = [A.alloc([256], BF16) for _ in range(2)]
    sg2 = [A.alloc([2, 128], F32) for _ in range(2)]
    kTp = [A.alloc([256], BF16) for _ in range(2)]
    vbp = [A.alloc([2, 128], BF16) for _ in range(2)]
    kTs = A.alloc([1536], BF16)
    vbs = A.alloc([12, 128], BF16)
    ckb = A.alloc([4, 128], BF16)
    ropa = A.alloc([8, 256], F32)
    ropo = A.alloc([2, 256], F32)
    xf = [A.alloc([128], F32) for _ in range(2)]
    rt1 = [A.alloc([128], F32) for _ in range(2)]
    rt2 = [A.alloc([128], F32) for _ in range(2)]
    xb16 = [A.alloc([128], BF16) for _ in range(2)]
    pbuf = [A.alloc([1536], BF16) for _ in range(2)]
    pT = [A.alloc([1536], BF16) for _ in range(2)]
    pbufp = [[A.alloc([256], BF16) for _ in range(2)] for _ in range(2)]
    pTp = [[A.alloc([256], BF16) for _ in range(2)] for _ in range(2)]
    ast = [A.alloc([16], F32) for _ in range(4)]
    of = [A.alloc([128], F32) for _ in range(4)]
    o1 = [A.alloc([128], F32) for _ in range(4)]
    on = [A.alloc([128], F32) for _ in range(4)]
    ob = [A.alloc([128], BF16) for _ in range(4)]
    ajunk = A.alloc([128], BF16)

    dma('sp', ropa, I['ropeall'].rearrange("(j p) n -> p j n", p=128), (), ['ropa'])
    dma('sp', ropo, I['ropeown'].rearrange("(j p) n -> p j n", p=128), (), ['ropo'])

    ctr = {'x': 0}

    def rope(src_ps, tab, dst16, rkeys, wkey):
        i = ctr['x'] % 2
        ctr['x'] += 1
        cp('act', xf[i], src_ps, rkeys, [('xf', i)])
        tt('dve', rt1[i], xf[i], tab[:, 0:128], ALU.mult, [('xf', i), 'ropa', 'ropo'], [('rt1', i)])
        xv = xf[i].rearrange("p (g h f) -> p g h f", h=2, f=16)
        sv = tab[:, 128:256].rearrange("p (g h f) -> p g h f", h=2, f=16)
        r2 = rt2[i].rearrange("p (g h f) -> p g h f", h=2, f=16)
        tt('pool', r2[:, :, 0, :], xv[:, :, 1, :], sv[:, :, 0, :], ALU.mult, [('xf', i), 'ropa', 'ropo'], [('rt2', i, 0)])
        tt('pool', r2[:, :, 1, :], xv[:, :, 0, :], sv[:, :, 1, :], ALU.mult, [('xf', i), 'ropa', 'ropo'], [('rt2', i, 1)])
        tt('dve', dst16, rt1[i], rt2[i], ALU.add, [('rt1', i), ('rt2', i, 0), ('rt2', i, 1)], [wkey])

    def drive2(gens):
        gens = [g for g in gens if g is not None]
        while gens:
            for g in list(gens):
                try:
                    next(g)
                except StopIteration:
                    gens.remove(g)

    def attn_j(h, par, j, kind, mixcol0):
        ai = par * 2 + j
        if kind == 'p':
            ntk, kT_, vb_, kkey, vkey = 2, kTp[par], vbp[par], ('kTp', par), ('vbp', par)
            sb0, pb0, ob_ = 2 + j, 4 + j, 6 + j
            pbs, pTs = pbufp[j], pTp[j]
            pkey = ('pp', j)
        else:
            ntk, kT_, vb_, kkey, vkey = 12, kTs, vbs, 'kTs', 'vbs'
            sb0, pb0, ob_ = 2, 5, 7
            pbs, pTs = pbuf, pT
            pkey = ('ps_', 0)
        Tk = ntk * 128
        qT_ = qT2[par]
        s0c = sb0 * 512
        p0c = pb0 * 512
        for m in range(2):
            for n0 in range(0, Tk, 512):
                w_ = min(512, Tk - n0)
                mm(PS[:, s0c + n0:s0c + n0 + w_], qT_[64 * m:64 * m + 64, j * 128:(j + 1) * 128],
                   kT_[64 * m:64 * m + 64, n0:n0 + w_], True, True,
                   [('qT', par, j), kkey], [('ps', sb0 + n0 // 512)])
            sck = [('ps', sb0 + b_) for b_ in range((Tk + 511) // 512)]
            red(ast[ai][:, m:m + 1], PS[:, s0c:s0c + Tk], ALU.max, sck, [('ast', ai, 'mx', m)])
            ts('dve', ast[ai][:, 2 + m:3 + m], ast[ai][:, m:m + 1], -0.125, None, ALU.mult, None,
               [('ast', ai, 'mx', m)], [('ast', ai, 'nb', m)])
            act(pbs[m][:, 0:Tk], PS[:, s0c:s0c + Tk], AF.Exp, sck + [('ast', ai, 'nb', m)],
                [('pbuf', pkey, m), ('ast', ai, 'sum', m)], bias=ast[ai][:, 2 + m:3 + m], scale=0.125,
                accum=ast[ai][:, 4 + m:5 + m])
            yield
            ptv = PS[:, p0c:p0c + 1024].bitcast(BF16)
            for t in range(ntk):
                tr(ptv[:, t * 128:(t + 1) * 128], pbs[m][:, t * 128:(t + 1) * 128], identb,
                   [('pbuf', pkey, m), 'identb'], [('ps', pb0 + t // 8)])
            if ntk <= 2:
                cp('dve', pTs[m][:, 0:Tk], ptv[:, 0:Tk], [('ps', pb0)], [('pT', pkey, m, 0)])
                ptk = [('pT', pkey, m, 0)]
            else:
                cp('dve', pTs[m][:, 0:768], ptv[:, 0:768], [('ps', pb0)], [('pT', pkey, m, 0)])
                cp('act', pTs[m][:, 768:1536], ptv[:, 768:1536], [('ps', pb0), ('ps', pb0 + 1)], [('pT', pkey, m, 1)])
                ptk = [('pT', pkey, m, 0), ('pT', pkey, m, 1)]
            for t in range(ntk):
                mm(bank(ob_, m * 128, (m + 1) * 128), pTs[m][:, t * 128:(t + 1) * 128], vb_[:, t, :],
                   t == 0, t == ntk - 1, ptk + [vkey], [('ps', ob_)])
            yield
        a_ = ast[ai]
        recip(a_[:, 6:8], a_[:, 4:6], [('ast', ai, 'sum', 0), ('ast', ai, 'sum', 1)], [('ast', ai, 'rs')])
        tt('dve', a_[:, 8:9], a_[:, 7:8], neglam, ALU.mult, [('ast', ai, 'rs'), 'neglam'], [('ast', ai, 'c2')])
        act(o1[ai], bank(ob_, 0, 128), AF.Identity, [('ps', ob_), ('ast', ai, 'rs')], [('o1', ai)], scale=a_[:, 6:7])
        stt('dve', of[ai], bank(ob_, 128, 256), a_[:, 8:9], o1[ai], ALU.mult, ALU.add,
            [('ps', ob_), ('ast', ai, 'c2'), ('o1', ai)], [('of', ai)])
        yield
        act(ajunk, of[ai], AF.Square, [('of', ai)], ['ajunk', ('ast', ai, 'ss')], accum=a_[:, 9:10])
        ts('dve', a_[:, 10:11], a_[:, 9:10], 1.0 / 128, 1e-5, ALU.mult, ALU.add, [('ast', ai, 'ss')], [('ast', ai, 'ms')])
        act(a_[:, 11:12], a_[:, 10:11], AF.Sqrt, [('ast', ai, 'ms')], [('ast', ai, 'sd')])
        recip(a_[:, 12:13], a_[:, 11:12], [('ast', ai, 'sd')], [('ast', ai, 'rstd')])
        yield
        stt('dve', on[ai], of[ai], a_[:, 12:13], sgl, ALU.mult, ALU.mult, [('of', ai), ('ast', ai, 'rstd'), 'sgl'], [('on', ai)])
        tt('pool', ob[ai], on[ai], sg2[par][:, j, :], ALU.mult, [('on', ai), ('sg', par, j)], [('ob', ai)])
        pso = bankb(ob_)
        tr(pso[:, 512:640], ob[ai], identb, [('ob', ai), 'identb'], [('ps', ob_)])
        cp('act', mixT[:, h, mixcol0 + j * 128: mixcol0 + (j + 1) * 128], pso[:, 512:640], [('ps', ob_)],
           [('mixT', h, (mixcol0 // 128) + j)])
        yield

    def attn_gen(h, par, kind, mixcol0):
        if kind == 'p':
            g0, g1 = attn_j(h, par, 0, kind, mixcol0), attn_j(h, par, 1, kind, mixcol0)
            gens = [g0, g1]
            while gens:
                for g in list(gens):
                    try:
                        next(g)
                    except StopIteration:
                        gens.remove(g)
                yield
        else:
            for j in range(2):
                for _ in attn_j(h, par, j, kind, mixcol0):
                    yield

    def proj_gen(h, par, kind, s_):
        wb = wA[h % 2]
        wk = [('wA', h % 2, j4) for j4 in range(4)]
        pst = bankb(1)
        if kind == 'p':
            for j in range(2):
                g = s_ * 2 + j
                bi = g % 2
                for kc in range(8):
                    mm(bank(0), hTp[:, kc, g * 128:(g + 1) * 128], wb[:, kc, :], kc == 0, kc == 7,
                       [('xph', g)] + wk, [('ps', 0)])
                cp('dve', qkb[bi], bank(0, 0, 256), [('ps', 0)], [('qkb', bi)])
                cp('act', kvf[bi], bank(0, 128, 384), [('ps', 0)], [('kvf', bi)])
                dma('sp', O['nk'][g * 128:(g + 1) * 128, h * 128:(h + 1) * 128], kvf[bi][:, 0:128], [('kvf', bi)], (), is_out=True)
                dma('sp', O['nv'][g * 128:(g + 1) * 128, h * 128:(h + 1) * 128], kvf[bi][:, 128:256], [('kvf', bi)], (), is_out=True)
                cp('dve', vbp[par][:, j, :], bank(0, 256, 384), [('ps', 0)], [('vbp', par)])
                act(sg2[par][:, j, :], bank(0, 384, 512), AF.Silu, [('ps', 0)], [('sg', par, j)])
                yield
                tr(pst[:, 0:128], qkb[bi][:, 0:128], identb, [('qkb', bi), 'identb'], [('ps', 1)])
                tr(pst[:, 128:256], qkb[bi][:, 128:256], identb, [('qkb', bi), 'identb'], [('ps', 1)])
                cp('act', qT2[par][:, j * 128:(j + 1) * 128], pst[:, 0:128], [('ps', 1)], [('qT', par, j)])
                cp('act', kTp[par][:, j * 128:(j + 1) * 128], pst[:, 128:256], [('ps', 1)], [('kTp', par)])
                yield
        else:
            dma('pool', ckb, I['ck'].rearrange("(j p) c -> p j c", p=128)[:, :, h * 128:(h + 1) * 128], (), ['ckb'])
            dma('pool', vbs[:, 0:4, :], I['cv'].rearrange("(j p) c -> p j c", p=128)[:, :, h * 128:(h + 1) * 128], (), ['vbs'])
            for t in range(4):
                tr(pst[:, 512 + t * 128:512 + (t + 1) * 128], ckb[:, t, :], identb, ['ckb', 'identb'], [('ps', 1)])
            cp('dve', kTs[:, 0:512], pst[:, 512:1024], [('ps', 1)], ['kTs'])
            yield
            for j in range(8):
                for kc in range(8):
                    mm(bank(0, 0, 256), hTs[:, kc, j * 128:(j + 1) * 128], wb[:, kc, 128:384], kc == 0, kc == 7,
                       [('xsh', j)] + wk, [('ps', 0)])
                bi = j % 2
                cp('dve', vbs[:, 4 + j, :], bank(0, 128, 256), [('ps', 0)], ['vbs'])
                rope(bank(0, 0, 128), ropa[:, j, :], xb16[bi], [('ps', 0)], ('xb16', bi))
                yield
                tr(pst[:, 0:128], xb16[bi], identb, [('xb16', bi), 'identb'], [('ps', 1)])
                cp('act', kTs[:, 512 + j * 128:512 + (j + 1) * 128], pst[:, 0:128], [('ps', 1)], ['kTs'])
                yield
            for j in range(2):
                for kc in range(8):
                    mm(bank(0), hTo[:, kc, j * 128:(j + 1) * 128], wb[:, kc, :], kc == 0, kc == 7,
                       [('xoh', j)] + wk, [('ps', 0)])
                bi = j % 2
                act(sg2[par][:, j, :], bank(0, 384, 512), AF.Silu, [('ps', 0)], [('sg', par, j)])
                rope(bank(0, 0, 128), ropo[:, j, :], xb16[bi], [('ps', 0)], ('xb16', bi))
                yield
                tr(pst[:, 128:256], xb16[bi], identb, [('xb16', bi), 'identb'], [('ps', 1)])
                cp('act', qT2[par][:, j * 128:(j + 1) * 128], pst[:, 128:256], [('ps', 1)], [('qT', par, j)])
                yield

    ajobs = []
    for h in range(NHEAD_A):
        for s_ in range(4):
            ajobs.append((h, 'p', s_))
        ajobs.append((h, 's', 0))
    loaded = set()

    def load_w(h):
        if h in loaded or h >= NHEAD_A:
            return
        loaded.add(h)
        for j4, base in enumerate([0, 1024, 2048, 3072]):
            dma('pool', wA[h % 2][:, :, j4 * 128:(j4 + 1) * 128], w_in_v[:, :, base + h * 128: base + (h + 1) * 128], (),
                [('wA', h % 2, j4)])
    if ajobs:
        load_w(0)
        drive2([proj_gen(ajobs[0][0], 0, ajobs[0][1], ajobs[0][2])])
        for n, (h, kind, s_) in enumerate(ajobs):
            par = n % 2
            ag = attn_gen(h, par, kind, s_ * 256 if kind == 'p' else 1024)
            pg = None
            if n + 1 < len(ajobs):
                h2, kind2, s2 = ajobs[n + 1]
                load_w(h2)
                pg = proj_gen(h2, (n + 1) % 2, kind2, s2)
            drive2([ag, pg])
    if STOP == 'A':
        return
    S.barrier()
    A.off = mA

    wR = [A.alloc([8, 512], BF16) for _ in range(2)]
    rkv = A.alloc([3, 1024], F32)
    kk = A.alloc([1024], F32)
    t1 = A.alloc([1024], F32)
    prodb = A.alloc([1024], BF16)
    sqb = prodb
    vb16 = A.alloc([1024], BF16)
    VT = A.alloc([8, 128], BF16)
    sgb = A.alloc([8, 128], F32)
    yacc = A.alloc([8, 128], F32)
    wL = yacc.rearrange("p a b -> p (a b)").bitcast(BF16).rearrange("p (a b) -> p a b", b=256)
    bsum = A.alloc([16], F32)
    lnxg = [A.alloc([128], F32) for _ in range(2)]
    lnxb = [A.alloc([128], F32) for _ in range(2)]
    sgw = A.alloc([256], F32)
    Pp = A.alloc([256], F32)
    csb = A.alloc([256], F32)
    Winv = A.alloc([256], F32)
    av = A.alloc([256], F32)
    tmpa = A.alloc([256], F32)
    tmpb = A.alloc([256], F32)
    BW = A.alloc([256], BF16)
    KW = A.alloc([256], BF16)
    Wb2 = [A.alloc([2, 130], F32) for _ in range(2)]
    Kt2 = [A.alloc([256], BF16) for _ in range(2)]
    Bt2 = [A.alloc([256], BF16) for _ in range(2)]
    ARb2 = [A.alloc([2, 256], BF16) for _ in range(2)]
    BKT2 = [A.alloc([2, 256], BF16) for _ in range(2)]
    ATb2 = [A.alloc([2, 128], BF16) for _ in range(2)]
    NM = [A.alloc([512], BF16) for _ in range(4)]
    lo = [0]

    def lalloc(n):
        a_ = lvl[:, lo[0]:lo[0] + n].bitcast(F32)
        lo[0] += n
        assert lo[0] <= LVL_WORDS
        return a_
    X0 = [lalloc(128) for _ in range(4)]
    X0T = [lalloc(128) for _ in range(4)]
    XX = [[lalloc(256) for _ in range(2)] for _ in range(4)]
    Zb = [[lalloc(128) for _ in range(2)] for _ in range(4)]
    Zh = [A.alloc([128], BF16) for _ in range(4)]
    GT = A.alloc([8, 64], BF16)
    Hs = A.alloc([8, 64], F32)
    Qb = A.alloc([8, 128], BF16)
    Sf = [A.alloc([64], F32) for _ in range(2)]
    Sb = [A.alloc([64], BF16) for _ in range(2)]
    s0raw = A.alloc([128], F32)
    stg = [A.alloc([128], F32) for _ in range(2)]
    gst = A.alloc([8, 16], F32)
    ysq = t1.rearrange("p (a b) -> p a b", b=128)
    ybon = kk.rearrange("p (a b) -> p a b", b=128)
    obR = A.alloc([8, 128], BF16)

    dma('pool', wL, w_in_v[:, :, 4096 + 3072:4096 + 3328], (), ['wL'])
    for q in range(2):
        memset('dve', Wb2[q][:, :, 0:1], 1.0, [('Wbpad0', q)])
        memset('dve', Wb2[q][:, :, 129:130], 1.0, [('Wbpad1', q)])

    T = 1024
    units = [(hTp, 'xph', 0, 'p'), (hTs, 'xsh', 1024, 's')][:max(1, min(2, NSEQ_R))]

    def proj_shift(w_ap, wkeys, hT, hkey, ci, dst, dkey, kind, b0):
        for n0 in range(0, T, 512):
            bnk = b0 + n0 // 512
            hk = [(hkey, n0 // 128 + q) for q in range(4)]
            for kc in range(8):
                mm(bank(bnk), w_ap[:, kc, :], hT[:, kc, n0:n0 + 512], kc == 0, kc == 7, hk + wkeys, [('ps', bnk)])
        for n0 in range(0, T, 512):
            bnk = b0 + n0 // 512
            act(dst[:, n0:n0 + 512], bank(bnk), AF.Identity, [('ps', bnk), 'c0v'], [dkey], scale=c0v[:, ci:ci + 1])
        psv = PS[:, b0 * 512:b0 * 512 + 1024]
        pk = [('ps', b0), ('ps', b0 + 1)]
        blocks = [(0, T)] if kind == 's' else [(q * 256, (q + 1) * 256) for q in range(4)]
        for (s_, e_) in blocks:
            kk_ = [('ps', b0 + s_ // 512)] if (s_ // 512 == (e_ - 1) // 512) else pk
            stt('dve', dst[:, s_ + 1:e_], psv[:, s_:e_ - 1], mu[:, ci, 0:1], dst[:, s_ + 1:e_], ALU.mult, ALU.add,
                kk_ + ['mu', dkey], [dkey])
            stt('dve', dst[:, s_:e_ - 1], psv[:, s_ + 1:e_], mu[:, ci, 1:2], dst[:, s_:e_ - 1], ALU.mult, ALU.add,
                kk_ + ['mu', dkey], [dkey])

    for (hT, hkey, lc0, kind) in units:
        for c in range(2):
            proj_shift(wL[:, :, c * 128:(c + 1) * 128], ['wL'], hT, hkey, 24 + c, t1, 't1', kind, 2 * c)
            if c == 0:
                act(twT[:, lc0:lc0 + T], t1[:, 0:T], AF.Tanh, ['t1'], [('twT', kind)])
            else:
                cp('act', laT[:, lc0:lc0 + T], t1[:, 0:T], ['t1'], [('laT', kind)])
    S.barrier()

    def prep_gen(hp, kind, lc0, d, seg, par):
        Wb, Kt, Bt, ARb, BKT, ATb = Wb2[par], Kt2[par], Bt2[par], ARb2[par], BKT2[par], ATb2[par]
        kT_, rT_ = rkv[:, 1, :], rkv[:, 0, :]
        c0_ = seg * 256
        lc = lc0 + c0_
        mm(bank(0, 0, 256), W2b[64 * d:64 * d + 64, hp * 128:(hp + 1) * 128], twT[64 * d:64 * d + 64, lc:lc + 256],
           True, True, ['W2b', ('twT', kind)], [('ps', 0)])
        act(sgw, bank(0, 0, 256), AF.Sigmoid, [('ps', 0), 'w0v'], ['sgw'], bias=w0v[:, hp * 2 + d:hp * 2 + d + 1])
        mm(bank(1, 0, 256), A2b[64 * d:64 * d + 64, hp * 128:(hp + 1) * 128], laT[64 * d:64 * d + 64, lc:lc + 256],
           True, True, ['A2b', ('laT', kind)], [('ps', 1)])
        act(av, bank(1, 0, 256), AF.Sigmoid, [('ps', 1), 'a0v'], ['av'], bias=a0v[:, hp * 2 + d:hp * 2 + d + 1])
        yield
        for t in range(2):
            S.op('dve', (lambda t=t: (lambda e: e.tensor_tensor_scan(
                out=Pp[:, t * 128:(t + 1) * 128], data0=onesf, data1=sgw[:, t * 128:(t + 1) * 128],
                initial=0.0, op0=ALU.mult, op1=ALU.add)))(), ['sgw', 'onesf'], [('Pp', t)])
        if d == 0:
            cs = Pp
            csk = [('Pp', 0), ('Pp', 1)]
        else:
            for t in range(2):
                stt('dve', csb[:, t * 128:(t + 1) * 128], sgw[:, t * 128:(t + 1) * 128],
                    Pp[:, t * 128 + 127:t * 128 + 128], Pp[:, t * 128:(t + 1) * 128], ALU.add, ALU.subtract,
                    ['sgw', ('Pp', t)], [('csb', t)])
            cs = csb
            csk = [('csb', 0), ('csb', 1)]
        yield
        act(Wb[:, :, 1:129], cs.rearrange("p (a b) -> p a b", b=128), AF.Exp, csk, [('Wb', par)], scale=-DC)
        act(Winv, cs, AF.Exp, csk, ['Winv'], scale=DC)
        ts('dve', tmpa, av, kav[:, hp:hp + 1], omka[:, hp:hp + 1], ALU.mult, ALU.add, ['av', 'kav', 'omka'], ['tmpa'])
        tt('pool', tmpb, kk[:, c0_:c0_ + 256], av, ALU.mult, ['kk', 'av'], ['tmpb'])
        yield
        tt('dve', tmpa, tmpa, kT_[:, c0_:c0_ + 256], ALU.mult, ['tmpa', ('rkv', 1)], ['tmpa'])
        tt('dve', Kt, tmpa, Winv, ALU.mult, ['tmpa', 'Winv'], [('Kt', par)])
        tt('dve', Bt, tmpb, Winv, ALU.mult, ['tmpb', 'Winv'], [('Bt', par)])
        yield
        Wprev = Wb[:, :, 0:128] if d == 0 else Wb[:, :, 2:130]
        stt('dve', ARb[:, :, 0:128], kk[:, c0_:c0_ + 256].rearrange("p (a b) -> p a b", b=128), -1.0, Wprev,
            ALU.mult, ALU.mult, ['kk', ('Wb', par), ('Wbpad0', par), ('Wbpad1', par)], [('ARb', par, 'a')])
        tt('dve', ARb[:, :, 128:256], rT_[:, c0_:c0_ + 256].rearrange("p (a b) -> p a b", b=128), Wb[:, :, 1:129],
           ALU.mult, [('rkv', 0), ('Wb', par)], [('ARb', par, 'r')])
        yield
        pst2 = bankb(2)
        for t in range(2):
            wc = Wb[:, t, 128:129] if d == 0 else Wb[:, t, 1:2]
            ts('dve', BW[:, t * 128:(t + 1) * 128], Bt[:, t * 128:(t + 1) * 128], wc, None, ALU.mult, None,
               [('Bt', par), ('Wb', par)], [('BW', t)])
            ts('dve', KW[:, t * 128:(t + 1) * 128], Kt[:, t * 128:(t + 1) * 128], wc, None, ALU.mult, None,
               [('Kt', par), ('Wb', par)], [('KW', t)])
            yield
        for t in range(2):
            tr(pst2[:, t * 384:t * 384 + 128], BW[:, t * 128:(t + 1) * 128], identb, [('BW', t), 'identb'], [('ps', 2)])
            tr(pst2[:, t * 384 + 128:t * 384 + 256], KW[:, t * 128:(t + 1) * 128], identb, [('KW', t), 'identb'], [('ps', 2)])
            tr(pst2[:, t * 384 + 256:t * 384 + 384], ARb[:, t, 0:128], identb, [('ARb', par, 'a'), 'identb'], [('ps', 2)])
        for t in range(2):
            cp('act', BKT[:, t, :], pst2[:, t * 384:t * 384 + 256], [('ps', 2)], [('BKT', par, t)])
            cp('act', ATb[:, t, :], pst2[:, t * 384 + 256:t * 384 + 384], [('ps', 2)], [('ATb', par, t)])
        yield

    def rest_gen(hp, kind, d, seg, par, state_in):
        Wb, Kt, Bt, ARb, BKT, ATb = Wb2[par], Kt2[par], Bt2[par], ARb2[par], BKT2[par], ATb2[par]
        gt0 = seg * 2
        P = []
        for t in range(2):
            for e in range(2):
                zi = t * 2 + e
                P.append(dict(t=t, e=e, zi=zi, si=zi, pb=64 * e, bM=4 + zi, gt=gt0 + t))
        for p in P:
            t, e, pb, bM, si = p['t'], p['e'], p['pb'], p['bM'], p['si']
            mm(bank(bM, 0, 256), Bt[pb:pb + 64, t * 128:(t + 1) * 128], ARb[pb:pb + 64, t, :], True, True,
               [('Bt', par), ('ARb', par, 'a'), ('ARb', par, 'r')], [('ps', bM)])
            mm(bank(bM, 256, 512), Kt[pb:pb + 64, t * 128:(t + 1) * 128], ARb[pb:pb + 64, t, :], True, True,
               [('Kt', par), ('ARb', par, 'a'), ('ARb', par, 'r')], [('ps', bM)])
        for p in P:
            bM, si = p['bM'], p['si']
            tt('dve', NM[si], bank(bM), maskNM[:, d, :], ALU.mult, [('ps', bM), 'maskNM'], [('NM', si)])
            tt('dve', X0[si].bitcast(LEVEL_DT), bank(bM, 0, 128), maskNM[:, d, 0:128], ALU.mult, [('ps', bM), 'maskNM'], [('X0', si)])
        yield
        for p in P:
            t, e, pb, bM, si, gt = p['t'], p['e'], p['pb'], p['bM'], p['si'], p['gt']
            mm(bank(bM, 0, 128), ARb[pb:pb + 64, t, 0:128], Bt[pb:pb + 64, t * 128:(t + 1) * 128], True, True,
               [('Bt', par), ('ARb', par, 'a')], [('ps', bM)])
            mm(bank(bM, 384, 448), NM[si][:, 256:384], VT[:, gt, e * 64:(e + 1) * 64], True, True,
               [('NM', si), 'VT'], [('ps', bM)])
        for p in P:
            t, e, bM, si, zi = p['t'], p['e'], p['bM'], p['si'], p['zi']
            tt('dve', X0T[si].bitcast(LEVEL_DT), bank(bM, 0, 128), maskT[:, d, :], ALU.mult, [('ps', bM), 'maskT'], [('X0T', si)])
            cp('act', Zb[zi][0].bitcast(LEVEL_DT)[:, 64:128], bank(bM, 384, 448), [('ps', bM)], [('Zb', zi, 0, 'u')])
            cp('pool', Zb[zi][0].bitcast(LEVEL_DT)[:, 0:64], ATb[:, t, e * 64:(e + 1) * 64], [('ATb', par, t)], [('Zb', zi, 0, 'a')])
        yield
        for j in range(7):
            for p in P:
                bM, si, zi = p['bM'], p['si'], p['zi']
                Xj = X0[si] if j == 0 else XX[si][j % 2][:, 0:128]
                XjT = X0T[si] if j == 0 else XX[si][j % 2][:, 128:256]
                xk = [('X0', si), ('X0T', si)] if j == 0 else [('XX', si, j % 2)]
                zc = Zb[zi][j % 2]
                zck = [('Zb', zi, j % 2, 'a'), ('Zb', zi, j % 2, 'u')]
                Xr, XTr, zr = Xj.bitcast(LEVEL_DT), XjT.bitcast(LEVEL_DT), zc.bitcast(LEVEL_DT)
                mm(bank(bM, 0, 128), Xr, zr, True, True, xk + zck, [('ps', bM)])
                if j < 6:
                    mm(bank(bM, 128, 256), XTr, Xr, True, True, xk, [('ps', bM)])
                    mm(bank(bM, 256, 384), Xr, XTr, True, True, xk, [('ps', bM)])
            for p in P:
                bM, si, zi = p['bM'], p['si'], p['zi']
                zc = Zb[zi][j % 2]
                zn = Zb[zi][(j + 1) % 2]
                zck = [('Zb', zi, j % 2, 'a'), ('Zb', zi, j % 2, 'u')]
                znk = [('Zb', zi, (j + 1) % 2, 'a'), ('Zb', zi, (j + 1) % 2, 'u')]
                tt('dve', zn.bitcast(LEVEL_DT), bank(bM, 0, 128), zc, ALU.add, [('ps', bM)] + zck, znk)
                if j < 6:
                    cp('act', XX[si][(j + 1) % 2].bitcast(LEVEL_DT), bank(bM, 128, 384), [('ps', bM)], [('XX', si, (j + 1) % 2)])
            yield
        for p in P:
            si, zi = p['si'], p['zi']
            cp('pool', Zh[si], Zb[zi][1], [('Zb', zi, 1, 'a'), ('Zb', zi, 1, 'u')], [('Zh', si)])
        for p in P:
            t, e, pb, bM, si, gt = p['t'], p['e'], p['pb'], p['bM'], p['si'], p['gt']
            Z = Zh[si]
            zk = [('Zh', si)]
            mm(bank(bM, 448, 512)[pb:pb + 64, :], Z[:, 0:64], BKT[:, t, e * 64:(e + 1) * 64], True, True,
               zk + [('BKT', par, t)], [('ps', bM)])
            mm(bank(bM, 384, 448)[pb:pb + 64, :], BKT[:, t, e * 64:(e + 1) * 64], Z[:, 64:128], True, False,
               zk + [('BKT', par, t)], [('ps', bM)])
            mm(bank(bM, 384, 448)[pb:pb + 64, :], BKT[:, t, 128 + e * 64:128 + (e + 1) * 64],
               VT[:, gt, e * 64:(e + 1) * 64], False, True, ['VT', ('BKT', par, t)], [('ps', bM)])
            mm(bank(bM, 0, 128)[pb:pb + 64, :], Z[:, 0:64], NM[si][:, 128:256], True, True,
               zk + [('NM', si)], [('ps', bM)])
            mm(bank(bM, 128, 192), NM[si][:, 128:256], Z[:, 64:128], True, False, zk + [('NM', si)], [('ps', bM)])
            mm(bank(bM, 128, 192), NM[si][:, 384:512], VT[:, gt, e * 64:(e + 1) * 64], False, True,
               ['VT', ('NM', si)], [('ps', bM)])
        yield
        for p in P:
            t, e, pb, bM, si, gt = p['t'], p['e'], p['pb'], p['bM'], p['si'], p['gt']
            wc = Wb[pb:pb + 64, t, 128:129] if d == 0 else Wb[pb:pb + 64, t, 1:2]
            stt('dve', GT[pb:pb + 64, gt, :], identf[pb:pb + 64, pb:pb + 64], wc, bank(bM, 448, 512)[pb:pb + 64, :],
                ALU.mult, ALU.add, [('ps', bM), 'identf', ('Wb', par)], [('GT', gt, e)])
            tt('dve', Qb[pb:pb + 64, gt, :], bank(bM, 0, 128)[pb:pb + 64, :], ARb[pb:pb + 64, t, 128:256], ALU.add,
               [('ps', bM), ('ARb', par, 'r')], [('Qb', gt, e)])
            cp('act', Hs[pb:pb + 64, gt, :], bank(bM, 384, 448)[pb:pb + 64, :], [('ps', bM)], [('Hs', gt, e)])
            if d == 0:
                cp('act', yacc[:, gt, e * 64:(e + 1) * 64], bank(bM, 128, 192), [('ps', bM)], [('yacc', gt, e)])
            else:
                tt('dve', yacc[:, gt, e * 64:(e + 1) * 64], bank(bM, 128, 192), yacc[:, gt, e * 64:(e + 1) * 64],
                   ALU.add, [('ps', bM), ('yacc', gt, e)], [('yacc', gt, e)])
        yield
        have_state = state_in
        order = [0, 1] if d == 0 else [1, 0]
        for t in order:
            gt = gt0 + t
            for e in range(2):
                pb = 64 * e
                if have_state:
                    mm(bank(3, e * 64, e * 64 + 64), Qb[pb:pb + 64, gt, :], Sb[d][pb:pb + 64, :], True, True,
                       [('Qb', gt, e), ('Sb', d, e)], [('ps', 3)])
                    mm(bank(3, 128 + e * 64, 192 + e * 64)[pb:pb + 64, :], GT[pb:pb + 64, gt, :], Sb[d][pb:pb + 64, :],
                       True, True, [('GT', gt, e), ('Sb', d, e)], [('ps', 3)])
            for e in range(2):
                pb = 64 * e
                if have_state:
                    tt('dve', yacc[:, gt, e * 64:(e + 1) * 64], bank(3, e * 64, e * 64 + 64),
                       yacc[:, gt, e * 64:(e + 1) * 64], ALU.add, [('ps', 3), ('yacc', gt, e)], [('yacc', gt, e)])
                    tt('dve', Sf[d][pb:pb + 64, :], bank(3, 128 + e * 64, 192 + e * 64)[pb:pb + 64, :], Hs[pb:pb + 64, gt, :],
                       ALU.add, [('ps', 3), ('Hs', gt, e)], [('Sf', d, e)])
                else:
                    cp('dve', Sf[d][pb:pb + 64, :], Hs[pb:pb + 64, gt, :], [('Hs', gt, e)], [('Sf', d, e)])
                cp('act', Sb[d][pb:pb + 64, :], Sf[d][pb:pb + 64, :], [('Sf', d, e)], [('Sb', d, e)])
            have_state = True
            yield
        if kind == 'p':
            q = (seg + d) % 2
            tr(bank(3, 256, 384)[0:64, :], Sf[d], identf, [('Sf', d, 0), ('Sf', d, 1), 'identf'], [('ps', 3)])
            cp('act', stg[q][0:64, :], bank(3, 256, 384)[0:64, :], [('ps', 3)], [('stg', q)])
            dma('sp', O['ns'][seg, d, 2 * hp:2 * hp + 2, :, :].rearrange("e v k -> v e k"),
                stg[q][0:64, :].rearrange("p (e k) -> p e k", e=2), [('stg', q)], (), is_out=True)
            yield

    def drive(a, b):
        gens = [g for g in (a, b) if g is not None]
        while gens:
            for g in list(gens):
                try:
                    next(g)
                except StopIteration:
                    gens.remove(g)

    jobno = [0]
    for hp in range(NHP):
        wr = wR[hp % 2]
        for a_, base in enumerate([4096, 4096 + 1024, 4096 + 2048, 4096 + 3328]):
            dma('pool', wr[:, :, a_ * 128:(a_ + 1) * 128], w_in_v[:, :, base + hp * 128: base + (hp + 1) * 128], (),
                [('wR', hp % 2, a_)])
        dma('sp', lnxg[hp % 2], I['lnxg'][0:1, hp * 128:(hp + 1) * 128].partition_broadcast(128), (), [('lnxg', hp % 2)])
        dma('sp', lnxb[hp % 2], I['lnxb'][0:1, hp * 128:(hp + 1) * 128].partition_broadcast(128), (), [('lnxb', hp % 2)])
        for (hT, hkey, lc0, kind) in units:
            nt = 8
            for a_ in range(3):
                proj_shift(wr[:, :, a_ * 128:(a_ + 1) * 128], [('wR', hp % 2, a_)], hT, hkey, a_ * 8 + hp, rkv[:, a_, :],
                           ('rkv', a_), kind, 2 * a_)
            rT_, kT_, vT_ = rkv[:, 0, :], rkv[:, 1, :], rkv[:, 2, :]
            for t in range(nt):
                bnk = 6 + (t // 4) % 2
                for kc in range(8):
                    mm(bank(bnk, (t % 4) * 128, (t % 4 + 1) * 128), hT[:, kc, t * 128:(t + 1) * 128],
                       wr[:, kc, 384:512], kc == 0, kc == 7, [(hkey, t), ('wR', hp % 2, 3)], [('ps', bnk)])
                if t % 4 == 3:
                    act(sgb[:, t - 3:t + 1, :], bank(bnk).rearrange("p (a b) -> p a b", b=128), AF.Silu, [('ps', bnk)],
                        [('sgb', q) for q in range(t - 3, t + 1)])
            ts('dve', t1[:, 0:T], kT_[:, 0:T], kkv[:, hp:hp + 1], None, ALU.mult, None, [('rkv', 1), 'kkv'], ['t1'])
            act(sqb[:, 0:T], t1[:, 0:T], AF.Square, ['t1'], ['prodb'])
            for n0 in range(0, T, 512):
                bnk = (n0 // 512) % 2
                mm(bank(bnk), bonesb, sqb[:, n0:n0 + 512], True, True, ['bonesb', 'prodb'], [('ps', bnk)])
                act(kk[:, n0:n0 + 512], bank(bnk), AF.Sqrt, [('ps', bnk), 'cst'], ['kk'], bias=cst[:, 0:1])
            recip(kk[:, 0:T], kk[:, 0:T], ['kk'], ['kk'])
            tt('dve', kk[:, 0:T], kk[:, 0:T], t1[:, 0:T], ALU.mult, ['kk', 't1'], ['kk'])
            stt('dve', prodb[:, 0:T], rT_[:, 0:T], rkv_[:, hp:hp + 1], kT_[:, 0:T], ALU.mult, ALU.mult,
                [('rkv', 0), ('rkv', 1), 'rkv_'], ['prodb'])
            for t in range(nt):
                mm(bank(1, t * 2, t * 2 + 2), prodb[:, t * 128:(t + 1) * 128], hindb, True, True, ['prodb', 'hindb'], [('ps', 1)])
            cp('act', bsum[:, 0:nt * 2], bank(1, 0, nt * 2), [('ps', 1)], ['bsum'])
            cp('act', vb16[:, 0:T], vT_[:, 0:T], [('rkv', 2)], ['vb16'])
            pst2 = bankb(2)
            for t in range(nt):
                tr(pst2[:, t * 128:(t + 1) * 128], vb16[:, t * 128:(t + 1) * 128], identb, ['vb16', 'identb'], [('ps', 2)])
            cp('dve', VT[:, 0:nt, :], pst2[:, 0:nt * 128].rearrange("p (a b) -> p a b", b=128), [('ps', 2)], ['VT'])
            jobs = []
            for d in range(ND):
                segs = list(range(4)) if d == 0 else list(range(3, -1, -1))
                for i_, seg in enumerate(segs):
                    st_in = (kind == 's')
                    jobs.append((d, seg, st_in, (kind == 's' and i_ == 0)))
            pars = []
            for _ in jobs:
                pars.append(jobno[0] % 2)
                jobno[0] += 1
            pg = prep_gen(hp, kind, lc0, jobs[0][0], jobs[0][1], pars[0])
            drive(pg, None)
            for n, (d, seg, st_in, load_s0) in enumerate(jobs):
                if load_s0:
                    dma('sp', s0raw[0:64, :].rearrange("p (e k) -> p e k", e=2),
                        I['s0'][d, 2 * hp:2 * hp + 2, :, :].rearrange("e v k -> v e k"), (), ['s0raw'])
                    tr(bank(3, 0, 64), s0raw[0:64, :], identf[0:64, 0:64], ['s0raw', 'identf'], [('ps', 3)])
                    for e in range(2):
                        pb = 64 * e
                        cp('dve', Sf[d][pb:pb + 64, :], bank(3, 0, 64)[pb:pb + 64, :], [('ps', 3)], [('Sf', d, e)])
                        cp('act', Sb[d][pb:pb + 64, :], bank(3, 0, 64)[pb:pb + 64, :], [('ps', 3)], [('Sb', d, e)])
                rg = rest_gen(hp, kind, d, seg, pars[n], st_in)
                ng = None
                if n + 1 < len(jobs):
                    ng = prep_gen(hp, kind, lc0, jobs[n + 1][0], jobs[n + 1][1], pars[n + 1])
                drive(rg, ng)
            n2 = nt * 2
            yk = [('yacc', t, e) for t in range(nt) for e in range(2)]
            yv = yacc[:, 0:nt, :].rearrange("p a (e f) -> p (a e) f", e=2)
            red(gst[:, 0, 0:n2], yv, ALU.add, yk, [('gst', 0)])
            act(ysq[:, 0:nt, :], yacc[:, 0:nt, :], AF.Square, yk, ['t1'])
            red(gst[:, 1, 0:n2], ysq[:, 0:nt, :].rearrange("p a (e f) -> p (a e) f", e=2), ALU.add, ['t1'], [('gst', 1)])
            ts('dve', gst[:, 2, 0:n2], gst[:, 0, 0:n2], 1.0 / 64, None, ALU.mult, None, [('gst', 0)], [('gst', 2)])
            tt('dve', gst[:, 3, 0:n2], gst[:, 2, 0:n2], gst[:, 2, 0:n2], ALU.mult, [('gst', 2)], [('gst', 3)])
            stt('dve', gst[:, 4, 0:n2], gst[:, 1, 0:n2], 1.0 / 64, gst[:, 3, 0:n2], ALU.mult, ALU.subtract,
                [('gst', 1), ('gst', 3)], [('gst', 4)])
            ts('dve', gst[:, 4, 0:n2], gst[:, 4, 0:n2], 64e-5, None, ALU.add, None, [('gst', 4)], [('gst', 4)])
            act(gst[:, 5, 0:n2], gst[:, 4, 0:n2], AF.Sqrt, [('gst', 4)], [('gst', 5)])
            recip(gst[:, 6, 0:n2], gst[:, 5, 0:n2], [('gst', 5)], [('gst', 6)])
            ysv = ysq[:, 0:nt, :].rearrange("p a (e f) -> p (a e) f", e=2)
            tt('dve', ysv, yv, gst[:, 2, 0:n2].unsqueeze(2).to_broadcast([128, n2, 64]), ALU.subtract, yk + [('gst', 2)], ['t1'])
            tt('dve', ysv, ysv, gst[:, 6, 0:n2].unsqueeze(2).to_broadcast([128, n2, 64]), ALU.mult, ['t1', ('gst', 6)], ['t1'])
            tt('dve', ysq[:, 0:nt, :], ysq[:, 0:nt, :], lnxg[hp % 2].unsqueeze(1).to_broadcast([128, nt, 128]),
               ALU.mult, ['t1', ('lnxg', hp % 2)], ['t1'])
            tt('dve', ysq[:, 0:nt, :], ysq[:, 0:nt, :], lnxb[hp % 2].unsqueeze(1).to_broadcast([128, nt, 128]),
               ALU.add, ['t1', ('lnxb', hp % 2)], ['t1'])
            tt('dve', ybon[:, 0:nt, :].rearrange("p a (e f) -> p (a e) f", e=2),
               VT[:, 0:nt, :].rearrange("p a (e f) -> p (a e) f", e=2),
               bsum[:, 0:n2].unsqueeze(2).to_broadcast([128, n2, 64]), ALU.mult, ['VT', 'bsum'], ['kk'])
            tt('dve', ysq[:, 0:nt, :], ysq[:, 0:nt, :], ybon[:, 0:nt, :], ALU.add, ['t1', 'kk'], ['t1'])
            tt('dve', obR[:, 0:nt, :], ysq[:, 0:nt, :], sgb[:, 0:nt, :], ALU.mult, ['t1'] + [('sgb', t) for t in range(nt)], ['obR'])
            if kind == 'p':
                pst2 = bankb(2)
                for t in range(nt):
                    tr(pst2[:, t * 128:(t + 1) * 128], obR[:, t, :], identb, ['obR', 'identb'], [('ps', 2)])
                cp('act', mixT[:, 8 + hp, 0:1024], pst2[:, 0:1024], [('ps', 2)], [('mixT', 8 + hp, g) for g in range(8)])
            else:
                for t in range(8):
                    mm(bank(0, 0, 256), obR[:, t, :], selb[:, t, :], t == 0, t == 7, ['obR', 'selb'], [('ps', 0)])
                cp('act', mixT[:, 8 + hp, 1024:1280], bank(0, 0, 256), [('ps', 0)], [('mixT', 8 + hp, 8), ('mixT', 8 + hp, 9)])
    if DEBUG:
        dump_all(dict(yacc=yacc, Sf0=Sf[0], GT=GT, Hs=Hs))
    if STOP == 'R':
        return
    S.barrier()
    A.off = mA

    wout = A.alloc([16, 1024], BF16)
    fgbc = A.alloc([1024], F32)
    gatebc = A.alloc([2, 1024], F32)
    bgbc = A.alloc([1024], F32)
    scbc = A.alloc([8, 2, 128], BF16)
    wadg = A.alloc([8, 1024], BF16)
    xr = [A.alloc([1024], F32) for _ in range(2)]
    yv_ = [A.alloc([1024], F32) for _ in range(2)]
    ojunk = A.alloc([1024], BF16)
    ost = [A.alloc([4], F32) for _ in range(2)]
    wout_v = I['w_out'].rearrange("(c p) n -> p c n", p=128)
    wada_v = I['w_ada'].rearrange("(kc p) n -> p kc n", p=128)
    for n in range(2):
        dma('pool', wadg[:, :, n * 512:(n + 1) * 512], wada_v[:, :, 2048 + n * 512:2048 + (n + 1) * 512], (), [('wadg', n)])
    for c4 in range(4):
        dma('pool', wout[:, c4 * 4:(c4 + 1) * 4, :], wout_v[:, c4 * 4:(c4 + 1) * 4, :], (), [('wout', c4)])
    dma('sp', fgbc, I['fg'][0:1, :].partition_broadcast(128), (), ['fgbc'])
    dma('sp', bgbc, I['bgate'][0:1, :].partition_broadcast(128), (), ['bgbc'])
    cp('dve', scbc, scp.unsqueeze(3).to_broadcast([128, 8, 2, 128]), ['scp'], ['scbc'])
    for v in range(2):
        for n in range(2):
            for kc in range(8):
                mm(bank(2 + n), scbc[:, kc, v, :], wadg[:, kc, n * 512:(n + 1) * 512],
                   kc == 0, kc == 7, ['scbc', ('wadg', n)], [('ps', 2 + n)])
            tt('dve', gatebc[:, v, n * 512:(n + 1) * 512], bank(2 + n), bgbc[:, n * 512:(n + 1) * 512], ALU.add,
               [('ps', 2 + n), 'bgbc'], [('gatebc', v, n)])
    otiles = [('xp', g, 'yp', 0) for g in range(8)] + [('xo', g, 'ys', 1) for g in range(2)]
    for ti, (src, g, dst, v) in enumerate(otiles):
        b = ti % 2
        mg = g if src == 'xp' else 8 + g
        dma('sp', xr[b], I[src][g * 128:(g + 1) * 128, :], (), [('xr', b)])
        for n in range(2):
            for c in range(16):
                mm(bank(n), mixT[:, c, mg * 128:(mg + 1) * 128], wout[:, c, n * 512:(n + 1) * 512], c == 0, c == 15,
                   [('mixT', c, mg), ('wout', c // 4)], [('ps', n)])
            tt('dve', yv_[b][:, n * 512:(n + 1) * 512], bank(n), gatebc[:, v, n * 512:(n + 1) * 512], ALU.mult,
               [('ps', n), ('gatebc', v, n)], [('yv', b, n)])
            tt('pool', yv_[b][:, n * 512:(n + 1) * 512], yv_[b][:, n * 512:(n + 1) * 512], xr[b][:, n * 512:(n + 1) * 512], ALU.add,
               [('yv', b, n), ('xr', b)], [('yv', b, n)])
        act(ojunk, yv_[b], AF.Square, [('yv', b, 0), ('yv', b, 1)], ['ojunk', ('ost', b)], accum=ost[b][:, 0:1])
        ts('dve', ost[b][:, 1:2], ost[b][:, 0:1], 1.0 / 1024, 1e-6, ALU.mult, ALU.add, [('ost', b)], [('ostb', b)])
        act(ost[b][:, 2:3], ost[b][:, 1:2], AF.Sqrt, [('ostb', b)], [('ostc', b)])
        recip(ost[b][:, 3:4], ost[b][:, 2:3], [('ostc', b)], [('ostd', b)])
        stt('dve', xr[b], yv_[b], ost[b][:, 3:4], fgbc, ALU.mult, ALU.mult, [('yv', b, 0), ('yv', b, 1), ('ostd', b), 'fgbc', ('xr', b)], [('xr', b)])
        dma('sp', O[dst][g * 128:(g + 1) * 128, :], xr[b], [('xr', b)], (), is_out=True)


_NC = None


def _rope_tab(pos_rows, pos_cols):
    n_freq = 16
    inv = (10000.0 ** (-np.arange(n_freq, dtype=np.float32) / n_freq)).astype(np.float32)
    T = len(pos_rows)
    tab = np.zeros((T, 256), np.float32)
    for s_, pos in enumerate([pos_rows, pos_cols]):
        ang = pos.astype(np.float32)[:, None] * inv[None, :]
        c, sn = np.cos(ang).astype(np.float32), np.sin(ang).astype(np.float32)
        for m in range(2):
            for hf in range(2):
                o = m * 64 + s_ * 32 + hf * 16
                tab[:, o:o + 16] = c
                tab[:, 128 + o:128 + o + 16] = -sn if hf == 0 else sn
    return tab


def kernel(x_prompt, x_sample, cache_k, cache_v, state_rwkv, c, c_ctx, norm_g, w_ada, b_ada,
           w_in, lam_q1, lam_k1, lam_q2, lam_k2, subln_g, shift_mu, decay_w0, decay_w2,
           iclr_a0, iclr_a2, k_k, k_a, r_k, lnx_g, lnx_b, w_out, final_g):
    global _NC
    f = lambda a: np.ascontiguousarray(np.asarray(a, dtype=np.float32))
    x_prompt, x_sample, cache_k, cache_v, state_rwkv = map(f, (x_prompt, x_sample, cache_k, cache_v, state_rwkv))
    c, c_ctx = f(c), f(c_ctx)
    if _NC is None:
        _NC = build()
    nc = _NC

    def fm(v, nch):
        return np.ascontiguousarray(f(v).reshape(nch, 128).T)

    i = np.arange(128)
    su = (i[:, None] < i[None, :]).astype(np.float32)
    ui = (i[:, None] <= i[None, :]).astype(np.float32)
    sl = (i[:, None] > i[None, :]).astype(np.float32)
    li = (i[:, None] >= i[None, :]).astype(np.float32)
    maskNM = np.stack([np.concatenate([su, ui, su, ui], 1), np.concatenate([sl, li, sl, li], 1)])
    maskT = np.stack([sl, su])
    bones = np.kron(np.eye(2, dtype=np.float32), np.ones((64, 64), np.float32))
    hind = np.kron(np.eye(2, dtype=np.float32), np.ones((64, 1), np.float32))
    tok = np.arange(1024)
    ropeall = _rope_tab(tok // 64, tok % 64)
    shared = {
        "w_in": f(w_in)[0], "w_ada": f(w_ada)[0], "w_out": f(w_out)[0],
        "bada_fm": fm(f(b_ada)[0], 24), "bgate": f(b_ada)[0:1, 2048:3072], "normg_fm": fm(f(norm_g)[0], 8),
        "fg": f(final_g)[None, :],
        "lamv": np.concatenate([f(lam_q1)[0], f(lam_k1)[0], f(lam_q2)[0], f(lam_k2)[0]])[None, :],
        "sublng": f(subln_g)[0:1],
        "mu_fm": np.ascontiguousarray(np.stack([fm(f(shift_mu)[0, 0], 26), fm(f(shift_mu)[0, 1], 26)], -1).reshape(128, 52)),
        "w0_fm": np.ascontiguousarray(np.stack([fm(f(decay_w0)[0, 0], 8), fm(f(decay_w0)[0, 1], 8)], -1).reshape(128, 16)),
        "a0_fm": np.ascontiguousarray(np.stack([fm(f(iclr_a0)[0, 0], 8), fm(f(iclr_a0)[0, 1], 8)], -1).reshape(128, 16)),
        "w2": f(decay_w2)[0].reshape(128, 1024), "a2": f(iclr_a2)[0].reshape(128, 1024),
        "kk_fm": fm(f(k_k)[0], 8), "ka_fm": fm(f(k_a)[0], 8), "rk_fm": fm(f(r_k)[0].reshape(-1), 8),
        "lnxg": f(lnx_g)[0:1], "lnxb": f(lnx_b)[0:1],
        "ident": np.eye(128, dtype=np.float32), "maskNM": maskNM, "maskT": maskT, "bones": bones, "hind": hind,
        "ropeall": ropeall,
    }
    in_maps = []
    for core in range(8):
        b, q = core // 4, core % 4
        sel = np.zeros((1024, 256), np.float32)
        sel[q * 256 + np.arange(256), np.arange(256)] = 1.0
        cT = np.stack([fm(c_ctx, 8), fm(c[b], 8)], -1).reshape(128, 16)
        m = dict(shared)
        m.update({
            "xp": x_prompt[core * 4:(core + 1) * 4].reshape(1024, 1024),
            "xs": x_sample[b], "xo": x_sample[b, q * 256:(q + 1) * 256],
            "ck": cache_k[b, 0].reshape(512, 1024), "cv": cache_v[b, 0].reshape(512, 1024),
            "s0": state_rwkv[b, 0], "cT": np.ascontiguousarray(cT), "selT": sel,
            "ropeown": np.ascontiguousarray(ropeall[q * 256:(q + 1) * 256]),
        })
        in_maps.append({k: np.ascontiguousarray(v, dtype=np.float32) for k, v in m.items()})
    res = run_bass_kernel_spmd(nc, in_maps, core_ids=list(range(8)))
    R = res.results
    y_prompt = np.concatenate([R[i]["yp"].reshape(4, 256, 1024) for i in range(8)], 0)
    y_sample = np.stack([np.concatenate([R[b * 4 + q]["ys"] for q in range(4)], 0) for b in range(2)], 0)
    new_k = np.concatenate([R[i]["nk"].reshape(4, 1, 256, 8, 2, 64) for i in range(8)], 0)
    new_v = np.concatenate([R[i]["nv"].reshape(4, 1, 256, 8, 128) for i in range(8)], 0)
    new_s = np.concatenate([R[i]["ns"].reshape(4, 1, 2, 16, 64, 64) for i in range(8)], 0)
    return (y_prompt.astype(np.float32), y_sample.astype(np.float32), new_k.astype(np.float32),
            new_v.astype(np.float32), new_s.astype(np.float32))
```

```python
import math
from contextlib import ExitStack
import numpy as np
import concourse.bass as bass
import concourse.mybir as mybir
from concourse.bass_utils import run_bass_kernel_spmd

F32 = mybir.dt.float32
F32R = mybir.dt.float32r
LEVEL_DT = F32
BF16 = mybir.dt.bfloat16
AF = mybir.ActivationFunctionType
ALU = mybir.AluOpType
AX = mybir.AxisListType

ENG = ['pe', 'dve', 'act', 'pool', 'sp']
NDMA = 40
DC = math.exp(-0.5)


class Sched:
    def __init__(s, nc, stack):
        s.nc = nc
        s.ops = {e: [] for e in ENG}
        s.cnt = {e: 0 for e in ENG}
        s.known = {e: {f: 0 for f in ENG} for e in ENG}
        s.snap = {e: [] for e in ENG}
        s.kdma = {e: {} for e in ENG}
        s.lastw = {}
        s.rd_e = {}
        s.rd_d = {}
        s.sems = {e: stack.enter_context(nc.semaphore('c_' + e)) for e in ENG}
        s.dsems = [stack.enter_context(nc.semaphore('d%d' % i)) for i in range(2 * NDMA)]
        s.dcnt = {'sp': 0, 'pool': 0, 'act': 0}
        s.dcount = 0
        s.dlast = {}
        s.out_events = []

    def op(s, eng, fn, r=(), w=(), dma=False, is_out=False, noinc=False, cc=False):
        r = [(k[0], k[1]) if (isinstance(k, tuple) and k[0] == 'ps') else k for k in r]
        w = [(k[0], k[1]) if (isinstance(k, tuple) and k[0] == 'ps') else k for k in w]
        psr = [k for k in r if isinstance(k, tuple) and k[0] == 'ps']
        if psr:
            r = [k for k in r if not (isinstance(k, tuple) and k[0] == 'ps')]
            w = list(w) + [k for k in psr if k not in w]
        deps = []
        for k in r:
            ev = s.lastw.get(k)
            if ev is not None:
                deps.append((ev, True))
        for k in w:
            ev = s.lastw.get(k)
            if ev is not None:
                deps.append((ev, False))
            for f, c in s.rd_e.get(k, {}).items():
                deps.append((('E', f, c), True))
            for ev in s.rd_d.get(k, ()):
                deps.append((ev, False))
        waits = {}
        for ev, raw in deps:
            if ev[0] == 'E':
                _, f, c = ev
                if f == eng and not dma:
                    if (not raw) or eng == 'pe':
                        continue
                if s.known[eng][f] >= c:
                    continue
                waits[('E', f)] = max(waits.get(('E', f), 0), c)
            else:
                _, si, v = ev
                if s.kdma[eng].get(si, 0) >= v:
                    continue
                waits[('D', si)] = max(waits.get(('D', si), 0), v)
        if cc:
            ev = ('D', 'cc', 1)
            s.dlast['cc'] = 1
        elif dma:
            qn = s.dcnt[eng]
            si = qn % NDMA + (NDMA if eng == 'pool' else 0)
            v = 16 * (qn // NDMA + 1)
            if qn >= NDMA and s.kdma[eng].get(si, 0) < v - 16:
                waits[('D', si)] = max(waits.get(('D', si), 0), v - 16)
            s.dcnt[eng] += 1
            s.dcount += 1
            s.dlast[si] = v
            ev = ('D', si, v)
        for (t, x), v in waits.items():
            if t == 'E':
                kn = s.known[eng]
                if kn[x] < v:
                    kn[x] = v
                sn = s.snap[x][v - 1]
                for f2, c2 in sn.items():
                    if kn[f2] < c2:
                        kn[f2] = c2
            else:
                s.kdma[eng][x] = v
        if not (dma or cc):
            if noinc:
                ev = ('E', eng, s.cnt[eng] + 1)
            else:
                s.cnt[eng] += 1
                ev = ('E', eng, s.cnt[eng])
                s.snap[eng].append(dict(s.known[eng]))
        s.ops[eng].append((list(waits.items()), fn, None if noinc else ev))
        for k in r:
            if ev[0] == 'E':
                s.rd_e.setdefault(k, {})[eng] = ev[2]
            else:
                s.rd_d.setdefault(k, []).append(ev)
        for k in w:
            s.lastw[k] = ev
            s.rd_e[k] = {}
            s.rd_d[k] = []
        if is_out:
            s.out_events.append(ev)
        return ev

    def barrier(s):
        waits = {}
        for f in ENG:
            if f != 'sp' and s.cnt[f] > 0:
                waits[('E', f)] = s.cnt[f]
        for si, v in s.dlast.items():
            waits[('D', si)] = v
        s.cnt['sp'] += 1
        c = s.cnt['sp']
        for f in ENG:
            if f != 'sp':
                s.known['sp'][f] = s.cnt[f]
        s.kdma['sp'] = dict(s.dlast)
        s.snap['sp'].append(dict(s.known['sp']))
        s.ops['sp'].append((list(waits.items()), (lambda e: e.nop()), ('E', 'sp', c)))
        for f in ENG:
            if f == 'sp':
                continue
            s.ops[f].append(([(('E', 'sp'), c)], None, None))
            for g in ENG:
                if g != f:
                    s.known[f][g] = max(s.known[f][g], s.cnt[g])
            s.kdma[f] = dict(s.dlast)
        s.lastw.clear()
        s.rd_e.clear()
        s.rd_d.clear()

    def finish(s):
        waits = {}
        for ev in s.out_events:
            waits[('D', ev[1])] = max(waits.get(('D', ev[1]), 0), ev[2])
        s.ops['sp'].append((list(waits.items()), None, None))

    def emit(s, block):
        def mk(engname):
            def body(e):
                for waits, fn, ev in s.ops[engname]:
                    for (t, x), v in waits:
                        sem = s.sems[x] if t == 'E' else (s.ccsem if x == 'cc' else s.dsems[x])
                        e.wait_ge(sem, v)
                    if fn is None:
                        continue
                    ins = fn(e)
                    if ev is None:
                        continue
                    if ev[0] == 'E':
                        ins.then_inc(s.sems[engname], 1)
                    elif ev[1] == 'cc':
                        ins.then_inc(s.ccsem, 1)
                    else:
                        ins.then_inc(s.dsems[ev[1]], 16)
            return body
        block.tensor(mk('pe'))
        block.vector(mk('dve'))
        block.scalar(mk('act'))
        block.gpsimd(mk('pool'))
        block.sync(mk('sp'))


class Arena:
    def __init__(s, ar, nwords):
        s.ar = ar
        s.off = 0
        s.n = nwords
        s.peak = 0

    def alloc(s, shape, dt):
        n = 1
        for x in shape:
            n *= x
        words = n if dt == F32 else (n + 1) // 2
        words = (words + 7) // 8 * 8
        a = s.ar[:, s.off:s.off + words]
        s.off += words
        s.peak = max(s.peak, s.off)
        assert s.off <= s.n, ("arena overflow", s.off, s.n)
        if dt == BF16:
            a = a.bitcast(BF16)
        a = a[:, 0:n]
        if len(shape) == 2:
            a = a.rearrange("p (a b) -> p a b", b=shape[1])
        elif len(shape) == 3:
            a = a.rearrange("p (a b c) -> p a b c", b=shape[1], c=shape[2])
        elif len(shape) == 4:
            a = a.rearrange("p (a b c d) -> p a b c d", b=shape[1], c=shape[2], d=shape[3])
        return a


IN_SPECS = [
    ("xp", [1024, 1024]), ("xs", [1024, 1024]), ("xo", [256, 1024]),
    ("ck", [512, 1024]), ("cv", [512, 1024]), ("s0", [2, 4, 64, 64]),
    ("cT", [128, 16]), ("w_in", [1024, 8448]), ("w_ada", [1024, 3072]), ("w_out", [2048, 1024]),
    ("bada_fm", [128, 24]), ("bgate", [1, 1024]), ("normg_fm", [128, 8]), ("fg", [1, 1024]),
    ("lamv", [1, 256]), ("sublng", [1, 128]), ("mu_fm", [128, 64]), ("w0_fm", [128, 20]),
    ("a0_fm", [128, 20]), ("w2", [128, 1280]), ("a2", [128, 1280]), ("kk_fm", [128, 10]),
    ("ka_fm", [128, 10]), ("rk_fm", [128, 10]), ("lnxg", [1, 1280]), ("lnxb", [1, 1280]), ("wrs", [1024, 1024]),
    ("ident", [128, 128]), ("maskNM", [2, 128, 512]), ("maskT", [2, 128, 128]),
    ("bones", [128, 128]), ("hind", [128, 2]), ("selT", [1024, 256]),
    ("ropeall", [1024, 256]), ("ropeown", [256, 256]),
]
OUT_SPECS = [
    ("yp", [1024, 1024]), ("ys", [256, 1024]), ("nk", [1024, 1024]), ("nv", [1024, 1024]),
    ("ns", [4, 2, 16, 64, 64]),
]

ARENA_WORDS = 52480 - 4096
LVL_WORDS = 4096
NHEAD_A = 8
NHP = 8
NSEQ_R = 5
ND = 2
DEBUG = False
A_MODE = 'all'
DUMPS = []
STOP = None


def build():
    nc = bass.Bass("TRN2", target_bir_lowering=False)
    I = {n: nc.dram_tensor(n, sh, F32, kind="ExternalInput").ap() for n, sh in IN_SPECS}
    O = {n: nc.dram_tensor(n, sh, F32, kind="ExternalOutput").ap() for n, sh in OUT_SPECS}
    with ExitStack() as stack:
        ar = stack.enter_context(nc.sbuf_tensor("arena", [128, ARENA_WORDS], F32))
        PS = stack.enter_context(nc.psum_tensor("ps", [128, 4096], F32))
        lvl = stack.enter_context(nc.sbuf_tensor("lvl", [128, LVL_WORDS], LEVEL_DT))
        S = Sched(nc, stack)
        S.ccsem = stack.enter_context(nc.semaphore('ccsem'))
        A = Arena(ar, ARENA_WORDS)
        block = stack.enter_context(nc.Block())
        _program(nc, S, A, PS, I, O, lvl)
        S.finish()
        S.emit(block)
    return nc


def _program(nc, S, A, PS, I, O, lvl):
    def dma(eng, out, in_, r, w, is_out=False):
        S.op(eng, lambda e: e.dma_start(out=out, in_=in_), r, w, dma=True, is_out=is_out)

    def mm(out, lhsT, rhs, start, stop, r, w):
        S.op('pe', lambda e: e.matmul(out, lhsT, rhs, start=start, stop=stop), r, w, noinc=(not stop))

    def tr(out, in_, ident, r, w):
        S.op('pe', lambda e: e.transpose(out, in_, ident), r, w)

    def act(out, in_, func, r, w, bias=None, scale=None, accum=None):
        def f(e):
            kw = {}
            if bias is not None:
                kw['bias'] = bias
            if scale is not None:
                kw['scale'] = scale
            if accum is not None:
                kw['accum_out'] = accum
            return e.activation(out=out, in_=in_, func=func, **kw)
        S.op('act', f, r, w)

    def tt(eng, out, in0, in1, op, r, w):
        S.op(eng, lambda e: e.tensor_tensor(out=out, in0=in0, in1=in1, op=op), r, w)

    def ts(eng, out, in0, s1, s2, op0, op1, r, w):
        if s2 is None:
            S.op(eng, lambda e: e.tensor_scalar(out=out, in0=in0, scalar1=s1, scalar2=None, op0=op0), r, w)
        else:
            S.op(eng, lambda e: e.tensor_scalar(out=out, in0=in0, scalar1=s1, scalar2=s2, op0=op0, op1=op1), r, w)

    def stt(eng, out, in0, sc, in1, op0, op1, r, w):
        S.op(eng, lambda e: e.scalar_tensor_tensor(out=out, in0=in0, scalar=sc, in1=in1, op0=op0, op1=op1), r, w)

    def cp(eng, out, in_, r, w):
        if eng == 'act':
            act(out, in_, AF.Identity, r, w)
        else:
            S.op(eng, lambda e: e.tensor_copy(out=out, in_=in_), r, w)

    def red(out, in_, op, r, w):
        S.op('dve', lambda e: e.tensor_reduce(out=out, in_=in_, axis=AX.X, op=op), r, w)

    def recip(out, in_, r, w):
        S.op('dve', lambda e: e.reciprocal(out=out, in_=in_), r, w)

    def memset(eng, out, val, w):
        S.op(eng, lambda e: e.memset(out, val), (), w)

    def bank(b, c0=0, c1=512):
        return PS[:, b * 512 + c0: b * 512 + c1]

    def bankb(b):
        return PS[:, b * 512:(b + 1) * 512].bitcast(BF16)

    w_in_v = I['w_in'].rearrange("(kc p) n -> p kc n", p=128)

    def dump_all(bufs):
        S.barrier()
        for name, ap in bufs.items():
            sh = list(ap.shape)
            dt = nc.dram_tensor('dbg_' + name, sh, ap.dtype, kind="ExternalOutput").ap()
            DUMPS.append('dbg_' + name)
            dma('sp', dt, ap, (), (), is_out=True)

    identf = A.alloc([128], F32)
    identb = A.alloc([128], BF16)
    onesf = A.alloc([128], F32)
    maskNM = A.alloc([2, 512], BF16)
    maskT = A.alloc([2, 128], BF16)
    bonesb = A.alloc([128], BF16)
    hindb = A.alloc([2], BF16)
    selb = A.alloc([8, 256], BF16)
    cst = A.alloc([4], F32)
    hTp = A.alloc([8, 1024], BF16)
    hTs = A.alloc([8, 1024], BF16)
    hTo = A.alloc([8, 256], BF16)
    mixT = A.alloc([16, 1280], BF16)
    modfm = A.alloc([24, 2], F32)
    scale1 = A.alloc([8, 2], F32)
    neglam = A.alloc([1], F32)
    sgl = A.alloc([128], F32)
    mu = A.alloc([32, 2], F32)
    c0v = A.alloc([32], F32)
    w0v = A.alloc([20], F32)
    a0v = A.alloc([20], F32)
    kkv = A.alloc([10], F32)
    kav = A.alloc([10], F32)
    omka = A.alloc([10], F32)
    rkv_ = A.alloc([10], F32)
    W2b = A.alloc([1280], BF16)
    A2b = A.alloc([1280], BF16)
    twT = A.alloc([2048], BF16)
    laT = A.alloc([2048], BF16)

    dma('sp', identf, I['ident'], (), ['identf'])
    dma('pool', identb, I['ident'], (), ['identb'])
    dma('pool', maskNM, I['maskNM'].rearrange("d p n -> p d n"), (), ['maskNM'])
    dma('pool', maskT, I['maskT'].rearrange("d p n -> p d n"), (), ['maskT'])
    dma('pool', bonesb, I['bones'], (), ['bonesb'])
    dma('pool', hindb, I['hind'], (), ['hindb'])
    dma('pool', selb, I['selT'].rearrange("(j p) n -> p j n", p=128), (), ['selb'])
    dma('sp', mu, I['mu_fm'].rearrange("p (c j) -> p c j", j=2), (), ['mu'])
    dma('sp', w0v, I['w0_fm'], (), ['w0v'])
    dma('sp', a0v, I['a0_fm'], (), ['a0v'])
    dma('sp', kkv, I['kk_fm'], (), ['kkv'])
    dma('sp', kav, I['ka_fm'], (), ['kav'])
    dma('sp', rkv_, I['rk_fm'], (), ['rkv_'])
    dma('pool', W2b, I['w2'], (), ['W2b'])
    dma('pool', A2b, I['a2'], (), ['A2b'])
    memset('dve', onesf, 1.0, ['onesf'])
    memset('dve', cst[:, 0:1], 1e-12, ['cst'])
    tt('dve', c0v, mu[:, :, 0], mu[:, :, 1], ALU.add, ['mu'], ['c0v'])
    ts('dve', c0v, c0v, -1.0, 1.0, ALU.mult, ALU.add, ['c0v'], ['c0v'])
    ts('dve', omka, kav, -1.0, 1.0, ALU.mult, ALU.add, ['kav'], ['omka'])

    scp = A.alloc([8, 2], BF16)
    m0 = A.off
    cT = A.alloc([16], F32)
    sc = A.alloc([8, 2], F32)
    wadaf = [A.alloc([8, 512], F32) for _ in range(4)]
    bada = A.alloc([24], F32)
    normg = A.alloc([8], F32)
    lamt = A.alloc([4, 64], F32)
    lamp = A.alloc([2, 64], F32)
    lams = A.alloc([4], F32)

    dma('sp', cT, I['cT'], (), ['cT'])
    dma('sp', bada, I['bada_fm'], (), ['bada'])
    dma('sp', normg, I['normg_fm'], (), ['normg'])
    dma('sp', lamt.rearrange("p a b -> p (a b)"), I['lamv'][0:1, :].partition_broadcast(128), (), ['lamt'])
    dma('sp', sgl, I['sublng'][0:1, :].partition_broadcast(128), (), ['sgl'])
    wada_v = I['w_ada'].rearrange("(kc p) n -> p kc n", p=128)
    for n in range(4):
        dma('sp', wadaf[n], wada_v[:, :, n * 512:(n + 1) * 512], (), [('wada', n)])
    act(sc, cT.rearrange("p (c v) -> p c v", v=2), AF.Silu, ['cT'], ['sc'])
    cp('dve', scp, sc, ['sc'], ['scp'])
    for fc in range(16):
        for kc in range(8):
            mm(bank(0, fc * 2, fc * 2 + 2), wadaf[fc // 4][:, kc, (fc % 4) * 128:(fc % 4 + 1) * 128], sc[:, kc, :],
               kc == 0, kc == 7, ['sc', ('wada', fc // 4)], [('ps', 0)])
    tt('dve', modfm[:, 0:16, :], bank(0, 0, 32).rearrange("p (a b) -> p a b", b=2),
       bada[:, 0:16].unsqueeze(2).to_broadcast([128, 16, 2]), ALU.add, [('ps', 0), 'bada'], ['modfm'])
    ts('dve', scale1, modfm[:, 8:16, :], 1.0, None, ALU.add, None, ['modfm'], ['scale1'])
    tt('dve', scale1, scale1, normg.unsqueeze(2).to_broadcast([128, 8, 2]), ALU.mult, ['scale1', 'normg'], ['scale1'])
    tt('dve', lamp[:, 0, :], lamt[:, 0, :], lamt[:, 1, :], ALU.mult, ['lamt'], ['lamp'])
    tt('dve', lamp[:, 1, :], lamt[:, 2, :], lamt[:, 3, :], ALU.mult, ['lamt', 'lamp'], ['lamp'])
    red(lams[:, 0:2], lamp, ALU.add, ['lamp'], ['lams'])
    act(lams[:, 2:4], lams[:, 0:2], AF.Exp, ['lams'], ['lams2'])
    lam_init = 0.8 - 0.6 * math.exp(-0.3 * 0)
    tt('dve', neglam, lams[:, 3:4], lams[:, 2:3], ALU.subtract, ['lams2'], ['neglam'])
    ts('dve', neglam, neglam, -lam_init, None, ALU.add, None, ['neglam'], ['neglam'])
    ts('dve', sgl, sgl, 1.0 - lam_init, None, ALU.mult, None, ['sgl'], ['sgl'])

    if STOP == '0':
        return
    xt = [A.alloc([1024], F32) for _ in range(2)]
    xn = [A.alloc([1024], BF16) for _ in range(2)]
    junk = A.alloc([1024], BF16)
    st1 = [A.alloc([4], F32) for _ in range(2)]
    tiles = [('xp', g, hTp, g, 0) for g in range(8)] + [('xs', g, hTs, g, 1) for g in range(8)] + \
            [('xo', g, hTo, g, 1) for g in range(2)]
    for ti, (src, g, hT, tg, v) in enumerate(tiles):
        b = ti % 2
        dma('sp', xt[b], I[src][g * 128:(g + 1) * 128, :], (), [('xt', b)])
        act(junk, xt[b], AF.Square, [('xt', b)], ['junk', ('st1', b)], accum=st1[b][:, 0:1])
        ts('dve', st1[b][:, 1:2], st1[b][:, 0:1], 1.0 / 1024, 1e-6, ALU.mult, ALU.add, [('st1', b)], [('st1b', b)])
        act(st1[b][:, 2:3], st1[b][:, 1:2], AF.Sqrt, [('st1b', b)], [('st1c', b)])
        recip(st1[b][:, 3:4], st1[b][:, 2:3], [('st1c', b)], [('st1d', b)])
        ts('dve', xn[b], xt[b], st1[b][:, 3:4], None, ALU.mult, None, [('xt', b), ('st1d', b)], [('xn', b)])
        pb_ = bankb(3 + b)
        for kc in range(8):
            tr(pb_[:, kc * 128:(kc + 1) * 128], xn[b][:, kc * 128:(kc + 1) * 128], identb,
               [('xn', b), 'identb'], [('ps', 3 + b)])
        for kc in range(8):
            act(hT[:, kc, tg * 128:(tg + 1) * 128], pb_[:, kc * 128:(kc + 1) * 128], AF.Identity,
                [('ps', 3 + b), 'scale1', 'modfm'], [(src + 'h', tg)],
                bias=modfm[:, kc, v:v + 1], scale=scale1[:, kc, v:v + 1])
    if STOP == '1':
        return
    S.barrier()
    A.off = m0
    if STOP == '1b':
        return

    mA = A.off
    wA = [A.alloc([8, 512], BF16) for _ in range(2)]
    qkb = [A.alloc([256], BF16) for _ in range(2)]
    kvf = [A.alloc([256], F32) for _ in range(2)]
    qT2 = [A.alloc([256], BF16) for _ in range(2)]
    sg2 = [A.alloc([2, 128], F32) for _ in range(2)]
    kTp = [A.alloc([256], BF16) for _ in range(2)]
    vbp = [A.alloc([2, 128], BF16) for _ in range(2)]
    kTs = A.alloc([1536], BF16)
    vbs = A.alloc([12, 128], BF16)
    ckb = A.alloc([4, 128], BF16)
    ropa = A.alloc([8, 256], F32)
    ropo = A.alloc([2, 256], F32)
    xf = [A.alloc([128], F32) for _ in range(2)]
    rt1 = [A.alloc([128], F32) for _ in range(2)]
    rt2 = [A.alloc([128], F32) for _ in range(2)]
    xb16 = [A.alloc([128], BF16) for _ in range(2)]
    pbuf = [A.alloc([1536], BF16) for _ in range(2)]
    pT = [A.alloc([1536], BF16) for _ in range(2)]
    pbufp = [[A.alloc([256], BF16) for _ in range(2)] for _ in range(2)]
    pTp = [[A.alloc([256], BF16) for _ in range(2)] for _ in range(2)]
    ast = [A.alloc([16], F32) for _ in range(4)]
    of = [A.alloc([128], F32) for _ in range(4)]
    o1 = [A.alloc([128], F32) for _ in range(4)]
    on = [A.alloc([128], F32) for _ in range(4)]
    ob = [A.alloc([128], BF16) for _ in range(4)]
    ajunk = A.alloc([128], BF16)

    dma('sp', ropa, I['ropeall'].rearrange("(j p) n -> p j n", p=128), (), ['ropa'])
    dma('sp', ropo, I['ropeown'].rearrange("(j p) n -> p j n", p=128), (), ['ropo'])

    ctr = {'x': 0}

    def rope(src_ps, tab, dst16, rkeys, wkey):
        i = ctr['x'] % 2
        ctr['x'] += 1
        cp('act', xf[i], src_ps, rkeys, [('xf', i)])
        tt('dve', rt1[i], xf[i], tab[:, 0:128], ALU.mult, [('xf', i), 'ropa', 'ropo'], [('rt1', i)])
        xv = xf[i].rearrange("p (g h f) -> p g h f", h=2, f=16)
        sv = tab[:, 128:256].rearrange("p (g h f) -> p g h f", h=2, f=16)
        r2 = rt2[i].rearrange("p (g h f) -> p g h f", h=2, f=16)
        tt('pool', r2[:, :, 0, :], xv[:, :, 1, :], sv[:, :, 0, :], ALU.mult, [('xf', i), 'ropa', 'ropo'], [('rt2', i, 0)])
        tt('pool', r2[:, :, 1, :], xv[:, :, 0, :], sv[:, :, 1, :], ALU.mult, [('xf', i), 'ropa', 'ropo'], [('rt2', i, 1)])
        tt('dve', dst16, rt1[i], rt2[i], ALU.add, [('rt1', i), ('rt2', i, 0), ('rt2', i, 1)], [wkey])

    def drive2(gens):
        gens = [g for g in gens if g is not None]
        while gens:
            for g in list(gens):
                try:
                    next(g)
                except StopIteration:
                    gens.remove(g)

    def attn_j(h, par, j, kind, mixcol0):
        ai = par * 2 + j
        if kind == 'p':
            ntk, kT_, vb_, kkey, vkey = 2, kTp[par], vbp[par], ('kTp', par), ('vbp', par)
            sb0, pb0, ob_ = 2 + j, 4 + j, 6 + j
            pbs, pTs = pbufp[j], pTp[j]
            pkey = ('pp', j)
        else:
            ntk, kT_, vb_, kkey, vkey = 12, kTs, vbs, 'kTs', 'vbs'
            sb0, pb0, ob_ = 2, 5, 7
            pbs, pTs = pbuf, pT
            pkey = ('ps_', 0)
        Tk = ntk * 128
        qT_ = qT2[par]
        s0c = sb0 * 512
        p0c = pb0 * 512
        for m in range(2):
            for n0 in range(0, Tk, 512):
                w_ = min(512, Tk - n0)
                mm(PS[:, s0c + n0:s0c + n0 + w_], qT_[64 * m:64 * m + 64, j * 128:(j + 1) * 128],
                   kT_[64 * m:64 * m + 64, n0:n0 + w_], True, True,
                   [('qT', par, j), kkey], [('ps', sb0 + n0 // 512)])
            sck = [('ps', sb0 + b_) for b_ in range((Tk + 511) // 512)]
            red(ast[ai][:, m:m + 1], PS[:, s0c:s0c + Tk], ALU.max, sck, [('ast', ai, 'mx', m)])
            ts('dve', ast[ai][:, 2 + m:3 + m], ast[ai][:, m:m + 1], -0.125, None, ALU.mult, None,
               [('ast', ai, 'mx', m)], [('ast', ai, 'nb', m)])
            act(pbs[m][:, 0:Tk], PS[:, s0c:s0c + Tk], AF.Exp, sck + [('ast', ai, 'nb', m)],
                [('pbuf', pkey, m), ('ast', ai, 'sum', m)], bias=ast[ai][:, 2 + m:3 + m], scale=0.125,
                accum=ast[ai][:, 4 + m:5 + m])
            yield
            ptv = PS[:, p0c:p0c + 1024].bitcast(BF16)
            for t in range(ntk):
                tr(ptv[:, t * 128:(t + 1) * 128], pbs[m][:, t * 128:(t + 1) * 128], identb,
                   [('pbuf', pkey, m), 'identb'], [('ps', pb0 + t // 8)])
            if ntk <= 2:
                cp('dve', pTs[m][:, 0:Tk], ptv[:, 0:Tk], [('ps', pb0)], [('pT', pkey, m, 0)])
                ptk = [('pT', pkey, m, 0)]
            else:
                cp('dve', pTs[m][:, 0:768], ptv[:, 0:768], [('ps', pb0)], [('pT', pkey, m, 0)])
                cp('act', pTs[m][:, 768:1536], ptv[:, 768:1536], [('ps', pb0), ('ps', pb0 + 1)], [('pT', pkey, m, 1)])
                ptk = [('pT', pkey, m, 0), ('pT', pkey, m, 1)]
            for t in range(ntk):
                mm(bank(ob_, m * 128, (m + 1) * 128), pTs[m][:, t * 128:(t + 1) * 128], vb_[:, t, :],
                   t == 0, t == ntk - 1, ptk + [vkey], [('ps', ob_)])
            yield
        a_ = ast[ai]
        recip(a_[:, 6:8], a_[:, 4:6], [('ast', ai, 'sum', 0), ('ast', ai, 'sum', 1)], [('ast', ai, 'rs')])
        tt('dve', a_[:, 8:9], a_[:, 7:8], neglam, ALU.mult, [('ast', ai, 'rs'), 'neglam'], [('ast', ai, 'c2')])
        act(o1[ai], bank(ob_, 0, 128), AF.Identity, [('ps', ob_), ('ast', ai, 'rs')], [('o1', ai)], scale=a_[:, 6:7])
        stt('dve', of[ai], bank(ob_, 128, 256), a_[:, 8:9], o1[ai], ALU.mult, ALU.add,
            [('ps', ob_), ('ast', ai, 'c2'), ('o1', ai)], [('of', ai)])
        yield
        act(ajunk, of[ai], AF.Square, [('of', ai)], ['ajunk', ('ast', ai, 'ss')], accum=a_[:, 9:10])
        ts('dve', a_[:, 10:11], a_[:, 9:10], 1.0 / 128, 1e-5, ALU.mult, ALU.add, [('ast', ai, 'ss')], [('ast', ai, 'ms')])
        act(a_[:, 11:12], a_[:, 10:11], AF.Sqrt, [('ast', ai, 'ms')], [('ast', ai, 'sd')])
        recip(a_[:, 12:13], a_[:, 11:12], [('ast', ai, 'sd')], [('ast', ai, 'rstd')])
        yield
        stt('dve', on[ai], of[ai], a_[:, 12:13], sgl, ALU.mult, ALU.mult, [('of', ai), ('ast', ai, 'rstd'), 'sgl'], [('on', ai)])
        tt('pool', ob[ai], on[ai], sg2[par][:, j, :], ALU.mult, [('on', ai), ('sg', par, j)], [('ob', ai)])
        pso = bankb(ob_)
        tr(pso[:, 512:640], ob[ai], identb, [('ob', ai), 'identb'], [('ps', ob_)])
        cp('act', mixT[:, h, mixcol0 + j * 128: mixcol0 + (j + 1) * 128], pso[:, 512:640], [('ps', ob_)],
           [('mixT', h, (mixcol0 // 128) + j)])
        yield

    def attn_gen(h, par, kind, mixcol0):
        if kind == 'p':
            g0, g1 = attn_j(h, par, 0, kind, mixcol0), attn_j(h, par, 1, kind, mixcol0)
            gens = [g0, g1]
            while gens:
                for g in list(gens):
                    try:
                        next(g)
                    except StopIteration:
                        gens.remove(g)
                yield
        else:
            for j in range(2):
                for _ in attn_j(h, par, j, kind, mixcol0):
                    yield

    def proj_gen(h, par, kind, s_):
        wb = wA[h % 2]
        wk = [('wA', h % 2, j4) for j4 in range(4)]
        pst = bankb(1)
        if kind == 'p':
            for j in range(2):
                g = s_ * 2 + j
                bi = g % 2
                for kc in range(8):
                    mm(bank(0), hTp[:, kc, g * 128:(g + 1) * 128], wb[:, kc, :], kc == 0, kc == 7,
                       [('xph', g)] + wk, [('ps', 0)])
                cp('dve', qkb[bi], bank(0, 0, 256), [('ps', 0)], [('qkb', bi)])
                cp('act', kvf[bi], bank(0, 128, 384), [('ps', 0)], [('kvf', bi)])
                dma('sp', O['nk'][g * 128:(g + 1) * 128, h * 128:(h + 1) * 128], kvf[bi][:, 0:128], [('kvf', bi)], (), is_out=True)
                dma('sp', O['nv'][g * 128:(g + 1) * 128, h * 128:(h + 1) * 128], kvf[bi][:, 128:256], [('kvf', bi)], (), is_out=True)
                cp('dve', vbp[par][:, j, :], bank(0, 256, 384), [('ps', 0)], [('vbp', par)])
                act(sg2[par][:, j, :], bank(0, 384, 512), AF.Silu, [('ps', 0)], [('sg', par, j)])
                yield
                tr(pst[:, 0:128], qkb[bi][:, 0:128], identb, [('qkb', bi), 'identb'], [('ps', 1)])
                tr(pst[:, 128:256], qkb[bi][:, 128:256], identb, [('qkb', bi), 'identb'], [('ps', 1)])
                cp('act', qT2[par][:, j * 128:(j + 1) * 128], pst[:, 0:128], [('ps', 1)], [('qT', par, j)])
                cp('act', kTp[par][:, j * 128:(j + 1) * 128], pst[:, 128:256], [('ps', 1)], [('kTp', par)])
                yield
        else:
            dma('pool', ckb, I['ck'].rearrange("(j p) c -> p j c", p=128)[:, :, h * 128:(h + 1) * 128], (), ['ckb'])
            dma('pool', vbs[:, 0:4, :], I['cv'].rearrange("(j p) c -> p j c", p=128)[:, :, h * 128:(h + 1) * 128], (), ['vbs'])
            for t in range(4):
                tr(pst[:, 512 + t * 128:512 + (t + 1) * 128], ckb[:, t, :], identb, ['ckb', 'identb'], [('ps', 1)])
            cp('dve', kTs[:, 0:512], pst[:, 512:1024], [('ps', 1)], ['kTs'])
            yield
            for j in range(8):
                for kc in range(8):
                    mm(bank(0, 0, 256), hTs[:, kc, j * 128:(j + 1) * 128], wb[:, kc, 128:384], kc == 0, kc == 7,
                       [('xsh', j)] + wk, [('ps', 0)])
                bi = j % 2
                cp('dve', vbs[:, 4 + j, :], bank(0, 128, 256), [('ps', 0)], ['vbs'])
                rope(bank(0, 0, 128), ropa[:, j, :], xb16[bi], [('ps', 0)], ('xb16', bi))
                yield
                tr(pst[:, 0:128], xb16[bi], identb, [('xb16', bi), 'identb'], [('ps', 1)])
                cp('act', kTs[:, 512 + j * 128:512 + (j + 1) * 128], pst[:, 0:128], [('ps', 1)], ['kTs'])
                yield
            for j in range(2):
                for kc in range(8):
                    mm(bank(0), hTo[:, kc, j * 128:(j + 1) * 128], wb[:, kc, :], kc == 0, kc == 7,
                       [('xoh', j)] + wk, [('ps', 0)])
                bi = j % 2
                act(sg2[par][:, j, :], bank(0, 384, 512), AF.Silu, [('ps', 0)], [('sg', par, j)])
                rope(bank(0, 0, 128), ropo[:, j, :], xb16[bi], [('ps', 0)], ('xb16', bi))
                yield
                tr(pst[:, 128:256], xb16[bi], identb, [('xb16', bi), 'identb'], [('ps', 1)])
                cp('act', qT2[par][:, j * 128:(j + 1) * 128], pst[:, 128:256], [('ps', 1)], [('qT', par, j)])
                yield

    ajobs = []
    for h in range(NHEAD_A):
        for s_ in range(4):
            ajobs.append((h, 'p', s_))
        ajobs.append((h, 's', 0))
    loaded = set()

    def load_w(h):
        if h in loaded or h >= NHEAD_A:
            return
        loaded.add(h)
        for j4, base in enumerate([0, 1024, 2048, 3072]):
            dma('pool', wA[h % 2][:, :, j4 * 128:(j4 + 1) * 128], w_in_v[:, :, base + h * 128: base + (h + 1) * 128], (),
                [('wA', h % 2, j4)])
    if ajobs:
        load_w(0)
        drive2([proj_gen(ajobs[0][0], 0, ajobs[0][1], ajobs[0][2])])
        for n, (h, kind, s_) in enumerate(ajobs):
            par = n % 2
            ag = attn_gen(h, par, kind, s_ * 256 if kind == 'p' else 1024)
            pg = None
            if n + 1 < len(ajobs):
                h2, kind2, s2 = ajobs[n + 1]
                load_w(h2)
                pg = proj_gen(h2, (n + 1) % 2, kind2, s2)
            drive2([ag, pg])
    if STOP == 'A':
        return
    S.barrier()
    A.off = mA

    wR = [A.alloc([8, 512], BF16) for _ in range(2)]
    rkv = A.alloc([3, 1024], F32)
    kk = A.alloc([1024], F32)
    t1 = A.alloc([1024], F32)
    prodb = A.alloc([1024], BF16)
    sqb = prodb
    vb16 = A.alloc([1024], BF16)
    VT = A.alloc([8, 128], BF16)
    sgb = A.alloc([8, 128], F32)
    yacc = A.alloc([8, 128], F32)
    wL = yacc.rearrange("p a b -> p (a b)").bitcast(BF16).rearrange("p (a b) -> p a b", b=256)
    bsum = A.alloc([16], F32)
    lnxg = [A.alloc([128], F32) for _ in range(2)]
    lnxb = [A.alloc([128], F32) for _ in range(2)]
    sgw = A.alloc([256], F32)
    Pp = A.alloc([256], F32)
    csb = A.alloc([256], F32)
    Winv = A.alloc([256], F32)
    av = A.alloc([256], F32)
    tmpa = A.alloc([256], F32)
    tmpb = A.alloc([256], F32)
    BW = A.alloc([256], BF16)
    KW = A.alloc([256], BF16)
    Wb2 = [A.alloc([2, 130], F32) for _ in range(2)]
    Kt2 = [A.alloc([256], BF16) for _ in range(2)]
    Bt2 = [A.alloc([256], BF16) for _ in range(2)]
    ARb2 = [A.alloc([2, 256], BF16) for _ in range(2)]
    BKT2 = [A.alloc([2, 256], BF16) for _ in range(2)]
    ATb2 = [A.alloc([2, 128], BF16) for _ in range(2)]
    NM = [A.alloc([512], BF16) for _ in range(4)]
    lo = [0]

    def lalloc(n):
        a_ = lvl[:, lo[0]:lo[0] + n].bitcast(F32)
        lo[0] += n
        assert lo[0] <= LVL_WORDS
        return a_
    X0 = [lalloc(128) for _ in range(4)]
    X0T = [lalloc(128) for _ in range(4)]
    XX = [[lalloc(256) for _ in range(2)] for _ in range(4)]
    Zb = [[lalloc(128) for _ in range(2)] for _ in range(4)]
    Zh = [A.alloc([128], BF16) for _ in range(4)]
    GT = A.alloc([8, 64], BF16)
    Hs = A.alloc([8, 64], F32)
    Qb = A.alloc([8, 128], BF16)
    Sf = [A.alloc([64], F32) for _ in range(2)]
    Sb = [A.alloc([64], BF16) for _ in range(2)]
    s0raw = A.alloc([128], F32)
    stg = [A.alloc([128], F32) for _ in range(2)]
    gst = A.alloc([8, 16], F32)
    ysq = t1.rearrange("p (a b) -> p a b", b=128)
    ybon = kk.rearrange("p (a b) -> p a b", b=128)
    obR = A.alloc([8, 128], BF16)

    dma('pool', wL, w_in_v[:, :, 4096 + 3072:4096 + 3328], (), ['wL'])
    for q in range(2):
        memset('dve', Wb2[q][:, :, 0:1], 1.0, [('Wbpad0', q)])
        memset('dve', Wb2[q][:, :, 129:130], 1.0, [('Wbpad1', q)])

    T = 1024
    units = [(hTp, 'xph', 0, 'p'), (hTs, 'xsh', 1024, 's')]

    def proj_shift(w_ap, wkeys, hT, hkey, ci, dst, dkey, kind, b0):
        for n0 in range(0, T, 512):
            bnk = b0 + n0 // 512
            hk = [(hkey, n0 // 128 + q) for q in range(4)]
            for kc in range(8):
                mm(bank(bnk), w_ap[:, kc, :], hT[:, kc, n0:n0 + 512], kc == 0, kc == 7, hk + wkeys, [('ps', bnk)])
        for n0 in range(0, T, 512):
            bnk = b0 + n0 // 512
            act(dst[:, n0:n0 + 512], bank(bnk), AF.Identity, [('ps', bnk), 'c0v'], [dkey], scale=c0v[:, ci:ci + 1])
        psv = PS[:, b0 * 512:b0 * 512 + 1024]
        pk = [('ps', b0), ('ps', b0 + 1)]
        blocks = [(0, T)] if kind == 's' else [(q * 256, (q + 1) * 256) for q in range(4)]
        for (s_, e_) in blocks:
            kk_ = [('ps', b0 + s_ // 512)] if (s_ // 512 == (e_ - 1) // 512) else pk
            stt('dve', dst[:, s_ + 1:e_], psv[:, s_:e_ - 1], mu[:, ci, 0:1], dst[:, s_ + 1:e_], ALU.mult, ALU.add,
                kk_ + ['mu', dkey], [dkey])
            stt('dve', dst[:, s_:e_ - 1], psv[:, s_ + 1:e_], mu[:, ci, 1:2], dst[:, s_:e_ - 1], ALU.mult, ALU.add,
                kk_ + ['mu', dkey], [dkey])

    for (hT, hkey, lc0, kind) in units:
        for c in range(2):
            proj_shift(wL[:, :, c * 128:(c + 1) * 128], ['wL'], hT, hkey, 24 + c, t1, 't1', kind, 2 * c)
            if c == 0:
                act(twT[:, lc0:lc0 + T], t1[:, 0:T], AF.Tanh, ['t1'], [('twT', kind)])
            else:
                cp('act', laT[:, lc0:lc0 + T], t1[:, 0:T], ['t1'], [('laT', kind)])
    S.barrier()

    def prep_gen(hp, kind, lc0, d, seg, par):
        Wb, Kt, Bt, ARb, BKT, ATb = Wb2[par], Kt2[par], Bt2[par], ARb2[par], BKT2[par], ATb2[par]
        kT_, rT_ = rkv[:, 1, :], rkv[:, 0, :]
        c0_ = seg * 256
        lc = lc0 + c0_
        mm(bank(0, 0, 256), W2b[64 * d:64 * d + 64, hp * 128:(hp + 1) * 128], twT[64 * d:64 * d + 64, lc:lc + 256],
           True, True, ['W2b', ('twT', kind)], [('ps', 0)])
        act(sgw, bank(0, 0, 256), AF.Sigmoid, [('ps', 0), 'w0v'], ['sgw'], bias=w0v[:, hp * 2 + d:hp * 2 + d + 1])
        mm(bank(1, 0, 256), A2b[64 * d:64 * d + 64, hp * 128:(hp + 1) * 128], laT[64 * d:64 * d + 64, lc:lc + 256],
           True, True, ['A2b', ('laT', kind)], [('ps', 1)])
        act(av, bank(1, 0, 256), AF.Sigmoid, [('ps', 1), 'a0v'], ['av'], bias=a0v[:, hp * 2 + d:hp * 2 + d + 1])
        yield
        for t in range(2):
            S.op('dve', (lambda t=t: (lambda e: e.tensor_tensor_scan(
                out=Pp[:, t * 128:(t + 1) * 128], data0=onesf, data1=sgw[:, t * 128:(t + 1) * 128],
                initial=0.0, op0=ALU.mult, op1=ALU.add)))(), ['sgw', 'onesf'], [('Pp', t)])
        if d == 0:
            cs = Pp
            csk = [('Pp', 0), ('Pp', 1)]
        else:
            for t in range(2):
                stt('dve', csb[:, t * 128:(t + 1) * 128], sgw[:, t * 128:(t + 1) * 128],
                    Pp[:, t * 128 + 127:t * 128 + 128], Pp[:, t * 128:(t + 1) * 128], ALU.add, ALU.subtract,
                    ['sgw', ('Pp', t)], [('csb', t)])
            cs = csb
            csk = [('csb', 0), ('csb', 1)]
        yield
        act(Wb[:, :, 1:129], cs.rearrange("p (a b) -> p a b", b=128), AF.Exp, csk, [('Wb', par)], scale=-DC)
        act(Winv, cs, AF.Exp, csk, ['Winv'], scale=DC)
        ts('dve', tmpa, av, kav[:, hp:hp + 1], omka[:, hp:hp + 1], ALU.mult, ALU.add, ['av', 'kav', 'omka'], ['tmpa'])
        tt('pool', tmpb, kk[:, c0_:c0_ + 256], av, ALU.mult, ['kk', 'av'], ['tmpb'])
        yield
        tt('dve', tmpa, tmpa, kT_[:, c0_:c0_ + 256], ALU.mult, ['tmpa', ('rkv', 1)], ['tmpa'])
        tt('dve', Kt, tmpa, Winv, ALU.mult, ['tmpa', 'Winv'], [('Kt', par)])
        tt('dve', Bt, tmpb, Winv, ALU.mult, ['tmpb', 'Winv'], [('Bt', par)])
        yield
        Wprev = Wb[:, :, 0:128] if d == 0 else Wb[:, :, 2:130]
        stt('dve', ARb[:, :, 0:128], kk[:, c0_:c0_ + 256].rearrange("p (a b) -> p a b", b=128), -1.0, Wprev,
            ALU.mult, ALU.mult, ['kk', ('Wb', par), ('Wbpad0', par), ('Wbpad1', par)], [('ARb', par, 'a')])
        tt('dve', ARb[:, :, 128:256], rT_[:, c0_:c0_ + 256].rearrange("p (a b) -> p a b", b=128), Wb[:, :, 1:129],
           ALU.mult, [('rkv', 0), ('Wb', par)], [('ARb', par, 'r')])
        yield
        pst2 = bankb(2)
        for t in range(2):
            wc = Wb[:, t, 128:129] if d == 0 else Wb[:, t, 1:2]
            ts('dve', BW[:, t * 128:(t + 1) * 128], Bt[:, t * 128:(t + 1) * 128], wc, None, ALU.mult, None,
               [('Bt', par), ('Wb', par)], [('BW', t)])
            ts('dve', KW[:, t * 128:(t + 1) * 128], Kt[:, t * 128:(t + 1) * 128], wc, None, ALU.mult, None,
               [('Kt', par), ('Wb', par)], [('KW', t)])
            yield
        for t in range(2):
            tr(pst2[:, t * 384:t * 384 + 128], BW[:, t * 128:(t + 1) * 128], identb, [('BW', t), 'identb'], [('ps', 2)])
            tr(pst2[:, t * 384 + 128:t * 384 + 256], KW[:, t * 128:(t + 1) * 128], identb, [('KW', t), 'identb'], [('ps', 2)])
            tr(pst2[:, t * 384 + 256:t * 384 + 384], ARb[:, t, 0:128], identb, [('ARb', par, 'a'), 'identb'], [('ps', 2)])
        for t in range(2):
            cp('act', BKT[:, t, :], pst2[:, t * 384:t * 384 + 256], [('ps', 2)], [('BKT', par, t)])
            cp('act', ATb[:, t, :], pst2[:, t * 384 + 256:t * 384 + 384], [('ps', 2)], [('ATb', par, t)])
        yield

    def rest_gen(hp, kind, d, seg, par, state_in):
        Wb, Kt, Bt, ARb, BKT, ATb = Wb2[par], Kt2[par], Bt2[par], ARb2[par], BKT2[par], ATb2[par]
        gt0 = seg * 2
        P = []
        for t in range(2):
            for e in range(2):
                zi = t * 2 + e
                P.append(dict(t=t, e=e, zi=zi, si=zi, pb=64 * e, bM=4 + zi, gt=gt0 + t))
        for p in P:
            t, e, pb, bM, si = p['t'], p['e'], p['pb'], p['bM'], p['si']
            mm(bank(bM, 0, 256), Bt[pb:pb + 64, t * 128:(t + 1) * 128], ARb[pb:pb + 64, t, :], True, True,
               [('Bt', par), ('ARb', par, 'a'), ('ARb', par, 'r')], [('ps', bM)])
            mm(bank(bM, 256, 512), Kt[pb:pb + 64, t * 128:(t + 1) * 128], ARb[pb:pb + 64, t, :], True, True,
               [('Kt', par), ('ARb', par, 'a'), ('ARb', par, 'r')], [('ps', bM)])
        for p in P:
            bM, si = p['bM'], p['si']
            tt('dve', NM[si], bank(bM), maskNM[:, d, :], ALU.mult, [('ps', bM), 'maskNM'], [('NM', si)])
            tt('dve', X0[si].bitcast(LEVEL_DT), bank(bM, 0, 128), maskNM[:, d, 0:128], ALU.mult, [('ps', bM), 'maskNM'], [('X0', si)])
        yield
        for p in P:
            t, e, pb, bM, si, gt = p['t'], p['e'], p['pb'], p['bM'], p['si'], p['gt']
            mm(bank(bM, 0, 128), ARb[pb:pb + 64, t, 0:128], Bt[pb:pb + 64, t * 128:(t + 1) * 128], True, True,
               [('Bt', par), ('ARb', par, 'a')], [('ps', bM)])
            mm(bank(bM, 384, 448), NM[si][:, 256:384], VT[:, gt, e * 64:(e + 1) * 64], True, True,
               [('NM', si), 'VT'], [('ps', bM)])
        for p in P:
            t, e, bM, si, zi = p['t'], p['e'], p['bM'], p['si'], p['zi']
            tt('dve', X0T[si].bitcast(LEVEL_DT), bank(bM, 0, 128), maskT[:, d, :], ALU.mult, [('ps', bM), 'maskT'], [('X0T', si)])
            cp('act', Zb[zi][0].bitcast(LEVEL_DT)[:, 64:128], bank(bM, 384, 448), [('ps', bM)], [('Zb', zi, 0, 'u')])
            cp('pool', Zb[zi][0].bitcast(LEVEL_DT)[:, 0:64], ATb[:, t, e * 64:(e + 1) * 64], [('ATb', par, t)], [('Zb', zi, 0, 'a')])
        yield
        for j in range(7):
            for p in P:
                bM, si, zi = p['bM'], p['si'], p['zi']
                Xj = X0[si] if j == 0 else XX[si][j % 2][:, 0:128]
                XjT = X0T[si] if j == 0 else XX[si][j % 2][:, 128:256]
                xk = [('X0', si), ('X0T', si)] if j == 0 else [('XX', si, j % 2)]
                zc = Zb[zi][j % 2]
                zck = [('Zb', zi, j % 2, 'a'), ('Zb', zi, j % 2, 'u')]
                Xr, XTr, zr = Xj.bitcast(LEVEL_DT), XjT.bitcast(LEVEL_DT), zc.bitcast(LEVEL_DT)
                mm(bank(bM, 0, 128), Xr, zr, True, True, xk + zck, [('ps', bM)])
                if j < 6:
                    mm(bank(bM, 128, 256), XTr, Xr, True, True, xk, [('ps', bM)])
                    mm(bank(bM, 256, 384), Xr, XTr, True, True, xk, [('ps', bM)])
            for p in P:
                bM, si, zi = p['bM'], p['si'], p['zi']
                zc = Zb[zi][j % 2]
                zn = Zb[zi][(j + 1) % 2]
                zck = [('Zb', zi, j % 2, 'a'), ('Zb', zi, j % 2, 'u')]
                znk = [('Zb', zi, (j + 1) % 2, 'a'), ('Zb', zi, (j + 1) % 2, 'u')]
                tt('dve', zn.bitcast(LEVEL_DT), bank(bM, 0, 128), zc, ALU.add, [('ps', bM)] + zck, znk)
                if j < 6:
                    cp('act', XX[si][(j + 1) % 2].bitcast(LEVEL_DT), bank(bM, 128, 384), [('ps', bM)], [('XX', si, (j + 1) % 2)])
            yield
        for p in P:
            si, zi = p['si'], p['zi']
            cp('pool', Zh[si], Zb[zi][1], [('Zb', zi, 1, 'a'), ('Zb', zi, 1, 'u')], [('Zh', si)])
        for p in P:
            t, e, pb, bM, si, gt = p['t'], p['e'], p['pb'], p['bM'], p['si'], p['gt']
            Z = Zh[si]
            zk = [('Zh', si)]
            mm(bank(bM, 448, 512)[pb:pb + 64, :], Z[:, 0:64], BKT[:, t, e * 64:(e + 1) * 64], True, True,
               zk + [('BKT', par, t)], [('ps', bM)])
            mm(bank(bM, 384, 448)[pb:pb + 64, :], BKT[:, t, e * 64:(e + 1) * 64], Z[:, 64:128], True, False,
               zk + [('BKT', par, t)], [('ps', bM)])
            mm(bank(bM, 384, 448)[pb:pb + 64, :], BKT[:, t, 128 + e * 64:128 + (e + 1) * 64],
               VT[:, gt, e * 64:(e + 1) * 64], False, True, ['VT', ('BKT', par, t)], [('ps', bM)])
            mm(bank(bM, 0, 128)[pb:pb + 64, :], Z[:, 0:64], NM[si][:, 128:256], True, True,
               zk + [('NM', si)], [('ps', bM)])
            mm(bank(bM, 128, 192), NM[si][:, 128:256], Z[:, 64:128], True, False, zk + [('NM', si)], [('ps', bM)])
            mm(bank(bM, 128, 192), NM[si][:, 384:512], VT[:, gt, e * 64:(e + 1) * 64], False, True,
               ['VT', ('NM', si)], [('ps', bM)])
        yield
        for p in P:
            t, e, pb, bM, si, gt = p['t'], p['e'], p['pb'], p['bM'], p['si'], p['gt']
            wc = Wb[pb:pb + 64, t, 128:129] if d == 0 else Wb[pb:pb + 64, t, 1:2]
            stt('dve', GT[pb:pb + 64, gt, :], identf[pb:pb + 64, pb:pb + 64], wc, bank(bM, 448, 512)[pb:pb + 64, :],
                ALU.mult, ALU.add, [('ps', bM), 'identf', ('Wb', par)], [('GT', gt, e)])
            tt('dve', Qb[pb:pb + 64, gt, :], bank(bM, 0, 128)[pb:pb + 64, :], ARb[pb:pb + 64, t, 128:256], ALU.add,
               [('ps', bM), ('ARb', par, 'r')], [('Qb', gt, e)])
            cp('act', Hs[pb:pb + 64, gt, :], bank(bM, 384, 448)[pb:pb + 64, :], [('ps', bM)], [('Hs', gt, e)])
            if d == 0:
                cp('act', yacc[:, gt, e * 64:(e + 1) * 64], bank(bM, 128, 192), [('ps', bM)], [('yacc', gt, e)])
            else:
                tt('dve', yacc[:, gt, e * 64:(e + 1) * 64], bank(bM, 128, 192), yacc[:, gt, e * 64:(e + 1) * 64],
                   ALU.add, [('ps', bM), ('yacc', gt, e)], [('yacc', gt, e)])
        yield
        have_state = state_in
        order = [0, 1] if d == 0 else [1, 0]
        for t in order:
            gt = gt0 + t
            for e in range(2):
                pb = 64 * e
                if have_state:
                    mm(bank(3, e * 64, e * 64 + 64), Qb[pb:pb + 64, gt, :], Sb[d][pb:pb + 64, :], True, True,
                       [('Qb', gt, e), ('Sb', d, e)], [('ps', 3)])
                    mm(bank(3, 128 + e * 64, 192 + e * 64)[pb:pb + 64, :], GT[pb:pb + 64, gt, :], Sb[d][pb:pb + 64, :],
                       True, True, [('GT', gt, e), ('Sb', d, e)], [('ps', 3)])
            for e in range(2):
                pb = 64 * e
                if have_state:
                    tt('dve', yacc[:, gt, e * 64:(e + 1) * 64], bank(3, e * 64, e * 64 + 64),
                       yacc[:, gt, e * 64:(e + 1) * 64], ALU.add, [('ps', 3), ('yacc', gt, e)], [('yacc', gt, e)])
                    tt('dve', Sf[d][pb:pb + 64, :], bank(3, 128 + e * 64, 192 + e * 64)[pb:pb + 64, :], Hs[pb:pb + 64, gt, :],
                       ALU.add, [('ps', 3), ('Hs', gt, e)], [('Sf', d, e)])
                else:
                    cp('dve', Sf[d][pb:pb + 64, :], Hs[pb:pb + 64, gt, :], [('Hs', gt, e)], [('Sf', d, e)])
                cp('act', Sb[d][pb:pb + 64, :], Sf[d][pb:pb + 64, :], [('Sf', d, e)], [('Sb', d, e)])
            have_state = True
            yield
        if kind == 'p':
            q = (seg + d) % 2
            tr(bank(3, 256, 384)[0:64, :], Sf[d], identf, [('Sf', d, 0), ('Sf', d, 1), 'identf'], [('ps', 3)])
            cp('act', stg[q][0:64, :], bank(3, 256, 384)[0:64, :], [('ps', 3)], [('stg', q)])
            dma('sp', O['ns'][seg, d, 2 * hp:2 * hp + 2, :, :].rearrange("e v k -> v e k"),
                stg[q][0:64, :].rearrange("p (e k) -> p e k", e=2), [('stg', q)], (), is_out=True)
            yield

    def drive(a, b):
        gens = [g for g in (a, b) if g is not None]
        while gens:
            for g in list(gens):
                try:
                    next(g)
                except StopIteration:
                    gens.remove(g)

    jobno = [0]
    ag_in = nc.dram_tensor("ag_in", [1024, 256], BF16)
    ag_out = nc.dram_tensor("ag_out", [4096, 256], BF16)
    wrs_v = I['wrs'].rearrange("(kc p) n -> p kc n", p=128)
    unit_of = {'p': (hTp, 'xph', 0), 's': (hTs, 'xsh', 1024)}
    tasks = ([('s', 8), ('s', 9)] if NSEQ_R >= 2 else []) + [('p', h_) for h_ in range(NHP)]

    def ci_of(a_, hp):
        return a_ * 8 + hp if hp < 8 else 26 + a_ * 2 + (hp - 8)
    for ti_, (kind, hp) in enumerate(tasks):
        tp = ti_ % 2
        wr = wR[tp]
        for a_, base in enumerate([4096, 4096 + 1024, 4096 + 2048, 4096 + 3328]):
            if hp < 8:
                wsrc = w_in_v[:, :, base + hp * 128: base + (hp + 1) * 128]
            else:
                wsrc = wrs_v[:, :, (hp - 8) * 512 + a_ * 128:(hp - 8) * 512 + (a_ + 1) * 128]
            dma('pool', wr[:, :, a_ * 128:(a_ + 1) * 128], wsrc, (), [('wR', tp, a_)])
        dma('sp', lnxg[tp], I['lnxg'][0:1, hp * 128:(hp + 1) * 128].partition_broadcast(128), (), [('lnxg', tp)])
        dma('sp', lnxb[tp], I['lnxb'][0:1, hp * 128:(hp + 1) * 128].partition_broadcast(128), (), [('lnxb', tp)])
        if ti_ == 2 and tasks[0][0] == 's':
            S.op('pool', lambda e: e.collective_compute("AllGather", ALU.bypass, replica_groups=[[0, 1, 2, 3], [4, 5, 6, 7]],
                                                        ins=[ag_in.ap().opt()], outs=[ag_out.ap().opt()]),
                 [('ag_in', 0), ('ag_in', 1)], ['ag_out'], dma=True, cc=True)
        for (hT, hkey, lc0) in [unit_of[kind]]:
            nt = 8
            for a_ in range(3):
                proj_shift(wr[:, :, a_ * 128:(a_ + 1) * 128], [('wR', tp, a_)], hT, hkey, ci_of(a_, hp), rkv[:, a_, :],
                           ('rkv', a_), kind, 2 * a_)
            rT_, kT_, vT_ = rkv[:, 0, :], rkv[:, 1, :], rkv[:, 2, :]
            for t in range(nt):
                bnk = 6 + (t // 4) % 2
                for kc in range(8):
                    mm(bank(bnk, (t % 4) * 128, (t % 4 + 1) * 128), hT[:, kc, t * 128:(t + 1) * 128],
                       wr[:, kc, 384:512], kc == 0, kc == 7, [(hkey, t), ('wR', tp, 3)], [('ps', bnk)])
                if t % 4 == 3:
                    act(sgb[:, t - 3:t + 1, :], bank(bnk).rearrange("p (a b) -> p a b", b=128), AF.Silu, [('ps', bnk)],
                        [('sgb', q) for q in range(t - 3, t + 1)])
            ts('dve', t1[:, 0:T], kT_[:, 0:T], kkv[:, hp:hp + 1], None, ALU.mult, None, [('rkv', 1), 'kkv'], ['t1'])
            act(sqb[:, 0:T], t1[:, 0:T], AF.Square, ['t1'], ['prodb'])
            for n0 in range(0, T, 512):
                bnk = (n0 // 512) % 2
                mm(bank(bnk), bonesb, sqb[:, n0:n0 + 512], True, True, ['bonesb', 'prodb'], [('ps', bnk)])
                act(kk[:, n0:n0 + 512], bank(bnk), AF.Sqrt, [('ps', bnk), 'cst'], ['kk'], bias=cst[:, 0:1])
            recip(kk[:, 0:T], kk[:, 0:T], ['kk'], ['kk'])
            tt('dve', kk[:, 0:T], kk[:, 0:T], t1[:, 0:T], ALU.mult, ['kk', 't1'], ['kk'])
            stt('dve', prodb[:, 0:T], rT_[:, 0:T], rkv_[:, hp:hp + 1], kT_[:, 0:T], ALU.mult, ALU.mult,
                [('rkv', 0), ('rkv', 1), 'rkv_'], ['prodb'])
            for t in range(nt):
                mm(bank(1, t * 2, t * 2 + 2), prodb[:, t * 128:(t + 1) * 128], hindb, True, True, ['prodb', 'hindb'], [('ps', 1)])
            cp('act', bsum[:, 0:nt * 2], bank(1, 0, nt * 2), [('ps', 1)], ['bsum'])
            cp('act', vb16[:, 0:T], vT_[:, 0:T], [('rkv', 2)], ['vb16'])
            pst2 = bankb(2)
            for t in range(nt):
                tr(pst2[:, t * 128:(t + 1) * 128], vb16[:, t * 128:(t + 1) * 128], identb, ['vb16', 'identb'], [('ps', 2)])
            cp('dve', VT[:, 0:nt, :], pst2[:, 0:nt * 128].rearrange("p (a b) -> p a b", b=128), [('ps', 2)], ['VT'])
            jobs = []
            for d in range(ND):
                segs = list(range(4)) if d == 0 else list(range(3, -1, -1))
                for i_, seg in enumerate(segs):
                    st_in = (kind == 's')
                    jobs.append((d, seg, st_in, (kind == 's' and i_ == 0)))
            pars = []
            for _ in jobs:
                pars.append(jobno[0] % 2)
                jobno[0] += 1
            pg = prep_gen(hp, kind, lc0, jobs[0][0], jobs[0][1], pars[0])
            drive(pg, None)
            for n, (d, seg, st_in, load_s0) in enumerate(jobs):
                if load_s0:
                    dma('sp', s0raw[0:64, :].rearrange("p (e k) -> p e k", e=2),
                        I['s0'][d, 2 * (hp - 8):2 * (hp - 8) + 2, :, :].rearrange("e v k -> v e k"), (), ['s0raw'])
                    tr(bank(3, 0, 64), s0raw[0:64, :], identf[0:64, 0:64], ['s0raw', 'identf'], [('ps', 3)])
                    for e in range(2):
                        pb = 64 * e
                        cp('dve', Sf[d][pb:pb + 64, :], bank(3, 0, 64)[pb:pb + 64, :], [('ps', 3)], [('Sf', d, e)])
                        cp('act', Sb[d][pb:pb + 64, :], bank(3, 0, 64)[pb:pb + 64, :], [('ps', 3)], [('Sb', d, e)])
                rg = rest_gen(hp, kind, d, seg, pars[n], st_in)
                ng = None
                if n + 1 < len(jobs):
                    ng = prep_gen(hp, kind, lc0, jobs[n + 1][0], jobs[n + 1][1], pars[n + 1])
                drive(rg, ng)
            n2 = nt * 2
            yk = [('yacc', t, e) for t in range(nt) for e in range(2)]
            yv = yacc[:, 0:nt, :].rearrange("p a (e f) -> p (a e) f", e=2)
            red(gst[:, 0, 0:n2], yv, ALU.add, yk, [('gst', 0)])
            act(ysq[:, 0:nt, :], yacc[:, 0:nt, :], AF.Square, yk, ['t1'])
            red(gst[:, 1, 0:n2], ysq[:, 0:nt, :].rearrange("p a (e f) -> p (a e) f", e=2), ALU.add, ['t1'], [('gst', 1)])
            ts('dve', gst[:, 2, 0:n2], gst[:, 0, 0:n2], 1.0 / 64, None, ALU.mult, None, [('gst', 0)], [('gst', 2)])
            tt('dve', gst[:, 3, 0:n2], gst[:, 2, 0:n2], gst[:, 2, 0:n2], ALU.mult, [('gst', 2)], [('gst', 3)])
            stt('dve', gst[:, 4, 0:n2], gst[:, 1, 0:n2], 1.0 / 64, gst[:, 3, 0:n2], ALU.mult, ALU.subtract,
                [('gst', 1), ('gst', 3)], [('gst', 4)])
            ts('dve', gst[:, 4, 0:n2], gst[:, 4, 0:n2], 64e-5, None, ALU.add, None, [('gst', 4)], [('gst', 4)])
            act(gst[:, 5, 0:n2], gst[:, 4, 0:n2], AF.Sqrt, [('gst', 4)], [('gst', 5)])
            recip(gst[:, 6, 0:n2], gst[:, 5, 0:n2], [('gst', 5)], [('gst', 6)])
            ysv = ysq[:, 0:nt, :].rearrange("p a (e f) -> p (a e) f", e=2)
            tt('dve', ysv, yv, gst[:, 2, 0:n2].unsqueeze(2).to_broadcast([128, n2, 64]), ALU.subtract, yk + [('gst', 2)], ['t1'])
            tt('dve', ysv, ysv, gst[:, 6, 0:n2].unsqueeze(2).to_broadcast([128, n2, 64]), ALU.mult, ['t1', ('gst', 6)], ['t1'])
            tt('dve', ysq[:, 0:nt, :], ysq[:, 0:nt, :], lnxg[tp].unsqueeze(1).to_broadcast([128, nt, 128]),
               ALU.mult, ['t1', ('lnxg', tp)], ['t1'])
            tt('dve', ysq[:, 0:nt, :], ysq[:, 0:nt, :], lnxb[tp].unsqueeze(1).to_broadcast([128, nt, 128]),
               ALU.add, ['t1', ('lnxb', tp)], ['t1'])
            tt('dve', ybon[:, 0:nt, :].rearrange("p a (e f) -> p (a e) f", e=2),
               VT[:, 0:nt, :].rearrange("p a (e f) -> p (a e) f", e=2),
               bsum[:, 0:n2].unsqueeze(2).to_broadcast([128, n2, 64]), ALU.mult, ['VT', 'bsum'], ['kk'])
            tt('dve', ysq[:, 0:nt, :], ysq[:, 0:nt, :], ybon[:, 0:nt, :], ALU.add, ['t1', 'kk'], ['t1'])
            tt('dve', obR[:, 0:nt, :], ysq[:, 0:nt, :], sgb[:, 0:nt, :], ALU.mult, ['t1'] + [('sgb', t) for t in range(nt)], ['obR'])
            if kind == 'p':
                pst2 = bankb(2)
                for t in range(nt):
                    tr(pst2[:, t * 128:(t + 1) * 128], obR[:, t, :], identb, ['obR', 'identb'], [('ps', 2)])
                cp('act', mixT[:, 8 + hp, 0:1024], pst2[:, 0:1024], [('ps', 2)], [('mixT', 8 + hp, g) for g in range(8)])
            else:
                i_ = hp - 8
                dma('sp', ag_in.ap().rearrange("(t p) (i f) -> p t i f", p=128, i=2)[:, :, i_, :], obR[:, 0:8, :], ['obR'],
                    [('ag_in', i_)])
    if DEBUG:
        dump_all(dict(yacc=yacc, Sf0=Sf[0], GT=GT, Hs=Hs))
    if STOP == 'R':
        return
    S.barrier()
    A.off = mA

    Gt = A.alloc([32, 256], BF16)
    dma('sp', Gt, ag_out.ap().rearrange("(rt p) f -> p rt f", p=128), ['ag_out'], ['Gt'])
    for r_ in range(4):
        for i_ in range(2):
            hq = 2 * r_ + i_
            bq = 4 + (hq % 4)
            for t in range(8):
                mm(bank(bq, 0, 256), Gt[:, r_ * 8 + t, i_ * 128:(i_ + 1) * 128], selb[:, t, :], t == 0, t == 7,
                   ['Gt', 'selb'], [('ps', bq)])
            cp('act', mixT[:, 8 + hq, 1024:1280], bank(bq, 0, 256), [('ps', bq)], [('mixT', 8 + hq, 8), ('mixT', 8 + hq, 9)])
    S.barrier()
    A.off = mA
    wout = A.alloc([16, 1024], BF16)
    fgbc = A.alloc([1024], F32)
    gatebc = A.alloc([2, 1024], F32)
    bgbc = A.alloc([1024], F32)
    scbc = A.alloc([8, 2, 128], BF16)
    wadg = A.alloc([8, 1024], BF16)
    xr = [A.alloc([1024], F32) for _ in range(2)]
    yv_ = [A.alloc([1024], F32) for _ in range(2)]
    ojunk = A.alloc([1024], BF16)
    ost = [A.alloc([4], F32) for _ in range(2)]
    wout_v = I['w_out'].rearrange("(c p) n -> p c n", p=128)
    wada_v = I['w_ada'].rearrange("(kc p) n -> p kc n", p=128)
    for n in range(2):
        dma('pool', wadg[:, :, n * 512:(n + 1) * 512], wada_v[:, :, 2048 + n * 512:2048 + (n + 1) * 512], (), [('wadg', n)])
    for c4 in range(4):
        dma('pool', wout[:, c4 * 4:(c4 + 1) * 4, :], wout_v[:, c4 * 4:(c4 + 1) * 4, :], (), [('wout', c4)])
    dma('sp', fgbc, I['fg'][0:1, :].partition_broadcast(128), (), ['fgbc'])
    dma('sp', bgbc, I['bgate'][0:1, :].partition_broadcast(128), (), ['bgbc'])
    cp('dve', scbc, scp.unsqueeze(3).to_broadcast([128, 8, 2, 128]), ['scp'], ['scbc'])
    for v in range(2):
        for n in range(2):
            for kc in range(8):
                mm(bank(2 + n), scbc[:, kc, v, :], wadg[:, kc, n * 512:(n + 1) * 512],
                   kc == 0, kc == 7, ['scbc', ('wadg', n)], [('ps', 2 + n)])
            tt('dve', gatebc[:, v, n * 512:(n + 1) * 512], bank(2 + n), bgbc[:, n * 512:(n + 1) * 512], ALU.add,
               [('ps', 2 + n), 'bgbc'], [('gatebc', v, n)])
    otiles = [('xp', g, 'yp', 0) for g in range(8)] + [('xo', g, 'ys', 1) for g in range(2)]
    for ti, (src, g, dst, v) in enumerate(otiles):
        b = ti % 2
        mg = g if src == 'xp' else 8 + g
        dma('sp', xr[b], I[src][g * 128:(g + 1) * 128, :], (), [('xr', b)])
        for n in range(2):
            for c in range(16):
                mm(bank(n), mixT[:, c, mg * 128:(mg + 1) * 128], wout[:, c, n * 512:(n + 1) * 512], c == 0, c == 15,
                   [('mixT', c, mg), ('wout', c // 4)], [('ps', n)])
            tt('dve', yv_[b][:, n * 512:(n + 1) * 512], bank(n), gatebc[:, v, n * 512:(n + 1) * 512], ALU.mult,
               [('ps', n), ('gatebc', v, n)], [('yv', b, n)])
            tt('pool', yv_[b][:, n * 512:(n + 1) * 512], yv_[b][:, n * 512:(n + 1) * 512], xr[b][:, n * 512:(n + 1) * 512], ALU.add,
               [('yv', b, n), ('xr', b)], [('yv', b, n)])
        act(ojunk, yv_[b], AF.Square, [('yv', b, 0), ('yv', b, 1)], ['ojunk', ('ost', b)], accum=ost[b][:, 0:1])
        ts('dve', ost[b][:, 1:2], ost[b][:, 0:1], 1.0 / 1024, 1e-6, ALU.mult, ALU.add, [('ost', b)], [('ostb', b)])
        act(ost[b][:, 2:3], ost[b][:, 1:2], AF.Sqrt, [('ostb', b)], [('ostc', b)])
        recip(ost[b][:, 3:4], ost[b][:, 2:3], [('ostc', b)], [('ostd', b)])
        stt('dve', xr[b], yv_[b], ost[b][:, 3:4], fgbc, ALU.mult, ALU.mult, [('yv', b, 0), ('yv', b, 1), ('ostd', b), 'fgbc', ('xr', b)], [('xr', b)])
        dma('sp', O[dst][g * 128:(g + 1) * 128, :], xr[b], [('xr', b)], (), is_out=True)


_NC = None


def _rope_tab(pos_rows, pos_cols):
    n_freq = 16
    inv = (10000.0 ** (-np.arange(n_freq, dtype=np.float32) / n_freq)).astype(np.float32)
    T = len(pos_rows)
    tab = np.zeros((T, 256), np.float32)
    for s_, pos in enumerate([pos_rows, pos_cols]):
        ang = pos.astype(np.float32)[:, None] * inv[None, :]
        c, sn = np.cos(ang).astype(np.float32), np.sin(ang).astype(np.float32)
        for m in range(2):
            for hf in range(2):
                o = m * 64 + s_ * 32 + hf * 16
                tab[:, o:o + 16] = c
                tab[:, 128 + o:128 + o + 16] = -sn if hf == 0 else sn
    return tab


def kernel(x_prompt, x_sample, cache_k, cache_v, state_rwkv, c, c_ctx, norm_g, w_ada, b_ada,
           w_in, lam_q1, lam_k1, lam_q2, lam_k2, subln_g, shift_mu, decay_w0, decay_w2,
           iclr_a0, iclr_a2, k_k, k_a, r_k, lnx_g, lnx_b, w_out, final_g):
    global _NC
    f = lambda a: np.ascontiguousarray(np.asarray(a, dtype=np.float32))
    x_prompt, x_sample, cache_k, cache_v, state_rwkv = map(f, (x_prompt, x_sample, cache_k, cache_v, state_rwkv))
    c, c_ctx = f(c), f(c_ctx)
    if _NC is None:
        _NC = build()
    nc = _NC

    def fm(v, nch):
        return np.ascontiguousarray(f(v).reshape(nch, 128).T)

    i = np.arange(128)
    su = (i[:, None] < i[None, :]).astype(np.float32)
    ui = (i[:, None] <= i[None, :]).astype(np.float32)
    sl = (i[:, None] > i[None, :]).astype(np.float32)
    li = (i[:, None] >= i[None, :]).astype(np.float32)
    maskNM = np.stack([np.concatenate([su, ui, su, ui], 1), np.concatenate([sl, li, sl, li], 1)])
    maskT = np.stack([sl, su])
    bones = np.kron(np.eye(2, dtype=np.float32), np.ones((64, 64), np.float32))
    hind = np.kron(np.eye(2, dtype=np.float32), np.ones((64, 1), np.float32))
    tok = np.arange(1024)
    ropeall = _rope_tab(tok // 64, tok % 64)
    W_in = f(w_in)[0]
    smu, sw0, sa0 = f(shift_mu)[0], f(decay_w0)[0], f(iclr_a0)[0]
    sw2, sa2 = f(decay_w2)[0].reshape(128, 1024), f(iclr_a2)[0].reshape(128, 1024)
    skk, ska, srk = f(k_k)[0], f(k_a)[0], f(r_k)[0].reshape(-1)
    slg, slb = f(lnx_g)[0], f(lnx_b)[0]
    shared = {
        "w_in": W_in, "w_ada": f(w_ada)[0], "w_out": f(w_out)[0],
        "bada_fm": fm(f(b_ada)[0], 24), "bgate": f(b_ada)[0:1, 2048:3072], "normg_fm": fm(f(norm_g)[0], 8),
        "fg": f(final_g)[None, :],
        "lamv": np.concatenate([f(lam_q1)[0], f(lam_k1)[0], f(lam_q2)[0], f(lam_k2)[0]])[None, :],
        "sublng": f(subln_g)[0:1],
        "ident": np.eye(128, dtype=np.float32), "maskNM": maskNM, "maskT": maskT, "bones": bones, "hind": hind,
        "ropeall": ropeall,
    }

    def cols(v, hp):
        return v[hp * 128:(hp + 1) * 128]
    in_maps = []
    for core in range(8):
        b, q = core // 4, core % 4
        hps = list(range(8)) + [2 * q, 2 * q + 1]
        sel = np.zeros((1024, 256), np.float32)
        sel[q * 256 + np.arange(256), np.arange(256)] = 1.0
        cT = np.stack([fm(c_ctx, 8), fm(c[b], 8)], -1).reshape(128, 16)
        chunks = [smu[:, ci * 128:(ci + 1) * 128] for ci in range(26)]
        for a_ in range(3):
            for i_ in range(2):
                ci = a_ * 8 + hps[8 + i_]
                chunks.append(smu[:, ci * 128:(ci + 1) * 128])
        mu_fm = np.stack([np.stack([ch[0], ch[1]], -1) for ch in chunks], 1).reshape(128, 64)
        w0_fm = np.stack([np.stack([cols(sw0[0], h_), cols(sw0[1], h_)], -1) for h_ in hps], 1).reshape(128, 20)
        a0_fm = np.stack([np.stack([cols(sa0[0], h_), cols(sa0[1], h_)], -1) for h_ in hps], 1).reshape(128, 20)
        ext = lambda v: np.stack([cols(v, h_) for h_ in hps], 1)
        extc = lambda m: np.concatenate([m[:, h_ * 128:(h_ + 1) * 128] for h_ in hps], 1)
        wrs = np.concatenate([W_in[:, base + h_ * 128: base + (h_ + 1) * 128]
                              for h_ in hps[8:] for base in (4096, 4096 + 1024, 4096 + 2048, 4096 + 3328)], 1)
        m = dict(shared)
        m.update({
            "xp": x_prompt[core * 4:(core + 1) * 4].reshape(1024, 1024),
            "xs": x_sample[b], "xo": x_sample[b, q * 256:(q + 1) * 256],
            "ck": cache_k[b, 0].reshape(512, 1024), "cv": cache_v[b, 0].reshape(512, 1024),
            "s0": state_rwkv[b, 0][:, 4 * q:4 * q + 4], "cT": np.ascontiguousarray(cT), "selT": sel,
            "ropeown": np.ascontiguousarray(ropeall[q * 256:(q + 1) * 256]),
            "mu_fm": mu_fm, "w0_fm": w0_fm, "a0_fm": a0_fm, "w2": extc(sw2), "a2": extc(sa2),
            "kk_fm": ext(skk), "ka_fm": ext(ska), "rk_fm": ext(srk),
            "lnxg": extc(slg[None, :]), "lnxb": extc(slb[None, :]), "wrs": wrs,
        })
        in_maps.append({k: np.ascontiguousarray(v, dtype=np.float32) for k, v in m.items()})
    res = run_bass_kernel_spmd(nc, in_maps, core_ids=list(range(8)))
    R = res.results
    y_prompt = np.concatenate([R[i]["yp"].reshape(4, 256, 1024) for i in range(8)], 0)
    y_sample = np.stack([np.concatenate([R[b * 4 + q]["ys"] for q in range(4)], 0) for b in range(2)], 0)
    new_k = np.concatenate([R[i]["nk"].reshape(4, 1, 256, 8, 2, 64) for i in range(8)], 0)
    new_v = np.concatenate([R[i]["nv"].reshape(4, 1, 256, 8, 128) for i in range(8)], 0)
    new_s = np.concatenate([R[i]["ns"].reshape(4, 1, 2, 16, 64, 64) for i in range(8)], 0)
    return (y_prompt.astype(np.float32), y_sample.astype(np.float32), new_k.astype(np.float32),
            new_v.astype(np.float32), new_s.astype(np.float32))
```

```python
import math
from contextlib import ExitStack
import numpy as np
import concourse.bass as bass
import concourse.mybir as mybir
from concourse.bass_utils import run_bass_kernel_spmd

F32 = mybir.dt.float32
F32R = mybir.dt.float32r
LEVEL_DT = F32
BF16 = mybir.dt.bfloat16
AF = mybir.ActivationFunctionType
ALU = mybir.AluOpType
AX = mybir.AxisListType

ENG = ['pe', 'dve', 'act', 'pool', 'sp']
NDMA = 40
DC = math.exp(-0.5)


class Sched:
    def __init__(s, nc, stack):
        s.nc = nc
        s.ops = {e: [] for e in ENG}
        s.cnt = {e: 0 for e in ENG}
        s.known = {e: {f: 0 for f in ENG} for e in ENG}
        s.snap = {e: [] for e in ENG}
        s.kdma = {e: {} for e in ENG}
        s.lastw = {}
        s.rd_e = {}
        s.rd_d = {}
        s.sems = {e: stack.enter_context(nc.semaphore('c_' + e)) for e in ENG}
        s.dsems = [stack.enter_context(nc.semaphore('d%d' % i)) for i in range(2 * NDMA)]
        s.dcnt = {'sp': 0, 'pool': 0, 'act': 0}
        s.dcount = 0
        s.dlast = {}
        s.out_events = []

    def op(s, eng, fn, r=(), w=(), dma=False, is_out=False, noinc=False, cc=False):
        r = [(k[0], k[1]) if (isinstance(k, tuple) and k[0] == 'ps') else k for k in r]
        w = [(k[0], k[1]) if (isinstance(k, tuple) and k[0] == 'ps') else k for k in w]
        psr = [k for k in r if isinstance(k, tuple) and k[0] == 'ps']
        if psr:
            r = [k for k in r if not (isinstance(k, tuple) and k[0] == 'ps')]
            w = list(w) + [k for k in psr if k not in w]
        deps = []
        for k in r:
            ev = s.lastw.get(k)
            if ev is not None:
                deps.append((ev, True))
        for k in w:
            ev = s.lastw.get(k)
            if ev is not None:
                deps.append((ev, False))
            for f, c in s.rd_e.get(k, {}).items():
                deps.append((('E', f, c), True))
            for ev in s.rd_d.get(k, ()):
                deps.append((ev, False))
        waits = {}
        for ev, raw in deps:
            if ev[0] == 'E':
                _, f, c = ev
                if f == eng and not dma:
                    if (not raw) or eng == 'pe':
                        continue
                if s.known[eng][f] >= c:
                    continue
                waits[('E', f)] = max(waits.get(('E', f), 0), c)
            else:
                _, si, v = ev
                if s.kdma[eng].get(si, 0) >= v:
                    continue
                waits[('D', si)] = max(waits.get(('D', si), 0), v)
        if cc:
            ev = ('D', 'cc', 1)
            s.dlast['cc'] = 1
        elif dma:
            qn = s.dcnt[eng]
            si = qn % NDMA + (NDMA if eng == 'pool' else 0)
            v = 16 * (qn // NDMA + 1)
            if qn >= NDMA and s.kdma[eng].get(si, 0) < v - 16:
                waits[('D', si)] = max(waits.get(('D', si), 0), v - 16)
            s.dcnt[eng] += 1
            s.dcount += 1
            s.dlast[si] = v
            ev = ('D', si, v)
        for (t, x), v in waits.items():
            if t == 'E':
                kn = s.known[eng]
                if kn[x] < v:
                    kn[x] = v
                sn = s.snap[x][v - 1]
                for f2, c2 in sn.items():
                    if kn[f2] < c2:
                        kn[f2] = c2
            else:
                s.kdma[eng][x] = v
        if not (dma or cc):
            if noinc:
                ev = ('E', eng, s.cnt[eng] + 1)
            else:
                s.cnt[eng] += 1
                ev = ('E', eng, s.cnt[eng])
                s.snap[eng].append(dict(s.known[eng]))
        s.ops[eng].append((list(waits.items()), fn, None if noinc else ev))
        for k in r:
            if ev[0] == 'E':
                s.rd_e.setdefault(k, {})[eng] = ev[2]
            else:
                s.rd_d.setdefault(k, []).append(ev)
        for k in w:
            s.lastw[k] = ev
            s.rd_e[k] = {}
            s.rd_d[k] = []
        if is_out:
            s.out_events.append(ev)
        return ev

    def barrier(s):
        waits = {}
        for f in ENG:
            if f != 'sp' and s.cnt[f] > 0:
                waits[('E', f)] = s.cnt[f]
        for si, v in s.dlast.items():
            waits[('D', si)] = v
        s.cnt['sp'] += 1
        c = s.cnt['sp']
        for f in ENG:
            if f != 'sp':
                s.known['sp'][f] = s.cnt[f]
        s.kdma['sp'] = dict(s.dlast)
        s.snap['sp'].append(dict(s.known['sp']))
        s.ops['sp'].append((list(waits.items()), (lambda e: e.nop()), ('E', 'sp', c)))
        for f in ENG:
            if f == 'sp':
                continue
            s.ops[f].append(([(('E', 'sp'), c)], None, None))
            for g in ENG:
                if g != f:
                    s.known[f][g] = max(s.known[f][g], s.cnt[g])
            s.kdma[f] = dict(s.dlast)
        s.lastw.clear()
        s.rd_e.clear()
        s.rd_d.clear()

    def finish(s):
        waits = {}
        for ev in s.out_events:
            waits[('D', ev[1])] = max(waits.get(('D', ev[1]), 0), ev[2])
        s.ops['sp'].append((list(waits.items()), None, None))

    def emit(s, block):
        def mk(engname):
            def body(e):
                for waits, fn, ev in s.ops[engname]:
                    for (t, x), v in waits:
                        sem = s.sems[x] if t == 'E' else (s.ccsem if x == 'cc' else s.dsems[x])
                        e.wait_ge(sem, v)
                    if fn is None:
                        continue
                    ins = fn(e)
                    if ev is None:
                        continue
                    if ev[0] == 'E':
                        ins.then_inc(s.sems[engname], 1)
                    elif ev[1] == 'cc':
                        ins.then_inc(s.ccsem, 1)
                    else:
                        ins.then_inc(s.dsems[ev[1]], 16)
            return body
        block.tensor(mk('pe'))
        block.vector(mk('dve'))
        block.scalar(mk('act'))
        block.gpsimd(mk('pool'))
        block.sync(mk('sp'))


class Arena:
    def __init__(s, ar, nwords):
        s.ar = ar
        s.off = 0
        s.n = nwords
        s.peak = 0

    def alloc(s, shape, dt):
        n = 1
        for x in shape:
            n *= x
        words = n if dt == F32 else (n + 1) // 2
        words = (words + 7) // 8 * 8
        a = s.ar[:, s.off:s.off + words]
        s.off += words
        s.peak = max(s.peak, s.off)
        assert s.off <= s.n, ("arena overflow", s.off, s.n)
        if dt == BF16:
            a = a.bitcast(BF16)
        a = a[:, 0:n]
        if len(shape) == 2:
            a = a.rearrange("p (a b) -> p a b", b=shape[1])
        elif len(shape) == 3:
            a = a.rearrange("p (a b c) -> p a b c", b=shape[1], c=shape[2])
        elif len(shape) == 4:
            a = a.rearrange("p (a b c d) -> p a b c d", b=shape[1], c=shape[2], d=shape[3])
        return a


IN_SPECS = [
    ("xp", [1024, 1024]), ("xs", [1024, 1024]), ("xo", [256, 1024]),
    ("ck", [512, 1024]), ("cv", [512, 1024]), ("s0", [2, 4, 64, 64]),
    ("cT", [128, 16]), ("w_in", [1024, 8448]), ("w_ada", [1024, 3072]), ("w_out", [2048, 1024]),
    ("bada_fm", [128, 24]), ("bgate", [1, 1024]), ("normg_fm", [128, 8]), ("fg", [1, 1024]),
    ("lamv", [1, 256]), ("sublng", [1, 128]), ("mu_fm", [128, 64]), ("w0_fm", [128, 20]),
    ("a0_fm", [128, 20]), ("w2", [128, 1280]), ("a2", [128, 1280]), ("kk_fm", [128, 10]),
    ("ka_fm", [128, 10]), ("rk_fm", [128, 10]), ("lnxg", [1, 1280]), ("lnxb", [1, 1280]), ("wrs", [1024, 1024]),
    ("ident", [128, 128]), ("maskNM", [2, 128, 512]), ("maskT", [2, 128, 128]),
    ("bones", [128, 128]), ("hind", [128, 2]), ("selT", [1024, 256]),
    ("ropeall", [1024, 256]), ("ropeown", [256, 256]),
]
OUT_SPECS = [
    ("yp", [1024, 1024]), ("ys", [256, 1024]), ("nk", [1024, 1024]), ("nv", [1024, 1024]),
    ("ns", [4, 2, 16, 64, 64]),
]

ARENA_WORDS = 52480 - 4096
LVL_WORDS = 4096
NHEAD_A = 8
NHP = 8
NSEQ_R = 5
ND = 2
DEBUG = False
A_MODE = 'all'
DUMPS = []
STOP = None


def build():
    nc = bass.Bass("TRN2", target_bir_lowering=False)
    I = {n: nc.dram_tensor(n, sh, F32, kind="ExternalInput").ap() for n, sh in IN_SPECS}
    O = {n: nc.dram_tensor(n, sh, F32, kind="ExternalOutput").ap() for n, sh in OUT_SPECS}
    with ExitStack() as stack:
        ar = stack.enter_context(nc.sbuf_tensor("arena", [128, ARENA_WORDS], F32))
        PS = stack.enter_context(nc.psum_tensor("ps", [128, 4096], F32))
        lvl = stack.enter_context(nc.sbuf_tensor("lvl", [128, LVL_WORDS], LEVEL_DT))
        S = Sched(nc, stack)
        S.ccsem = stack.enter_context(nc.semaphore('ccsem'))
        A = Arena(ar, ARENA_WORDS)
        block = stack.enter_context(nc.Block())
        _program(nc, S, A, PS, I, O, lvl)
        S.finish()
        S.emit(block)
    return nc


def _program(nc, S, A, PS, I, O, lvl):
    def dma(eng, out, in_, r, w, is_out=False):
        S.op(eng, lambda e: e.dma_start(out=out, in_=in_), r, w, dma=True, is_out=is_out)

    def mm(out, lhsT, rhs, start, stop, r, w):
        S.op('pe', lambda e: e.matmul(out, lhsT, rhs, start=start, stop=stop), r, w, noinc=(not stop))

    def tr(out, in_, ident, r, w):
        S.op('pe', lambda e: e.transpose(out, in_, ident), r, w)

    def act(out, in_, func, r, w, bias=None, scale=None, accum=None):
        def f(e):
            kw = {}
            if bias is not None:
                kw['bias'] = bias
            if scale is not None:
                kw['scale'] = scale
            if accum is not None:
                kw['accum_out'] = accum
            return e.activation(out=out, in_=in_, func=func, **kw)
        S.op('act', f, r, w)

    def tt(eng, out, in0, in1, op, r, w):
        S.op(eng, lambda e: e.tensor_tensor(out=out, in0=in0, in1=in1, op=op), r, w)

    def ts(eng, out, in0, s1, s2, op0, op1, r, w):
        if s2 is None:
            S.op(eng, lambda e: e.tensor_scalar(out=out, in0=in0, scalar1=s1, scalar2=None, op0=op0), r, w)
        else:
            S.op(eng, lambda e: e.tensor_scalar(out=out, in0=in0, scalar1=s1, scalar2=s2, op0=op0, op1=op1), r, w)

    def stt(eng, out, in0, sc, in1, op0, op1, r, w):
        S.op(eng, lambda e: e.scalar_tensor_tensor(out=out, in0=in0, scalar=sc, in1=in1, op0=op0, op1=op1), r, w)

    def cp(eng, out, in_, r, w):
        if eng == 'act':
            act(out, in_, AF.Identity, r, w)
        else:
            S.op(eng, lambda e: e.tensor_copy(out=out, in_=in_), r, w)

    def red(out, in_, op, r, w):
        S.op('dve', lambda e: e.tensor_reduce(out=out, in_=in_, axis=AX.X, op=op), r, w)

    def recip(out, in_, r, w):
        S.op('dve', lambda e: e.reciprocal(out=out, in_=in_), r, w)

    def memset(eng, out, val, w):
        S.op(eng, lambda e: e.memset(out, val), (), w)

    def bank(b, c0=0, c1=512):
        return PS[:, b * 512 + c0: b * 512 + c1]

    def bankb(b):
        return PS[:, b * 512:(b + 1) * 512].bitcast(BF16)

    w_in_v = I['w_in'].rearrange("(kc p) n -> p kc n", p=128)

    def dump_all(bufs):
        S.barrier()
        for name, ap in bufs.items():
            sh = list(ap.shape)
            dt = nc.dram_tensor('dbg_' + name, sh, ap.dtype, kind="ExternalOutput").ap()
            DUMPS.append('dbg_' + name)
            dma('sp', dt, ap, (), (), is_out=True)

    identf = A.alloc([128], F32)
    identb = A.alloc([128], BF16)
    onesf = A.alloc([128], F32)
    maskNM = A.alloc([2, 512], BF16)
    maskT = A.alloc([2, 128], BF16)
    bonesb = A.alloc([128], BF16)
    hindb = A.alloc([2], BF16)
    selb = A.alloc([8, 256], BF16)
    cst = A.alloc([4], F32)
    hTp = A.alloc([8, 1024], BF16)
    hTs = A.alloc([8, 1024], BF16)
    hTo = A.alloc([8, 256], BF16)
    mixT = A.alloc([16, 1280], BF16)
    modfm = A.alloc([24, 2], F32)
    scale1 = A.alloc([8, 2], F32)
    neglam = A.alloc([1], F32)
    sgl = A.alloc([128], F32)
    mu = A.alloc([32, 2], F32)
    c0v = A.alloc([32], F32)
    w0v = A.alloc([20], F32)
    a0v = A.alloc([20], F32)
    kkv = A.alloc([10], F32)
    kav = A.alloc([10], F32)
    omka = A.alloc([10], F32)
    rkv_ = A.alloc([10], F32)
    W2b = A.alloc([1280], BF16)
    A2b = A.alloc([1280], BF16)
    twT = A.alloc([2048], BF16)
    laT = A.alloc([2048], BF16)

    dma('sp', identf, I['ident'], (), ['identf'])
    dma('pool', identb, I['ident'], (), ['identb'])
    dma('pool', maskNM, I['maskNM'].rearrange("d p n -> p d n"), (), ['maskNM'])
    dma('pool', maskT, I['maskT'].rearrange("d p n -> p d n"), (), ['maskT'])
    dma('pool', bonesb, I['bones'], (), ['bonesb'])
    dma('pool', hindb, I['hind'], (), ['hindb'])
    dma('pool', selb, I['selT'].rearrange("(j p) n -> p j n", p=128), (), ['selb'])
    dma('sp', mu, I['mu_fm'].rearrange("p (c j) -> p c j", j=2), (), ['mu'])
    dma('sp', w0v, I['w0_fm'], (), ['w0v'])
    dma('sp', a0v, I['a0_fm'], (), ['a0v'])
    dma('sp', kkv, I['kk_fm'], (), ['kkv'])
    dma('sp', kav, I['ka_fm'], (), ['kav'])
    dma('sp', rkv_, I['rk_fm'], (), ['rkv_'])
    dma('pool', W2b, I['w2'], (), ['W2b'])
    dma('pool', A2b, I['a2'], (), ['A2b'])
    memset('dve', onesf, 1.0, ['onesf'])
    memset('dve', cst[:, 0:1], 1e-12, ['cst'])
    tt('dve', c0v, mu[:, :, 0], mu[:, :, 1], ALU.add, ['mu'], ['c0v'])
    ts('dve', c0v, c0v, -1.0, 1.0, ALU.mult, ALU.add, ['c0v'], ['c0v'])
    ts('dve', omka, kav, -1.0, 1.0, ALU.mult, ALU.add, ['kav'], ['omka'])

    scp = A.alloc([8, 2], BF16)
    m0 = A.off
    cT = A.alloc([16], F32)
    sc = A.alloc([8, 2], F32)
    wadaf = [A.alloc([8, 512], F32) for _ in range(4)]
    bada = A.alloc([24], F32)
    normg = A.alloc([8], F32)
    lamt = A.alloc([4, 64], F32)
    lamp = A.alloc([2, 64], F32)
    lams = A.alloc([4], F32)

    dma('sp', cT, I['cT'], (), ['cT'])
    dma('sp', bada, I['bada_fm'], (), ['bada'])
    dma('sp', normg, I['normg_fm'], (), ['normg'])
    dma('sp', lamt.rearrange("p a b -> p (a b)"), I['lamv'][0:1, :].partition_broadcast(128), (), ['lamt'])
    dma('sp', sgl, I['sublng'][0:1, :].partition_broadcast(128), (), ['sgl'])
    wada_v = I['w_ada'].rearrange("(kc p) n -> p kc n", p=128)
    for n in range(4):
        dma('sp', wadaf[n], wada_v[:, :, n * 512:(n + 1) * 512], (), [('wada', n)])
    act(sc, cT.rearrange("p (c v) -> p c v", v=2), AF.Silu, ['cT'], ['sc'])
    cp('dve', scp, sc, ['sc'], ['scp'])
    for fc in range(16):
        for kc in range(8):
            mm(bank(0, fc * 2, fc * 2 + 2), wadaf[fc // 4][:, kc, (fc % 4) * 128:(fc % 4 + 1) * 128], sc[:, kc, :],
               kc == 0, kc == 7, ['sc', ('wada', fc // 4)], [('ps', 0)])
    tt('dve', modfm[:, 0:16, :], bank(0, 0, 32).rearrange("p (a b) -> p a b", b=2),
       bada[:, 0:16].unsqueeze(2).to_broadcast([128, 16, 2]), ALU.add, [('ps', 0), 'bada'], ['modfm'])
    ts('dve', scale1, modfm[:, 8:16, :], 1.0, None, ALU.add, None, ['modfm'], ['scale1'])
    tt('dve', scale1, scale1, normg.unsqueeze(2).to_broadcast([128, 8, 2]), ALU.mult, ['scale1', 'normg'], ['scale1'])
    tt('dve', lamp[:, 0, :], lamt[:, 0, :], lamt[:, 1, :], ALU.mult, ['lamt'], ['lamp'])
    tt('dve', lamp[:, 1, :], lamt[:, 2, :], lamt[:, 3, :], ALU.mult, ['lamt', 'lamp'], ['lamp'])
    red(lams[:, 0:2], lamp, ALU.add, ['lamp'], ['lams'])
    act(lams[:, 2:4], lams[:, 0:2], AF.Exp, ['lams'], ['lams2'])
    lam_init = 0.8 - 0.6 * math.exp(-0.3 * 0)
    tt('dve', neglam, lams[:, 3:4], lams[:, 2:3], ALU.subtract, ['lams2'], ['neglam'])
    ts('dve', neglam, neglam, -lam_init, None, ALU.add, None, ['neglam'], ['neglam'])
    ts('dve', sgl, sgl, 1.0 - lam_init, None, ALU.mult, None, ['sgl'], ['sgl'])

    if STOP == '0':
        return
    xt = [A.alloc([1024], F32) for _ in range(2)]
    xn = [A.alloc([1024], BF16) for _ in range(2)]
    junk = A.alloc([1024], BF16)
    st1 = [A.alloc([4], F32) for _ in range(2)]
    tiles = [('xp', g, hTp, g, 0) for g in range(8)] + [('xs', g, hTs, g, 1) for g in range(8)] + \
            [('xo', g, hTo, g, 1) for g in range(2)]
    for ti, (src, g, hT, tg, v) in enumerate(tiles):
        b = ti % 2
        dma('sp', xt[b], I[src][g * 128:(g + 1) * 128, :], (), [('xt', b)])
        act(junk, xt[b], AF.Square, [('xt', b)], ['junk', ('st1', b)], accum=st1[b][:, 0:1])
        ts('dve', st1[b][:, 1:2], st1[b][:, 0:1], 1.0 / 1024, 1e-6, ALU.mult, ALU.add, [('st1', b)], [('st1b', b)])
        act(st1[b][:, 2:3], st1[b][:, 1:2], AF.Sqrt, [('st1b', b)], [('st1c', b)])
        recip(st1[b][:, 3:4], st1[b][:, 2:3], [('st1c', b)], [('st1d', b)])
        ts('dve', xn[b], xt[b], st1[b][:, 3:4], None, ALU.mult, None, [('xt', b), ('st1d', b)], [('xn', b)])
        pb_ = bankb(3 + b)
        for kc in range(8):
            tr(pb_[:, kc * 128:(kc + 1) * 128], xn[b][:, kc * 128:(kc + 1) * 128], identb,
               [('xn', b), 'identb'], [('ps', 3 + b)])
        for kc in range(8):
            act(hT[:, kc, tg * 128:(tg + 1) * 128], pb_[:, kc * 128:(kc + 1) * 128], AF.Identity,
                [('ps', 3 + b), 'scale1', 'modfm'], [(src + 'h', tg)],
                bias=modfm[:, kc, v:v + 1], scale=scale1[:, kc, v:v + 1])
    if STOP == '1':
        return
    S.barrier()
    A.off = m0
    if STOP == '1b':
        return

    mA = A.off
    wA = [A.alloc([8, 512], BF16) for _ in range(2)]
    qkb = [A.alloc([256], BF16) for _ in range(2)]
    kvf = [A.alloc([256], F32) for _ in range(2)]
    qT2 = [A.alloc([256], BF16) for _ in range(2)]
    sg2 = [A.alloc([2, 128], F32) for _ in range(2)]
    kTp = [A.alloc([256], BF16) for _ in range(2)]
    vbp = [A.alloc([2, 128], BF16) for _ in range(2)]
    kTs = A.alloc([1536], BF16)
    vbs = A.alloc([12, 128], BF16)
    ckb = A.alloc([4, 128], BF16)
    ropa = A.alloc([8, 256], F32)
    ropo = A.alloc([2, 256], F32)
    xf = [A.alloc([128], F32) for _ in range(2)]
    rt1 = [A.alloc([128], F32) for _ in range(2)]
    rt2 = [A.alloc([128], F32) for _ in range(2)]
    xb16 = [A.alloc([128], BF16) for _ in range(2)]
    pbuf = [A.alloc([1536], BF16) for _ in range(2)]
    pT = [A.alloc([1536], BF16) for _ in range(2)]
    pbufp = [[A.alloc([256], BF16) for _ in range(2)] for _ in range(2)]
    pTp = [[A.alloc([256], BF16) for _ in range(2)] for _ in range(2)]
    ast = [A.alloc([16], F32) for _ in range(4)]
    of = [A.alloc([128], F32) for _ in range(4)]
    o1 = [A.alloc([128], F32) for _ in range(4)]
    on = [A.alloc([128], F32) for _ in range(4)]
    ob = [A.alloc([128], BF16) for _ in range(4)]
    ajunk = A.alloc([128], BF16)

    dma('sp', ropa, I['ropeall'].rearrange("(j p) n -> p j n", p=128), (), ['ropa'])
    dma('sp', ropo, I['ropeown'].rearrange("(j p) n -> p j n", p=128), (), ['ropo'])

    ctr = {'x': 0}

    def rope(src_ps, tab, dst16, rkeys, wkey):
        i = ctr['x'] % 2
        ctr['x'] += 1
        cp('act', xf[i], src_ps, rkeys, [('xf', i)])
        tt('dve', rt1[i], xf[i], tab[:, 0:128], ALU.mult, [('xf', i), 'ropa', 'ropo'], [('rt1', i)])
        xv = xf[i].rearrange("p (g h f) -> p g h f", h=2, f=16)
        sv = tab[:, 128:256].rearrange("p (g h f) -> p g h f", h=2, f=16)
        r2 = rt2[i].rearrange("p (g h f) -> p g h f", h=2, f=16)
        tt('pool', r2[:, :, 0, :], xv[:, :, 1, :], sv[:, :, 0, :], ALU.mult, [('xf', i), 'ropa', 'ropo'], [('rt2', i, 0)])
        tt('pool', r2[:, :, 1, :], xv[:, :, 0, :], sv[:, :, 1, :], ALU.mult, [('xf', i), 'ropa', 'ropo'], [('rt2', i, 1)])
        tt('dve', dst16, rt1[i], rt2[i], ALU.add, [('rt1', i), ('rt2', i, 0), ('rt2', i, 1)], [wkey])

    def drive2(gens):
        gens = [g for g in gens if g is not None]
        while gens:
            for g in list(gens):
                try:
                    next(g)
                except StopIteration:
                    gens.remove(g)

    def attn_unit(par, j, m, kind):
        ai = par * 2 + j
        if kind == 'p':
            ntk, kT_, vb_, kkey, vkey = 2, kTp[par], vbp[par], ('kTp', par), ('vbp', par)
            sb0 = 2 + 2 * j + m
            ob_ = 6 + j
            pbs, pTs, pkey = pbufp[j], pTp[j], ('pp', j)
            tcol = 512
        else:
            ntk, kT_, vb_, kkey, vkey = 12, kTs, vbs, 'kTs', 'vbs'
            sb0 = 2 + 3 * m
            ob_ = sb0 + 2
            pbs, pTs, pkey = pbuf, pT, ('ps_', 0)
            tcol = 0
        Tk = ntk * 128
        qT_ = qT2[par]
        s0c = sb0 * 512
        nsb = (Tk + 511) // 512
        sck = [('ps', sb0 + b_) for b_ in range(nsb)]
        for n0 in range(0, Tk, 512):
            w_ = min(512, Tk - n0)
            mm(PS[:, s0c + n0:s0c + n0 + w_], qT_[64 * m:64 * m + 64, j * 128:(j + 1) * 128],
               kT_[64 * m:64 * m + 64, n0:n0 + w_], True, True, [('qT', par, j), kkey], [('ps', sb0 + n0 // 512)])
        red(ast[ai][:, m:m + 1], PS[:, s0c:s0c + Tk], ALU.max, sck, [('ast', ai, 'mx', m)])
        ts('dve', ast[ai][:, 2 + m:3 + m], ast[ai][:, m:m + 1], -0.125, None, ALU.mult, None,
           [('ast', ai, 'mx', m)], [('ast', ai, 'nb', m)])
        act(pbs[m][:, 0:Tk], PS[:, s0c:s0c + Tk], AF.Exp, sck + [('ast', ai, 'nb', m)],
            [('pbuf', pkey, m), ('ast', ai, 'sum', m)], bias=ast[ai][:, 2 + m:3 + m], scale=0.125,
            accum=ast[ai][:, 4 + m:5 + m])
        yield
        ptv = PS[:, s0c:s0c + 1024].bitcast(BF16)
        for t in range(ntk):
            c_ = tcol + t * 128
            tr(ptv[:, c_:c_ + 128], pbs[m][:, t * 128:(t + 1) * 128], identb,
               [('pbuf', pkey, m), 'identb'], [('ps', sb0 + (c_ // 1024))])
        if ntk <= 2:
            cp('dve', pTs[m][:, 0:Tk], ptv[:, tcol:tcol + Tk], [('ps', sb0)], [('pT', pkey, m, 0)])
            ptk = [('pT', pkey, m, 0)]
        else:
            cp('dve', pTs[m][:, 0:768], ptv[:, 0:768], [('ps', sb0)], [('pT', pkey, m, 0)])
            cp('act', pTs[m][:, 768:1536], ptv[:, 768:1536], [('ps', sb0), ('ps', sb0 + 1)], [('pT', pkey, m, 1)])
            ptk = [('pT', pkey, m, 0), ('pT', pkey, m, 1)]
        yield
        oc = m * 128 if kind == 'p' else 0
        for t in range(ntk):
            mm(bank(ob_, oc, oc + 128), pTs[m][:, t * 128:(t + 1) * 128], vb_[:, t, :],
               t == 0, t == ntk - 1, ptk + [vkey], [('ps', ob_)])
        yield

    def attn_comb(h, par, j, kind, mixcol0):
        ai = par * 2 + j
        if kind == 'p':
            o1src, o2src, k1, k2, ob_ = bank(6 + j, 0, 128), bank(6 + j, 128, 256), ('ps', 6 + j), ('ps', 6 + j), 6 + j
        else:
            o1src, o2src, k1, k2, ob_ = bank(4, 0, 128), bank(7, 0, 128), ('ps', 4), ('ps', 7), 4
        a_ = ast[ai]
        recip(a_[:, 6:8], a_[:, 4:6], [('ast', ai, 'sum', 0), ('ast', ai, 'sum', 1)], [('ast', ai, 'rs')])
        tt('dve', a_[:, 8:9], a_[:, 7:8], neglam, ALU.mult, [('ast', ai, 'rs'), 'neglam'], [('ast', ai, 'c2')])
        act(o1[ai], o1src, AF.Identity, [k1, ('ast', ai, 'rs')], [('o1', ai)], scale=a_[:, 6:7])
        stt('dve', of[ai], o2src, a_[:, 8:9], o1[ai], ALU.mult, ALU.add, [k2, ('ast', ai, 'c2'), ('o1', ai)], [('of', ai)])
        yield
        act(ajunk, of[ai], AF.Square, [('of', ai)], ['ajunk', ('ast', ai, 'ss')], accum=a_[:, 9:10])
        ts('dve', a_[:, 10:11], a_[:, 9:10], 1.0 / 128, 1e-5, ALU.mult, ALU.add, [('ast', ai, 'ss')], [('ast', ai, 'ms')])
        act(a_[:, 11:12], a_[:, 10:11], AF.Sqrt, [('ast', ai, 'ms')], [('ast', ai, 'sd')])
        recip(a_[:, 12:13], a_[:, 11:12], [('ast', ai, 'sd')], [('ast', ai, 'rstd')])
        yield
        stt('dve', on[ai], of[ai], a_[:, 12:13], sgl, ALU.mult, ALU.mult, [('of', ai), ('ast', ai, 'rstd'), 'sgl'], [('on', ai)])
        tt('pool', ob[ai], on[ai], sg2[par][:, j, :], ALU.mult, [('on', ai), ('sg', par, j)], [('ob', ai)])
        pso = bankb(ob_)
        tr(pso[:, 512:640], ob[ai], identb, [('ob', ai), 'identb'], [('ps', ob_)])
        cp('act', mixT[:, h, mixcol0 + j * 128: mixcol0 + (j + 1) * 128], pso[:, 512:640], [('ps', ob_)],
           [('mixT', h, (mixcol0 // 128) + j)])
        yield

    def rr(gens):
        gens = list(gens)
        while gens:
            for g in list(gens):
                try:
                    next(g)
                except StopIteration:
                    gens.remove(g)
            yield

    def attn_gen(h, par, kind, mixcol0):
        if kind == 'p':
            for _ in rr([attn_unit(par, j, m, kind) for j in range(2) for m in range(2)]):
                yield
            for _ in rr([attn_comb(h, par, j, kind, mixcol0) for j in range(2)]):
                yield
        else:
            for j in range(2):
                for _ in rr([attn_unit(par, j, m, kind) for m in range(2)]):
                    yield
                for _ in attn_comb(h, par, j, kind, mixcol0):
                    yield

    def proj_gen(h, par, kind, s_):
        wb = wA[h % 2]
        wk = [('wA', h % 2, j4) for j4 in range(4)]
        pst = bankb(1)
        if kind == 'p':
            for j in range(2):
                g = s_ * 2 + j
                bi = g % 2
                for kc in range(8):
                    mm(bank(0), hTp[:, kc, g * 128:(g + 1) * 128], wb[:, kc, :], kc == 0, kc == 7,
                       [('xph', g)] + wk, [('ps', 0)])
                cp('dve', qkb[bi], bank(0, 0, 256), [('ps', 0)], [('qkb', bi)])
                cp('act', kvf[bi], bank(0, 128, 384), [('ps', 0)], [('kvf', bi)])
                dma('sp', O['nk'][g * 128:(g + 1) * 128, h * 128:(h + 1) * 128], kvf[bi][:, 0:128], [('kvf', bi)], (), is_out=True)
                dma('sp', O['nv'][g * 128:(g + 1) * 128, h * 128:(h + 1) * 128], kvf[bi][:, 128:256], [('kvf', bi)], (), is_out=True)
                cp('dve', vbp[par][:, j, :], bank(0, 256, 384), [('ps', 0)], [('vbp', par)])
                act(sg2[par][:, j, :], bank(0, 384, 512), AF.Silu, [('ps', 0)], [('sg', par, j)])
                yield
                tr(pst[:, 0:128], qkb[bi][:, 0:128], identb, [('qkb', bi), 'identb'], [('ps', 1)])
                tr(pst[:, 128:256], qkb[bi][:, 128:256], identb, [('qkb', bi), 'identb'], [('ps', 1)])
                cp('act', qT2[par][:, j * 128:(j + 1) * 128], pst[:, 0:128], [('ps', 1)], [('qT', par, j)])
                cp('act', kTp[par][:, j * 128:(j + 1) * 128], pst[:, 128:256], [('ps', 1)], [('kTp', par)])
                yield
        else:
            dma('pool', ckb, I['ck'].rearrange("(j p) c -> p j c", p=128)[:, :, h * 128:(h + 1) * 128], (), ['ckb'])
            dma('pool', vbs[:, 0:4, :], I['cv'].rearrange("(j p) c -> p j c", p=128)[:, :, h * 128:(h + 1) * 128], (), ['vbs'])
            for t in range(4):
                tr(pst[:, 512 + t * 128:512 + (t + 1) * 128], ckb[:, t, :], identb, ['ckb', 'identb'], [('ps', 1)])
            cp('dve', kTs[:, 0:512], pst[:, 512:1024], [('ps', 1)], ['kTs'])
            yield
            for j in range(8):
                for kc in range(8):
                    mm(bank(0, 0, 256), hTs[:, kc, j * 128:(j + 1) * 128], wb[:, kc, 128:384], kc == 0, kc == 7,
                       [('xsh', j)] + wk, [('ps', 0)])
                bi = j % 2
                cp('dve', vbs[:, 4 + j, :], bank(0, 128, 256), [('ps', 0)], ['vbs'])
                rope(bank(0, 0, 128), ropa[:, j, :], xb16[bi], [('ps', 0)], ('xb16', bi))
                yield
                tr(pst[:, 0:128], xb16[bi], identb, [('xb16', bi), 'identb'], [('ps', 1)])
                cp('act', kTs[:, 512 + j * 128:512 + (j + 1) * 128], pst[:, 0:128], [('ps', 1)], ['kTs'])
                yield
            for j in range(2):
                for kc in range(8):
                    mm(bank(0), hTo[:, kc, j * 128:(j + 1) * 128], wb[:, kc, :], kc == 0, kc == 7,
                       [('xoh', j)] + wk, [('ps', 0)])
                bi = j % 2
                act(sg2[par][:, j, :], bank(0, 384, 512), AF.Silu, [('ps', 0)], [('sg', par, j)])
                rope(bank(0, 0, 128), ropo[:, j, :], xb16[bi], [('ps', 0)], ('xb16', bi))
                yield
                tr(pst[:, 128:256], xb16[bi], identb, [('xb16', bi), 'identb'], [('ps', 1)])
                cp('act', qT2[par][:, j * 128:(j + 1) * 128], pst[:, 128:256], [('ps', 1)], [('qT', par, j)])
                yield

    ajobs = []
    for h in range(NHEAD_A):
        for s_ in range(4):
            ajobs.append((h, 'p', s_))
        ajobs.append((h, 's', 0))
    loaded = set()

    def load_w(h):
        if h in loaded or h >= NHEAD_A:
            return
        loaded.add(h)
        for j4, base in enumerate([0, 1024, 2048, 3072]):
            dma('pool', wA[h % 2][:, :, j4 * 128:(j4 + 1) * 128], w_in_v[:, :, base + h * 128: base + (h + 1) * 128], (),
                [('wA', h % 2, j4)])
    if ajobs:
        load_w(0)
        drive2([proj_gen(ajobs[0][0], 0, ajobs[0][1], ajobs[0][2])])
        for n, (h, kind, s_) in enumerate(ajobs):
            par = n % 2
            ag = attn_gen(h, par, kind, s_ * 256 if kind == 'p' else 1024)
            pg = None
            if n + 1 < len(ajobs):
                h2, kind2, s2 = ajobs[n + 1]
                load_w(h2)
                pg = proj_gen(h2, (n + 1) % 2, kind2, s2)
            drive2([ag, pg])
    if STOP == 'A':
        return
    S.barrier()
    A.off = mA

    wR = [A.alloc([8, 512], BF16) for _ in range(2)]
    rkv = A.alloc([3, 1024], F32)
    kk = A.alloc([1024], F32)
    t1 = A.alloc([1024], F32)
    prodb = A.alloc([1024], BF16)
    sqb = prodb
    vb16 = A.alloc([1024], BF16)
    VT = A.alloc([8, 128], BF16)
    sgb = A.alloc([8, 128], F32)
    yacc = A.alloc([8, 128], F32)
    wL = yacc.rearrange("p a b -> p (a b)").bitcast(BF16).rearrange("p (a b) -> p a b", b=256)
    bsum = A.alloc([16], F32)
    lnxg = [A.alloc([128], F32) for _ in range(2)]
    lnxb = [A.alloc([128], F32) for _ in range(2)]
    sgw = A.alloc([256], F32)
    Pp = A.alloc([256], F32)
    csb = A.alloc([256], F32)
    Winv = A.alloc([256], F32)
    av = A.alloc([256], F32)
    tmpa = A.alloc([256], F32)
    tmpb = A.alloc([256], F32)
    BW = A.alloc([256], BF16)
    KW = A.alloc([256], BF16)
    Wb2 = [A.alloc([2, 130], F32) for _ in range(2)]
    Kt2 = [A.alloc([256], BF16) for _ in range(2)]
    Bt2 = [A.alloc([256], BF16) for _ in range(2)]
    ARb2 = [A.alloc([2, 256], BF16) for _ in range(2)]
    BKT2 = [A.alloc([2, 256], BF16) for _ in range(2)]
    ATb2 = [A.alloc([2, 128], BF16) for _ in range(2)]
    NM = [A.alloc([512], BF16) for _ in range(4)]
    lo = [0]

    def lalloc(n):
        a_ = lvl[:, lo[0]:lo[0] + n].bitcast(F32)
        lo[0] += n
        assert lo[0] <= LVL_WORDS
        return a_
    X0 = [lalloc(128) for _ in range(4)]
    X0T = [lalloc(128) for _ in range(4)]
    XX = [[lalloc(256) for _ in range(2)] for _ in range(4)]
    Zb = [[lalloc(128) for _ in range(2)] for _ in range(4)]
    Zh = [A.alloc([128], BF16) for _ in range(4)]
    GT = A.alloc([8, 64], BF16)
    Hs = A.alloc([8, 64], F32)
    Qb = A.alloc([8, 128], BF16)
    Sf = [A.alloc([64], F32) for _ in range(2)]
    Sb = [A.alloc([64], BF16) for _ in range(2)]
    s0raw = A.alloc([128], F32)
    stg = [A.alloc([128], F32) for _ in range(2)]
    gst = A.alloc([8, 16], F32)
    ysq = t1.rearrange("p (a b) -> p a b", b=128)
    ybon = kk.rearrange("p (a b) -> p a b", b=128)
    obR = A.alloc([8, 128], BF16)

    dma('pool', wL, w_in_v[:, :, 4096 + 3072:4096 + 3328], (), ['wL'])
    for q in range(2):
        memset('dve', Wb2[q][:, :, 0:1], 1.0, [('Wbpad0', q)])
        memset('dve', Wb2[q][:, :, 129:130], 1.0, [('Wbpad1', q)])

    T = 1024
    units = [(hTp, 'xph', 0, 'p'), (hTs, 'xsh', 1024, 's')]

    def proj_shift(w_ap, wkeys, hT, hkey, ci, dst, dkey, kind, b0):
        for n0 in range(0, T, 512):
            bnk = b0 + n0 // 512
            hk = [(hkey, n0 // 128 + q) for q in range(4)]
            for kc in range(8):
                mm(bank(bnk), w_ap[:, kc, :], hT[:, kc, n0:n0 + 512], kc == 0, kc == 7, hk + wkeys, [('ps', bnk)])
        for n0 in range(0, T, 512):
            bnk = b0 + n0 // 512
            act(dst[:, n0:n0 + 512], bank(bnk), AF.Identity, [('ps', bnk), 'c0v'], [dkey], scale=c0v[:, ci:ci + 1])
        psv = PS[:, b0 * 512:b0 * 512 + 1024]
        pk = [('ps', b0), ('ps', b0 + 1)]
        blocks = [(0, T)] if kind == 's' else [(q * 256, (q + 1) * 256) for q in range(4)]
        for (s_, e_) in blocks:
            kk_ = [('ps', b0 + s_ // 512)] if (s_ // 512 == (e_ - 1) // 512) else pk
            stt('dve', dst[:, s_ + 1:e_], psv[:, s_:e_ - 1], mu[:, ci, 0:1], dst[:, s_ + 1:e_], ALU.mult, ALU.add,
                kk_ + ['mu', dkey], [dkey])
            stt('dve', dst[:, s_:e_ - 1], psv[:, s_ + 1:e_], mu[:, ci, 1:2], dst[:, s_:e_ - 1], ALU.mult, ALU.add,
                kk_ + ['mu', dkey], [dkey])

    for (hT, hkey, lc0, kind) in units:
        for c in range(2):
            proj_shift(wL[:, :, c * 128:(c + 1) * 128], ['wL'], hT, hkey, 24 + c, t1, 't1', kind, 2 * c)
            if c == 0:
                act(twT[:, lc0:lc0 + T], t1[:, 0:T], AF.Tanh, ['t1'], [('twT', kind)])
            else:
                cp('act', laT[:, lc0:lc0 + T], t1[:, 0:T], ['t1'], [('laT', kind)])
    S.barrier()

    def prep_gen(hp, kind, lc0, d, seg, par):
        Wb, Kt, Bt, ARb, BKT, ATb = Wb2[par], Kt2[par], Bt2[par], ARb2[par], BKT2[par], ATb2[par]
        kT_, rT_ = rkv[:, 1, :], rkv[:, 0, :]
        c0_ = seg * 256
        lc = lc0 + c0_
        mm(bank(0, 0, 256), W2b[64 * d:64 * d + 64, hp * 128:(hp + 1) * 128], twT[64 * d:64 * d + 64, lc:lc + 256],
           True, True, ['W2b', ('twT', kind)], [('ps', 0)])
        act(sgw, bank(0, 0, 256), AF.Sigmoid, [('ps', 0), 'w0v'], ['sgw'], bias=w0v[:, hp * 2 + d:hp * 2 + d + 1])
        mm(bank(1, 0, 256), A2b[64 * d:64 * d + 64, hp * 128:(hp + 1) * 128], laT[64 * d:64 * d + 64, lc:lc + 256],
           True, True, ['A2b', ('laT', kind)], [('ps', 1)])
        act(av, bank(1, 0, 256), AF.Sigmoid, [('ps', 1), 'a0v'], ['av'], bias=a0v[:, hp * 2 + d:hp * 2 + d + 1])
        yield
        for t in range(2):
            S.op('dve', (lambda t=t: (lambda e: e.tensor_tensor_scan(
                out=Pp[:, t * 128:(t + 1) * 128], data0=onesf, data1=sgw[:, t * 128:(t + 1) * 128],
                initial=0.0, op0=ALU.mult, op1=ALU.add)))(), ['sgw', 'onesf'], [('Pp', t)])
        if d == 0:
            cs = Pp
            csk = [('Pp', 0), ('Pp', 1)]
        else:
            for t in range(2):
                stt('dve', csb[:, t * 128:(t + 1) * 128], sgw[:, t * 128:(t + 1) * 128],
                    Pp[:, t * 128 + 127:t * 128 + 128], Pp[:, t * 128:(t + 1) * 128], ALU.add, ALU.subtract,
                    ['sgw', ('Pp', t)], [('csb', t)])
            cs = csb
            csk = [('csb', 0), ('csb', 1)]
        yield
        act(Wb[:, :, 1:129], cs.rearrange("p (a b) -> p a b", b=128), AF.Exp, csk, [('Wb', par)], scale=-DC)
        act(Winv, cs, AF.Exp, csk, ['Winv'], scale=DC)
        ts('dve', tmpa, av, kav[:, hp:hp + 1], omka[:, hp:hp + 1], ALU.mult, ALU.add, ['av', 'kav', 'omka'], ['tmpa'])
        tt('pool', tmpb, kk[:, c0_:c0_ + 256], av, ALU.mult, ['kk', 'av'], ['tmpb'])
        yield
        tt('dve', tmpa, tmpa, kT_[:, c0_:c0_ + 256], ALU.mult, ['tmpa', ('rkv', 1)], ['tmpa'])
        tt('dve', Kt, tmpa, Winv, ALU.mult, ['tmpa', 'Winv'], [('Kt', par)])
        tt('dve', Bt, tmpb, Winv, ALU.mult, ['tmpb', 'Winv'], [('Bt', par)])
        yield
        Wprev = Wb[:, :, 0:128] if d == 0 else Wb[:, :, 2:130]
        stt('dve', ARb[:, :, 0:128], kk[:, c0_:c0_ + 256].rearrange("p (a b) -> p a b", b=128), -1.0, Wprev,
            ALU.mult, ALU.mult, ['kk', ('Wb', par), ('Wbpad0', par), ('Wbpad1', par)], [('ARb', par, 'a')])
        tt('dve', ARb[:, :, 128:256], rT_[:, c0_:c0_ + 256].rearrange("p (a b) -> p a b", b=128), Wb[:, :, 1:129],
           ALU.mult, [('rkv', 0), ('Wb', par)], [('ARb', par, 'r')])
        yield
        pst2 = bankb(2)
        for t in range(2):
            wc = Wb[:, t, 128:129] if d == 0 else Wb[:, t, 1:2]
            ts('dve', BW[:, t * 128:(t + 1) * 128], Bt[:, t * 128:(t + 1) * 128], wc, None, ALU.mult, None,
               [('Bt', par), ('Wb', par)], [('BW', t)])
            ts('dve', KW[:, t * 128:(t + 1) * 128], Kt[:, t * 128:(t + 1) * 128], wc, None, ALU.mult, None,
               [('Kt', par), ('Wb', par)], [('KW', t)])
            yield
        for t in range(2):
            tr(pst2[:, t * 384:t * 384 + 128], BW[:, t * 128:(t + 1) * 128], identb, [('BW', t), 'identb'], [('ps', 2)])
            tr(pst2[:, t * 384 + 128:t * 384 + 256], KW[:, t * 128:(t + 1) * 128], identb, [('KW', t), 'identb'], [('ps', 2)])
            tr(pst2[:, t * 384 + 256:t * 384 + 384], ARb[:, t, 0:128], identb, [('ARb', par, 'a'), 'identb'], [('ps', 2)])
        for t in range(2):
            cp('act', BKT[:, t, :], pst2[:, t * 384:t * 384 + 256], [('ps', 2)], [('BKT', par, t)])
            cp('act', ATb[:, t, :], pst2[:, t * 384 + 256:t * 384 + 384], [('ps', 2)], [('ATb', par, t)])
        yield

    def rest_gen(hp, kind, d, seg, par, state_in):
        Wb, Kt, Bt, ARb, BKT, ATb = Wb2[par], Kt2[par], Bt2[par], ARb2[par], BKT2[par], ATb2[par]
        gt0 = seg * 2
        P = []
        for t in range(2):
            for e in range(2):
                zi = t * 2 + e
                P.append(dict(t=t, e=e, zi=zi, si=zi, pb=64 * e, bM=4 + zi, gt=gt0 + t))
        for p in P:
            t, e, pb, bM, si = p['t'], p['e'], p['pb'], p['bM'], p['si']
            mm(bank(bM, 0, 256), Bt[pb:pb + 64, t * 128:(t + 1) * 128], ARb[pb:pb + 64, t, :], True, True,
               [('Bt', par), ('ARb', par, 'a'), ('ARb', par, 'r')], [('ps', bM)])
            mm(bank(bM, 256, 512), Kt[pb:pb + 64, t * 128:(t + 1) * 128], ARb[pb:pb + 64, t, :], True, True,
               [('Kt', par), ('ARb', par, 'a'), ('ARb', par, 'r')], [('ps', bM)])
        for p in P:
            bM, si = p['bM'], p['si']
            tt('dve', NM[si], bank(bM), maskNM[:, d, :], ALU.mult, [('ps', bM), 'maskNM'], [('NM', si)])
            tt('dve', X0[si].bitcast(LEVEL_DT), bank(bM, 0, 128), maskNM[:, d, 0:128], ALU.mult, [('ps', bM), 'maskNM'], [('X0', si)])
        yield
        for p in P:
            t, e, pb, bM, si, gt = p['t'], p['e'], p['pb'], p['bM'], p['si'], p['gt']
            mm(bank(bM, 0, 128), ARb[pb:pb + 64, t, 0:128], Bt[pb:pb + 64, t * 128:(t + 1) * 128], True, True,
               [('Bt', par), ('ARb', par, 'a')], [('ps', bM)])
            mm(bank(bM, 384, 448), NM[si][:, 256:384], VT[:, gt, e * 64:(e + 1) * 64], True, True,
               [('NM', si), 'VT'], [('ps', bM)])
        for p in P:
            t, e, bM, si, zi = p['t'], p['e'], p['bM'], p['si'], p['zi']
            tt('dve', X0T[si].bitcast(LEVEL_DT), bank(bM, 0, 128), maskT[:, d, :], ALU.mult, [('ps', bM), 'maskT'], [('X0T', si)])
            cp('act', Zb[zi][0].bitcast(LEVEL_DT)[:, 64:128], bank(bM, 384, 448), [('ps', bM)], [('Zb', zi, 0, 'u')])
            cp('pool', Zb[zi][0].bitcast(LEVEL_DT)[:, 0:64], ATb[:, t, e * 64:(e + 1) * 64], [('ATb', par, t)], [('Zb', zi, 0, 'a')])
        yield
        for j in range(7):
            for p in P:
                bM, si, zi = p['bM'], p['si'], p['zi']
                Xj = X0[si] if j == 0 else XX[si][j % 2][:, 0:128]
                XjT = X0T[si] if j == 0 else XX[si][j % 2][:, 128:256]
                xk = [('X0', si), ('X0T', si)] if j == 0 else [('XX', si, j % 2)]
                zc = Zb[zi][j % 2]
                zck = [('Zb', zi, j % 2, 'a'), ('Zb', zi, j % 2, 'u')]
                Xr, XTr, zr = Xj.bitcast(LEVEL_DT), XjT.bitcast(LEVEL_DT), zc.bitcast(LEVEL_DT)
                mm(bank(bM, 0, 128), Xr, zr, True, True, xk + zck, [('ps', bM)])
                if j < 6:
                    mm(bank(bM, 128, 256), XTr, Xr, True, True, xk, [('ps', bM)])
                    mm(bank(bM, 256, 384), Xr, XTr, True, True, xk, [('ps', bM)])
            for p in P:
                bM, si, zi = p['bM'], p['si'], p['zi']
                zc = Zb[zi][j % 2]
                zn = Zb[zi][(j + 1) % 2]
                zck = [('Zb', zi, j % 2, 'a'), ('Zb', zi, j % 2, 'u')]
                znk = [('Zb', zi, (j + 1) % 2, 'a'), ('Zb', zi, (j + 1) % 2, 'u')]
                tt('dve', zn.bitcast(LEVEL_DT), bank(bM, 0, 128), zc, ALU.add, [('ps', bM)] + zck, znk)
                if j < 6:
                    cp('act', XX[si][(j + 1) % 2].bitcast(LEVEL_DT), bank(bM, 128, 384), [('ps', bM)], [('XX', si, (j + 1) % 2)])
            yield
        for p in P:
            si, zi = p['si'], p['zi']
            cp('pool', Zh[si], Zb[zi][1], [('Zb', zi, 1, 'a'), ('Zb', zi, 1, 'u')], [('Zh', si)])
        for p in P:
            t, e, pb, bM, si, gt = p['t'], p['e'], p['pb'], p['bM'], p['si'], p['gt']
            Z = Zh[si]
            zk = [('Zh', si)]
            mm(bank(bM, 448, 512)[pb:pb + 64, :], Z[:, 0:64], BKT[:, t, e * 64:(e + 1) * 64], True, True,
               zk + [('BKT', par, t)], [('ps', bM)])
            mm(bank(bM, 384, 448)[pb:pb + 64, :], BKT[:, t, e * 64:(e + 1) * 64], Z[:, 64:128], True, False,
               zk + [('BKT', par, t)], [('ps', bM)])
            mm(bank(bM, 384, 448)[pb:pb + 64, :], BKT[:, t, 128 + e * 64:128 + (e + 1) * 64],
               VT[:, gt, e * 64:(e + 1) * 64], False, True, ['VT', ('BKT', par, t)], [('ps', bM)])
            mm(bank(bM, 0, 128)[pb:pb + 64, :], Z[:, 0:64], NM[si][:, 128:256], True, True,
               zk + [('NM', si)], [('ps', bM)])
            mm(bank(bM, 128, 192), NM[si][:, 128:256], Z[:, 64:128], True, False, zk + [('NM', si)], [('ps', bM)])
            mm(bank(bM, 128, 192), NM[si][:, 384:512], VT[:, gt, e * 64:(e + 1) * 64], False, True,
               ['VT', ('NM', si)], [('ps', bM)])
        yield
        for p in P:
            t, e, pb, bM, si, gt = p['t'], p['e'], p['pb'], p['bM'], p['si'], p['gt']
            wc = Wb[pb:pb + 64, t, 128:129] if d == 0 else Wb[pb:pb + 64, t, 1:2]
            stt('dve', GT[pb:pb + 64, gt, :], identf[pb:pb + 64, pb:pb + 64], wc, bank(bM, 448, 512)[pb:pb + 64, :],
                ALU.mult, ALU.add, [('ps', bM), 'identf', ('Wb', par)], [('GT', gt, e)])
            tt('dve', Qb[pb:pb + 64, gt, :], bank(bM, 0, 128)[pb:pb + 64, :], ARb[pb:pb + 64, t, 128:256], ALU.add,
               [('ps', bM), ('ARb', par, 'r')], [('Qb', gt, e)])
            cp('act', Hs[pb:pb + 64, gt, :], bank(bM, 384, 448)[pb:pb + 64, :], [('ps', bM)], [('Hs', gt, e)])
            if d == 0:
                cp('act', yacc[:, gt, e * 64:(e + 1) * 64], bank(bM, 128, 192), [('ps', bM)], [('yacc', gt, e)])
            else:
                tt('dve', yacc[:, gt, e * 64:(e + 1) * 64], bank(bM, 128, 192), yacc[:, gt, e * 64:(e + 1) * 64],
                   ALU.add, [('ps', bM), ('yacc', gt, e)], [('yacc', gt, e)])
        yield
        have_state = state_in
        order = [0, 1] if d == 0 else [1, 0]
        for t in order:
            gt = gt0 + t
            for e in range(2):
                pb = 64 * e
                if have_state:
                    mm(bank(3, e * 64, e * 64 + 64), Qb[pb:pb + 64, gt, :], Sb[d][pb:pb + 64, :], True, True,
                       [('Qb', gt, e), ('Sb', d, e)], [('ps', 3)])
                    mm(bank(3, 128 + e * 64, 192 + e * 64)[pb:pb + 64, :], GT[pb:pb + 64, gt, :], Sb[d][pb:pb + 64, :],
                       True, True, [('GT', gt, e), ('Sb', d, e)], [('ps', 3)])
            for e in range(2):
                pb = 64 * e
                if have_state:
                    tt('dve', yacc[:, gt, e * 64:(e + 1) * 64], bank(3, e * 64, e * 64 + 64),
                       yacc[:, gt, e * 64:(e + 1) * 64], ALU.add, [('ps', 3), ('yacc', gt, e)], [('yacc', gt, e)])
                    tt('dve', Sf[d][pb:pb + 64, :], bank(3, 128 + e * 64, 192 + e * 64)[pb:pb + 64, :], Hs[pb:pb + 64, gt, :],
                       ALU.add, [('ps', 3), ('Hs', gt, e)], [('Sf', d, e)])
                else:
                    cp('dve', Sf[d][pb:pb + 64, :], Hs[pb:pb + 64, gt, :], [('Hs', gt, e)], [('Sf', d, e)])
                cp('act', Sb[d][pb:pb + 64, :], Sf[d][pb:pb + 64, :], [('Sf', d, e)], [('Sb', d, e)])
            have_state = True
            yield
        if kind == 'p':
            q = (seg + d) % 2
            tr(bank(3, 256, 384)[0:64, :], Sf[d], identf, [('Sf', d, 0), ('Sf', d, 1), 'identf'], [('ps', 3)])
            cp('act', stg[q][0:64, :], bank(3, 256, 384)[0:64, :], [('ps', 3)], [('stg', q)])
            dma('sp', O['ns'][seg, d, 2 * hp:2 * hp + 2, :, :].rearrange("e v k -> v e k"),
                stg[q][0:64, :].rearrange("p (e k) -> p e k", e=2), [('stg', q)], (), is_out=True)
            yield

    def drive(a, b):
        gens = [g for g in (a, b) if g is not None]
        while gens:
            for g in list(gens):
                try:
                    next(g)
                except StopIteration:
                    gens.remove(g)

    jobno = [0]
    ag_in = nc.dram_tensor("ag_in", [1024, 256], BF16)
    ag_out = nc.dram_tensor("ag_out", [4096, 256], BF16)
    wrs_v = I['wrs'].rearrange("(kc p) n -> p kc n", p=128)
    unit_of = {'p': (hTp, 'xph', 0), 's': (hTs, 'xsh', 1024)}
    tasks = ([('s', 8), ('s', 9)] if NSEQ_R >= 2 else []) + [('p', h_) for h_ in range(NHP)]

    def ci_of(a_, hp):
        return a_ * 8 + hp if hp < 8 else 26 + a_ * 2 + (hp - 8)
    for ti_, (kind, hp) in enumerate(tasks):
        tp = ti_ % 2
        wr = wR[tp]
        for a_, base in enumerate([4096, 4096 + 1024, 4096 + 2048, 4096 + 3328]):
            if hp < 8:
                wsrc = w_in_v[:, :, base + hp * 128: base + (hp + 1) * 128]
            else:
                wsrc = wrs_v[:, :, (hp - 8) * 512 + a_ * 128:(hp - 8) * 512 + (a_ + 1) * 128]
            dma('pool', wr[:, :, a_ * 128:(a_ + 1) * 128], wsrc, (), [('wR', tp, a_)])
        dma('sp', lnxg[tp], I['lnxg'][0:1, hp * 128:(hp + 1) * 128].partition_broadcast(128), (), [('lnxg', tp)])
        dma('sp', lnxb[tp], I['lnxb'][0:1, hp * 128:(hp + 1) * 128].partition_broadcast(128), (), [('lnxb', tp)])
        if ti_ == 2 and tasks[0][0] == 's':
            S.op('pool', lambda e: e.collective_compute("AllGather", ALU.bypass, replica_groups=[[0, 1, 2, 3], [4, 5, 6, 7]],
                                                        ins=[ag_in.ap().opt()], outs=[ag_out.ap().opt()]),
                 [('ag_in', 0), ('ag_in', 1)], ['ag_out'], dma=True, cc=True)
        for (hT, hkey, lc0) in [unit_of[kind]]:
            nt = 8
            for a_ in range(3):
                proj_shift(wr[:, :, a_ * 128:(a_ + 1) * 128], [('wR', tp, a_)], hT, hkey, ci_of(a_, hp), rkv[:, a_, :],
                           ('rkv', a_), kind, 2 * a_)
            rT_, kT_, vT_ = rkv[:, 0, :], rkv[:, 1, :], rkv[:, 2, :]
            for t in range(nt):
                bnk = 6 + (t // 4) % 2
                for kc in range(8):
                    mm(bank(bnk, (t % 4) * 128, (t % 4 + 1) * 128), hT[:, kc, t * 128:(t + 1) * 128],
                       wr[:, kc, 384:512], kc == 0, kc == 7, [(hkey, t), ('wR', tp, 3)], [('ps', bnk)])
                if t % 4 == 3:
                    act(sgb[:, t - 3:t + 1, :], bank(bnk).rearrange("p (a b) -> p a b", b=128), AF.Silu, [('ps', bnk)],
                        [('sgb', q) for q in range(t - 3, t + 1)])
            ts('dve', t1[:, 0:T], kT_[:, 0:T], kkv[:, hp:hp + 1], None, ALU.mult, None, [('rkv', 1), 'kkv'], ['t1'])
            act(sqb[:, 0:T], t1[:, 0:T], AF.Square, ['t1'], ['prodb'])
            for n0 in range(0, T, 512):
                bnk = (n0 // 512) % 2
                mm(bank(bnk), bonesb, sqb[:, n0:n0 + 512], True, True, ['bonesb', 'prodb'], [('ps', bnk)])
                act(kk[:, n0:n0 + 512], bank(bnk), AF.Sqrt, [('ps', bnk), 'cst'], ['kk'], bias=cst[:, 0:1])
            recip(kk[:, 0:T], kk[:, 0:T], ['kk'], ['kk'])
            tt('dve', kk[:, 0:T], kk[:, 0:T], t1[:, 0:T], ALU.mult, ['kk', 't1'], ['kk'])
            stt('dve', prodb[:, 0:T], rT_[:, 0:T], rkv_[:, hp:hp + 1], kT_[:, 0:T], ALU.mult, ALU.mult,
                [('rkv', 0), ('rkv', 1), 'rkv_'], ['prodb'])
            for t in range(nt):
                mm(bank(1, t * 2, t * 2 + 2), prodb[:, t * 128:(t + 1) * 128], hindb, True, True, ['prodb', 'hindb'], [('ps', 1)])
            cp('act', bsum[:, 0:nt * 2], bank(1, 0, nt * 2), [('ps', 1)], ['bsum'])
            cp('act', vb16[:, 0:T], vT_[:, 0:T], [('rkv', 2)], ['vb16'])
            pst2 = bankb(2)
            for t in range(nt):
                tr(pst2[:, t * 128:(t + 1) * 128], vb16[:, t * 128:(t + 1) * 128], identb, ['vb16', 'identb'], [('ps', 2)])
            cp('dve', VT[:, 0:nt, :], pst2[:, 0:nt * 128].rearrange("p (a b) -> p a b", b=128), [('ps', 2)], ['VT'])
            jobs = []
            for d in range(ND):
                segs = list(range(4)) if d == 0 else list(range(3, -1, -1))
                for i_, seg in enumerate(segs):
                    st_in = (kind == 's')
                    jobs.append((d, seg, st_in, (kind == 's' and i_ == 0)))
            pars = []
            for _ in jobs:
                pars.append(jobno[0] % 2)
                jobno[0] += 1
            pg = prep_gen(hp, kind, lc0, jobs[0][0], jobs[0][1], pars[0])
            drive(pg, None)
            for n, (d, seg, st_in, load_s0) in enumerate(jobs):
                if load_s0:
                    dma('sp', s0raw[0:64, :].rearrange("p (e k) -> p e k", e=2),
                        I['s0'][d, 2 * (hp - 8):2 * (hp - 8) + 2, :, :].rearrange("e v k -> v e k"), (), ['s0raw'])
                    tr(bank(3, 0, 64), s0raw[0:64, :], identf[0:64, 0:64], ['s0raw', 'identf'], [('ps', 3)])
                    for e in range(2):
                        pb = 64 * e
                        cp('dve', Sf[d][pb:pb + 64, :], bank(3, 0, 64)[pb:pb + 64, :], [('ps', 3)], [('Sf', d, e)])
                        cp('act', Sb[d][pb:pb + 64, :], bank(3, 0, 64)[pb:pb + 64, :], [('ps', 3)], [('Sb', d, e)])
                rg = rest_gen(hp, kind, d, seg, pars[n], st_in)
                ng = None
                if n + 1 < len(jobs):
                    ng = prep_gen(hp, kind, lc0, jobs[n + 1][0], jobs[n + 1][1], pars[n + 1])
                drive(rg, ng)
            n2 = nt * 2
            yk = [('yacc', t, e) for t in range(nt) for e in range(2)]
            yv = yacc[:, 0:nt, :].rearrange("p a (e f) -> p (a e) f", e=2)
            red(gst[:, 0, 0:n2], yv, ALU.add, yk, [('gst', 0)])
            act(ysq[:, 0:nt, :], yacc[:, 0:nt, :], AF.Square, yk, ['t1'])
            red(gst[:, 1, 0:n2], ysq[:, 0:nt, :].rearrange("p a (e f) -> p (a e) f", e=2), ALU.add, ['t1'], [('gst', 1)])
            ts('dve', gst[:, 2, 0:n2], gst[:, 0, 0:n2], 1.0 / 64, None, ALU.mult, None, [('gst', 0)], [('gst', 2)])
            tt('dve', gst[:, 3, 0:n2], gst[:, 2, 0:n2], gst[:, 2, 0:n2], ALU.mult, [('gst', 2)], [('gst', 3)])
            stt('dve', gst[:, 4, 0:n2], gst[:, 1, 0:n2], 1.0 / 64, gst[:, 3, 0:n2], ALU.mult, ALU.subtract,
                [('gst', 1), ('gst', 3)], [('gst', 4)])
            ts('dve', gst[:, 4, 0:n2], gst[:, 4, 0:n2], 64e-5, None, ALU.add, None, [('gst', 4)], [('gst', 4)])
            act(gst[:, 5, 0:n2], gst[:, 4, 0:n2], AF.Sqrt, [('gst', 4)], [('gst', 5)])
            recip(gst[:, 6, 0:n2], gst[:, 5, 0:n2], [('gst', 5)], [('gst', 6)])
            ysv = ysq[:, 0:nt, :].rearrange("p a (e f) -> p (a e) f", e=2)
            tt('dve', ysv, yv, gst[:, 2, 0:n2].unsqueeze(2).to_broadcast([128, n2, 64]), ALU.subtract, yk + [('gst', 2)], ['t1'])
            tt('dve', ysv, ysv, gst[:, 6, 0:n2].unsqueeze(2).to_broadcast([128, n2, 64]), ALU.mult, ['t1', ('gst', 6)], ['t1'])
            tt('dve', ysq[:, 0:nt, :], ysq[:, 0:nt, :], lnxg[tp].unsqueeze(1).to_broadcast([128, nt, 128]),
               ALU.mult, ['t1', ('lnxg', tp)], ['t1'])
            tt('dve', ysq[:, 0:nt, :], ysq[:, 0:nt, :], lnxb[tp].unsqueeze(1).to_broadcast([128, nt, 128]),
               ALU.add, ['t1', ('lnxb', tp)], ['t1'])
            tt('dve', ybon[:, 0:nt, :].rearrange("p a (e f) -> p (a e) f", e=2),
               VT[:, 0:nt, :].rearrange("p a (e f) -> p (a e) f", e=2),
               bsum[:, 0:n2].unsqueeze(2).to_broadcast([128, n2, 64]), ALU.mult, ['VT', 'bsum'], ['kk'])
            tt('dve', ysq[:, 0:nt, :], ysq[:, 0:nt, :], ybon[:, 0:nt, :], ALU.add, ['t1', 'kk'], ['t1'])
            tt('dve', obR[:, 0:nt, :], ysq[:, 0:nt, :], sgb[:, 0:nt, :], ALU.mult, ['t1'] + [('sgb', t) for t in range(nt)], ['obR'])
            if kind == 'p':
                pst2 = bankb(2)
                for t in range(nt):
                    tr(pst2[:, t * 128:(t + 1) * 128], obR[:, t, :], identb, ['obR', 'identb'], [('ps', 2)])
                cp('act', mixT[:, 8 + hp, 0:1024], pst2[:, 0:1024], [('ps', 2)], [('mixT', 8 + hp, g) for g in range(8)])
            else:
                i_ = hp - 8
                dma('sp', ag_in.ap().rearrange("(t p) (i f) -> p t i f", p=128, i=2)[:, :, i_, :], obR[:, 0:8, :], ['obR'],
                    [('ag_in', i_)])
    if DEBUG:
        dump_all(dict(yacc=yacc, Sf0=Sf[0], GT=GT, Hs=Hs))
    if STOP == 'R':
        return
    S.barrier()
    A.off = mA

    Gt = A.alloc([32, 256], BF16)
    dma('sp', Gt, ag_out.ap().rearrange("(rt p) f -> p rt f", p=128), ['ag_out'], ['Gt'])
    for r_ in range(4):
        for i_ in range(2):
            hq = 2 * r_ + i_
            bq = 4 + (hq % 4)
            for t in range(8):
                mm(bank(bq, 0, 256), Gt[:, r_ * 8 + t, i_ * 128:(i_ + 1) * 128], selb[:, t, :], t == 0, t == 7,
                   ['Gt', 'selb'], [('ps', bq)])
            cp('act', mixT[:, 8 + hq, 1024:1280], bank(bq, 0, 256), [('ps', bq)], [('mixT', 8 + hq, 8), ('mixT', 8 + hq, 9)])
    S.barrier()
    A.off = mA
    wout = A.alloc([16, 1024], BF16)
    fgbc = A.alloc([1024], F32)
    gatebc = A.alloc([2, 1024], F32)
    bgbc = A.alloc([1024], F32)
    scbc = A.alloc([8, 2, 128], BF16)
    wadg = A.alloc([8, 1024], BF16)
    xr = [A.alloc([1024], F32) for _ in range(2)]
    yv_ = [A.alloc([1024], F32) for _ in range(2)]
    ojunk = A.alloc([1024], BF16)
    ost = [A.alloc([4], F32) for _ in range(2)]
    wout_v = I['w_out'].rearrange("(c p) n -> p c n", p=128)
    wada_v = I['w_ada'].rearrange("(kc p) n -> p kc n", p=128)
    for n in range(2):
        dma('pool', wadg[:, :, n * 512:(n + 1) * 512], wada_v[:, :, 2048 + n * 512:2048 + (n + 1) * 512], (), [('wadg', n)])
    for c4 in range(4):
        dma('pool', wout[:, c4 * 4:(c4 + 1) * 4, :], wout_v[:, c4 * 4:(c4 + 1) * 4, :], (), [('wout', c4)])
    dma('sp', fgbc, I['fg'][0:1, :].partition_broadcast(128), (), ['fgbc'])
    dma('sp', bgbc, I['bgate'][0:1, :].partition_broadcast(128), (), ['bgbc'])
    cp('dve', scbc, scp.unsqueeze(3).to_broadcast([128, 8, 2, 128]), ['scp'], ['scbc'])
    for v in range(2):
        for n in range(2):
            for kc in range(8):
                mm(bank(2 + n), scbc[:, kc, v, :], wadg[:, kc, n * 512:(n + 1) * 512],
                   kc == 0, kc == 7, ['scbc', ('wadg', n)], [('ps', 2 + n)])
            tt('dve', gatebc[:, v, n * 512:(n + 1) * 512], bank(2 + n), bgbc[:, n * 512:(n + 1) * 512], ALU.add,
               [('ps', 2 + n), 'bgbc'], [('gatebc', v, n)])
    otiles = [('xp', g, 'yp', 0) for g in range(8)] + [('xo', g, 'ys', 1) for g in range(2)]
    for ti, (src, g, dst, v) in enumerate(otiles):
        b = ti % 2
        mg = g if src == 'xp' else 8 + g
        dma('sp', xr[b], I[src][g * 128:(g + 1) * 128, :], (), [('xr', b)])
        for n in range(2):
            for c in range(16):
                mm(bank(n), mixT[:, c, mg * 128:(mg + 1) * 128], wout[:, c, n * 512:(n + 1) * 512], c == 0, c == 15,
                   [('mixT', c, mg), ('wout', c // 4)], [('ps', n)])
            tt('dve', yv_[b][:, n * 512:(n + 1) * 512], bank(n), gatebc[:, v, n * 512:(n + 1) * 512], ALU.mult,
               [('ps', n), ('gatebc', v, n)], [('yv', b, n)])
            tt('pool', yv_[b][:, n * 512:(n + 1) * 512], yv_[b][:, n * 512:(n + 1) * 512], xr[b][:, n * 512:(n + 1) * 512], ALU.add,
               [('yv', b, n), ('xr', b)], [('yv', b, n)])
        act(ojunk, yv_[b], AF.Square, [('yv', b, 0), ('yv', b, 1)], ['ojunk', ('ost', b)], accum=ost[b][:, 0:1])
        ts('dve', ost[b][:, 1:2], ost[b][:, 0:1], 1.0 / 1024, 1e-6, ALU.mult, ALU.add, [('ost', b)], [('ostb', b)])
        act(ost[b][:, 2:3], ost[b][:, 1:2], AF.Sqrt, [('ostb', b)], [('ostc', b)])
        recip(ost[b][:, 3:4], ost[b][:, 2:3], [('ostc', b)], [('ostd', b)])
        stt('dve', xr[b], yv_[b], ost[b][:, 3:4], fgbc, ALU.mult, ALU.mult, [('yv', b, 0), ('yv', b, 1), ('ostd', b), 'fgbc', ('xr', b)], [('xr', b)])
        dma('sp', O[dst][g * 128:(g + 1) * 128, :], xr[b], [('xr', b)], (), is_out=True)


_NC = None


def _rope_tab(pos_rows, pos_cols):
    n_freq = 16
    inv = (10000.0 ** (-np.arange(n_freq, dtype=np.float32) / n_freq)).astype(np.float32)
    T = len(pos_rows)
    tab = np.zeros((T, 256), np.float32)
    for s_, pos in enumerate([pos_rows, pos_cols]):
        ang = pos.astype(np.float32)[:, None] * inv[None, :]
        c, sn = np.cos(ang).astype(np.float32), np.sin(ang).astype(np.float32)
        for m in range(2):
            for hf in range(2):
                o = m * 64 + s_ * 32 + hf * 16
                tab[:, o:o + 16] = c
                tab[:, 128 + o:128 + o + 16] = -sn if hf == 0 else sn
    return tab


def kernel(x_prompt, x_sample, cache_k, cache_v, state_rwkv, c, c_ctx, norm_g, w_ada, b_ada,
           w_in, lam_q1, lam_k1, lam_q2, lam_k2, subln_g, shift_mu, decay_w0, decay_w2,
           iclr_a0, iclr_a2, k_k, k_a, r_k, lnx_g, lnx_b, w_out, final_g):
    global _NC
    f = lambda a: np.ascontiguousarray(np.asarray(a, dtype=np.float32))
    x_prompt, x_sample, cache_k, cache_v, state_rwkv = map(f, (x_prompt, x_sample, cache_k, cache_v, state_rwkv))
    c, c_ctx = f(c), f(c_ctx)
    if _NC is None:
        _NC = build()
    nc = _NC

    def fm(v, nch):
        return np.ascontiguousarray(f(v).reshape(nch, 128).T)

    i = np.arange(128)
    su = (i[:, None] < i[None, :]).astype(np.float32)
    ui = (i[:, None] <= i[None, :]).astype(np.float32)
    sl = (i[:, None] > i[None, :]).astype(np.float32)
    li = (i[:, None] >= i[None, :]).astype(np.float32)
    maskNM = np.stack([np.concatenate([su, ui, su, ui], 1), np.concatenate([sl, li, sl, li], 1)])
    maskT = np.stack([sl, su])
    bones = np.kron(np.eye(2, dtype=np.float32), np.ones((64, 64), np.float32))
    hind = np.kron(np.eye(2, dtype=np.float32), np.ones((64, 1), np.float32))
    tok = np.arange(1024)
    ropeall = _rope_tab(tok // 64, tok % 64)
    W_in = f(w_in)[0]
    smu, sw0, sa0 = f(shift_mu)[0], f(decay_w0)[0], f(iclr_a0)[0]
    sw2, sa2 = f(decay_w2)[0].reshape(128, 1024), f(iclr_a2)[0].reshape(128, 1024)
    skk, ska, srk = f(k_k)[0], f(k_a)[0], f(r_k)[0].reshape(-1)
    slg, slb = f(lnx_g)[0], f(lnx_b)[0]
    shared = {
        "w_in": W_in, "w_ada": f(w_ada)[0], "w_out": f(w_out)[0],
        "bada_fm": fm(f(b_ada)[0], 24), "bgate": f(b_ada)[0:1, 2048:3072], "normg_fm": fm(f(norm_g)[0], 8),
        "fg": f(final_g)[None, :],
        "lamv": np.concatenate([f(lam_q1)[0], f(lam_k1)[0], f(lam_q2)[0], f(lam_k2)[0]])[None, :],
        "sublng": f(subln_g)[0:1],
        "ident": np.eye(128, dtype=np.float32), "maskNM": maskNM, "maskT": maskT, "bones": bones, "hind": hind,
        "ropeall": ropeall,
    }

    def cols(v, hp):
        return v[hp * 128:(hp + 1) * 128]
    in_maps = []
    for core in range(8):
        b, q = core // 4, core % 4
        hps = list(range(8)) + [2 * q, 2 * q + 1]
        sel = np.zeros((1024, 256), np.float32)
        sel[q * 256 + np.arange(256), np.arange(256)] = 1.0
        cT = np.stack([fm(c_ctx, 8), fm(c[b], 8)], -1).reshape(128, 16)
        chunks = [smu[:, ci * 128:(ci + 1) * 128] for ci in range(26)]
        for a_ in range(3):
            for i_ in range(2):
                ci = a_ * 8 + hps[8 + i_]
                chunks.append(smu[:, ci * 128:(ci + 1) * 128])
        mu_fm = np.stack([np.stack([ch[0], ch[1]], -1) for ch in chunks], 1).reshape(128, 64)
        w0_fm = np.stack([np.stack([cols(sw0[0], h_), cols(sw0[1], h_)], -1) for h_ in hps], 1).reshape(128, 20)
        a0_fm = np.stack([np.stack([cols(sa0[0], h_), cols(sa0[1], h_)], -1) for h_ in hps], 1).reshape(128, 20)
        ext = lambda v: np.stack([cols(v, h_) for h_ in hps], 1)
        extc = lambda m: np.concatenate([m[:, h_ * 128:(h_ + 1) * 128] for h_ in hps], 1)
        wrs = np.concatenate([W_in[:, base + h_ * 128: base + (h_ + 1) * 128]
                              for h_ in hps[8:] for base in (4096, 4096 + 1024, 4096 + 2048, 4096 + 3328)], 1)
        m = dict(shared)
        m.update({
            "xp": x_prompt[core * 4:(core + 1) * 4].reshape(1024, 1024),
            "xs": x_sample[b], "xo": x_sample[b, q * 256:(q + 1) * 256],
            "ck": cache_k[b, 0].reshape(512, 1024), "cv": cache_v[b, 0].reshape(512, 1024),
            "s0": state_rwkv[b, 0][:, 4 * q:4 * q + 4], "cT": np.ascontiguousarray(cT), "selT": sel,
            "ropeown": np.ascontiguousarray(ropeall[q * 256:(q + 1) * 256]),
            "mu_fm": mu_fm, "w0_fm": w0_fm, "a0_fm": a0_fm, "w2": extc(sw2), "a2": extc(sa2),
            "kk_fm": ext(skk), "ka_fm": ext(ska), "rk_fm": ext(srk),
            "lnxg": extc(slg[None, :]), "lnxb": extc(slb[None, :]), "wrs": wrs,
        })
        in_maps.append({k: np.ascontiguousarray(v, dtype=np.float32) for k, v in m.items()})
    res = run_bass_kernel_spmd(nc, in_maps, core_ids=list(range(8)))
    R = res.results
    y_prompt = np.concatenate([R[i]["yp"].reshape(4, 256, 1024) for i in range(8)], 0)
    y_sample = np.stack([np.concatenate([R[b * 4 + q]["ys"] for q in range(4)], 0) for b in range(2)], 0)
    new_k = np.concatenate([R[i]["nk"].reshape(4, 1, 256, 8, 2, 64) for i in range(8)], 0)
    new_v = np.concatenate([R[i]["nv"].reshape(4, 1, 256, 8, 128) for i in range(8)], 0)
    new_s = np.concatenate([R[i]["ns"].reshape(4, 1, 2, 16, 64, 64) for i in range(8)], 0)
    return (y_prompt.astype(np.float32), y_sample.astype(np.float32), new_k.astype(np.float32),
            new_v.astype(np.float32), new_s.astype(np.float32))
```

```python
import math
from contextlib import ExitStack
import numpy as np
import concourse.bass as bass
import concourse.mybir as mybir
from concourse.bass_utils import run_bass_kernel_spmd

F32 = mybir.dt.float32
F32R = mybir.dt.float32r
LEVEL_DT = F32
BF16 = mybir.dt.bfloat16
AF = mybir.ActivationFunctionType
ALU = mybir.AluOpType
AX = mybir.AxisListType

ENG = ['pe', 'dve', 'act', 'pool', 'sp']
NDMA = 40
DC = math.exp(-0.5)


class Sched:
    def __init__(s, nc, stack):
        s.nc = nc
        s.ops = {e: [] for e in ENG}
        s.cnt = {e: 0 for e in ENG}
        s.known = {e: {f: 0 for f in ENG} for e in ENG}
        s.snap = {e: [] for e in ENG}
        s.kdma = {e: {} for e in ENG}
        s.lastw = {}
        s.rd_e = {}
        s.rd_d = {}
        s.sems = {e: stack.enter_context(nc.semaphore('c_' + e)) for e in ENG}
        s.dsems = [stack.enter_context(nc.semaphore('d%d' % i)) for i in range(2 * NDMA)]
        s.dcnt = {'sp': 0, 'pool': 0, 'act': 0}
        s.dcount = 0
        s.dlast = {}
        s.out_events = []

    def op(s, eng, fn, r=(), w=(), dma=False, is_out=False, noinc=False, cc=False):
        r = [(k[0], k[1]) if (isinstance(k, tuple) and k[0] == 'ps') else k for k in r]
        w = [(k[0], k[1]) if (isinstance(k, tuple) and k[0] == 'ps') else k for k in w]
        psr = [k for k in r if isinstance(k, tuple) and k[0] == 'ps']
        if psr:
            r = [k for k in r if not (isinstance(k, tuple) and k[0] == 'ps')]
            w = list(w) + [k for k in psr if k not in w]
        deps = []
        for k in r:
            ev = s.lastw.get(k)
            if ev is not None:
                deps.append((ev, True))
        for k in w:
            ev = s.lastw.get(k)
            if ev is not None:
                deps.append((ev, False))
            for f, c in s.rd_e.get(k, {}).items():
                deps.append((('E', f, c), True))
            for ev in s.rd_d.get(k, ()):
                deps.append((ev, False))
        waits = {}
        for ev, raw in deps:
            if ev[0] == 'E':
                _, f, c = ev
                if f == eng and not dma:
                    if (not raw) or eng == 'pe':
                        continue
                if s.known[eng][f] >= c:
                    continue
                waits[('E', f)] = max(waits.get(('E', f), 0), c)
            else:
                _, si, v = ev
                if s.kdma[eng].get(si, 0) >= v:
                    continue
                waits[('D', si)] = max(waits.get(('D', si), 0), v)
        if cc:
            ev = ('D', 'cc', 1)
            s.dlast['cc'] = 1
        elif dma:
            qn = s.dcnt[eng]
            si = qn % NDMA + (NDMA if eng == 'pool' else 0)
            v = 16 * (qn // NDMA + 1)
            if qn >= NDMA and s.kdma[eng].get(si, 0) < v - 16:
                waits[('D', si)] = max(waits.get(('D', si), 0), v - 16)
            s.dcnt[eng] += 1
            s.dcount += 1
            s.dlast[si] = v
            ev = ('D', si, v)
        for (t, x), v in waits.items():
            if t == 'E':
                kn = s.known[eng]
                if kn[x] < v:
                    kn[x] = v
                sn = s.snap[x][v - 1]
                for f2, c2 in sn.items():
                    if kn[f2] < c2:
                        kn[f2] = c2
            else:
                s.kdma[eng][x] = v
        if not (dma or cc):
            if noinc:
                ev = ('E', eng, s.cnt[eng] + 1)
            else:
                s.cnt[eng] += 1
                ev = ('E', eng, s.cnt[eng])
                s.snap[eng].append(dict(s.known[eng]))
        s.ops[eng].append((list(waits.items()), fn, None if noinc else ev))
        for k in r:
            if ev[0] == 'E':
                s.rd_e.setdefault(k, {})[eng] = ev[2]
            else:
                s.rd_d.setdefault(k, []).append(ev)
        for k in w:
            s.lastw[k] = ev
            s.rd_e[k] = {}
            s.rd_d[k] = []
        if is_out:
            s.out_events.append(ev)
        return ev

    def barrier(s):
        waits = {}
        for f in ENG:
            if f != 'sp' and s.cnt[f] > 0:
                waits[('E', f)] = s.cnt[f]
        for si, v in s.dlast.items():
            waits[('D', si)] = v
        s.cnt['sp'] += 1
        c = s.cnt['sp']
        for f in ENG:
            if f != 'sp':
                s.known['sp'][f] = s.cnt[f]
        s.kdma['sp'] = dict(s.dlast)
        s.snap['sp'].append(dict(s.known['sp']))
        s.ops['sp'].append((list(waits.items()), (lambda e: e.nop()), ('E', 'sp', c)))
        for f in ENG:
            if f == 'sp':
                continue
            s.ops[f].append(([(('E', 'sp'), c)], None, None))
            for g in ENG:
                if g != f:
                    s.known[f][g] = max(s.known[f][g], s.cnt[g])
            s.kdma[f] = dict(s.dlast)
        s.lastw.clear()
        s.rd_e.clear()
        s.rd_d.clear()

    def finish(s):
        waits = {}
        for ev in s.out_events:
            waits[('D', ev[1])] = max(waits.get(('D', ev[1]), 0), ev[2])
        s.ops['sp'].append((list(waits.items()), None, None))

    def emit(s, block):
        def mk(engname):
            def body(e):
                for waits, fn, ev in s.ops[engname]:
                    for (t, x), v in waits:
                        sem = s.sems[x] if t == 'E' else (s.ccsem if x == 'cc' else s.dsems[x])
                        e.wait_ge(sem, v)
                    if fn is None:
                        continue
                    ins = fn(e)
                    if ev is None:
                        continue
                    if ev[0] == 'E':
                        ins.then_inc(s.sems[engname], 1)
                    elif ev[1] == 'cc':
                        ins.then_inc(s.ccsem, 1)
                    else:
                        ins.then_inc(s.dsems[ev[1]], 16)
            return body
        block.tensor(mk('pe'))
        block.vector(mk('dve'))
        block.scalar(mk('act'))
        block.gpsimd(mk('pool'))
        block.sync(mk('sp'))


class Arena:
    def __init__(s, ar, nwords):
        s.ar = ar
        s.off = 0
        s.n = nwords
        s.peak = 0

    def alloc(s, shape, dt):
        n = 1
        for x in shape:
            n *= x
        words = n if dt == F32 else (n + 1) // 2
        words = (words + 7) // 8 * 8
        a = s.ar[:, s.off:s.off + words]
        s.off += words
        s.peak = max(s.peak, s.off)
        assert s.off <= s.n, ("arena overflow", s.off, s.n)
        if dt == BF16:
            a = a.bitcast(BF16)
        a = a[:, 0:n]
        if len(shape) == 2:
            a = a.rearrange("p (a b) -> p a b", b=shape[1])
        elif len(shape) == 3:
            a = a.rearrange("p (a b c) -> p a b c", b=shape[1], c=shape[2])
        elif len(shape) == 4:
            a = a.rearrange("p (a b c d) -> p a b c d", b=shape[1], c=shape[2], d=shape[3])
        return a


IN_SPECS = [
    ("xp", [1024, 1024]), ("xs", [1024, 1024]), ("xo", [256, 1024]),
    ("ck", [512, 1024]), ("cv", [512, 1024]), ("s0", [2, 4, 64, 64]),
    ("cT", [128, 16]), ("w_in", [1024, 8448]), ("w_ada", [1024, 3072]), ("w_out", [2048, 1024]),
    ("bada_fm", [128, 24]), ("bgate", [1, 1024]), ("normg_fm", [128, 8]), ("fg", [1, 1024]),
    ("lamv", [1, 256]), ("sublng", [1, 128]), ("mu_fm", [128, 64]), ("w0_fm", [128, 20]),
    ("a0_fm", [128, 20]), ("w2", [128, 1280]), ("a2", [128, 1280]), ("kk_fm", [128, 10]),
    ("ka_fm", [128, 10]), ("rk_fm", [128, 10]), ("lnxg", [1, 1280]), ("lnxb", [1, 1280]), ("wrs", [1024, 1024]),
    ("ident", [128, 128]), ("maskNM", [2, 128, 512]), ("maskT", [2, 128, 128]),
    ("bones", [128, 128]), ("hind", [128, 2]), ("selT", [1024, 256]),
    ("ropeall", [1024, 256]), ("ropeown", [256, 256]),
]
OUT_SPECS = [
    ("yp", [1024, 1024]), ("ys", [256, 1024]), ("nk", [1024, 1024]), ("nv", [1024, 1024]),
    ("ns", [4, 2, 16, 64, 64]),
]

ARENA_WORDS = 52480 - 4096
LVL_WORDS = 4096
NHEAD_A = 8
NHP = 8
NSEQ_R = 5
ND = 2
DEBUG = False
A_MODE = 'all'
DUMPS = []
STOP = None


def build():
    nc = bass.Bass("TRN2", target_bir_lowering=False)
    I = {n: nc.dram_tensor(n, sh, F32, kind="ExternalInput").ap() for n, sh in IN_SPECS}
    O = {n: nc.dram_tensor(n, sh, F32, kind="ExternalOutput").ap() for n, sh in OUT_SPECS}
    with ExitStack() as stack:
        ar = stack.enter_context(nc.sbuf_tensor("arena", [128, ARENA_WORDS], F32))
        PS = stack.enter_context(nc.psum_tensor("ps", [128, 4096], F32))
        lvl = stack.enter_context(nc.sbuf_tensor("lvl", [128, LVL_WORDS], LEVEL_DT))
        S = Sched(nc, stack)
        S.ccsem = stack.enter_context(nc.semaphore('ccsem'))
        A = Arena(ar, ARENA_WORDS)
        block = stack.enter_context(nc.Block())
        _program(nc, S, A, PS, I, O, lvl)
        S.finish()
        S.emit(block)
    return nc


def _program(nc, S, A, PS, I, O, lvl):
    def dma(eng, out, in_, r, w, is_out=False):
        S.op(eng, lambda e: e.dma_start(out=out, in_=in_), r, w, dma=True, is_out=is_out)

    def mm(out, lhsT, rhs, start, stop, r, w):
        S.op('pe', lambda e: e.matmul(out, lhsT, rhs, start=start, stop=stop), r, w, noinc=(not stop))

    def tr(out, in_, ident, r, w):
        S.op('pe', lambda e: e.transpose(out, in_, ident), r, w)

    def act(out, in_, func, r, w, bias=None, scale=None, accum=None):
        def f(e):
            kw = {}
            if bias is not None:
                kw['bias'] = bias
            if scale is not None:
                kw['scale'] = scale
            if accum is not None:
                kw['accum_out'] = accum
            return e.activation(out=out, in_=in_, func=func, **kw)
        S.op('act', f, r, w)

    def tt(eng, out, in0, in1, op, r, w):
        S.op(eng, lambda e: e.tensor_tensor(out=out, in0=in0, in1=in1, op=op), r, w)

    def ts(eng, out, in0, s1, s2, op0, op1, r, w):
        if s2 is None:
            S.op(eng, lambda e: e.tensor_scalar(out=out, in0=in0, scalar1=s1, scalar2=None, op0=op0), r, w)
        else:
            S.op(eng, lambda e: e.tensor_scalar(out=out, in0=in0, scalar1=s1, scalar2=s2, op0=op0, op1=op1), r, w)

    def stt(eng, out, in0, sc, in1, op0, op1, r, w):
        S.op(eng, lambda e: e.scalar_tensor_tensor(out=out, in0=in0, scalar=sc, in1=in1, op0=op0, op1=op1), r, w)

    def cp(eng, out, in_, r, w):
        if eng == 'act':
            act(out, in_, AF.Identity, r, w)
        else:
            S.op(eng, lambda e: e.tensor_copy(out=out, in_=in_), r, w)

    def red(out, in_, op, r, w):
        S.op('dve', lambda e: e.tensor_reduce(out=out, in_=in_, axis=AX.X, op=op), r, w)

    def recip(out, in_, r, w):
        S.op('dve', lambda e: e.reciprocal(out=out, in_=in_), r, w)

    def memset(eng, out, val, w):
        S.op(eng, lambda e: e.memset(out, val), (), w)

    def bank(b, c0=0, c1=512):
        return PS[:, b * 512 + c0: b * 512 + c1]

    def bankb(b):
        return PS[:, b * 512:(b + 1) * 512].bitcast(BF16)

    w_in_v = I['w_in'].rearrange("(kc p) n -> p kc n", p=128)

    def dump_all(bufs):
        S.barrier()
        for name, ap in bufs.items():
            sh = list(ap.shape)
            dt = nc.dram_tensor('dbg_' + name, sh, ap.dtype, kind="ExternalOutput").ap()
            DUMPS.append('dbg_' + name)
            dma('sp', dt, ap, (), (), is_out=True)

    identf = A.alloc([128], F32)
    identb = A.alloc([128], BF16)
    onesf = A.alloc([128], F32)
    maskNM = A.alloc([2, 512], BF16)
    maskT = A.alloc([2, 128], BF16)
    bonesb = A.alloc([128], BF16)
    hindb = A.alloc([2], BF16)
    selb = A.alloc([8, 256], BF16)
    cst = A.alloc([4], F32)
    hTp = A.alloc([8, 1024], BF16)
    hTs = A.alloc([8, 1024], BF16)
    hTo = A.alloc([8, 256], BF16)
    mixT = A.alloc([16, 1280], BF16)
    modfm = A.alloc([24, 2], F32)
    scale1 = A.alloc([8, 2], F32)
    neglam = A.alloc([1], F32)
    sgl = A.alloc([128], F32)
    mu = A.alloc([32, 2], F32)
    c0v = A.alloc([32], F32)
    w0v = A.alloc([20], F32)
    a0v = A.alloc([20], F32)
    kkv = A.alloc([10], F32)
    kav = A.alloc([10], F32)
    omka = A.alloc([10], F32)
    rkv_ = A.alloc([10], F32)
    W2b = A.alloc([1280], BF16)
    A2b = A.alloc([1280], BF16)
    twT = A.alloc([2048], BF16)
    laT = A.alloc([2048], BF16)

    dma('sp', identf, I['ident'], (), ['identf'])
    dma('pool', identb, I['ident'], (), ['identb'])
    dma('pool', maskNM, I['maskNM'].rearrange("d p n -> p d n"), (), ['maskNM'])
    dma('pool', maskT, I['maskT'].rearrange("d p n -> p d n"), (), ['maskT'])
    dma('pool', bonesb, I['bones'], (), ['bonesb'])
    dma('pool', hindb, I['hind'], (), ['hindb'])
    dma('pool', selb, I['selT'].rearrange("(j p) n -> p j n", p=128), (), ['selb'])
    dma('sp', mu, I['mu_fm'].rearrange("p (c j) -> p c j", j=2), (), ['mu'])
    dma('sp', w0v, I['w0_fm'], (), ['w0v'])
    dma('sp', a0v, I['a0_fm'], (), ['a0v'])
    dma('sp', kkv, I['kk_fm'], (), ['kkv'])
    dma('sp', kav, I['ka_fm'], (), ['kav'])
    dma('sp', rkv_, I['rk_fm'], (), ['rkv_'])
    dma('pool', W2b, I['w2'], (), ['W2b'])
    dma('pool', A2b, I['a2'], (), ['A2b'])
    memset('dve', onesf, 1.0, ['onesf'])
    memset('dve', cst[:, 0:1], 1e-12, ['cst'])
    tt('dve', c0v, mu[:, :, 0], mu[:, :, 1], ALU.add, ['mu'], ['c0v'])
    ts('dve', c0v, c0v, -1.0, 1.0, ALU.mult, ALU.add, ['c0v'], ['c0v'])
    ts('dve', omka, kav, -1.0, 1.0, ALU.mult, ALU.add, ['kav'], ['omka'])

    scp = A.alloc([8, 2], BF16)
    m0 = A.off
    cT = A.alloc([16], F32)
    sc = A.alloc([8, 2], F32)
    wadaf = [A.alloc([8, 512], F32) for _ in range(4)]
    bada = A.alloc([24], F32)
    normg = A.alloc([8], F32)
    lamt = A.alloc([4, 64], F32)
    lamp = A.alloc([2, 64], F32)
    lams = A.alloc([4], F32)

    dma('sp', cT, I['cT'], (), ['cT'])
    dma('sp', bada, I['bada_fm'], (), ['bada'])
    dma('sp', normg, I['normg_fm'], (), ['normg'])
    dma('sp', lamt.rearrange("p a b -> p (a b)"), I['lamv'][0:1, :].partition_broadcast(128), (), ['lamt'])
    dma('sp', sgl, I['sublng'][0:1, :].partition_broadcast(128), (), ['sgl'])
    wada_v = I['w_ada'].rearrange("(kc p) n -> p kc n", p=128)
    for n in range(4):
        dma('sp', wadaf[n], wada_v[:, :, n * 512:(n + 1) * 512], (), [('wada', n)])
    act(sc, cT.rearrange("p (c v) -> p c v", v=2), AF.Silu, ['cT'], ['sc'])
    cp('dve', scp, sc, ['sc'], ['scp'])
    for fc in range(16):
        for kc in range(8):
            mm(bank(0, fc * 2, fc * 2 + 2), wadaf[fc // 4][:, kc, (fc % 4) * 128:(fc % 4 + 1) * 128], sc[:, kc, :],
               kc == 0, kc == 7, ['sc', ('wada', fc // 4)], [('ps', 0)])
    tt('dve', modfm[:, 0:16, :], bank(0, 0, 32).rearrange("p (a b) -> p a b", b=2),
       bada[:, 0:16].unsqueeze(2).to_broadcast([128, 16, 2]), ALU.add, [('ps', 0), 'bada'], ['modfm'])
    ts('dve', scale1, modfm[:, 8:16, :], 1.0, None, ALU.add, None, ['modfm'], ['scale1'])
    tt('dve', scale1, scale1, normg.unsqueeze(2).to_broadcast([128, 8, 2]), ALU.mult, ['scale1', 'normg'], ['scale1'])
    tt('dve', lamp[:, 0, :], lamt[:, 0, :], lamt[:, 1, :], ALU.mult, ['lamt'], ['lamp'])
    tt('dve', lamp[:, 1, :], lamt[:, 2, :], lamt[:, 3, :], ALU.mult, ['lamt', 'lamp'], ['lamp'])
    red(lams[:, 0:2], lamp, ALU.add, ['lamp'], ['lams'])
    act(lams[:, 2:4], lams[:, 0:2], AF.Exp, ['lams'], ['lams2'])
    lam_init = 0.8 - 0.6 * math.exp(-0.3 * 0)
    tt('dve', neglam, lams[:, 3:4], lams[:, 2:3], ALU.subtract, ['lams2'], ['neglam'])
    ts('dve', neglam, neglam, -lam_init, None, ALU.add, None, ['neglam'], ['neglam'])
    ts('dve', sgl, sgl, 1.0 - lam_init, None, ALU.mult, None, ['sgl'], ['sgl'])

    if STOP == '0':
        return
    xt = [A.alloc([1024], F32) for _ in range(2)]
    xn = [A.alloc([1024], BF16) for _ in range(2)]
    junk = A.alloc([1024], BF16)
    st1 = [A.alloc([4], F32) for _ in range(2)]
    tiles = [('xp', g, hTp, g, 0) for g in range(8)] + [('xs', g, hTs, g, 1) for g in range(8)] + \
            [('xo', g, hTo, g, 1) for g in range(2)]
    for ti, (src, g, hT, tg, v) in enumerate(tiles):
        b = ti % 2
        dma('sp', xt[b], I[src][g * 128:(g + 1) * 128, :], (), [('xt', b)])
        act(junk, xt[b], AF.Square, [('xt', b)], ['junk', ('st1', b)], accum=st1[b][:, 0:1])
        ts('dve', st1[b][:, 1:2], st1[b][:, 0:1], 1.0 / 1024, 1e-6, ALU.mult, ALU.add, [('st1', b)], [('st1b', b)])
        act(st1[b][:, 2:3], st1[b][:, 1:2], AF.Sqrt, [('st1b', b)], [('st1c', b)])
        recip(st1[b][:, 3:4], st1[b][:, 2:3], [('st1c', b)], [('st1d', b)])
        ts('dve', xn[b], xt[b], st1[b][:, 3:4], None, ALU.mult, None, [('xt', b), ('st1d', b)], [('xn', b)])
        pb_ = bankb(3 + b)
        for kc in range(8):
            tr(pb_[:, kc * 128:(kc + 1) * 128], xn[b][:, kc * 128:(kc + 1) * 128], identb,
               [('xn', b), 'identb'], [('ps', 3 + b)])
        for kc in range(8):
            if kc % 2 == 0:
                act(hT[:, kc, tg * 128:(tg + 1) * 128], pb_[:, kc * 128:(kc + 1) * 128], AF.Identity,
                    [('ps', 3 + b), 'scale1', 'modfm'], [(src + 'h', tg)],
                    bias=modfm[:, kc, v:v + 1], scale=scale1[:, kc, v:v + 1])
        for kc in range(8):
            if kc % 2 == 1:
                ts('dve', hT[:, kc, tg * 128:(tg + 1) * 128], pb_[:, kc * 128:(kc + 1) * 128], scale1[:, kc, v:v + 1],
                   modfm[:, kc, v:v + 1], ALU.mult, ALU.add, [('ps', 3 + b), 'scale1', 'modfm'], [(src + 'h', tg)])
    if STOP == '1':
        return
    S.barrier()
    A.off = m0
    if STOP == '1b':
        return

    mA = A.off
    wA = [A.alloc([8, 512], BF16) for _ in range(2)]
    qkb = [A.alloc([256], BF16) for _ in range(2)]
    kvf = [A.alloc([256], F32) for _ in range(2)]
    qT2 = [A.alloc([256], BF16) for _ in range(2)]
    sg2 = [A.alloc([2, 128], F32) for _ in range(2)]
    kTp = [A.alloc([256], BF16) for _ in range(2)]
    vbp = [A.alloc([2, 128], BF16) for _ in range(2)]
    kTs = A.alloc([1536], BF16)
    vbs = A.alloc([12, 128], BF16)
    ckb = A.alloc([4, 128], BF16)
    ropa = A.alloc([8, 256], F32)
    ropo = A.alloc([2, 256], F32)
    xf = [A.alloc([128], F32) for _ in range(2)]
    rt1 = [A.alloc([128], F32) for _ in range(2)]
    rt2 = [A.alloc([128], F32) for _ in range(2)]
    xb16 = [A.alloc([128], BF16) for _ in range(2)]
    pbuf = [A.alloc([1536], BF16) for _ in range(2)]
    pT = [A.alloc([1536], BF16) for _ in range(2)]
    pbufp = [[A.alloc([256], BF16) for _ in range(2)] for _ in range(2)]
    pTp = [[A.alloc([256], BF16) for _ in range(2)] for _ in range(2)]
    ast = [A.alloc([16], F32) for _ in range(4)]
    of = [A.alloc([128], F32) for _ in range(4)]
    o1 = [A.alloc([128], F32) for _ in range(4)]
    on = [A.alloc([128], F32) for _ in range(4)]
    ob = [A.alloc([128], BF16) for _ in range(4)]
    ajunk = A.alloc([128], BF16)

    dma('sp', ropa, I['ropeall'].rearrange("(j p) n -> p j n", p=128), (), ['ropa'])
    dma('sp', ropo, I['ropeown'].rearrange("(j p) n -> p j n", p=128), (), ['ropo'])

    ctr = {'x': 0}

    def rope(src_ps, tab, dst16, rkeys, wkey):
        i = ctr['x'] % 2
        ctr['x'] += 1
        cp('act', xf[i], src_ps, rkeys, [('xf', i)])
        tt('dve', rt1[i], xf[i], tab[:, 0:128], ALU.mult, [('xf', i), 'ropa', 'ropo'], [('rt1', i)])
        xv = xf[i].rearrange("p (g h f) -> p g h f", h=2, f=16)
        sv = tab[:, 128:256].rearrange("p (g h f) -> p g h f", h=2, f=16)
        r2 = rt2[i].rearrange("p (g h f) -> p g h f", h=2, f=16)
        tt('pool', r2[:, :, 0, :], xv[:, :, 1, :], sv[:, :, 0, :], ALU.mult, [('xf', i), 'ropa', 'ropo'], [('rt2', i, 0)])
        tt('pool', r2[:, :, 1, :], xv[:, :, 0, :], sv[:, :, 1, :], ALU.mult, [('xf', i), 'ropa', 'ropo'], [('rt2', i, 1)])
        tt('dve', dst16, rt1[i], rt2[i], ALU.add, [('rt1', i), ('rt2', i, 0), ('rt2', i, 1)], [wkey])

    def drive2(gens):
        gens = [g for g in gens if g is not None]
        while gens:
            for g in list(gens):
                try:
                    next(g)
                except StopIteration:
                    gens.remove(g)

    def attn_unit(par, j, m, kind):
        ai = par * 2 + j
        if kind == 'p':
            ntk, kT_, vb_, kkey, vkey = 2, kTp[par], vbp[par], ('kTp', par), ('vbp', par)
            sb0 = 2 + 2 * j + m
            ob_ = 6 + j
            pbs, pTs, pkey = pbufp[j], pTp[j], ('pp', j)
            tcol = 512
        else:
            ntk, kT_, vb_, kkey, vkey = 12, kTs, vbs, 'kTs', 'vbs'
            sb0 = 2 + 3 * m
            ob_ = sb0 + 2
            pbs, pTs, pkey = pbuf, pT, ('ps_', 0)
            tcol = 0
        Tk = ntk * 128
        qT_ = qT2[par]
        s0c = sb0 * 512
        nsb = (Tk + 511) // 512
        sck = [('ps', sb0 + b_) for b_ in range(nsb)]
        for n0 in range(0, Tk, 512):
            w_ = min(512, Tk - n0)
            mm(PS[:, s0c + n0:s0c + n0 + w_], qT_[64 * m:64 * m + 64, j * 128:(j + 1) * 128],
               kT_[64 * m:64 * m + 64, n0:n0 + w_], True, True, [('qT', par, j), kkey], [('ps', sb0 + n0 // 512)])
        red(ast[ai][:, m:m + 1], PS[:, s0c:s0c + Tk], ALU.max, sck, [('ast', ai, 'mx', m)])
        ts('dve', ast[ai][:, 2 + m:3 + m], ast[ai][:, m:m + 1], -0.125, None, ALU.mult, None,
           [('ast', ai, 'mx', m)], [('ast', ai, 'nb', m)])
        act(pbs[m][:, 0:Tk], PS[:, s0c:s0c + Tk], AF.Exp, sck + [('ast', ai, 'nb', m)],
            [('pbuf', pkey, m), ('ast', ai, 'sum', m)], bias=ast[ai][:, 2 + m:3 + m], scale=0.125,
            accum=ast[ai][:, 4 + m:5 + m])
        yield
        ptv = PS[:, s0c:s0c + 1024].bitcast(BF16)
        for t in range(ntk):
            c_ = tcol + t * 128
            tr(ptv[:, c_:c_ + 128], pbs[m][:, t * 128:(t + 1) * 128], identb,
               [('pbuf', pkey, m), 'identb'], [('ps', sb0 + (c_ // 1024))])
        if ntk <= 2:
            cp('dve', pTs[m][:, 0:Tk], ptv[:, tcol:tcol + Tk], [('ps', sb0)], [('pT', pkey, m, 0)])
            ptk = [('pT', pkey, m, 0)]
        else:
            cp('dve', pTs[m][:, 0:768], ptv[:, 0:768], [('ps', sb0)], [('pT', pkey, m, 0)])
            cp('act', pTs[m][:, 768:1536], ptv[:, 768:1536], [('ps', sb0), ('ps', sb0 + 1)], [('pT', pkey, m, 1)])
            ptk = [('pT', pkey, m, 0), ('pT', pkey, m, 1)]
        yield
        oc = m * 128 if kind == 'p' else 0
        for t in range(ntk):
            mm(bank(ob_, oc, oc + 128), pTs[m][:, t * 128:(t + 1) * 128], vb_[:, t, :],
               t == 0, t == ntk - 1, ptk + [vkey], [('ps', ob_)])
        yield

    def attn_comb(h, par, j, kind, mixcol0):
        ai = par * 2 + j
        if kind == 'p':
            o1src, o2src, k1, k2, ob_ = bank(6 + j, 0, 128), bank(6 + j, 128, 256), ('ps', 6 + j), ('ps', 6 + j), 6 + j
        else:
            o1src, o2src, k1, k2, ob_ = bank(4, 0, 128), bank(7, 0, 128), ('ps', 4), ('ps', 7), 4
        a_ = ast[ai]
        recip(a_[:, 6:8], a_[:, 4:6], [('ast', ai, 'sum', 0), ('ast', ai, 'sum', 1)], [('ast', ai, 'rs')])
        tt('dve', a_[:, 8:9], a_[:, 7:8], neglam, ALU.mult, [('ast', ai, 'rs'), 'neglam'], [('ast', ai, 'c2')])
        act(o1[ai], o1src, AF.Identity, [k1, ('ast', ai, 'rs')], [('o1', ai)], scale=a_[:, 6:7])
        stt('dve', of[ai], o2src, a_[:, 8:9], o1[ai], ALU.mult, ALU.add, [k2, ('ast', ai, 'c2'), ('o1', ai)], [('of', ai)])
        yield
        act(ajunk, of[ai], AF.Square, [('of', ai)], ['ajunk', ('ast', ai, 'ss')], accum=a_[:, 9:10])
        ts('dve', a_[:, 10:11], a_[:, 9:10], 1.0 / 128, 1e-5, ALU.mult, ALU.add, [('ast', ai, 'ss')], [('ast', ai, 'ms')])
        act(a_[:, 11:12], a_[:, 10:11], AF.Sqrt, [('ast', ai, 'ms')], [('ast', ai, 'sd')])
        recip(a_[:, 12:13], a_[:, 11:12], [('ast', ai, 'sd')], [('ast', ai, 'rstd')])
        yield
        stt('dve', on[ai], of[ai], a_[:, 12:13], sgl, ALU.mult, ALU.mult, [('of', ai), ('ast', ai, 'rstd'), 'sgl'], [('on', ai)])
        tt('pool', ob[ai], on[ai], sg2[par][:, j, :], ALU.mult, [('on', ai), ('sg', par, j)], [('ob', ai)])
        pso = bankb(ob_)
        tr(pso[:, 512:640], ob[ai], identb, [('ob', ai), 'identb'], [('ps', ob_)])
        cp('act', mixT[:, h, mixcol0 + j * 128: mixcol0 + (j + 1) * 128], pso[:, 512:640], [('ps', ob_)],
           [('mixT', h, (mixcol0 // 128) + j)])
        yield

    def rr(gens):
        gens = list(gens)
        while gens:
            for g in list(gens):
                try:
                    next(g)
                except StopIteration:
                    gens.remove(g)
            yield

    def attn_gen(h, par, kind, mixcol0):
        if kind == 'p':
            for _ in rr([attn_unit(par, j, m, kind) for j in range(2) for m in range(2)]):
                yield
            for _ in rr([attn_comb(h, par, j, kind, mixcol0) for j in range(2)]):
                yield
        else:
            for j in range(2):
                for _ in rr([attn_unit(par, j, m, kind) for m in range(2)]):
                    yield
                for _ in attn_comb(h, par, j, kind, mixcol0):
                    yield

    def proj_gen(h, par, kind, s_):
        wb = wA[h % 2]
        wk = [('wA', h % 2, j4) for j4 in range(4)]
        pst = bankb(1)
        if kind == 'p':
            for j in range(2):
                g = s_ * 2 + j
                bi = g % 2
                for kc in range(8):
                    mm(bank(0), hTp[:, kc, g * 128:(g + 1) * 128], wb[:, kc, :], kc == 0, kc == 7,
                       [('xph', g)] + wk, [('ps', 0)])
                cp('dve', qkb[bi], bank(0, 0, 256), [('ps', 0)], [('qkb', bi)])
                cp('act', kvf[bi], bank(0, 128, 384), [('ps', 0)], [('kvf', bi)])
                dma('sp', O['nk'][g * 128:(g + 1) * 128, h * 128:(h + 1) * 128], kvf[bi][:, 0:128], [('kvf', bi)], (), is_out=True)
                dma('sp', O['nv'][g * 128:(g + 1) * 128, h * 128:(h + 1) * 128], kvf[bi][:, 128:256], [('kvf', bi)], (), is_out=True)
                cp('dve', vbp[par][:, j, :], bank(0, 256, 384), [('ps', 0)], [('vbp', par)])
                act(sg2[par][:, j, :], bank(0, 384, 512), AF.Silu, [('ps', 0)], [('sg', par, j)])
                yield
                tr(pst[:, 0:128], qkb[bi][:, 0:128], identb, [('qkb', bi), 'identb'], [('ps', 1)])
                tr(pst[:, 128:256], qkb[bi][:, 128:256], identb, [('qkb', bi), 'identb'], [('ps', 1)])
                cp('act', qT2[par][:, j * 128:(j + 1) * 128], pst[:, 0:128], [('ps', 1)], [('qT', par, j)])
                cp('act', kTp[par][:, j * 128:(j + 1) * 128], pst[:, 128:256], [('ps', 1)], [('kTp', par)])
                yield
        else:
            dma('pool', ckb, I['ck'].rearrange("(j p) c -> p j c", p=128)[:, :, h * 128:(h + 1) * 128], (), ['ckb'])
            dma('pool', vbs[:, 0:4, :], I['cv'].rearrange("(j p) c -> p j c", p=128)[:, :, h * 128:(h + 1) * 128], (), ['vbs'])
            for t in range(4):
                tr(pst[:, 512 + t * 128:512 + (t + 1) * 128], ckb[:, t, :], identb, ['ckb', 'identb'], [('ps', 1)])
            cp('dve', kTs[:, 0:512], pst[:, 512:1024], [('ps', 1)], ['kTs'])
            yield
            for j in range(8):
                for kc in range(8):
                    mm(bank(0, 0, 256), hTs[:, kc, j * 128:(j + 1) * 128], wb[:, kc, 128:384], kc == 0, kc == 7,
                       [('xsh', j)] + wk, [('ps', 0)])
                bi = j % 2
                cp('dve', vbs[:, 4 + j, :], bank(0, 128, 256), [('ps', 0)], ['vbs'])
                rope(bank(0, 0, 128), ropa[:, j, :], xb16[bi], [('ps', 0)], ('xb16', bi))
                yield
                tr(pst[:, 0:128], xb16[bi], identb, [('xb16', bi), 'identb'], [('ps', 1)])
                cp('act', kTs[:, 512 + j * 128:512 + (j + 1) * 128], pst[:, 0:128], [('ps', 1)], ['kTs'])
                yield
            for j in range(2):
                for kc in range(8):
                    mm(bank(0), hTo[:, kc, j * 128:(j + 1) * 128], wb[:, kc, :], kc == 0, kc == 7,
                       [('xoh', j)] + wk, [('ps', 0)])
                bi = j % 2
                act(sg2[par][:, j, :], bank(0, 384, 512), AF.Silu, [('ps', 0)], [('sg', par, j)])
                rope(bank(0, 0, 128), ropo[:, j, :], xb16[bi], [('ps', 0)], ('xb16', bi))
                yield
                tr(pst[:, 128:256], xb16[bi], identb, [('xb16', bi), 'identb'], [('ps', 1)])
                cp('act', qT2[par][:, j * 128:(j + 1) * 128], pst[:, 128:256], [('ps', 1)], [('qT', par, j)])
                yield

    ajobs = []
    for h in range(NHEAD_A):
        for s_ in range(4):
            ajobs.append((h, 'p', s_))
        ajobs.append((h, 's', 0))
    loaded = set()

    def load_w(h):
        if h in loaded or h >= NHEAD_A:
            return
        loaded.add(h)
        for j4, base in enumerate([0, 1024, 2048, 3072]):
            dma('pool', wA[h % 2][:, :, j4 * 128:(j4 + 1) * 128], w_in_v[:, :, base + h * 128: base + (h + 1) * 128], (),
                [('wA', h % 2, j4)])
    if ajobs:
        load_w(0)
        drive2([proj_gen(ajobs[0][0], 0, ajobs[0][1], ajobs[0][2])])
        for n, (h, kind, s_) in enumerate(ajobs):
            par = n % 2
            ag = attn_gen(h, par, kind, s_ * 256 if kind == 'p' else 1024)
            pg = None
            if n + 1 < len(ajobs):
                h2, kind2, s2 = ajobs[n + 1]
                load_w(h2)
                pg = proj_gen(h2, (n + 1) % 2, kind2, s2)
            drive2([ag, pg])
    if STOP == 'A':
        return
    S.barrier()
    A.off = mA

    wR = [A.alloc([8, 512], BF16) for _ in range(2)]
    rkv = A.alloc([3, 1024], F32)
    kk = A.alloc([1024], F32)
    t1 = A.alloc([1024], F32)
    prodb = A.alloc([1024], BF16)
    sqb = prodb
    vb16 = A.alloc([1024], BF16)
    VT = A.alloc([8, 128], BF16)
    sgb = A.alloc([8, 128], F32)
    yacc = A.alloc([8, 128], F32)
    wL = yacc.rearrange("p a b -> p (a b)").bitcast(BF16).rearrange("p (a b) -> p a b", b=256)
    bsum = A.alloc([16], F32)
    lnxg = [A.alloc([128], F32) for _ in range(2)]
    lnxb = [A.alloc([128], F32) for _ in range(2)]
    sgw = A.alloc([256], F32)
    Pp = A.alloc([256], F32)
    csb = A.alloc([256], F32)
    Winv = A.alloc([256], F32)
    av = A.alloc([256], F32)
    tmpa = A.alloc([256], F32)
    tmpb = A.alloc([256], F32)
    BW = A.alloc([256], BF16)
    KW = A.alloc([256], BF16)
    Wb2 = [A.alloc([2, 130], F32) for _ in range(2)]
    Kt2 = [A.alloc([256], BF16) for _ in range(2)]
    Bt2 = [A.alloc([256], BF16) for _ in range(2)]
    ARb2 = [A.alloc([2, 256], BF16) for _ in range(2)]
    BKT2 = [A.alloc([2, 256], BF16) for _ in range(2)]
    ATb2 = [A.alloc([2, 128], BF16) for _ in range(2)]
    NM = [A.alloc([512], BF16) for _ in range(4)]
    lo = [0]

    def lalloc(n):
        a_ = lvl[:, lo[0]:lo[0] + n].bitcast(F32)
        lo[0] += n
        assert lo[0] <= LVL_WORDS
        return a_
    X0 = [lalloc(128) for _ in range(4)]
    X0T = [lalloc(128) for _ in range(4)]
    XX = [[lalloc(256) for _ in range(2)] for _ in range(4)]
    Zb = [[lalloc(128) for _ in range(2)] for _ in range(4)]
    Zh = [A.alloc([128], BF16) for _ in range(4)]
    GT = A.alloc([8, 64], BF16)
    Hs = A.alloc([8, 64], F32)
    Qb = A.alloc([8, 128], BF16)
    Sf = [A.alloc([64], F32) for _ in range(2)]
    Sb = [A.alloc([64], BF16) for _ in range(2)]
    s0raw = A.alloc([128], F32)
    stg = [A.alloc([128], F32) for _ in range(2)]
    gst = A.alloc([8, 16], F32)
    ysq = t1.rearrange("p (a b) -> p a b", b=128)
    ybon = kk.rearrange("p (a b) -> p a b", b=128)
    obR = A.alloc([8, 128], BF16)

    dma('pool', wL, w_in_v[:, :, 4096 + 3072:4096 + 3328], (), ['wL'])
    for q in range(2):
        memset('dve', Wb2[q][:, :, 0:1], 1.0, [('Wbpad0', q)])
        memset('dve', Wb2[q][:, :, 129:130], 1.0, [('Wbpad1', q)])

    T = 1024
    units = [(hTp, 'xph', 0, 'p'), (hTs, 'xsh', 1024, 's')]

    def proj_shift(w_ap, wkeys, hT, hkey, ci, dst, dkey, kind, b0):
        for n0 in range(0, T, 512):
            bnk = b0 + n0 // 512
            hk = [(hkey, n0 // 128 + q) for q in range(4)]
            for kc in range(8):
                mm(bank(bnk), w_ap[:, kc, :], hT[:, kc, n0:n0 + 512], kc == 0, kc == 7, hk + wkeys, [('ps', bnk)])
        for n0 in range(0, T, 512):
            bnk = b0 + n0 // 512
            act(dst[:, n0:n0 + 512], bank(bnk), AF.Identity, [('ps', bnk), 'c0v'], [dkey], scale=c0v[:, ci:ci + 1])
        psv = PS[:, b0 * 512:b0 * 512 + 1024]
        pk = [('ps', b0), ('ps', b0 + 1)]
        blocks = [(0, T)] if kind == 's' else [(q * 256, (q + 1) * 256) for q in range(4)]
        for (s_, e_) in blocks:
            kk_ = [('ps', b0 + s_ // 512)] if (s_ // 512 == (e_ - 1) // 512) else pk
            stt('dve', dst[:, s_ + 1:e_], psv[:, s_:e_ - 1], mu[:, ci, 0:1], dst[:, s_ + 1:e_], ALU.mult, ALU.add,
                kk_ + ['mu', dkey], [dkey])
            stt('dve', dst[:, s_:e_ - 1], psv[:, s_ + 1:e_], mu[:, ci, 1:2], dst[:, s_:e_ - 1], ALU.mult, ALU.add,
                kk_ + ['mu', dkey], [dkey])

    for (hT, hkey, lc0, kind) in units:
        for c in range(2):
            proj_shift(wL[:, :, c * 128:(c + 1) * 128], ['wL'], hT, hkey, 24 + c, t1, 't1', kind, 2 * c)
            if c == 0:
                act(twT[:, lc0:lc0 + T], t1[:, 0:T], AF.Tanh, ['t1'], [('twT', kind)])
            else:
                cp('act', laT[:, lc0:lc0 + T], t1[:, 0:T], ['t1'], [('laT', kind)])
    S.barrier()

    def prep_gen(hp, kind, lc0, d, seg, par):
        Wb, Kt, Bt, ARb, BKT, ATb = Wb2[par], Kt2[par], Bt2[par], ARb2[par], BKT2[par], ATb2[par]
        kT_, rT_ = rkv[:, 1, :], rkv[:, 0, :]
        c0_ = seg * 256
        lc = lc0 + c0_
        mm(bank(0, 0, 256), W2b[64 * d:64 * d + 64, hp * 128:(hp + 1) * 128], twT[64 * d:64 * d + 64, lc:lc + 256],
           True, True, ['W2b', ('twT', kind)], [('ps', 0)])
        act(sgw, bank(0, 0, 256), AF.Sigmoid, [('ps', 0), 'w0v'], ['sgw'], bias=w0v[:, hp * 2 + d:hp * 2 + d + 1])
        mm(bank(1, 0, 256), A2b[64 * d:64 * d + 64, hp * 128:(hp + 1) * 128], laT[64 * d:64 * d + 64, lc:lc + 256],
           True, True, ['A2b', ('laT', kind)], [('ps', 1)])
        act(av, bank(1, 0, 256), AF.Sigmoid, [('ps', 1), 'a0v'], ['av'], bias=a0v[:, hp * 2 + d:hp * 2 + d + 1])
        yield
        for t in range(2):
            S.op('dve', (lambda t=t: (lambda e: e.tensor_tensor_scan(
                out=Pp[:, t * 128:(t + 1) * 128], data0=onesf, data1=sgw[:, t * 128:(t + 1) * 128],
                initial=0.0, op0=ALU.mult, op1=ALU.add)))(), ['sgw', 'onesf'], [('Pp', t)])
        if d == 0:
            cs = Pp
            csk = [('Pp', 0), ('Pp', 1)]
        else:
            for t in range(2):
                stt('dve', csb[:, t * 128:(t + 1) * 128], sgw[:, t * 128:(t + 1) * 128],
                    Pp[:, t * 128 + 127:t * 128 + 128], Pp[:, t * 128:(t + 1) * 128], ALU.add, ALU.subtract,
                    ['sgw', ('Pp', t)], [('csb', t)])
            cs = csb
            csk = [('csb', 0), ('csb', 1)]
        yield
        act(Wb[:, :, 1:129], cs.rearrange("p (a b) -> p a b", b=128), AF.Exp, csk, [('Wb', par)], scale=-DC)
        act(Winv, cs, AF.Exp, csk, ['Winv'], scale=DC)
        ts('dve', tmpa, av, kav[:, hp:hp + 1], omka[:, hp:hp + 1], ALU.mult, ALU.add, ['av', 'kav', 'omka'], ['tmpa'])
        tt('pool', tmpb, kk[:, c0_:c0_ + 256], av, ALU.mult, ['kk', 'av'], ['tmpb'])
        yield
        tt('dve', tmpa, tmpa, kT_[:, c0_:c0_ + 256], ALU.mult, ['tmpa', ('rkv', 1)], ['tmpa'])
        tt('dve', Kt, tmpa, Winv, ALU.mult, ['tmpa', 'Winv'], [('Kt', par)])
        tt('dve', Bt, tmpb, Winv, ALU.mult, ['tmpb', 'Winv'], [('Bt', par)])
        yield
        Wprev = Wb[:, :, 0:128] if d == 0 else Wb[:, :, 2:130]
        stt('dve', ARb[:, :, 0:128], kk[:, c0_:c0_ + 256].rearrange("p (a b) -> p a b", b=128), -1.0, Wprev,
            ALU.mult, ALU.mult, ['kk', ('Wb', par), ('Wbpad0', par), ('Wbpad1', par)], [('ARb', par, 'a')])
        tt('dve', ARb[:, :, 128:256], rT_[:, c0_:c0_ + 256].rearrange("p (a b) -> p a b", b=128), Wb[:, :, 1:129],
           ALU.mult, [('rkv', 0), ('Wb', par)], [('ARb', par, 'r')])
        yield
        pst2 = bankb(2)
        for t in range(2):
            wc = Wb[:, t, 128:129] if d == 0 else Wb[:, t, 1:2]
            ts('dve', BW[:, t * 128:(t + 1) * 128], Bt[:, t * 128:(t + 1) * 128], wc, None, ALU.mult, None,
               [('Bt', par), ('Wb', par)], [('BW', t)])
            ts('dve', KW[:, t * 128:(t + 1) * 128], Kt[:, t * 128:(t + 1) * 128], wc, None, ALU.mult, None,
               [('Kt', par), ('Wb', par)], [('KW', t)])
            yield
        for t in range(2):
            tr(pst2[:, t * 384:t * 384 + 128], BW[:, t * 128:(t + 1) * 128], identb, [('BW', t), 'identb'], [('ps', 2)])
            tr(pst2[:, t * 384 + 128:t * 384 + 256], KW[:, t * 128:(t + 1) * 128], identb, [('KW', t), 'identb'], [('ps', 2)])
            tr(pst2[:, t * 384 + 256:t * 384 + 384], ARb[:, t, 0:128], identb, [('ARb', par, 'a'), 'identb'], [('ps', 2)])
        for t in range(2):
            cp('act', BKT[:, t, :], pst2[:, t * 384:t * 384 + 256], [('ps', 2)], [('BKT', par, t)])
            cp('act', ATb[:, t, :], pst2[:, t * 384 + 256:t * 384 + 384], [('ps', 2)], [('ATb', par, t)])
        yield

    def rest_gen(hp, kind, d, seg, par, state_in):
        Wb, Kt, Bt, ARb, BKT, ATb = Wb2[par], Kt2[par], Bt2[par], ARb2[par], BKT2[par], ATb2[par]
        gt0 = seg * 2
        P = []
        for t in range(2):
            for e in range(2):
                zi = t * 2 + e
                P.append(dict(t=t, e=e, zi=zi, si=zi, pb=64 * e, bM=4 + zi, gt=gt0 + t))
        for p in P:
            t, e, pb, bM, si = p['t'], p['e'], p['pb'], p['bM'], p['si']
            mm(bank(bM, 0, 256), Bt[pb:pb + 64, t * 128:(t + 1) * 128], ARb[pb:pb + 64, t, :], True, True,
               [('Bt', par), ('ARb', par, 'a'), ('ARb', par, 'r')], [('ps', bM)])
            mm(bank(bM, 256, 512), Kt[pb:pb + 64, t * 128:(t + 1) * 128], ARb[pb:pb + 64, t, :], True, True,
               [('Kt', par), ('ARb', par, 'a'), ('ARb', par, 'r')], [('ps', bM)])
        for p in P:
            bM, si = p['bM'], p['si']
            tt('dve', NM[si], bank(bM), maskNM[:, d, :], ALU.mult, [('ps', bM), 'maskNM'], [('NM', si)])
            tt('dve', X0[si].bitcast(LEVEL_DT), bank(bM, 0, 128), maskNM[:, d, 0:128], ALU.mult, [('ps', bM), 'maskNM'], [('X0', si)])
        yield
        for p in P:
            t, e, pb, bM, si, gt = p['t'], p['e'], p['pb'], p['bM'], p['si'], p['gt']
            mm(bank(bM, 0, 128), ARb[pb:pb + 64, t, 0:128], Bt[pb:pb + 64, t * 128:(t + 1) * 128], True, True,
               [('Bt', par), ('ARb', par, 'a')], [('ps', bM)])
            mm(bank(bM, 384, 448), NM[si][:, 256:384], VT[:, gt, e * 64:(e + 1) * 64], True, True,
               [('NM', si), 'VT'], [('ps', bM)])
        for p in P:
            t, e, bM, si, zi = p['t'], p['e'], p['bM'], p['si'], p['zi']
            tt('dve', X0T[si].bitcast(LEVEL_DT), bank(bM, 0, 128), maskT[:, d, :], ALU.mult, [('ps', bM), 'maskT'], [('X0T', si)])
            cp('act', Zb[zi][0].bitcast(LEVEL_DT)[:, 64:128], bank(bM, 384, 448), [('ps', bM)], [('Zb', zi, 0, 'u')])
            cp('pool', Zb[zi][0].bitcast(LEVEL_DT)[:, 0:64], ATb[:, t, e * 64:(e + 1) * 64], [('ATb', par, t)], [('Zb', zi, 0, 'a')])
        yield
        for j in range(7):
            for p in P:
                bM, si, zi = p['bM'], p['si'], p['zi']
                Xj = X0[si] if j == 0 else XX[si][j % 2][:, 0:128]
                XjT = X0T[si] if j == 0 else XX[si][j % 2][:, 128:256]
                xk = [('X0', si), ('X0T', si)] if j == 0 else [('XX', si, j % 2)]
                zc = Zb[zi][j % 2]
                zck = [('Zb', zi, j % 2, 'a'), ('Zb', zi, j % 2, 'u')]
                Xr, XTr, zr = Xj.bitcast(LEVEL_DT), XjT.bitcast(LEVEL_DT), zc.bitcast(LEVEL_DT)
                mm(bank(bM, 0, 128), Xr, zr, True, True, xk + zck, [('ps', bM)])
                if j < 6:
                    mm(bank(bM, 128, 256), XTr, Xr, True, True, xk, [('ps', bM)])
                    mm(bank(bM, 256, 384), Xr, XTr, True, True, xk, [('ps', bM)])
            for p in P:
                bM, si, zi = p['bM'], p['si'], p['zi']
                zc = Zb[zi][j % 2]
                zn = Zb[zi][(j + 1) % 2]
                zck = [('Zb', zi, j % 2, 'a'), ('Zb', zi, j % 2, 'u')]
                znk = [('Zb', zi, (j + 1) % 2, 'a'), ('Zb', zi, (j + 1) % 2, 'u')]
                tt('dve', zn.bitcast(LEVEL_DT), bank(bM, 0, 128), zc, ALU.add, [('ps', bM)] + zck, znk)
                if j < 6:
                    cp('act', XX[si][(j + 1) % 2].bitcast(LEVEL_DT), bank(bM, 128, 384), [('ps', bM)], [('XX', si, (j + 1) % 2)])
            yield
        for p in P:
            si, zi = p['si'], p['zi']
            cp('pool', Zh[si], Zb[zi][1], [('Zb', zi, 1, 'a'), ('Zb', zi, 1, 'u')], [('Zh', si)])
        for p in P:
            t, e, pb, bM, si, gt = p['t'], p['e'], p['pb'], p['bM'], p['si'], p['gt']
            Z = Zh[si]
            zk = [('Zh', si)]
            mm(bank(bM, 448, 512)[pb:pb + 64, :], Z[:, 0:64], BKT[:, t, e * 64:(e + 1) * 64], True, True,
               zk + [('BKT', par, t)], [('ps', bM)])
            mm(bank(bM, 384, 448)[pb:pb + 64, :], BKT[:, t, e * 64:(e + 1) * 64], Z[:, 64:128], True, False,
               zk + [('BKT', par, t)], [('ps', bM)])
            mm(bank(bM, 384, 448)[pb:pb + 64, :], BKT[:, t, 128 + e * 64:128 + (e + 1) * 64],
               VT[:, gt, e * 64:(e + 1) * 64], False, True, ['VT', ('BKT', par, t)], [('ps', bM)])
            mm(bank(bM, 0, 128)[pb:pb + 64, :], Z[:, 0:64], NM[si][:, 128:256], True, True,
               zk + [('NM', si)], [('ps', bM)])
            mm(bank(bM, 128, 192), NM[si][:, 128:256], Z[:, 64:128], True, False, zk + [('NM', si)], [('ps', bM)])
            mm(bank(bM, 128, 192), NM[si][:, 384:512], VT[:, gt, e * 64:(e + 1) * 64], False, True,
               ['VT', ('NM', si)], [('ps', bM)])
        yield
        for p in P:
            t, e, pb, bM, si, gt = p['t'], p['e'], p['pb'], p['bM'], p['si'], p['gt']
            wc = Wb[pb:pb + 64, t, 128:129] if d == 0 else Wb[pb:pb + 64, t, 1:2]
            stt('dve', GT[pb:pb + 64, gt, :], identf[pb:pb + 64, pb:pb + 64], wc, bank(bM, 448, 512)[pb:pb + 64, :],
                ALU.mult, ALU.add, [('ps', bM), 'identf', ('Wb', par)], [('GT', gt, e)])
            tt('dve', Qb[pb:pb + 64, gt, :], bank(bM, 0, 128)[pb:pb + 64, :], ARb[pb:pb + 64, t, 128:256], ALU.add,
               [('ps', bM), ('ARb', par, 'r')], [('Qb', gt, e)])
            cp('act', Hs[pb:pb + 64, gt, :], bank(bM, 384, 448)[pb:pb + 64, :], [('ps', bM)], [('Hs', gt, e)])
            if d == 0:
                cp('act', yacc[:, gt, e * 64:(e + 1) * 64], bank(bM, 128, 192), [('ps', bM)], [('yacc', gt, e)])
            else:
                tt('dve', yacc[:, gt, e * 64:(e + 1) * 64], bank(bM, 128, 192), yacc[:, gt, e * 64:(e + 1) * 64],
                   ALU.add, [('ps', bM), ('yacc', gt, e)], [('yacc', gt, e)])
        yield
        have_state = state_in
        order = [0, 1] if d == 0 else [1, 0]
        for t in order:
            gt = gt0 + t
            for e in range(2):
                pb = 64 * e
                if have_state:
                    mm(bank(3, e * 64, e * 64 + 64), Qb[pb:pb + 64, gt, :], Sb[d][pb:pb + 64, :], True, True,
                       [('Qb', gt, e), ('Sb', d, e)], [('ps', 3)])
                    mm(bank(3, 128 + e * 64, 192 + e * 64)[pb:pb + 64, :], GT[pb:pb + 64, gt, :], Sb[d][pb:pb + 64, :],
                       True, True, [('GT', gt, e), ('Sb', d, e)], [('ps', 3)])
            for e in range(2):
                pb = 64 * e
                if have_state:
                    tt('dve', yacc[:, gt, e * 64:(e + 1) * 64], bank(3, e * 64, e * 64 + 64),
                       yacc[:, gt, e * 64:(e + 1) * 64], ALU.add, [('ps', 3), ('yacc', gt, e)], [('yacc', gt, e)])
                    tt('dve', Sf[d][pb:pb + 64, :], bank(3, 128 + e * 64, 192 + e * 64)[pb:pb + 64, :], Hs[pb:pb + 64, gt, :],
                       ALU.add, [('ps', 3), ('Hs', gt, e)], [('Sf', d, e)])
                else:
                    cp('dve', Sf[d][pb:pb + 64, :], Hs[pb:pb + 64, gt, :], [('Hs', gt, e)], [('Sf', d, e)])
                cp('act', Sb[d][pb:pb + 64, :], Sf[d][pb:pb + 64, :], [('Sf', d, e)], [('Sb', d, e)])
            have_state = True
            yield
        if kind == 'p':
            q = (seg + d) % 2
            tr(bank(3, 256, 384)[0:64, :], Sf[d], identf, [('Sf', d, 0), ('Sf', d, 1), 'identf'], [('ps', 3)])
            cp('act', stg[q][0:64, :], bank(3, 256, 384)[0:64, :], [('ps', 3)], [('stg', q)])
            dma('sp', O['ns'][seg, d, 2 * hp:2 * hp + 2, :, :].rearrange("e v k -> v e k"),
                stg[q][0:64, :].rearrange("p (e k) -> p e k", e=2), [('stg', q)], (), is_out=True)
            yield

    def drive(a, b):
        gens = [g for g in (a, b) if g is not None]
        while gens:
            for g in list(gens):
                try:
                    next(g)
                except StopIteration:
                    gens.remove(g)

    jobno = [0]
    ag_in = nc.dram_tensor("ag_in", [1024, 256], BF16)
    ag_out = nc.dram_tensor("ag_out", [4096, 256], BF16)
    wrs_v = I['wrs'].rearrange("(kc p) n -> p kc n", p=128)
    unit_of = {'p': (hTp, 'xph', 0), 's': (hTs, 'xsh', 1024)}
    tasks = ([('s', 8), ('s', 9)] if NSEQ_R >= 2 else []) + [('p', h_) for h_ in range(NHP)]

    def ci_of(a_, hp):
        return a_ * 8 + hp if hp < 8 else 26 + a_ * 2 + (hp - 8)
    for ti_, (kind, hp) in enumerate(tasks):
        tp = ti_ % 2
        wr = wR[tp]
        for a_, base in enumerate([4096, 4096 + 1024, 4096 + 2048, 4096 + 3328]):
            if hp < 8:
                wsrc = w_in_v[:, :, base + hp * 128: base + (hp + 1) * 128]
            else:
                wsrc = wrs_v[:, :, (hp - 8) * 512 + a_ * 128:(hp - 8) * 512 + (a_ + 1) * 128]
            dma('pool', wr[:, :, a_ * 128:(a_ + 1) * 128], wsrc, (), [('wR', tp, a_)])
        dma('sp', lnxg[tp], I['lnxg'][0:1, hp * 128:(hp + 1) * 128].partition_broadcast(128), (), [('lnxg', tp)])
        dma('sp', lnxb[tp], I['lnxb'][0:1, hp * 128:(hp + 1) * 128].partition_broadcast(128), (), [('lnxb', tp)])
        if ti_ == 2 and tasks[0][0] == 's':
            S.op('pool', lambda e: e.collective_compute("AllGather", ALU.bypass, replica_groups=[[0, 1, 2, 3], [4, 5, 6, 7]],
                                                        ins=[ag_in.ap().opt()], outs=[ag_out.ap().opt()]),
                 [('ag_in', 0), ('ag_in', 1)], ['ag_out'], dma=True, cc=True)
        for (hT, hkey, lc0) in [unit_of[kind]]:
            nt = 8
            for a_ in range(3):
                proj_shift(wr[:, :, a_ * 128:(a_ + 1) * 128], [('wR', tp, a_)], hT, hkey, ci_of(a_, hp), rkv[:, a_, :],
                           ('rkv', a_), kind, 2 * a_)
            rT_, kT_, vT_ = rkv[:, 0, :], rkv[:, 1, :], rkv[:, 2, :]
            for t in range(nt):
                bnk = 6 + (t // 4) % 2
                for kc in range(8):
                    mm(bank(bnk, (t % 4) * 128, (t % 4 + 1) * 128), hT[:, kc, t * 128:(t + 1) * 128],
                       wr[:, kc, 384:512], kc == 0, kc == 7, [(hkey, t), ('wR', tp, 3)], [('ps', bnk)])
                if t % 4 == 3:
                    act(sgb[:, t - 3:t + 1, :], bank(bnk).rearrange("p (a b) -> p a b", b=128), AF.Silu, [('ps', bnk)],
                        [('sgb', q) for q in range(t - 3, t + 1)])
            ts('dve', t1[:, 0:T], kT_[:, 0:T], kkv[:, hp:hp + 1], None, ALU.mult, None, [('rkv', 1), 'kkv'], ['t1'])
            act(sqb[:, 0:T], t1[:, 0:T], AF.Square, ['t1'], ['prodb'])
            for n0 in range(0, T, 512):
                bnk = (n0 // 512) % 2
                mm(bank(bnk), bonesb, sqb[:, n0:n0 + 512], True, True, ['bonesb', 'prodb'], [('ps', bnk)])
                act(kk[:, n0:n0 + 512], bank(bnk), AF.Sqrt, [('ps', bnk), 'cst'], ['kk'], bias=cst[:, 0:1])
            recip(kk[:, 0:T], kk[:, 0:T], ['kk'], ['kk'])
            tt('dve', kk[:, 0:T], kk[:, 0:T], t1[:, 0:T], ALU.mult, ['kk', 't1'], ['kk'])
            stt('dve', prodb[:, 0:T], rT_[:, 0:T], rkv_[:, hp:hp + 1], kT_[:, 0:T], ALU.mult, ALU.mult,
                [('rkv', 0), ('rkv', 1), 'rkv_'], ['prodb'])
            for t in range(nt):
                mm(bank(1, t * 2, t * 2 + 2), prodb[:, t * 128:(t + 1) * 128], hindb, True, True, ['prodb', 'hindb'], [('ps', 1)])
            cp('act', bsum[:, 0:nt * 2], bank(1, 0, nt * 2), [('ps', 1)], ['bsum'])
            cp('act', vb16[:, 0:T], vT_[:, 0:T], [('rkv', 2)], ['vb16'])
            pst2 = bankb(2)
            for t in range(nt):
                tr(pst2[:, t * 128:(t + 1) * 128], vb16[:, t * 128:(t + 1) * 128], identb, ['vb16', 'identb'], [('ps', 2)])
            cp('dve', VT[:, 0:nt, :], pst2[:, 0:nt * 128].rearrange("p (a b) -> p a b", b=128), [('ps', 2)], ['VT'])
            jobs = []
            for d in range(ND):
                segs = list(range(4)) if d == 0 else list(range(3, -1, -1))
                for i_, seg in enumerate(segs):
                    st_in = (kind == 's')
                    jobs.append((d, seg, st_in, (kind == 's' and i_ == 0)))
            pars = []
            for _ in jobs:
                pars.append(jobno[0] % 2)
                jobno[0] += 1
            pg = prep_gen(hp, kind, lc0, jobs[0][0], jobs[0][1], pars[0])
            drive(pg, None)
            for n, (d, seg, st_in, load_s0) in enumerate(jobs):
                if load_s0:
                    dma('sp', s0raw[0:64, :].rearrange("p (e k) -> p e k", e=2),
                        I['s0'][d, 2 * (hp - 8):2 * (hp - 8) + 2, :, :].rearrange("e v k -> v e k"), (), ['s0raw'])
                    tr(bank(3, 0, 64), s0raw[0:64, :], identf[0:64, 0:64], ['s0raw', 'identf'], [('ps', 3)])
                    for e in range(2):
                        pb = 64 * e
                        cp('dve', Sf[d][pb:pb + 64, :], bank(3, 0, 64)[pb:pb + 64, :], [('ps', 3)], [('Sf', d, e)])
                        cp('act', Sb[d][pb:pb + 64, :], bank(3, 0, 64)[pb:pb + 64, :], [('ps', 3)], [('Sb', d, e)])
                rg = rest_gen(hp, kind, d, seg, pars[n], st_in)
                ng = None
                if n + 1 < len(jobs):
                    ng = prep_gen(hp, kind, lc0, jobs[n + 1][0], jobs[n + 1][1], pars[n + 1])
                drive(rg, ng)
            n2 = nt * 2
            yk = [('yacc', t, e) for t in range(nt) for e in range(2)]
            yv = yacc[:, 0:nt, :].rearrange("p a (e f) -> p (a e) f", e=2)
            red(gst[:, 0, 0:n2], yv, ALU.add, yk, [('gst', 0)])
            act(ysq[:, 0:nt, :], yacc[:, 0:nt, :], AF.Square, yk, ['t1'])
            red(gst[:, 1, 0:n2], ysq[:, 0:nt, :].rearrange("p a (e f) -> p (a e) f", e=2), ALU.add, ['t1'], [('gst', 1)])
            ts('dve', gst[:, 2, 0:n2], gst[:, 0, 0:n2], 1.0 / 64, None, ALU.mult, None, [('gst', 0)], [('gst', 2)])
            tt('dve', gst[:, 3, 0:n2], gst[:, 2, 0:n2], gst[:, 2, 0:n2], ALU.mult, [('gst', 2)], [('gst', 3)])
            stt('dve', gst[:, 4, 0:n2], gst[:, 1, 0:n2], 1.0 / 64, gst[:, 3, 0:n2], ALU.mult, ALU.subtract,
                [('gst', 1), ('gst', 3)], [('gst', 4)])
            ts('dve', gst[:, 4, 0:n2], gst[:, 4, 0:n2], 64e-5, None, ALU.add, None, [('gst', 4)], [('gst', 4)])
            act(gst[:, 5, 0:n2], gst[:, 4, 0:n2], AF.Sqrt, [('gst', 4)], [('gst', 5)])
            recip(gst[:, 6, 0:n2], gst[:, 5, 0:n2], [('gst', 5)], [('gst', 6)])
            ysv = ysq[:, 0:nt, :].rearrange("p a (e f) -> p (a e) f", e=2)
            tt('dve', ysv, yv, gst[:, 2, 0:n2].unsqueeze(2).to_broadcast([128, n2, 64]), ALU.subtract, yk + [('gst', 2)], ['t1'])
            tt('dve', ysv, ysv, gst[:, 6, 0:n2].unsqueeze(2).to_broadcast([128, n2, 64]), ALU.mult, ['t1', ('gst', 6)], ['t1'])
            tt('dve', ysq[:, 0:nt, :], ysq[:, 0:nt, :], lnxg[tp].unsqueeze(1).to_broadcast([128, nt, 128]),
               ALU.mult, ['t1', ('lnxg', tp)], ['t1'])
            tt('dve', ysq[:, 0:nt, :], ysq[:, 0:nt, :], lnxb[tp].unsqueeze(1).to_broadcast([128, nt, 128]),
               ALU.add, ['t1', ('lnxb', tp)], ['t1'])
            tt('dve', ybon[:, 0:nt, :].rearrange("p a (e f) -> p (a e) f", e=2),
               VT[:, 0:nt, :].rearrange("p a (e f) -> p (a e) f", e=2),
               bsum[:, 0:n2].unsqueeze(2).to_broadcast([128, n2, 64]), ALU.mult, ['VT', 'bsum'], ['kk'])
            tt('dve', ysq[:, 0:nt, :], ysq[:, 0:nt, :], ybon[:, 0:nt, :], ALU.add, ['t1', 'kk'], ['t1'])
            tt('dve', obR[:, 0:nt, :], ysq[:, 0:nt, :], sgb[:, 0:nt, :], ALU.mult, ['t1'] + [('sgb', t) for t in range(nt)], ['obR'])
            if kind == 'p':
                pst2 = bankb(2)
                for t in range(nt):
                    tr(pst2[:, t * 128:(t + 1) * 128], obR[:, t, :], identb, ['obR', 'identb'], [('ps', 2)])
                cp('act', mixT[:, 8 + hp, 0:1024], pst2[:, 0:1024], [('ps', 2)], [('mixT', 8 + hp, g) for g in range(8)])
            else:
                i_ = hp - 8
                dma('sp', ag_in.ap().rearrange("(t p) (i f) -> p t i f", p=128, i=2)[:, :, i_, :], obR[:, 0:8, :], ['obR'],
                    [('ag_in', i_)])
    if DEBUG:
        dump_all(dict(yacc=yacc, Sf0=Sf[0], GT=GT, Hs=Hs))
    if STOP == 'R':
        return
    S.barrier()
    A.off = mA

    Gt = A.alloc([32, 256], BF16)
    dma('sp', Gt, ag_out.ap().rearrange("(rt p) f -> p rt f", p=128), ['ag_out'], ['Gt'])
    for r_ in range(4):
        for i_ in range(2):
            hq = 2 * r_ + i_
            bq = 4 + (hq % 4)
            for t in range(8):
                mm(bank(bq, 0, 256), Gt[:, r_ * 8 + t, i_ * 128:(i_ + 1) * 128], selb[:, t, :], t == 0, t == 7,
                   ['Gt', 'selb'], [('ps', bq)])
            cp('act', mixT[:, 8 + hq, 1024:1280], bank(bq, 0, 256), [('ps', bq)], [('mixT', 8 + hq, 8), ('mixT', 8 + hq, 9)])
    S.barrier()
    A.off = mA
    wout = A.alloc([16, 1024], BF16)
    fgbc = A.alloc([1024], F32)
    gatebc = A.alloc([2, 1024], F32)
    bgbc = A.alloc([1024], F32)
    scbc = A.alloc([8, 2, 128], BF16)
    wadg = A.alloc([8, 1024], BF16)
    xr = [A.alloc([1024], F32) for _ in range(2)]
    yv_ = [A.alloc([1024], F32) for _ in range(2)]
    ojunk = A.alloc([1024], BF16)
    ost = [A.alloc([4], F32) for _ in range(2)]
    wout_v = I['w_out'].rearrange("(c p) n -> p c n", p=128)
    wada_v = I['w_ada'].rearrange("(kc p) n -> p kc n", p=128)
    for n in range(2):
        dma('pool', wadg[:, :, n * 512:(n + 1) * 512], wada_v[:, :, 2048 + n * 512:2048 + (n + 1) * 512], (), [('wadg', n)])
    for c4 in range(4):
        dma('pool', wout[:, c4 * 4:(c4 + 1) * 4, :], wout_v[:, c4 * 4:(c4 + 1) * 4, :], (), [('wout', c4)])
    dma('sp', fgbc, I['fg'][0:1, :].partition_broadcast(128), (), ['fgbc'])
    dma('sp', bgbc, I['bgate'][0:1, :].partition_broadcast(128), (), ['bgbc'])
    cp('dve', scbc, scp.unsqueeze(3).to_broadcast([128, 8, 2, 128]), ['scp'], ['scbc'])
    for v in range(2):
        for n in range(2):
            for kc in range(8):
                mm(bank(2 + n), scbc[:, kc, v, :], wadg[:, kc, n * 512:(n + 1) * 512],
                   kc == 0, kc == 7, ['scbc', ('wadg', n)], [('ps', 2 + n)])
            tt('dve', gatebc[:, v, n * 512:(n + 1) * 512], bank(2 + n), bgbc[:, n * 512:(n + 1) * 512], ALU.add,
               [('ps', 2 + n), 'bgbc'], [('gatebc', v, n)])
    otiles = [('xp', g, 'yp', 0) for g in range(8)] + [('xo', g, 'ys', 1) for g in range(2)]
    for ti, (src, g, dst, v) in enumerate(otiles):
        b = ti % 2
        mg = g if src == 'xp' else 8 + g
        dma('sp', xr[b], I[src][g * 128:(g + 1) * 128, :], (), [('xr', b)])
        for n in range(2):
            for c in range(16):
                mm(bank(n), mixT[:, c, mg * 128:(mg + 1) * 128], wout[:, c, n * 512:(n + 1) * 512], c == 0, c == 15,
                   [('mixT', c, mg), ('wout', c // 4)], [('ps', n)])
            tt('dve', yv_[b][:, n * 512:(n + 1) * 512], bank(n), gatebc[:, v, n * 512:(n + 1) * 512], ALU.mult,
               [('ps', n), ('gatebc', v, n)], [('yv', b, n)])
            tt('dve', yv_[b][:, n * 512:(n + 1) * 512], yv_[b][:, n * 512:(n + 1) * 512], xr[b][:, n * 512:(n + 1) * 512], ALU.add,
               [('yv', b, n), ('xr', b)], [('yv', b, n)])
        act(ojunk, yv_[b], AF.Square, [('yv', b, 0), ('yv', b, 1)], ['ojunk', ('ost', b)], accum=ost[b][:, 0:1])
        ts('dve', ost[b][:, 1:2], ost[b][:, 0:1], 1.0 / 1024, 1e-6, ALU.mult, ALU.add, [('ost', b)], [('ostb', b)])
        act(ost[b][:, 2:3], ost[b][:, 1:2], AF.Sqrt, [('ostb', b)], [('ostc', b)])
        recip(ost[b][:, 3:4], ost[b][:, 2:3], [('ostc', b)], [('ostd', b)])
        stt('dve', xr[b], yv_[b], ost[b][:, 3:4], fgbc, ALU.mult, ALU.mult, [('yv', b, 0), ('yv', b, 1), ('ostd', b), 'fgbc', ('xr', b)], [('xr', b)])
        dma('sp', O[dst][g * 128:(g + 1) * 128, :], xr[b], [('xr', b)], (), is_out=True)


_NC = None


def _rope_tab(pos_rows, pos_cols):
    n_freq = 16
    inv = (10000.0 ** (-np.arange(n_freq, dtype=np.float32) / n_freq)).astype(np.float32)
    T = len(pos_rows)
    tab = np.zeros((T, 256), np.float32)
    for s_, pos in enumerate([pos_rows, pos_cols]):
        ang = pos.astype(np.float32)[:, None] * inv[None, :]
        c, sn = np.cos(ang).astype(np.float32), np.sin(ang).astype(np.float32)
        for m in range(2):
            for hf in range(2):
                o = m * 64 + s_ * 32 + hf * 16
                tab[:, o:o + 16] = c
                tab[:, 128 + o:128 + o + 16] = -sn if hf == 0 else sn
    return tab


def kernel(x_prompt, x_sample, cache_k, cache_v, state_rwkv, c, c_ctx, norm_g, w_ada, b_ada,
           w_in, lam_q1, lam_k1, lam_q2, lam_k2, subln_g, shift_mu, decay_w0, decay_w2,
           iclr_a0, iclr_a2, k_k, k_a, r_k, lnx_g, lnx_b, w_out, final_g):
    global _NC
    f = lambda a: np.ascontiguousarray(np.asarray(a, dtype=np.float32))
    x_prompt, x_sample, cache_k, cache_v, state_rwkv = map(f, (x_prompt, x_sample, cache_k, cache_v, state_rwkv))
    c, c_ctx = f(c), f(c_ctx)
    if _NC is None:
        _NC = build()
    nc = _NC

    def fm(v, nch):
        return np.ascontiguousarray(f(v).reshape(nch, 128).T)

    i = np.arange(128)
    su = (i[:, None] < i[None, :]).astype(np.float32)
    ui = (i[:, None] <= i[None, :]).astype(np.float32)
    sl = (i[:, None] > i[None, :]).astype(np.float32)
    li = (i[:, None] >= i[None, :]).astype(np.float32)
    maskNM = np.stack([np.concatenate([su, ui, su, ui], 1), np.concatenate([sl, li, sl, li], 1)])
    maskT = np.stack([sl, su])
    bones = np.kron(np.eye(2, dtype=np.float32), np.ones((64, 64), np.float32))
    hind = np.kron(np.eye(2, dtype=np.float32), np.ones((64, 1), np.float32))
    tok = np.arange(1024)
    ropeall = _rope_tab(tok // 64, tok % 64)
    W_in = f(w_in)[0]
    smu, sw0, sa0 = f(shift_mu)[0], f(decay_w0)[0], f(iclr_a0)[0]
    sw2, sa2 = f(decay_w2)[0].reshape(128, 1024), f(iclr_a2)[0].reshape(128, 1024)
    skk, ska, srk = f(k_k)[0], f(k_a)[0], f(r_k)[0].reshape(-1)
    slg, slb = f(lnx_g)[0], f(lnx_b)[0]
    shared = {
        "w_in": W_in, "w_ada": f(w_ada)[0], "w_out": f(w_out)[0],
        "bada_fm": fm(f(b_ada)[0], 24), "bgate": f(b_ada)[0:1, 2048:3072], "normg_fm": fm(f(norm_g)[0], 8),
        "fg": f(final_g)[None, :],
        "lamv": np.concatenate([f(lam_q1)[0], f(lam_k1)[0], f(lam_q2)[0], f(lam_k2)[0]])[None, :],
        "sublng": f(subln_g)[0:1],
        "ident": np.eye(128, dtype=np.float32), "maskNM": maskNM, "maskT": maskT, "bones": bones, "hind": hind,
        "ropeall": ropeall,
    }

    def cols(v, hp):
        return v[hp * 128:(hp + 1) * 128]
    in_maps = []
    for core in range(8):
        b, q = core // 4, core % 4
        hps = list(range(8)) + [2 * q, 2 * q + 1]
        sel = np.zeros((1024, 256), np.float32)
        sel[q * 256 + np.arange(256), np.arange(256)] = 1.0
        cT = np.stack([fm(c_ctx, 8), fm(c[b], 8)], -1).reshape(128, 16)
        chunks = [smu[:, ci * 128:(ci + 1) * 128] for ci in range(26)]
        for a_ in range(3):
            for i_ in range(2):
                ci = a_ * 8 + hps[8 + i_]
                chunks.append(smu[:, ci * 128:(ci + 1) * 128])
        mu_fm = np.stack([np.stack([ch[0], ch[1]], -1) for ch in chunks], 1).reshape(128, 64)
        w0_fm = np.stack([np.stack([cols(sw0[0], h_), cols(sw0[1], h_)], -1) for h_ in hps], 1).reshape(128, 20)
        a0_fm = np.stack([np.stack([cols(sa0[0], h_), cols(sa0[1], h_)], -1) for h_ in hps], 1).reshape(128, 20)
        ext = lambda v: np.stack([cols(v, h_) for h_ in hps], 1)
        extc = lambda m: np.concatenate([m[:, h_ * 128:(h_ + 1) * 128] for h_ in hps], 1)
        wrs = np.concatenate([W_in[:, base + h_ * 128: base + (h_ + 1) * 128]
                              for h_ in hps[8:] for base in (4096, 4096 + 1024, 4096 + 2048, 4096 + 3328)], 1)
        m = dict(shared)
        m.update({
            "xp": x_prompt[core * 4:(core + 1) * 4].reshape(1024, 1024),
            "xs": x_sample[b], "xo": x_sample[b, q * 256:(q + 1) * 256],
            "ck": cache_k[b, 0].reshape(512, 1024), "cv": cache_v[b, 0].reshape(512, 1024),
            "s0": state_rwkv[b, 0][:, 4 * q:4 * q + 4], "cT": np.ascontiguousarray(cT), "selT": sel,
            "ropeown": np.ascontiguousarray(ropeall[q * 256:(q + 1) * 256]),
            "mu_fm": mu_fm, "w0_fm": w0_fm, "a0_fm": a0_fm, "w2": extc(sw2), "a2": extc(sa2),
            "kk_fm": ext(skk), "ka_fm": ext(ska), "rk_fm": ext(srk),
            "lnxg": extc(slg[None, :]), "lnxb": extc(slb[None, :]), "wrs": wrs,
        })
        in_maps.append({k: np.ascontiguousarray(v, dtype=np.float32) for k, v in m.items()})
    res = run_bass_kernel_spmd(nc, in_maps, core_ids=list(range(8)))
    R = res.results
    y_prompt = np.concatenate([R[i]["yp"].reshape(4, 256, 1024) for i in range(8)], 0)
    y_sample = np.stack([np.concatenate([R[b * 4 + q]["ys"] for q in range(4)], 0) for b in range(2)], 0)
    new_k = np.concatenate([R[i]["nk"].reshape(4, 1, 256, 8, 2, 64) for i in range(8)], 0)
    new_v = np.concatenate([R[i]["nv"].reshape(4, 1, 256, 8, 128) for i in range(8)], 0)
    new_s = np.concatenate([R[i]["ns"].reshape(4, 1, 2, 16, 64, 64) for i in range(8)], 0)
    return (y_prompt.astype(np.float32), y_sample.astype(np.float32), new_k.astype(np.float32),
            new_v.astype(np.float32), new_s.astype(np.float32))
```

```python
import math
from contextlib import ExitStack
import numpy as np
import concourse.bass as bass
import concourse.mybir as mybir
from concourse.bass_utils import run_bass_kernel_spmd

F32 = mybir.dt.float32
F32R = mybir.dt.float32r
LEVEL_DT = F32
BF16 = mybir.dt.bfloat16
AF = mybir.ActivationFunctionType
ALU = mybir.AluOpType
AX = mybir.AxisListType

ENG = ['pe', 'dve', 'act', 'pool', 'sp']
NDMA = 40
DC = math.exp(-0.5)


class Sched:
    def __init__(s, nc, stack):
        s.nc = nc
        s.ops = {e: [] for e in ENG}
        s.cnt = {e: 0 for e in ENG}
        s.known = {e: {f: 0 for f in ENG} for e in ENG}
        s.snap = {e: [] for e in ENG}
        s.kdma = {e: {} for e in ENG}
        s.lastw = {}
        s.rd_e = {}
        s.rd_d = {}
        s.sems = {e: stack.enter_context(nc.semaphore('c_' + e)) for e in ENG}
        s.dsems = [stack.enter_context(nc.semaphore('d%d' % i)) for i in range(2 * NDMA)]
        s.dcnt = {'sp': 0, 'pool': 0, 'act': 0}
        s.dcount = 0
        s.dlast = {}
        s.out_events = []

    def op(s, eng, fn, r=(), w=(), dma=False, is_out=False, noinc=False, cc=False):
        r = [(k[0], k[1]) if (isinstance(k, tuple) and k[0] == 'ps') else k for k in r]
        w = [(k[0], k[1]) if (isinstance(k, tuple) and k[0] == 'ps') else k for k in w]
        psr = [k for k in r if isinstance(k, tuple) and k[0] == 'ps']
        if psr:
            r = [k for k in r if not (isinstance(k, tuple) and k[0] == 'ps')]
            w = list(w) + [k for k in psr if k not in w]
        deps = []
        for k in r:
            ev = s.lastw.get(k)
            if ev is not None:
                deps.append((ev, True))
        for k in w:
            ev = s.lastw.get(k)
            if ev is not None:
                deps.append((ev, False))
            for f, c in s.rd_e.get(k, {}).items():
                deps.append((('E', f, c), True))
            for ev in s.rd_d.get(k, ()):
                deps.append((ev, False))
        waits = {}
        for ev, raw in deps:
            if ev[0] == 'E':
                _, f, c = ev
                if f == eng and not dma:
                    if (not raw) or eng == 'pe':
                        continue
                if s.known[eng][f] >= c:
                    continue
                waits[('E', f)] = max(waits.get(('E', f), 0), c)
            else:
                _, si, v = ev
                if s.kdma[eng].get(si, 0) >= v:
                    continue
                waits[('D', si)] = max(waits.get(('D', si), 0), v)
        if cc:
            ev = ('D', 'cc', 1)
            s.dlast['cc'] = 1
        elif dma:
            qn = s.dcnt[eng]
            si = qn % NDMA + (NDMA if eng == 'pool' else 0)
            v = 16 * (qn // NDMA + 1)
            if qn >= NDMA and s.kdma[eng].get(si, 0) < v - 16:
                waits[('D', si)] = max(waits.get(('D', si), 0), v - 16)
            s.dcnt[eng] += 1
            s.dcount += 1
            s.dlast[si] = v
            ev = ('D', si, v)
        for (t, x), v in waits.items():
            if t == 'E':
                kn = s.known[eng]
                if kn[x] < v:
                    kn[x] = v
                sn = s.snap[x][v - 1]
                for f2, c2 in sn.items():
                    if kn[f2] < c2:
                        kn[f2] = c2
            else:
                s.kdma[eng][x] = v
        if not (dma or cc):
            if noinc:
                ev = ('E', eng, s.cnt[eng] + 1)
            else:
                s.cnt[eng] += 1
                ev = ('E', eng, s.cnt[eng])
                s.snap[eng].append(dict(s.known[eng]))
        s.ops[eng].append((list(waits.items()), fn, None if noinc else ev))
        for k in r:
            if ev[0] == 'E':
                s.rd_e.setdefault(k, {})[eng] = ev[2]
            else:
                s.rd_d.setdefault(k, []).append(ev)
        for k in w:
            s.lastw[k] = ev
            s.rd_e[k] = {}
            s.rd_d[k] = []
        if is_out:
            s.out_events.append(ev)
        return ev

    def barrier(s):
        waits = {}
        for f in ENG:
            if f != 'sp' and s.cnt[f] > 0:
                waits[('E', f)] = s.cnt[f]
        for si, v in s.dlast.items():
            waits[('D', si)] = v
        s.cnt['sp'] += 1
        c = s.cnt['sp']
        for f in ENG:
            if f != 'sp':
                s.known['sp'][f] = s.cnt[f]
        s.kdma['sp'] = dict(s.dlast)
        s.snap['sp'].append(dict(s.known['sp']))
        s.ops['sp'].append((list(waits.items()), (lambda e: e.nop()), ('E', 'sp', c)))
        for f in ENG:
            if f == 'sp':
                continue
            s.ops[f].append(([(('E', 'sp'), c)], None, None))
            for g in ENG:
                if g != f:
                    s.known[f][g] = max(s.known[f][g], s.cnt[g])
            s.kdma[f] = dict(s.dlast)
        s.lastw.clear()
        s.rd_e.clear()
        s.rd_d.clear()

    def finish(s):
        waits = {}
        for ev in s.out_events:
            waits[('D', ev[1])] = max(waits.get(('D', ev[1]), 0), ev[2])
        s.ops['sp'].append((list(waits.items()), None, None))

    def emit(s, block):
        def mk(engname):
            def body(e):
                for waits, fn, ev in s.ops[engname]:
                    for (t, x), v in waits:
                        sem = s.sems[x] if t == 'E' else (s.ccsem if x == 'cc' else s.dsems[x])
                        e.wait_ge(sem, v)
                    if fn is None:
                        continue
                    ins = fn(e)
                    if ev is None:
                        continue
                    if ev[0] == 'E':
                        ins.then_inc(s.sems[engname], 1)
                    elif ev[1] == 'cc':
                        ins.then_inc(s.ccsem, 1)
                    else:
                        ins.then_inc(s.dsems[ev[1]], 16)
            return body
        block.tensor(mk('pe'))
        block.vector(mk('dve'))
        block.scalar(mk('act'))
        block.gpsimd(mk('pool'))
        block.sync(mk('sp'))


class Arena:
    def __init__(s, ar, nwords):
        s.ar = ar
        s.off = 0
        s.n = nwords
        s.peak = 0

    def alloc(s, shape, dt):
        n = 1
        for x in shape:
            n *= x
        words = n if dt == F32 else (n + 1) // 2
        words = (words + 7) // 8 * 8
        a = s.ar[:, s.off:s.off + words]
        s.off += words
        s.peak = max(s.peak, s.off)
        assert s.off <= s.n, ("arena overflow", s.off, s.n)
        if dt == BF16:
            a = a.bitcast(BF16)
        a = a[:, 0:n]
        if len(shape) == 2:
            a = a.rearrange("p (a b) -> p a b", b=shape[1])
        elif len(shape) == 3:
            a = a.rearrange("p (a b c) -> p a b c", b=shape[1], c=shape[2])
        elif len(shape) == 4:
            a = a.rearrange("p (a b c d) -> p a b c d", b=shape[1], c=shape[2], d=shape[3])
        return a


IN_SPECS = [
    ("xp", [1024, 1024]), ("xs", [1024, 1024]), ("xo", [256, 1024]),
    ("ck", [512, 1024]), ("cv", [512, 1024]), ("s0", [2, 4, 64, 64]),
    ("cT", [128, 16]), ("w_in", [1024, 8448]), ("w_ada", [1024, 3072]), ("w_out", [2048, 1024]),
    ("bada_fm", [128, 24]), ("bgate", [1, 1024]), ("normg_fm", [128, 8]), ("fg", [1, 1024]),
    ("lamv", [1, 256]), ("sublng", [1, 128]), ("mu_fm", [128, 64]), ("w0_fm", [128, 20]),
    ("a0_fm", [128, 20]), ("w2", [128, 1280]), ("a2", [128, 1280]), ("kk_fm", [128, 10]),
    ("ka_fm", [128, 10]), ("rk_fm", [128, 10]), ("lnxg", [1, 1280]), ("lnxb", [1, 1280]), ("wrs", [1024, 1024]),
    ("ident", [128, 128]), ("maskNM", [2, 128, 512]), ("maskT", [2, 128, 128]),
    ("bones", [128, 128]), ("hind", [128, 2]), ("selT", [1024, 256]),
    ("ropeall", [1024, 256]), ("ropeown", [256, 256]),
]
OUT_SPECS = [
    ("yp", [1024, 1024]), ("ys", [256, 1024]), ("nk", [1024, 1024]), ("nv", [1024, 1024]),
    ("ns", [4, 2, 16, 64, 64]),
]

ARENA_WORDS = 52480 - 4096
LVL_WORDS = 4096
NHEAD_A = 8
NHP = 8
NSEQ_R = 5
ND = 2
DEBUG = False
A_MODE = 'all'
DUMPS = []
STOP = None


def build():
    nc = bass.Bass("TRN2", target_bir_lowering=False)
    I = {n: nc.dram_tensor(n, sh, F32, kind="ExternalInput").ap() for n, sh in IN_SPECS}
    O = {n: nc.dram_tensor(n, sh, F32, kind="ExternalOutput").ap() for n, sh in OUT_SPECS}
    with ExitStack() as stack:
        ar = stack.enter_context(nc.sbuf_tensor("arena", [128, ARENA_WORDS], F32))
        PS = stack.enter_context(nc.psum_tensor("ps", [128, 4096], F32))
        lvl = stack.enter_context(nc.sbuf_tensor("lvl", [128, LVL_WORDS], LEVEL_DT))
        S = Sched(nc, stack)
        S.ccsem = stack.enter_context(nc.semaphore('ccsem'))
        A = Arena(ar, ARENA_WORDS)
        block = stack.enter_context(nc.Block())
        _program(nc, S, A, PS, I, O, lvl)
        S.finish()
        S.emit(block)
    return nc


def _program(nc, S, A, PS, I, O, lvl):
    def dma(eng, out, in_, r, w, is_out=False):
        S.op(eng, lambda e: e.dma_start(out=out, in_=in_), r, w, dma=True, is_out=is_out)

    def mm(out, lhsT, rhs, start, stop, r, w):
        S.op('pe', lambda e: e.matmul(out, lhsT, rhs, start=start, stop=stop), r, w, noinc=(not stop))

    def tr(out, in_, ident, r, w):
        S.op('pe', lambda e: e.transpose(out, in_, ident), r, w)

    def act(out, in_, func, r, w, bias=None, scale=None, accum=None):
        def f(e):
            kw = {}
            if bias is not None:
                kw['bias'] = bias
            if scale is not None:
                kw['scale'] = scale
            if accum is not None:
                kw['accum_out'] = accum
            return e.activation(out=out, in_=in_, func=func, **kw)
        S.op('act', f, r, w)

    def tt(eng, out, in0, in1, op, r, w):
        S.op(eng, lambda e: e.tensor_tensor(out=out, in0=in0, in1=in1, op=op), r, w)

    def ts(eng, out, in0, s1, s2, op0, op1, r, w):
        if s2 is None:
            S.op(eng, lambda e: e.tensor_scalar(out=out, in0=in0, scalar1=s1, scalar2=None, op0=op0), r, w)
        else:
            S.op(eng, lambda e: e.tensor_scalar(out=out, in0=in0, scalar1=s1, scalar2=s2, op0=op0, op1=op1), r, w)

    def stt(eng, out, in0, sc, in1, op0, op1, r, w):
        S.op(eng, lambda e: e.scalar_tensor_tensor(out=out, in0=in0, scalar=sc, in1=in1, op0=op0, op1=op1), r, w)

    def cp(eng, out, in_, r, w):
        if eng == 'act':
            act(out, in_, AF.Identity, r, w)
        else:
            S.op(eng, lambda e: e.tensor_copy(out=out, in_=in_), r, w)

    def red(out, in_, op, r, w):
        S.op('dve', lambda e: e.tensor_reduce(out=out, in_=in_, axis=AX.X, op=op), r, w)

    def recip(out, in_, r, w):
        S.op('dve', lambda e: e.reciprocal(out=out, in_=in_), r, w)

    def memset(eng, out, val, w):
        S.op(eng, lambda e: e.memset(out, val), (), w)

    def bank(b, c0=0, c1=512):
        return PS[:, b * 512 + c0: b * 512 + c1]

    def bankb(b):
        return PS[:, b * 512:(b + 1) * 512].bitcast(BF16)

    w_in_v = I['w_in'].rearrange("(kc p) n -> p kc n", p=128)

    def dump_all(bufs):
        S.barrier()
        for name, ap in bufs.items():
            sh = list(ap.shape)
            dt = nc.dram_tensor('dbg_' + name, sh, ap.dtype, kind="ExternalOutput").ap()
            DUMPS.append('dbg_' + name)
            dma('sp', dt, ap, (), (), is_out=True)

    identf = A.alloc([128], F32)
    identb = A.alloc([128], BF16)
    onesf = A.alloc([128], F32)
    maskNM = A.alloc([2, 512], BF16)
    maskT = A.alloc([2, 128], BF16)
    bonesb = A.alloc([128], BF16)
    hindb = A.alloc([2], BF16)
    selb = A.alloc([8, 256], BF16)
    cst = A.alloc([4], F32)
    hTp = A.alloc([8, 1024], BF16)
    hTs = A.alloc([8, 1024], BF16)
    hTo = A.alloc([8, 256], BF16)
    mixT = A.alloc([16, 1280], BF16)
    modfm = A.alloc([24, 2], F32)
    scale1 = A.alloc([8, 2], F32)
    neglam = A.alloc([1], F32)
    sgl = A.alloc([128], F32)
    mu = A.alloc([32, 2], F32)
    c0v = A.alloc([32], F32)
    w0v = A.alloc([20], F32)
    a0v = A.alloc([20], F32)
    w0h = A.alloc([20], F32)
    a0h = A.alloc([20], F32)
    kkv = A.alloc([10], F32)
    kav = A.alloc([10], F32)
    omka = A.alloc([10], F32)
    rkv_ = A.alloc([10], F32)
    W2b = A.alloc([1280], BF16)
    A2b = A.alloc([1280], BF16)
    twT = A.alloc([2048], BF16)
    laT = A.alloc([2048], BF16)

    dma('sp', identf, I['ident'], (), ['identf'])
    dma('pool', identb, I['ident'], (), ['identb'])
    dma('pool', maskNM, I['maskNM'].rearrange("d p n -> p d n"), (), ['maskNM'])
    dma('pool', maskT, I['maskT'].rearrange("d p n -> p d n"), (), ['maskT'])
    dma('pool', bonesb, I['bones'], (), ['bonesb'])
    dma('pool', hindb, I['hind'], (), ['hindb'])
    dma('pool', selb, I['selT'].rearrange("(j p) n -> p j n", p=128), (), ['selb'])
    dma('sp', mu, I['mu_fm'].rearrange("p (c j) -> p c j", j=2), (), ['mu'])
    dma('sp', w0v, I['w0_fm'], (), ['w0v'])
    dma('sp', a0v, I['a0_fm'], (), ['a0v'])
    dma('sp', kkv, I['kk_fm'], (), ['kkv'])
    dma('sp', kav, I['ka_fm'], (), ['kav'])
    dma('sp', rkv_, I['rk_fm'], (), ['rkv_'])
    dma('pool', W2b, I['w2'], (), ['W2b'])
    dma('pool', A2b, I['a2'], (), ['A2b'])
    memset('dve', onesf, 1.0, ['onesf'])
    memset('dve', cst[:, 0:1], 1e-12, ['cst'])
    memset('dve', cst[:, 1:2], -0.5, ['cstb'])
    tt('dve', c0v, mu[:, :, 0], mu[:, :, 1], ALU.add, ['mu'], ['c0v'])
    ts('dve', c0v, c0v, -1.0, 1.0, ALU.mult, ALU.add, ['c0v'], ['c0v'])
    ts('dve', omka, kav, -1.0, 1.0, ALU.mult, ALU.add, ['kav'], ['omka'])
    ts('dve', w0h, w0v, 0.5, None, ALU.mult, None, ['w0v'], ['w0h'])
    ts('dve', a0h, a0v, 0.5, None, ALU.mult, None, ['a0v'], ['a0h'])

    scp = A.alloc([8, 2], BF16)
    m0 = A.off
    cT = A.alloc([16], F32)
    sc = A.alloc([8, 2], F32)
    wadaf = [A.alloc([8, 512], F32) for _ in range(4)]
    bada = A.alloc([24], F32)
    normg = A.alloc([8], F32)
    lamt = A.alloc([4, 64], F32)
    lamp = A.alloc([2, 64], F32)
    lams = A.alloc([4], F32)

    dma('sp', cT, I['cT'], (), ['cT'])
    dma('sp', bada, I['bada_fm'], (), ['bada'])
    dma('sp', normg, I['normg_fm'], (), ['normg'])
    dma('sp', lamt.rearrange("p a b -> p (a b)"), I['lamv'][0:1, :].partition_broadcast(128), (), ['lamt'])
    dma('sp', sgl, I['sublng'][0:1, :].partition_broadcast(128), (), ['sgl'])
    wada_v = I['w_ada'].rearrange("(kc p) n -> p kc n", p=128)
    for n in range(4):
        dma('sp', wadaf[n], wada_v[:, :, n * 512:(n + 1) * 512], (), [('wada', n)])
    act(sc, cT.rearrange("p (c v) -> p c v", v=2), AF.Silu, ['cT'], ['sc'])
    cp('dve', scp, sc, ['sc'], ['scp'])
    for fc in range(16):
        for kc in range(8):
            mm(bank(0, fc * 2, fc * 2 + 2), wadaf[fc // 4][:, kc, (fc % 4) * 128:(fc % 4 + 1) * 128], sc[:, kc, :],
               kc == 0, kc == 7, ['sc', ('wada', fc // 4)], [('ps', 0)])
    tt('dve', modfm[:, 0:16, :], bank(0, 0, 32).rearrange("p (a b) -> p a b", b=2),
       bada[:, 0:16].unsqueeze(2).to_broadcast([128, 16, 2]), ALU.add, [('ps', 0), 'bada'], ['modfm'])
    ts('dve', scale1, modfm[:, 8:16, :], 1.0, None, ALU.add, None, ['modfm'], ['scale1'])
    tt('dve', scale1, scale1, normg.unsqueeze(2).to_broadcast([128, 8, 2]), ALU.mult, ['scale1', 'normg'], ['scale1'])
    tt('dve', lamp[:, 0, :], lamt[:, 0, :], lamt[:, 1, :], ALU.mult, ['lamt'], ['lamp'])
    tt('dve', lamp[:, 1, :], lamt[:, 2, :], lamt[:, 3, :], ALU.mult, ['lamt', 'lamp'], ['lamp'])
    red(lams[:, 0:2], lamp, ALU.add, ['lamp'], ['lams'])
    act(lams[:, 2:4], lams[:, 0:2], AF.Exp, ['lams'], ['lams2'])
    lam_init = 0.8 - 0.6 * math.exp(-0.3 * 0)
    tt('dve', neglam, lams[:, 3:4], lams[:, 2:3], ALU.subtract, ['lams2'], ['neglam'])
    ts('dve', neglam, neglam, -lam_init, None, ALU.add, None, ['neglam'], ['neglam'])
    ts('dve', sgl, sgl, 0.5 * (1.0 - lam_init), None, ALU.mult, None, ['sgl'], ['sgl'])

    if STOP == '0':
        return
    xt = [A.alloc([1024], F32) for _ in range(2)]
    xn = [A.alloc([1024], BF16) for _ in range(2)]
    junk = A.alloc([1024], BF16)
    st1 = [A.alloc([4], F32) for _ in range(2)]
    tiles = [('xp', g, hTp, g, 0) for g in range(8)] + [('xs', g, hTs, g, 1) for g in range(8)] + \
            [('xo', g, hTo, g, 1) for g in range(2)]
    for ti, (src, g, hT, tg, v) in enumerate(tiles):
        b = ti % 2
        dma('sp', xt[b], I[src][g * 128:(g + 1) * 128, :], (), [('xt', b)])
        act(junk, xt[b], AF.Square, [('xt', b)], ['junk', ('st1', b)], accum=st1[b][:, 0:1])
        ts('dve', st1[b][:, 1:2], st1[b][:, 0:1], 1.0 / 1024, 1e-6, ALU.mult, ALU.add, [('st1', b)], [('st1b', b)])
        act(st1[b][:, 2:3], st1[b][:, 1:2], AF.Sqrt, [('st1b', b)], [('st1c', b)])
        recip(st1[b][:, 3:4], st1[b][:, 2:3], [('st1c', b)], [('st1d', b)])
        ts('dve', xn[b], xt[b], st1[b][:, 3:4], None, ALU.mult, None, [('xt', b), ('st1d', b)], [('xn', b)])
        pb_ = bankb(3 + b)
        for kc in range(8):
            tr(pb_[:, kc * 128:(kc + 1) * 128], xn[b][:, kc * 128:(kc + 1) * 128], identb,
               [('xn', b), 'identb'], [('ps', 3 + b)])
        for kc in range(8):
            if kc % 2 == 0:
                act(hT[:, kc, tg * 128:(tg + 1) * 128], pb_[:, kc * 128:(kc + 1) * 128], AF.Identity,
                    [('ps', 3 + b), 'scale1', 'modfm'], [(src + 'h', tg)],
                    bias=modfm[:, kc, v:v + 1], scale=scale1[:, kc, v:v + 1])
        for kc in range(8):
            if kc % 2 == 1:
                ts('dve', hT[:, kc, tg * 128:(tg + 1) * 128], pb_[:, kc * 128:(kc + 1) * 128], scale1[:, kc, v:v + 1],
                   modfm[:, kc, v:v + 1], ALU.mult, ALU.add, [('ps', 3 + b), 'scale1', 'modfm'], [(src + 'h', tg)])
    if STOP == '1':
        return
    S.barrier()
    A.off = m0
    if STOP == '1b':
        return

    mA = A.off
    wA = [A.alloc([8, 512], BF16) for _ in range(2)]
    qkb = [A.alloc([256], BF16) for _ in range(2)]
    kvf = [A.alloc([256], F32) for _ in range(2)]
    qT2 = [A.alloc([256], BF16) for _ in range(2)]
    sg2 = [A.alloc([2, 128], F32) for _ in range(2)]
    kTp = [A.alloc([256], BF16) for _ in range(2)]
    vbp = [A.alloc([2, 128], BF16) for _ in range(2)]
    kTs = A.alloc([1536], BF16)
    vbs = A.alloc([12, 128], BF16)
    ckb = A.alloc([4, 128], BF16)
    ropa = A.alloc([8, 256], F32)
    ropo = A.alloc([2, 256], F32)
    xf = [A.alloc([128], F32) for _ in range(2)]
    rt1 = [A.alloc([128], F32) for _ in range(2)]
    rt2 = [A.alloc([128], F32) for _ in range(2)]
    xb16 = [A.alloc([128], BF16) for _ in range(2)]
    pbuf = [A.alloc([1536], BF16) for _ in range(2)]
    pT = [A.alloc([1536], BF16) for _ in range(2)]
    pbufp = [[A.alloc([256], BF16) for _ in range(2)] for _ in range(2)]
    pTp = [[A.alloc([256], BF16) for _ in range(2)] for _ in range(2)]
    ast = [A.alloc([16], F32) for _ in range(4)]
    of = [A.alloc([128], F32) for _ in range(4)]
    o1 = [A.alloc([128], F32) for _ in range(4)]
    on = [A.alloc([128], F32) for _ in range(4)]
    ob = [A.alloc([128], BF16) for _ in range(4)]
    ajunk = A.alloc([128], BF16)

    dma('sp', ropa, I['ropeall'].rearrange("(j p) n -> p j n", p=128), (), ['ropa'])
    dma('sp', ropo, I['ropeown'].rearrange("(j p) n -> p j n", p=128), (), ['ropo'])

    ctr = {'x': 0}

    def rope(src_ps, tab, dst16, rkeys, wkey):
        i = ctr['x'] % 2
        ctr['x'] += 1
        cp('act', xf[i], src_ps, rkeys, [('xf', i)])
        tt('dve', rt1[i], xf[i], tab[:, 0:128], ALU.mult, [('xf', i), 'ropa', 'ropo'], [('rt1', i)])
        xv = xf[i].rearrange("p (g h f) -> p g h f", h=2, f=16)
        sv = tab[:, 128:256].rearrange("p (g h f) -> p g h f", h=2, f=16)
        r2 = rt2[i].rearrange("p (g h f) -> p g h f", h=2, f=16)
        tt('pool', r2[:, :, 0, :], xv[:, :, 1, :], sv[:, :, 0, :], ALU.mult, [('xf', i), 'ropa', 'ropo'], [('rt2', i, 0)])
        tt('pool', r2[:, :, 1, :], xv[:, :, 0, :], sv[:, :, 1, :], ALU.mult, [('xf', i), 'ropa', 'ropo'], [('rt2', i, 1)])
        tt('dve', dst16, rt1[i], rt2[i], ALU.add, [('rt1', i), ('rt2', i, 0), ('rt2', i, 1)], [wkey])

    def drive2(gens):
        gens = [g for g in gens if g is not None]
        while gens:
            for g in list(gens):
                try:
                    next(g)
                except StopIteration:
                    gens.remove(g)

    def attn_unit(par, j, m, kind):
        ai = par * 2 + j
        if kind == 'p':
            ntk, kT_, vb_, kkey, vkey = 2, kTp[par], vbp[par], ('kTp', par), ('vbp', par)
            sb0 = 2 + 2 * j + m
            ob_ = 6 + j
            pbs, pTs, pkey = pbufp[j], pTp[j], ('pp', j)
            tcol = 512
        else:
            ntk, kT_, vb_, kkey, vkey = 12, kTs, vbs, 'kTs', 'vbs'
            sb0 = 2 + 3 * m
            ob_ = sb0 + 2
            pbs, pTs, pkey = pbuf, pT, ('ps_', 0)
            tcol = 0
        Tk = ntk * 128
        qT_ = qT2[par]
        s0c = sb0 * 512
        nsb = (Tk + 511) // 512
        sck = [('ps', sb0 + b_) for b_ in range(nsb)]
        for n0 in range(0, Tk, 512):
            w_ = min(512, Tk - n0)
            mm(PS[:, s0c + n0:s0c + n0 + w_], qT_[64 * m:64 * m + 64, j * 128:(j + 1) * 128],
               kT_[64 * m:64 * m + 64, n0:n0 + w_], True, True, [('qT', par, j), kkey], [('ps', sb0 + n0 // 512)])
        red(ast[ai][:, m:m + 1], PS[:, s0c:s0c + Tk], ALU.max, sck, [('ast', ai, 'mx', m)])
        ts('dve', ast[ai][:, 2 + m:3 + m], ast[ai][:, m:m + 1], -0.125, None, ALU.mult, None,
           [('ast', ai, 'mx', m)], [('ast', ai, 'nb', m)])
        act(pbs[m][:, 0:Tk], PS[:, s0c:s0c + Tk], AF.Exp, sck + [('ast', ai, 'nb', m)],
            [('pbuf', pkey, m), ('ast', ai, 'sum', m)], bias=ast[ai][:, 2 + m:3 + m], scale=0.125,
            accum=ast[ai][:, 4 + m:5 + m])
        yield
        ptv = PS[:, s0c:s0c + 1024].bitcast(BF16)
        for t in range(ntk):
            c_ = tcol + t * 128
            tr(ptv[:, c_:c_ + 128], pbs[m][:, t * 128:(t + 1) * 128], identb,
               [('pbuf', pkey, m), 'identb'], [('ps', sb0 + (c_ // 1024))])
        if ntk <= 2:
            cp('dve', pTs[m][:, 0:Tk], ptv[:, tcol:tcol + Tk], [('ps', sb0)], [('pT', pkey, m, 0)])
            ptk = [('pT', pkey, m, 0)]
        else:
            cp('dve', pTs[m][:, 0:768], ptv[:, 0:768], [('ps', sb0)], [('pT', pkey, m, 0)])
            cp('act', pTs[m][:, 768:1536], ptv[:, 768:1536], [('ps', sb0), ('ps', sb0 + 1)], [('pT', pkey, m, 1)])
            ptk = [('pT', pkey, m, 0), ('pT', pkey, m, 1)]
        yield
        oc = m * 128 if kind == 'p' else 0
        for t in range(ntk):
            mm(bank(ob_, oc, oc + 128), pTs[m][:, t * 128:(t + 1) * 128], vb_[:, t, :],
               t == 0, t == ntk - 1, ptk + [vkey], [('ps', ob_)])
        yield

    def attn_comb(h, par, j, kind, mixcol0):
        ai = par * 2 + j
        if kind == 'p':
            o1src, o2src, k1, k2, ob_ = bank(6 + j, 0, 128), bank(6 + j, 128, 256), ('ps', 6 + j), ('ps', 6 + j), 6 + j
        else:
            o1src, o2src, k1, k2, ob_ = bank(4, 0, 128), bank(7, 0, 128), ('ps', 4), ('ps', 7), 4
        a_ = ast[ai]
        recip(a_[:, 6:8], a_[:, 4:6], [('ast', ai, 'sum', 0), ('ast', ai, 'sum', 1)], [('ast', ai, 'rs')])
        tt('dve', a_[:, 8:9], a_[:, 7:8], neglam, ALU.mult, [('ast', ai, 'rs'), 'neglam'], [('ast', ai, 'c2')])
        act(o1[ai], o1src, AF.Identity, [k1, ('ast', ai, 'rs')], [('o1', ai)], scale=a_[:, 6:7])
        stt('dve', of[ai], o2src, a_[:, 8:9], o1[ai], ALU.mult, ALU.add, [k2, ('ast', ai, 'c2'), ('o1', ai)], [('of', ai)])
        yield
        act(ajunk, of[ai], AF.Square, [('of', ai)], ['ajunk', ('ast', ai, 'ss')], accum=a_[:, 9:10])
        ts('dve', a_[:, 10:11], a_[:, 9:10], 1.0 / 128, 1e-5, ALU.mult, ALU.add, [('ast', ai, 'ss')], [('ast', ai, 'ms')])
        tt('pool', a_[:, 12:13], a_[:, 10:11], cst[:, 1:2], ALU.pow, [('ast', ai, 'ms'), 'cstb'], [('ast', ai, 'rstd')])
        yield
        stt('dve', on[ai], of[ai], a_[:, 12:13], sgl, ALU.mult, ALU.mult, [('of', ai), ('ast', ai, 'rstd'), 'sgl'], [('on', ai)])
        tt('pool', ob[ai], on[ai], sg2[par][:, j, :], ALU.mult, [('on', ai), ('sg', par, j)], [('ob', ai)])
        pso = bankb(ob_)
        tr(pso[:, 512:640], ob[ai], identb, [('ob', ai), 'identb'], [('ps', ob_)])
        cp('act', mixT[:, h, mixcol0 + j * 128: mixcol0 + (j + 1) * 128], pso[:, 512:640], [('ps', ob_)],
           [('mixT', h, (mixcol0 // 128) + j)])
        yield

    def rr(gens):
        gens = list(gens)
        while gens:
            for g in list(gens):
                try:
                    next(g)
                except StopIteration:
                    gens.remove(g)
            yield

    def attn_gen(h, par, kind, mixcol0):
        if kind == 'p':
            for _ in rr([attn_unit(par, j, m, kind) for j in range(2) for m in range(2)]):
                yield
            for _ in rr([attn_comb(h, par, j, kind, mixcol0) for j in range(2)]):
                yield
        else:
            for j in range(2):
                for _ in rr([attn_unit(par, j, m, kind) for m in range(2)]):
                    yield
                for _ in attn_comb(h, par, j, kind, mixcol0):
                    yield

    def proj_gen(h, par, kind, s_):
        wb = wA[h % 2]
        wk = [('wA', h % 2, j4) for j4 in range(4)]
        pst = bankb(1)
        if kind == 'p':
            for j in range(2):
                g = s_ * 2 + j
                bi = g % 2
                for kc in range(8):
                    mm(bank(0), hTp[:, kc, g * 128:(g + 1) * 128], wb[:, kc, :], kc == 0, kc == 7,
                       [('xph', g)] + wk, [('ps', 0)])
                cp('dve', qkb[bi], bank(0, 0, 256), [('ps', 0)], [('qkb', bi)])
                cp('act', kvf[bi], bank(0, 128, 384), [('ps', 0)], [('kvf', bi)])
                dma('sp', O['nk'][g * 128:(g + 1) * 128, h * 128:(h + 1) * 128], kvf[bi][:, 0:128], [('kvf', bi)], (), is_out=True)
                dma('sp', O['nv'][g * 128:(g + 1) * 128, h * 128:(h + 1) * 128], kvf[bi][:, 128:256], [('kvf', bi)], (), is_out=True)
                cp('dve', vbp[par][:, j, :], bank(0, 256, 384), [('ps', 0)], [('vbp', par)])
                act(sg2[par][:, j, :], bank(0, 384, 512), AF.Tanh, [('ps', 0)], [('sg', par, j)], scale=0.5)
                stt('dve', sg2[par][:, j, :], sg2[par][:, j, :], 1.0, bank(0, 384, 512), ALU.add, ALU.mult, [('ps', 0), ('sg', par, j)], [('sg', par, j)])
                yield
                tr(pst[:, 0:128], qkb[bi][:, 0:128], identb, [('qkb', bi), 'identb'], [('ps', 1)])
                tr(pst[:, 128:256], qkb[bi][:, 128:256], identb, [('qkb', bi), 'identb'], [('ps', 1)])
                cp('act', qT2[par][:, j * 128:(j + 1) * 128], pst[:, 0:128], [('ps', 1)], [('qT', par, j)])
                cp('act', kTp[par][:, j * 128:(j + 1) * 128], pst[:, 128:256], [('ps', 1)], [('kTp', par)])
                yield
        else:
            dma('pool', ckb, I['ck'].rearrange("(j p) c -> p j c", p=128)[:, :, h * 128:(h + 1) * 128], (), ['ckb'])
            dma('pool', vbs[:, 0:4, :], I['cv'].rearrange("(j p) c -> p j c", p=128)[:, :, h * 128:(h + 1) * 128], (), ['vbs'])
            for t in range(4):
                tr(pst[:, 512 + t * 128:512 + (t + 1) * 128], ckb[:, t, :], identb, ['ckb', 'identb'], [('ps', 1)])
            cp('dve', kTs[:, 0:512], pst[:, 512:1024], [('ps', 1)], ['kTs'])
            yield
            for j in range(8):
                for kc in range(8):
                    mm(bank(0, 0, 256), hTs[:, kc, j * 128:(j + 1) * 128], wb[:, kc, 128:384], kc == 0, kc == 7,
                       [('xsh', j)] + wk, [('ps', 0)])
                bi = j % 2
                cp('dve', vbs[:, 4 + j, :], bank(0, 128, 256), [('ps', 0)], ['vbs'])
                rope(bank(0, 0, 128), ropa[:, j, :], xb16[bi], [('ps', 0)], ('xb16', bi))
                yield
                tr(pst[:, 0:128], xb16[bi], identb, [('xb16', bi), 'identb'], [('ps', 1)])
                cp('act', kTs[:, 512 + j * 128:512 + (j + 1) * 128], pst[:, 0:128], [('ps', 1)], ['kTs'])
                yield
            for j in range(2):
                for kc in range(8):
                    mm(bank(0), hTo[:, kc, j * 128:(j + 1) * 128], wb[:, kc, :], kc == 0, kc == 7,
                       [('xoh', j)] + wk, [('ps', 0)])
                bi = j % 2
                act(sg2[par][:, j, :], bank(0, 384, 512), AF.Tanh, [('ps', 0)], [('sg', par, j)], scale=0.5)
                stt('dve', sg2[par][:, j, :], sg2[par][:, j, :], 1.0, bank(0, 384, 512), ALU.add, ALU.mult, [('ps', 0), ('sg', par, j)], [('sg', par, j)])
                rope(bank(0, 0, 128), ropo[:, j, :], xb16[bi], [('ps', 0)], ('xb16', bi))
                yield
                tr(pst[:, 128:256], xb16[bi], identb, [('xb16', bi), 'identb'], [('ps', 1)])
                cp('act', qT2[par][:, j * 128:(j + 1) * 128], pst[:, 128:256], [('ps', 1)], [('qT', par, j)])
                yield

    ajobs = []
    for h in range(NHEAD_A):
        for s_ in range(4):
            ajobs.append((h, 'p', s_))
        ajobs.append((h, 's', 0))
    loaded = set()

    def load_w(h):
        if h in loaded or h >= NHEAD_A:
            return
        loaded.add(h)
        for j4, base in enumerate([0, 1024, 2048, 3072]):
            dma('pool', wA[h % 2][:, :, j4 * 128:(j4 + 1) * 128], w_in_v[:, :, base + h * 128: base + (h + 1) * 128], (),
                [('wA', h % 2, j4)])
    if ajobs:
        load_w(0)
        drive2([proj_gen(ajobs[0][0], 0, ajobs[0][1], ajobs[0][2])])
        for n, (h, kind, s_) in enumerate(ajobs):
            par = n % 2
            ag = attn_gen(h, par, kind, s_ * 256 if kind == 'p' else 1024)
            pg = None
            if n + 1 < len(ajobs):
                h2, kind2, s2 = ajobs[n + 1]
                load_w(h2)
                pg = proj_gen(h2, (n + 1) % 2, kind2, s2)
            drive2([ag, pg])
    if STOP == 'A':
        return
    S.barrier()
    A.off = mA

    wR = [A.alloc([8, 512], BF16) for _ in range(2)]
    rkv = A.alloc([3, 1024], F32)
    kk = A.alloc([1024], F32)
    t1 = A.alloc([1024], F32)
    prodb = A.alloc([1024], BF16)
    sqb = prodb
    vb16 = A.alloc([1024], BF16)
    VT = A.alloc([8, 128], BF16)
    sgb = A.alloc([8, 128], F32)
    yacc = A.alloc([8, 128], F32)
    wL = yacc.rearrange("p a b -> p (a b)").bitcast(BF16).rearrange("p (a b) -> p a b", b=256)
    bsum = A.alloc([16], F32)
    lnxg = [A.alloc([128], F32) for _ in range(2)]
    lnxb = [A.alloc([128], F32) for _ in range(2)]
    sgw = A.alloc([256], F32)
    Pp = A.alloc([256], F32)
    csb = A.alloc([256], F32)
    Winv = A.alloc([256], F32)
    av = A.alloc([256], F32)
    tmpa = A.alloc([256], F32)
    tmpb = A.alloc([256], F32)
    BW = A.alloc([256], BF16)
    KW = A.alloc([256], BF16)
    Wb2 = [A.alloc([2, 130], F32) for _ in range(2)]
    Kt2 = [A.alloc([256], BF16) for _ in range(2)]
    Bt2 = [A.alloc([256], BF16) for _ in range(2)]
    ARb2 = [A.alloc([2, 256], BF16) for _ in range(2)]
    BKT2 = [A.alloc([2, 256], BF16) for _ in range(2)]
    ATb2 = [A.alloc([2, 128], BF16) for _ in range(2)]
    NM = [A.alloc([512], BF16) for _ in range(4)]
    lo = [0]

    def lalloc(n):
        a_ = lvl[:, lo[0]:lo[0] + n].bitcast(F32)
        lo[0] += n
        assert lo[0] <= LVL_WORDS
        return a_
    X0 = [lalloc(128) for _ in range(4)]
    X0T = [lalloc(128) for _ in range(4)]
    XX = [[lalloc(256) for _ in range(2)] for _ in range(4)]
    Zb = [[lalloc(128) for _ in range(2)] for _ in range(4)]
    Zh = [A.alloc([128], BF16) for _ in range(4)]
    GT = A.alloc([8, 64], BF16)
    Hs = A.alloc([8, 64], F32)
    Qb = A.alloc([8, 128], BF16)
    Sf = [A.alloc([64], F32) for _ in range(2)]
    Sb = [A.alloc([64], BF16) for _ in range(2)]
    s0raw = A.alloc([128], F32)
    stg = [A.alloc([128], F32) for _ in range(2)]
    gst = A.alloc([8, 16], F32)
    ysq = t1.rearrange("p (a b) -> p a b", b=128)
    ybon = kk.rearrange("p (a b) -> p a b", b=128)
    obR = A.alloc([8, 128], BF16)

    dma('pool', wL, w_in_v[:, :, 4096 + 3072:4096 + 3328], (), ['wL'])
    for q in range(2):
        memset('dve', Wb2[q][:, :, 0:1], 1.0, [('Wbpad0', q)])
        memset('dve', Wb2[q][:, :, 129:130], 1.0, [('Wbpad1', q)])

    T = 1024
    units = [(hTp, 'xph', 0, 'p'), (hTs, 'xsh', 1024, 's')]

    def proj_shift(w_ap, wkeys, hT, hkey, ci, dst, dkey, kind, b0):
        for n0 in range(0, T, 512):
            bnk = b0 + n0 // 512
            hk = [(hkey, n0 // 128 + q) for q in range(4)]
            for kc in range(8):
                mm(bank(bnk), w_ap[:, kc, :], hT[:, kc, n0:n0 + 512], kc == 0, kc == 7, hk + wkeys, [('ps', bnk)])
        for n0 in range(0, T, 512):
            bnk = b0 + n0 // 512
            act(dst[:, n0:n0 + 512], bank(bnk), AF.Identity, [('ps', bnk), 'c0v'], [dkey], scale=c0v[:, ci:ci + 1])
        psv = PS[:, b0 * 512:b0 * 512 + 1024]
        pk = [('ps', b0), ('ps', b0 + 1)]
        blocks = [(0, T)] if kind == 's' else [(q * 256, (q + 1) * 256) for q in range(4)]
        for (s_, e_) in blocks:
            kk_ = [('ps', b0 + s_ // 512)] if (s_ // 512 == (e_ - 1) // 512) else pk
            stt('dve', dst[:, s_ + 1:e_], psv[:, s_:e_ - 1], mu[:, ci, 0:1], dst[:, s_ + 1:e_], ALU.mult, ALU.add,
                kk_ + ['mu', dkey], [dkey])
            stt('dve', dst[:, s_:e_ - 1], psv[:, s_ + 1:e_], mu[:, ci, 1:2], dst[:, s_:e_ - 1], ALU.mult, ALU.add,
                kk_ + ['mu', dkey], [dkey])

    for (hT, hkey, lc0, kind) in units:
        for c in range(2):
            proj_shift(wL[:, :, c * 128:(c + 1) * 128], ['wL'], hT, hkey, 24 + c, t1, 't1', kind, 2 * c)
            if c == 0:
                act(twT[:, lc0:lc0 + T], t1[:, 0:T], AF.Tanh, ['t1'], [('twT', kind)])
            else:
                cp('act', laT[:, lc0:lc0 + T], t1[:, 0:T], ['t1'], [('laT', kind)])
    S.barrier()

    def prep_gen(hp, kind, lc0, d, seg, par):
        Wb, Kt, Bt, ARb, BKT, ATb = Wb2[par], Kt2[par], Bt2[par], ARb2[par], BKT2[par], ATb2[par]
        kT_, rT_ = rkv[:, 1, :], rkv[:, 0, :]
        c0_ = seg * 256
        lc = lc0 + c0_
        mm(bank(0, 0, 256), W2b[64 * d:64 * d + 64, hp * 128:(hp + 1) * 128], twT[64 * d:64 * d + 64, lc:lc + 256],
           True, True, ['W2b', ('twT', kind)], [('ps', 0)])
        act(sgw, bank(0, 0, 256), AF.Tanh, [('ps', 0), 'w0h'], ['sgw'], bias=w0h[:, hp * 2 + d:hp * 2 + d + 1], scale=0.5)
        ts('dve', sgw, sgw, 0.5, 0.5, ALU.mult, ALU.add, ['sgw'], ['sgw'])
        mm(bank(1, 0, 256), A2b[64 * d:64 * d + 64, hp * 128:(hp + 1) * 128], laT[64 * d:64 * d + 64, lc:lc + 256],
           True, True, ['A2b', ('laT', kind)], [('ps', 1)])
        act(av, bank(1, 0, 256), AF.Tanh, [('ps', 1), 'a0h'], ['av'], bias=a0h[:, hp * 2 + d:hp * 2 + d + 1], scale=0.5)
        ts('dve', av, av, 0.5, 0.5, ALU.mult, ALU.add, ['av'], ['av'])
        yield
        for t in range(2):
            S.op('dve', (lambda t=t: (lambda e: e.tensor_tensor_scan(
                out=Pp[:, t * 128:(t + 1) * 128], data0=onesf, data1=sgw[:, t * 128:(t + 1) * 128],
                initial=0.0, op0=ALU.mult, op1=ALU.add)))(), ['sgw', 'onesf'], [('Pp', t)])
        if d == 0:
            cs = Pp
            csk = [('Pp', 0), ('Pp', 1)]
        else:
            for t in range(2):
                stt('dve', csb[:, t * 128:(t + 1) * 128], sgw[:, t * 128:(t + 1) * 128],
                    Pp[:, t * 128 + 127:t * 128 + 128], Pp[:, t * 128:(t + 1) * 128], ALU.add, ALU.subtract,
                    ['sgw', ('Pp', t)], [('csb', t)])
            cs = csb
            csk = [('csb', 0), ('csb', 1)]
        yield
        act(Wb[:, :, 1:129], cs.rearrange("p (a b) -> p a b", b=128), AF.Exp, csk, [('Wb', par)], scale=-DC)
        act(Winv, cs, AF.Exp, csk, ['Winv'], scale=DC)
        ts('dve', tmpa, av, kav[:, hp:hp + 1], omka[:, hp:hp + 1], ALU.mult, ALU.add, ['av', 'kav', 'omka'], ['tmpa'])
        tt('pool', tmpb, kk[:, c0_:c0_ + 256], av, ALU.mult, ['kk', 'av'], ['tmpb'])
        yield
        tt('dve', tmpa, tmpa, kT_[:, c0_:c0_ + 256], ALU.mult, ['tmpa', ('rkv', 1)], ['tmpa'])
        tt('dve', Kt, tmpa, Winv, ALU.mult, ['tmpa', 'Winv'], [('Kt', par)])
        tt('dve', Bt, tmpb, Winv, ALU.mult, ['tmpb', 'Winv'], [('Bt', par)])
        yield
        Wprev = Wb[:, :, 0:128] if d == 0 else Wb[:, :, 2:130]
        stt('dve', ARb[:, :, 0:128], kk[:, c0_:c0_ + 256].rearrange("p (a b) -> p a b", b=128), -1.0, Wprev,
            ALU.mult, ALU.mult, ['kk', ('Wb', par), ('Wbpad0', par), ('Wbpad1', par)], [('ARb', par, 'a')])
        tt('dve', ARb[:, :, 128:256], rT_[:, c0_:c0_ + 256].rearrange("p (a b) -> p a b", b=128), Wb[:, :, 1:129],
           ALU.mult, [('rkv', 0), ('Wb', par)], [('ARb', par, 'r')])
        yield
        pst2 = bankb(2)
        for t in range(2):
            wc = Wb[:, t, 128:129] if d == 0 else Wb[:, t, 1:2]
            ts('dve', BW[:, t * 128:(t + 1) * 128], Bt[:, t * 128:(t + 1) * 128], wc, None, ALU.mult, None,
               [('Bt', par), ('Wb', par)], [('BW', t)])
            ts('dve', KW[:, t * 128:(t + 1) * 128], Kt[:, t * 128:(t + 1) * 128], wc, None, ALU.mult, None,
               [('Kt', par), ('Wb', par)], [('KW', t)])
            yield
        for t in range(2):
            tr(pst2[:, t * 384:t * 384 + 128], BW[:, t * 128:(t + 1) * 128], identb, [('BW', t), 'identb'], [('ps', 2)])
            tr(pst2[:, t * 384 + 128:t * 384 + 256], KW[:, t * 128:(t + 1) * 128], identb, [('KW', t), 'identb'], [('ps', 2)])
            tr(pst2[:, t * 384 + 256:t * 384 + 384], ARb[:, t, 0:128], identb, [('ARb', par, 'a'), 'identb'], [('ps', 2)])
        for t in range(2):
            cp('act', BKT[:, t, :], pst2[:, t * 384:t * 384 + 256], [('ps', 2)], [('BKT', par, t)])
            cp('act', ATb[:, t, :], pst2[:, t * 384 + 256:t * 384 + 384], [('ps', 2)], [('ATb', par, t)])
        yield

    def rest_gen(hp, kind, d, seg, par, state_in):
        Wb, Kt, Bt, ARb, BKT, ATb = Wb2[par], Kt2[par], Bt2[par], ARb2[par], BKT2[par], ATb2[par]
        gt0 = seg * 2
        P = []
        for t in range(2):
            for e in range(2):
                zi = t * 2 + e
                P.append(dict(t=t, e=e, zi=zi, si=zi, pb=64 * e, bM=4 + zi, gt=gt0 + t))
        for p in P:
            t, e, pb, bM, si = p['t'], p['e'], p['pb'], p['bM'], p['si']
            mm(bank(bM, 0, 256), Bt[pb:pb + 64, t * 128:(t + 1) * 128], ARb[pb:pb + 64, t, :], True, True,
               [('Bt', par), ('ARb', par, 'a'), ('ARb', par, 'r')], [('ps', bM)])
            mm(bank(bM, 256, 512), Kt[pb:pb + 64, t * 128:(t + 1) * 128], ARb[pb:pb + 64, t, :], True, True,
               [('Kt', par), ('ARb', par, 'a'), ('ARb', par, 'r')], [('ps', bM)])
        for p in P:
            bM, si = p['bM'], p['si']
            tt('dve', NM[si], bank(bM), maskNM[:, d, :], ALU.mult, [('ps', bM), 'maskNM'], [('NM', si)])
            tt('dve', X0[si].bitcast(LEVEL_DT), bank(bM, 0, 128), maskNM[:, d, 0:128], ALU.mult, [('ps', bM), 'maskNM'], [('X0', si)])
        yield
        for p in P:
            t, e, pb, bM, si, gt = p['t'], p['e'], p['pb'], p['bM'], p['si'], p['gt']
            mm(bank(bM, 0, 128), ARb[pb:pb + 64, t, 0:128], Bt[pb:pb + 64, t * 128:(t + 1) * 128], True, True,
               [('Bt', par), ('ARb', par, 'a')], [('ps', bM)])
            mm(bank(bM, 384, 448), NM[si][:, 256:384], VT[:, gt, e * 64:(e + 1) * 64], True, True,
               [('NM', si), 'VT'], [('ps', bM)])
        for p in P:
            t, e, bM, si, zi = p['t'], p['e'], p['bM'], p['si'], p['zi']
            tt('dve', X0T[si].bitcast(LEVEL_DT), bank(bM, 0, 128), maskT[:, d, :], ALU.mult, [('ps', bM), 'maskT'], [('X0T', si)])
            cp('act', Zb[zi][0].bitcast(LEVEL_DT)[:, 64:128], bank(bM, 384, 448), [('ps', bM)], [('Zb', zi, 0, 'u')])
            cp('pool', Zb[zi][0].bitcast(LEVEL_DT)[:, 0:64], ATb[:, t, e * 64:(e + 1) * 64], [('ATb', par, t)], [('Zb', zi, 0, 'a')])
        yield
        for j in range(7):
            for p in P:
                bM, si, zi = p['bM'], p['si'], p['zi']
                Xj = X0[si] if j == 0 else XX[si][j % 2][:, 0:128]
                XjT = X0T[si] if j == 0 else XX[si][j % 2][:, 128:256]
                xk = [('X0', si), ('X0T', si)] if j == 0 else [('XX', si, j % 2)]
                zc = Zb[zi][j % 2]
                zck = [('Zb', zi, j % 2, 'a'), ('Zb', zi, j % 2, 'u')]
                Xr, XTr, zr = Xj.bitcast(LEVEL_DT), XjT.bitcast(LEVEL_DT), zc.bitcast(LEVEL_DT)
                mm(bank(bM, 0, 128), Xr, zr, True, True, xk + zck, [('ps', bM)])
                if j < 6:
                    mm(bank(bM, 128, 256), XTr, Xr, True, True, xk, [('ps', bM)])
                    mm(bank(bM, 256, 384), Xr, XTr, True, True, xk, [('ps', bM)])
            for p in P:
                bM, si, zi = p['bM'], p['si'], p['zi']
                zc = Zb[zi][j % 2]
                zn = Zb[zi][(j + 1) % 2]
                zck = [('Zb', zi, j % 2, 'a'), ('Zb', zi, j % 2, 'u')]
                znk = [('Zb', zi, (j + 1) % 2, 'a'), ('Zb', zi, (j + 1) % 2, 'u')]
                tt('dve', zn.bitcast(LEVEL_DT), bank(bM, 0, 128), zc, ALU.add, [('ps', bM)] + zck, znk)
                if j < 6:
                    cp('act', XX[si][(j + 1) % 2].bitcast(LEVEL_DT), bank(bM, 128, 384), [('ps', bM)], [('XX', si, (j + 1) % 2)])
            yield
        for p in P:
            si, zi = p['si'], p['zi']
            cp('pool', Zh[si], Zb[zi][1], [('Zb', zi, 1, 'a'), ('Zb', zi, 1, 'u')], [('Zh', si)])
        for p in P:
            t, e, pb, bM, si, gt = p['t'], p['e'], p['pb'], p['bM'], p['si'], p['gt']
            Z = Zh[si]
            zk = [('Zh', si)]
            mm(bank(bM, 448, 512)[pb:pb + 64, :], Z[:, 0:64], BKT[:, t, e * 64:(e + 1) * 64], True, True,
               zk + [('BKT', par, t)], [('ps', bM)])
            mm(bank(bM, 384, 448)[pb:pb + 64, :], BKT[:, t, e * 64:(e + 1) * 64], Z[:, 64:128], True, False,
               zk + [('BKT', par, t)], [('ps', bM)])
            mm(bank(bM, 384, 448)[pb:pb + 64, :], BKT[:, t, 128 + e * 64:128 + (e + 1) * 64],
               VT[:, gt, e * 64:(e + 1) * 64], False, True, ['VT', ('BKT', par, t)], [('ps', bM)])
            mm(bank(bM, 0, 128)[pb:pb + 64, :], Z[:, 0:64], NM[si][:, 128:256], True, True,
               zk + [('NM', si)], [('ps', bM)])
            mm(bank(bM, 128, 192), NM[si][:, 128:256], Z[:, 64:128], True, False, zk + [('NM', si)], [('ps', bM)])
            mm(bank(bM, 128, 192), NM[si][:, 384:512], VT[:, gt, e * 64:(e + 1) * 64], False, True,
               ['VT', ('NM', si)], [('ps', bM)])
        yield
        for p in P:
            t, e, pb, bM, si, gt = p['t'], p['e'], p['pb'], p['bM'], p['si'], p['gt']
            wc = Wb[pb:pb + 64, t, 128:129] if d == 0 else Wb[pb:pb + 64, t, 1:2]
            stt('dve', GT[pb:pb + 64, gt, :], identf[pb:pb + 64, pb:pb + 64], wc, bank(bM, 448, 512)[pb:pb + 64, :],
                ALU.mult, ALU.add, [('ps', bM), 'identf', ('Wb', par)], [('GT', gt, e)])
            tt('dve', Qb[pb:pb + 64, gt, :], bank(bM, 0, 128)[pb:pb + 64, :], ARb[pb:pb + 64, t, 128:256], ALU.add,
               [('ps', bM), ('ARb', par, 'r')], [('Qb', gt, e)])
            cp('act', Hs[pb:pb + 64, gt, :], bank(bM, 384, 448)[pb:pb + 64, :], [('ps', bM)], [('Hs', gt, e)])
            if d == 0:
                cp('act', yacc[:, gt, e * 64:(e + 1) * 64], bank(bM, 128, 192), [('ps', bM)], [('yacc', gt, e)])
            else:
                tt('dve', yacc[:, gt, e * 64:(e + 1) * 64], bank(bM, 128, 192), yacc[:, gt, e * 64:(e + 1) * 64],
                   ALU.add, [('ps', bM), ('yacc', gt, e)], [('yacc', gt, e)])
        yield
        have_state = state_in
        order = [0, 1] if d == 0 else [1, 0]
        for t in order:
            gt = gt0 + t
            for e in range(2):
                pb = 64 * e
                if have_state:
                    mm(bank(3, e * 64, e * 64 + 64), Qb[pb:pb + 64, gt, :], Sb[d][pb:pb + 64, :], True, True,
                       [('Qb', gt, e), ('Sb', d, e)], [('ps', 3)])
                    mm(bank(3, 128 + e * 64, 192 + e * 64)[pb:pb + 64, :], GT[pb:pb + 64, gt, :], Sb[d][pb:pb + 64, :],
                       True, True, [('GT', gt, e), ('Sb', d, e)], [('ps', 3)])
            for e in range(2):
                pb = 64 * e
                if have_state:
                    tt('dve', yacc[:, gt, e * 64:(e + 1) * 64], bank(3, e * 64, e * 64 + 64),
                       yacc[:, gt, e * 64:(e + 1) * 64], ALU.add, [('ps', 3), ('yacc', gt, e)], [('yacc', gt, e)])
                    tt('dve', Sf[d][pb:pb + 64, :], bank(3, 128 + e * 64, 192 + e * 64)[pb:pb + 64, :], Hs[pb:pb + 64, gt, :],
                       ALU.add, [('ps', 3), ('Hs', gt, e)], [('Sf', d, e)])
                else:
                    cp('dve', Sf[d][pb:pb + 64, :], Hs[pb:pb + 64, gt, :], [('Hs', gt, e)], [('Sf', d, e)])
                cp('act', Sb[d][pb:pb + 64, :], Sf[d][pb:pb + 64, :], [('Sf', d, e)], [('Sb', d, e)])
            have_state = True
            yield
        if kind == 'p':
            q = (seg + d) % 2
            tr(bank(3, 256, 384)[0:64, :], Sf[d], identf, [('Sf', d, 0), ('Sf', d, 1), 'identf'], [('ps', 3)])
            cp('act', stg[q][0:64, :], bank(3, 256, 384)[0:64, :], [('ps', 3)], [('stg', q)])
            dma('sp', O['ns'][seg, d, 2 * hp:2 * hp + 2, :, :].rearrange("e v k -> v e k"),
                stg[q][0:64, :].rearrange("p (e k) -> p e k", e=2), [('stg', q)], (), is_out=True)
            yield

    def drive(a, b):
        gens = [g for g in (a, b) if g is not None]
        while gens:
            for g in list(gens):
                try:
                    next(g)
                except StopIteration:
                    gens.remove(g)

    jobno = [0]
    ag_in = nc.dram_tensor("ag_in", [1024, 256], BF16)
    ag_out = nc.dram_tensor("ag_out", [4096, 256], BF16)
    wrs_v = I['wrs'].rearrange("(kc p) n -> p kc n", p=128)
    unit_of = {'p': (hTp, 'xph', 0), 's': (hTs, 'xsh', 1024)}
    tasks = ([('s', 8), ('s', 9)] if NSEQ_R >= 2 else []) + [('p', h_) for h_ in range(NHP)]

    def ci_of(a_, hp):
        return a_ * 8 + hp if hp < 8 else 26 + a_ * 2 + (hp - 8)
    for ti_, (kind, hp) in enumerate(tasks):
        tp = ti_ % 2
        wr = wR[tp]
        for a_, base in enumerate([4096, 4096 + 1024, 4096 + 2048, 4096 + 3328]):
            if hp < 8:
                wsrc = w_in_v[:, :, base + hp * 128: base + (hp + 1) * 128]
            else:
                wsrc = wrs_v[:, :, (hp - 8) * 512 + a_ * 128:(hp - 8) * 512 + (a_ + 1) * 128]
            dma('pool', wr[:, :, a_ * 128:(a_ + 1) * 128], wsrc, (), [('wR', tp, a_)])
        dma('sp', lnxg[tp], I['lnxg'][0:1, hp * 128:(hp + 1) * 128].partition_broadcast(128), (), [('lnxg', tp)])
        dma('sp', lnxb[tp], I['lnxb'][0:1, hp * 128:(hp + 1) * 128].partition_broadcast(128), (), [('lnxb', tp)])
        if ti_ == 2 and tasks[0][0] == 's':
            S.op('pool', lambda e: e.collective_compute("AllGather", ALU.bypass, replica_groups=[[0, 1, 2, 3], [4, 5, 6, 7]],
                                                        ins=[ag_in.ap().opt()], outs=[ag_out.ap().opt()]),
                 [('ag_in', 0), ('ag_in', 1)], ['ag_out'], dma=True, cc=True)
        for (hT, hkey, lc0) in [unit_of[kind]]:
            nt = 8
            for a_ in range(3):
                proj_shift(wr[:, :, a_ * 128:(a_ + 1) * 128], [('wR', tp, a_)], hT, hkey, ci_of(a_, hp), rkv[:, a_, :],
                           ('rkv', a_), kind, 2 * a_)
            rT_, kT_, vT_ = rkv[:, 0, :], rkv[:, 1, :], rkv[:, 2, :]
            for t in range(nt):
                bnk = 6 + (t // 4) % 2
                for kc in range(8):
                    mm(bank(bnk, (t % 4) * 128, (t % 4 + 1) * 128), hT[:, kc, t * 128:(t + 1) * 128],
                       wr[:, kc, 384:512], kc == 0, kc == 7, [(hkey, t), ('wR', tp, 3)], [('ps', bnk)])
                if t % 4 == 3:
                    act(sgb[:, t - 3:t + 1, :], bank(bnk).rearrange("p (a b) -> p a b", b=128), AF.Silu, [('ps', bnk)],
                        [('sgb', q) for q in range(t - 3, t + 1)])
            ts('dve', t1[:, 0:T], kT_[:, 0:T], kkv[:, hp:hp + 1], None, ALU.mult, None, [('rkv', 1), 'kkv'], ['t1'])
            act(sqb[:, 0:T], t1[:, 0:T], AF.Square, ['t1'], ['prodb'])
            for n0 in range(0, T, 512):
                bnk = (n0 // 512) % 2
                mm(bank(bnk), bonesb, sqb[:, n0:n0 + 512], True, True, ['bonesb', 'prodb'], [('ps', bnk)])
                act(kk[:, n0:n0 + 512], bank(bnk), AF.Sqrt, [('ps', bnk), 'cst'], ['kk'], bias=cst[:, 0:1])
            recip(kk[:, 0:T], kk[:, 0:T], ['kk'], ['kk'])
            tt('dve', kk[:, 0:T], kk[:, 0:T], t1[:, 0:T], ALU.mult, ['kk', 't1'], ['kk'])
            stt('dve', prodb[:, 0:T], rT_[:, 0:T], rkv_[:, hp:hp + 1], kT_[:, 0:T], ALU.mult, ALU.mult,
                [('rkv', 0), ('rkv', 1), 'rkv_'], ['prodb'])
            for t in range(nt):
                mm(bank(1, t * 2, t * 2 + 2), prodb[:, t * 128:(t + 1) * 128], hindb, True, True, ['prodb', 'hindb'], [('ps', 1)])
            cp('act', bsum[:, 0:nt * 2], bank(1, 0, nt * 2), [('ps', 1)], ['bsum'])
            cp('act', vb16[:, 0:T], vT_[:, 0:T], [('rkv', 2)], ['vb16'])
            pst2 = bankb(2)
            for t in range(nt):
                tr(pst2[:, t * 128:(t + 1) * 128], vb16[:, t * 128:(t + 1) * 128], identb, ['vb16', 'identb'], [('ps', 2)])
            cp('dve', VT[:, 0:nt, :], pst2[:, 0:nt * 128].rearrange("p (a b) -> p a b", b=128), [('ps', 2)], ['VT'])
            jobs = []
            for d in range(ND):
                segs = list(range(4)) if d == 0 else list(range(3, -1, -1))
                for i_, seg in enumerate(segs):
                    st_in = (kind == 's')
                    jobs.append((d, seg, st_in, (kind == 's' and i_ == 0)))
            pars = []
            for _ in jobs:
                pars.append(jobno[0] % 2)
                jobno[0] += 1
            pg = prep_gen(hp, kind, lc0, jobs[0][0], jobs[0][1], pars[0])
            drive(pg, None)
            for n, (d, seg, st_in, load_s0) in enumerate(jobs):
                if load_s0:
                    dma('sp', s0raw[0:64, :].rearrange("p (e k) -> p e k", e=2),
                        I['s0'][d, 2 * (hp - 8):2 * (hp - 8) + 2, :, :].rearrange("e v k -> v e k"), (), ['s0raw'])
                    tr(bank(3, 0, 64), s0raw[0:64, :], identf[0:64, 0:64], ['s0raw', 'identf'], [('ps', 3)])
                    for e in range(2):
                        pb = 64 * e
                        cp('dve', Sf[d][pb:pb + 64, :], bank(3, 0, 64)[pb:pb + 64, :], [('ps', 3)], [('Sf', d, e)])
                        cp('act', Sb[d][pb:pb + 64, :], bank(3, 0, 64)[pb:pb + 64, :], [('ps', 3)], [('Sb', d, e)])
                rg = rest_gen(hp, kind, d, seg, pars[n], st_in)
                ng = None
                if n + 1 < len(jobs):
                    ng = prep_gen(hp, kind, lc0, jobs[n + 1][0], jobs[n + 1][1], pars[n + 1])
                drive(rg, ng)
            n2 = nt * 2
            yk = [('yacc', t, e) for t in range(nt) for e in range(2)]
            yv = yacc[:, 0:nt, :].rearrange("p a (e f) -> p (a e) f", e=2)
            red(gst[:, 0, 0:n2], yv, ALU.add, yk, [('gst', 0)])
            act(ysq[:, 0:nt, :], yacc[:, 0:nt, :], AF.Square, yk, ['t1'])
            red(gst[:, 1, 0:n2], ysq[:, 0:nt, :].rearrange("p a (e f) -> p (a e) f", e=2), ALU.add, ['t1'], [('gst', 1)])
            ts('dve', gst[:, 2, 0:n2], gst[:, 0, 0:n2], 1.0 / 64, None, ALU.mult, None, [('gst', 0)], [('gst', 2)])
            tt('dve', gst[:, 3, 0:n2], gst[:, 2, 0:n2], gst[:, 2, 0:n2], ALU.mult, [('gst', 2)], [('gst', 3)])
            stt('dve', gst[:, 4, 0:n2], gst[:, 1, 0:n2], 1.0 / 64, gst[:, 3, 0:n2], ALU.mult, ALU.subtract,
                [('gst', 1), ('gst', 3)], [('gst', 4)])
            ts('dve', gst[:, 4, 0:n2], gst[:, 4, 0:n2], 64e-5, None, ALU.add, None, [('gst', 4)], [('gst', 4)])
            act(gst[:, 5, 0:n2], gst[:, 4, 0:n2], AF.Sqrt, [('gst', 4)], [('gst', 5)])
            recip(gst[:, 6, 0:n2], gst[:, 5, 0:n2], [('gst', 5)], [('gst', 6)])
            ysv = ysq[:, 0:nt, :].rearrange("p a (e f) -> p (a e) f", e=2)
            tt('dve', ysv, yv, gst[:, 2, 0:n2].unsqueeze(2).to_broadcast([128, n2, 64]), ALU.subtract, yk + [('gst', 2)], ['t1'])
            tt('dve', ysv, ysv, gst[:, 6, 0:n2].unsqueeze(2).to_broadcast([128, n2, 64]), ALU.mult, ['t1', ('gst', 6)], ['t1'])
            tt('dve', ysq[:, 0:nt, :], ysq[:, 0:nt, :], lnxg[tp].unsqueeze(1).to_broadcast([128, nt, 128]),
               ALU.mult, ['t1', ('lnxg', tp)], ['t1'])
            tt('dve', ysq[:, 0:nt, :], ysq[:, 0:nt, :], lnxb[tp].unsqueeze(1).to_broadcast([128, nt, 128]),
               ALU.add, ['t1', ('lnxb', tp)], ['t1'])
            tt('dve', ybon[:, 0:nt, :].rearrange("p a (e f) -> p (a e) f", e=2),
               VT[:, 0:nt, :].rearrange("p a (e f) -> p (a e) f", e=2),
               bsum[:, 0:n2].unsqueeze(2).to_broadcast([128, n2, 64]), ALU.mult, ['VT', 'bsum'], ['kk'])
            tt('dve', ysq[:, 0:nt, :], ysq[:, 0:nt, :], ybon[:, 0:nt, :], ALU.add, ['t1', 'kk'], ['t1'])
            tt('dve', obR[:, 0:nt, :], ysq[:, 0:nt, :], sgb[:, 0:nt, :], ALU.mult, ['t1'] + [('sgb', t) for t in range(nt)], ['obR'])
            if kind == 'p':
                pst2 = bankb(2)
                for t in range(nt):
                    tr(pst2[:, t * 128:(t + 1) * 128], obR[:, t, :], identb, ['obR', 'identb'], [('ps', 2)])
                cp('act', mixT[:, 8 + hp, 0:1024], pst2[:, 0:1024], [('ps', 2)], [('mixT', 8 + hp, g) for g in range(8)])
            else:
                i_ = hp - 8
                dma('sp', ag_in.ap().rearrange("(t p) (i f) -> p t i f", p=128, i=2)[:, :, i_, :], obR[:, 0:8, :], ['obR'],
                    [('ag_in', i_)])
    if DEBUG:
        dump_all(dict(yacc=yacc, Sf0=Sf[0], GT=GT, Hs=Hs))
    if STOP == 'R':
        return
    S.barrier()
    A.off = mA

    wout = A.alloc([16, 1024], BF16)
    fgbc = A.alloc([1024], F32)
    gatebc = A.alloc([2, 1024], F32)
    bgbc = A.alloc([1024], F32)
    scbc = A.alloc([8, 2, 128], BF16)
    wadg = A.alloc([8, 512], BF16)
    wout_v = I['w_out'].rearrange("(c p) n -> p c n", p=128)
    wada_v = I['w_ada'].rearrange("(kc p) n -> p kc n", p=128)
    dma('pool', wadg, wada_v[:, :, 2048:2560], (), [('wadg', 0)])
    for c4 in range(4):
        dma('pool', wout[:, c4 * 4:(c4 + 1) * 4, :], wout_v[:, c4 * 4:(c4 + 1) * 4, :], (), [('wout', c4)])
    dma('sp', fgbc, I['fg'][0:1, :].partition_broadcast(128), (), ['fgbc'])
    dma('sp', bgbc, I['bgate'][0:1, :].partition_broadcast(128), (), ['bgbc'])
    mO = A.off
    Gt = A.alloc([32, 256], BF16)
    dma('sp', Gt, ag_out.ap().rearrange("(rt p) f -> p rt f", p=128), ['ag_out'], ['Gt'])
    cp('dve', scbc, scp.unsqueeze(3).to_broadcast([128, 8, 2, 128]), ['scp'], ['scbc'])
    for n in range(2):
        if n == 1:
            dma('pool', wadg, wada_v[:, :, 2560:3072], [('wadg', 0)], [('wadg', 0)])
        for v in range(2):
            for kc in range(8):
                mm(bank(2 + v), scbc[:, kc, v, :], wadg[:, kc, :], kc == 0, kc == 7, ['scbc', ('wadg', 0)], [('ps', 2 + v)])
            tt('dve', gatebc[:, v, n * 512:(n + 1) * 512], bank(2 + v), bgbc[:, n * 512:(n + 1) * 512], ALU.add,
               [('ps', 2 + v), 'bgbc'], [('gatebc', v, n)])
    for r_ in range(4):
        for i_ in range(2):
            hq = 2 * r_ + i_
            bq = 4 + (hq % 4)
            for t in range(8):
                mm(bank(bq, 0, 256), Gt[:, r_ * 8 + t, i_ * 128:(i_ + 1) * 128], selb[:, t, :], t == 0, t == 7,
                   ['Gt', 'selb'], [('ps', bq)])
            cp('act', mixT[:, 8 + hq, 1024:1280], bank(bq, 0, 256), [('ps', bq)], [('mixT', 8 + hq, 8), ('mixT', 8 + hq, 9)])
    S.barrier()
    A.off = mO
    xr = [A.alloc([1024], F32) for _ in range(2)]
    yv_ = [A.alloc([1024], F32) for _ in range(2)]
    ojunk = A.alloc([1024], BF16)
    ost = [A.alloc([4], F32) for _ in range(2)]
    otiles = [('xp', g, 'yp', 0) for g in range(8)] + [('xo', g, 'ys', 1) for g in range(2)]
    for ti, (src, g, dst, v) in enumerate(otiles):
        b = ti % 2
        mg = g if src == 'xp' else 8 + g
        dma('sp', xr[b], I[src][g * 128:(g + 1) * 128, :], (), [('xr', b)])
        for n in range(2):
            for c in range(16):
                mm(bank(n), mixT[:, c, mg * 128:(mg + 1) * 128], wout[:, c, n * 512:(n + 1) * 512], c == 0, c == 15,
                   [('mixT', c, mg), ('wout', c // 4)], [('ps', n)])
            tt('dve', yv_[b][:, n * 512:(n + 1) * 512], bank(n), gatebc[:, v, n * 512:(n + 1) * 512], ALU.mult,
               [('ps', n), ('gatebc', v, n)], [('yv', b, n)])
            tt('dve', yv_[b][:, n * 512:(n + 1) * 512], yv_[b][:, n * 512:(n + 1) * 512], xr[b][:, n * 512:(n + 1) * 512], ALU.add,
               [('yv', b, n), ('xr', b)], [('yv', b, n)])
        act(ojunk, yv_[b], AF.Square, [('yv', b, 0), ('yv', b, 1)], ['ojunk', ('ost', b)], accum=ost[b][:, 0:1])
        ts('dve', ost[b][:, 1:2], ost[b][:, 0:1], 1.0 / 1024, 1e-6, ALU.mult, ALU.add, [('ost', b)], [('ostb', b)])
        act(ost[b][:, 2:3], ost[b][:, 1:2], AF.Sqrt, [('ostb', b)], [('ostc', b)])
        recip(ost[b][:, 3:4], ost[b][:, 2:3], [('ostc', b)], [('ostd', b)])
        stt('dve', xr[b], yv_[b], ost[b][:, 3:4], fgbc, ALU.mult, ALU.mult, [('yv', b, 0), ('yv', b, 1), ('ostd', b), 'fgbc', ('xr', b)], [('xr', b)])
        dma('sp', O[dst][g * 128:(g + 1) * 128, :], xr[b], [('xr', b)], (), is_out=True)


_NC = None


def _rope_tab(pos_rows, pos_cols):
    n_freq = 16
    inv = (10000.0 ** (-np.arange(n_freq, dtype=np.float32) / n_freq)).astype(np.float32)
    T = len(pos_rows)
    tab = np.zeros((T, 256), np.float32)
    for s_, pos in enumerate([pos_rows, pos_cols]):
        ang = pos.astype(np.float32)[:, None] * inv[None, :]
        c, sn = np.cos(ang).astype(np.float32), np.sin(ang).astype(np.float32)
        for m in range(2):
            for hf in range(2):
                o = m * 64 + s_ * 32 + hf * 16
                tab[:, o:o + 16] = c
                tab[:, 128 + o:128 + o + 16] = -sn if hf == 0 else sn
    return tab


def kernel(x_prompt, x_sample, cache_k, cache_v, state_rwkv, c, c_ctx, norm_g, w_ada, b_ada,
           w_in, lam_q1, lam_k1, lam_q2, lam_k2, subln_g, shift_mu, decay_w0, decay_w2,
           iclr_a0, iclr_a2, k_k, k_a, r_k, lnx_g, lnx_b, w_out, final_g):
    global _NC
    f = lambda a: np.ascontiguousarray(np.asarray(a, dtype=np.float32))
    x_prompt, x_sample, cache_k, cache_v, state_rwkv = map(f, (x_prompt, x_sample, cache_k, cache_v, state_rwkv))
    c, c_ctx = f(c), f(c_ctx)
    if _NC is None:
        _NC = build()
    nc = _NC

    def fm(v, nch):
        return np.ascontiguousarray(f(v).reshape(nch, 128).T)

    i = np.arange(128)
    su = (i[:, None] < i[None, :]).astype(np.float32)
    ui = (i[:, None] <= i[None, :]).astype(np.float32)
    sl = (i[:, None] > i[None, :]).astype(np.float32)
    li = (i[:, None] >= i[None, :]).astype(np.float32)
    maskNM = np.stack([np.concatenate([su, ui, su, ui], 1), np.concatenate([sl, li, sl, li], 1)])
    maskT = np.stack([sl, su])
    bones = np.kron(np.eye(2, dtype=np.float32), np.ones((64, 64), np.float32))
    hind = np.kron(np.eye(2, dtype=np.float32), np.ones((64, 1), np.float32))
    tok = np.arange(1024)
    ropeall = _rope_tab(tok // 64, tok % 64)
    W_in = f(w_in)[0]
    smu, sw0, sa0 = f(shift_mu)[0], f(decay_w0)[0], f(iclr_a0)[0]
    sw2, sa2 = f(decay_w2)[0].reshape(128, 1024), f(iclr_a2)[0].reshape(128, 1024)
    skk, ska, srk = f(k_k)[0], f(k_a)[0], f(r_k)[0].reshape(-1)
    slg, slb = f(lnx_g)[0], f(lnx_b)[0]
    shared = {
        "w_in": W_in, "w_ada": f(w_ada)[0], "w_out": f(w_out)[0],
        "bada_fm": fm(f(b_ada)[0], 24), "bgate": f(b_ada)[0:1, 2048:3072], "normg_fm": fm(f(norm_g)[0], 8),
        "fg": f(final_g)[None, :],
        "lamv": np.concatenate([f(lam_q1)[0], f(lam_k1)[0], f(lam_q2)[0], f(lam_k2)[0]])[None, :],
        "sublng": f(subln_g)[0:1],
        "ident": np.eye(128, dtype=np.float32), "maskNM": maskNM, "maskT": maskT, "bones": bones, "hind": hind,
        "ropeall": ropeall,
    }

    def cols(v, hp):
        return v[hp * 128:(hp + 1) * 128]
    in_maps = []
    for core in range(8):
        b, q = core // 4, core % 4
        hps = list(range(8)) + [2 * q, 2 * q + 1]
        sel = np.zeros((1024, 256), np.float32)
        sel[q * 256 + np.arange(256), np.arange(256)] = 1.0
        cT = np.stack([fm(c_ctx, 8), fm(c[b], 8)], -1).reshape(128, 16)
        chunks = [smu[:, ci * 128:(ci + 1) * 128] for ci in range(26)]
        for a_ in range(3):
            for i_ in range(2):
                ci = a_ * 8 + hps[8 + i_]
                chunks.append(smu[:, ci * 128:(ci + 1) * 128])
        mu_fm = np.stack([np.stack([ch[0], ch[1]], -1) for ch in chunks], 1).reshape(128, 64)
        w0_fm = np.stack([np.stack([cols(sw0[0], h_), cols(sw0[1], h_)], -1) for h_ in hps], 1).reshape(128, 20)
        a0_fm = np.stack([np.stack([cols(sa0[0], h_), cols(sa0[1], h_)], -1) for h_ in hps], 1).reshape(128, 20)
        ext = lambda v: np.stack([cols(v, h_) for h_ in hps], 1)
        extc = lambda m: np.concatenate([m[:, h_ * 128:(h_ + 1) * 128] for h_ in hps], 1)
        wrs = np.concatenate([W_in[:, base + h_ * 128: base + (h_ + 1) * 128]
                              for h_ in hps[8:] for base in (4096, 4096 + 1024, 4096 + 2048, 4096 + 3328)], 1)
        m = dict(shared)
        m.update({
            "xp": x_prompt[core * 4:(core + 1) * 4].reshape(1024, 1024),
            "xs": x_sample[b], "xo": x_sample[b, q * 256:(q + 1) * 256],
            "ck": cache_k[b, 0].reshape(512, 1024), "cv": cache_v[b, 0].reshape(512, 1024),
            "s0": state_rwkv[b, 0][:, 4 * q:4 * q + 4], "cT": np.ascontiguousarray(cT), "selT": sel,
            "ropeown": np.ascontiguousarray(ropeall[q * 256:(q + 1) * 256]),
            "mu_fm": mu_fm, "w0_fm": w0_fm, "a0_fm": a0_fm, "w2": extc(sw2), "a2": extc(sa2),
            "kk_fm": ext(skk), "ka_fm": ext(ska), "rk_fm": ext(srk),
            "lnxg": extc(slg[None, :]), "lnxb": extc(slb[None, :]), "wrs": wrs,
        })
        in_maps.append({k: np.ascontiguousarray(v, dtype=np.float32) for k, v in m.items()})
    res = run_bass_kernel_spmd(nc, in_maps, core_ids=list(range(8)))
    R = res.results
    y_prompt = np.concatenate([R[i]["yp"].reshape(4, 256, 1024) for i in range(8)], 0)
    y_sample = np.stack([np.concatenate([R[b * 4 + q]["ys"] for q in range(4)], 0) for b in range(2)], 0)
    new_k = np.concatenate([R[i]["nk"].reshape(4, 1, 256, 8, 2, 64) for i in range(8)], 0)
    new_v = np.concatenate([R[i]["nv"].reshape(4, 1, 256, 8, 128) for i in range(8)], 0)
    new_s = np.concatenate([R[i]["ns"].reshape(4, 1, 2, 16, 64, 64) for i in range(8)], 0)
    return (y_prompt.astype(np.float32), y_sample.astype(np.float32), new_k.astype(np.float32),
            new_v.astype(np.float32), new_s.astype(np.float32))
```

```python
import math
from contextlib import ExitStack
import numpy as np
import concourse.bass as bass
import concourse.mybir as mybir
from concourse.bass_utils import run_bass_kernel_spmd

F32 = mybir.dt.float32
F32R = mybir.dt.float32r
LEVEL_DT = F32
BF16 = mybir.dt.bfloat16
AF = mybir.ActivationFunctionType
ALU = mybir.AluOpType
AX = mybir.AxisListType

ENG = ['pe', 'dve', 'act', 'pool', 'sp']
NDMA = 40
DC = math.exp(-0.5)


class Sched:
    def __init__(s, nc, stack):
        s.nc = nc
        s.ops = {e: [] for e in ENG}
        s.cnt = {e: 0 for e in ENG}
        s.known = {e: {f: 0 for f in ENG} for e in ENG}
        s.snap = {e: [] for e in ENG}
        s.kdma = {e: {} for e in ENG}
        s.lastw = {}
        s.rd_e = {}
        s.rd_d = {}
        s.sems = {e: stack.enter_context(nc.semaphore('c_' + e)) for e in ENG}
        s.dsems = [stack.enter_context(nc.semaphore('d%d' % i)) for i in range(2 * NDMA)]
        s.dcnt = {'sp': 0, 'pool': 0, 'act': 0}
        s.dcount = 0
        s.dlast = {}
        s.out_events = []

    def op(s, eng, fn, r=(), w=(), dma=False, is_out=False, noinc=False, cc=False):
        r = [(k[0], k[1]) if (isinstance(k, tuple) and k[0] == 'ps') else k for k in r]
        w = [(k[0], k[1]) if (isinstance(k, tuple) and k[0] == 'ps') else k for k in w]
        psr = [k for k in r if isinstance(k, tuple) and k[0] == 'ps']
        if psr:
            r = [k for k in r if not (isinstance(k, tuple) and k[0] == 'ps')]
            w = list(w) + [k for k in psr if k not in w]
        deps = []
        for k in r:
            ev = s.lastw.get(k)
            if ev is not None:
                deps.append((ev, True))
        for k in w:
            ev = s.lastw.get(k)
            if ev is not None:
                deps.append((ev, False))
            for f, c in s.rd_e.get(k, {}).items():
                deps.append((('E', f, c), True))
            for ev in s.rd_d.get(k, ()):
                deps.append((ev, False))
        waits = {}
        for ev, raw in deps:
            if ev[0] == 'E':
                _, f, c = ev
                if f == eng and not dma:
                    if (not raw) or eng == 'pe':
                        continue
                if s.known[eng][f] >= c:
                    continue
                waits[('E', f)] = max(waits.get(('E', f), 0), c)
            else:
                _, si, v = ev
                if s.kdma[eng].get(si, 0) >= v:
                    continue
                waits[('D', si)] = max(waits.get(('D', si), 0), v)
        if cc:
            ev = ('D', 'cc', 1)
            s.dlast['cc'] = 1
        elif dma:
            qn = s.dcnt[eng]
            si = qn % NDMA + (NDMA if eng == 'pool' else 0)
            v = 16 * (qn // NDMA + 1)
            if qn >= NDMA and s.kdma[eng].get(si, 0) < v - 16:
                waits[('D', si)] = max(waits.get(('D', si), 0), v - 16)
            s.dcnt[eng] += 1
            s.dcount += 1
            s.dlast[si] = v
            ev = ('D', si, v)
        for (t, x), v in waits.items():
            if t == 'E':
                kn = s.known[eng]
                if kn[x] < v:
                    kn[x] = v
                sn = s.snap[x][v - 1]
                for f2, c2 in sn.items():
                    if kn[f2] < c2:
                        kn[f2] = c2
            else:
                s.kdma[eng][x] = v
        if not (dma or cc):
            if noinc:
                ev = ('E', eng, s.cnt[eng] + 1)
            else:
                s.cnt[eng] += 1
                ev = ('E', eng, s.cnt[eng])
                s.snap[eng].append(dict(s.known[eng]))
        s.ops[eng].append((list(waits.items()), fn, None if noinc else ev))
        for k in r:
            if ev[0] == 'E':
                s.rd_e.setdefault(k, {})[eng] = ev[2]
            else:
                s.rd_d.setdefault(k, []).append(ev)
        for k in w:
            s.lastw[k] = ev
            s.rd_e[k] = {}
            s.rd_d[k] = []
        if is_out:
            s.out_events.append(ev)
        return ev

    def barrier(s):
        waits = {}
        for f in ENG:
            if f != 'sp' and s.cnt[f] > 0:
                waits[('E', f)] = s.cnt[f]
        for si, v in s.dlast.items():
            waits[('D', si)] = v
        s.cnt['sp'] += 1
        c = s.cnt['sp']
        for f in ENG:
            if f != 'sp':
                s.known['sp'][f] = s.cnt[f]
        s.kdma['sp'] = dict(s.dlast)
        s.snap['sp'].append(dict(s.known['sp']))
        s.ops['sp'].append((list(waits.items()), (lambda e: e.nop()), ('E', 'sp', c)))
        for f in ENG:
            if f == 'sp':
                continue
            s.ops[f].append(([(('E', 'sp'), c)], None, None))
            for g in ENG:
                if g != f:
                    s.known[f][g] = max(s.known[f][g], s.cnt[g])
            s.kdma[f] = dict(s.dlast)
        s.lastw.clear()
        s.rd_e.clear()
        s.rd_d.clear()

    def finish(s):
        waits = {}
        for ev in s.out_events:
            waits[('D', ev[1])] = max(waits.get(('D', ev[1]), 0), ev[2])
        s.ops['sp'].append((list(waits.items()), None, None))

    def emit(s, block):
        def mk(engname):
            def body(e):
                for waits, fn, ev in s.ops[engname]:
                    for (t, x), v in waits:
                        sem = s.sems[x] if t == 'E' else (s.ccsem if x == 'cc' else s.dsems[x])
                        e.wait_ge(sem, v)
                    if fn is None:
                        continue
                    ins = fn(e)
                    if ev is None:
                        continue
                    if ev[0] == 'E':
                        ins.then_inc(s.sems[engname], 1)
                    elif ev[1] == 'cc':
                        ins.then_inc(s.ccsem, 1)
                    else:
                        ins.then_inc(s.dsems[ev[1]], 16)
            return body
        block.tensor(mk('pe'))
        block.vector(mk('dve'))
        block.scalar(mk('act'))
        block.gpsimd(mk('pool'))
        block.sync(mk('sp'))


class Arena:
    def __init__(s, ar, nwords):
        s.ar = ar
        s.off = 0
        s.n = nwords
        s.peak = 0

    def alloc(s, shape, dt):
        n = 1
        for x in shape:
            n *= x
        words = n if dt == F32 else (n + 1) // 2
        words = (words + 7) // 8 * 8
        a = s.ar[:, s.off:s.off + words]
        s.off += words
        s.peak = max(s.peak, s.off)
        assert s.off <= s.n, ("arena overflow", s.off, s.n)
        if dt == BF16:
            a = a.bitcast(BF16)
        a = a[:, 0:n]
        if len(shape) == 2:
            a = a.rearrange("p (a b) -> p a b", b=shape[1])
        elif len(shape) == 3:
            a = a.rearrange("p (a b c) -> p a b c", b=shape[1], c=shape[2])
        elif len(shape) == 4:
            a = a.rearrange("p (a b c d) -> p a b c d", b=shape[1], c=shape[2], d=shape[3])
        return a


IN_SPECS = [
    ("xp", [1024, 1024]), ("xs", [1024, 1024]), ("xo", [256, 1024]),
    ("ck", [512, 1024]), ("cv", [512, 1024]), ("s0", [2, 4, 64, 64]),
    ("cT", [128, 16]), ("w_in", [1024, 8448]), ("w_ada", [1024, 3072]), ("w_out", [2048, 1024]),
    ("bada_fm", [128, 24]), ("bgate", [1, 1024]), ("normg_fm", [128, 8]), ("fg", [1, 1024]),
    ("lamv", [1, 256]), ("sublng", [1, 128]), ("mu_fm", [128, 64]), ("w0_fm", [128, 20]),
    ("a0_fm", [128, 20]), ("w2", [128, 1280]), ("a2", [128, 1280]), ("kk_fm", [128, 10]),
    ("ka_fm", [128, 10]), ("rk_fm", [128, 10]), ("lnxg", [1, 1280]), ("lnxb", [1, 1280]), ("wrs", [1024, 1024]),
    ("ident", [128, 128]), ("maskNM", [2, 128, 512]), ("maskT", [2, 128, 128]),
    ("bones", [128, 128]), ("hind", [128, 2]), ("selT", [1024, 256]),
    ("ropeall", [1024, 256]), ("ropeown", [256, 256]),
]
OUT_SPECS = [
    ("yp", [1024, 1024]), ("ys", [256, 1024]), ("nk", [1024, 1024]), ("nv", [1024, 1024]),
    ("ns", [4, 2, 16, 64, 64]),
]

ARENA_WORDS = 52480 - 4096
LVL_WORDS = 4096
NHEAD_A = 8
NHP = 8
NSEQ_R = 5
ND = 2
DEBUG = False
A_MODE = 'all'
DUMPS = []
STOP = None


def build():
    nc = bass.Bass("TRN2", target_bir_lowering=False)
    I = {n: nc.dram_tensor(n, sh, F32, kind="ExternalInput").ap() for n, sh in IN_SPECS}
    O = {n: nc.dram_tensor(n, sh, F32, kind="ExternalOutput").ap() for n, sh in OUT_SPECS}
    with ExitStack() as stack:
        ar = stack.enter_context(nc.sbuf_tensor("arena", [128, ARENA_WORDS], F32))
        PS = stack.enter_context(nc.psum_tensor("ps", [128, 4096], F32))
        lvl = stack.enter_context(nc.sbuf_tensor("lvl", [128, LVL_WORDS], LEVEL_DT))
        S = Sched(nc, stack)
        S.ccsem = stack.enter_context(nc.semaphore('ccsem'))
        A = Arena(ar, ARENA_WORDS)
        block = stack.enter_context(nc.Block())
        _program(nc, S, A, PS, I, O, lvl)
        S.finish()
        S.emit(block)
    return nc


def _program(nc, S, A, PS, I, O, lvl):
    def dma(eng, out, in_, r, w, is_out=False):
        S.op(eng, lambda e: e.dma_start(out=out, in_=in_), r, w, dma=True, is_out=is_out)

    def mm(out, lhsT, rhs, start, stop, r, w):
        S.op('pe', lambda e: e.matmul(out, lhsT, rhs, start=start, stop=stop), r, w, noinc=(not stop))

    def tr(out, in_, ident, r, w):
        S.op('pe', lambda e: e.transpose(out, in_, ident), r, w)

    def act(out, in_, func, r, w, bias=None, scale=None, accum=None):
        def f(e):
            kw = {}
            if bias is not None:
                kw['bias'] = bias
            if scale is not None:
                kw['scale'] = scale
            if accum is not None:
                kw['accum_out'] = accum
            return e.activation(out=out, in_=in_, func=func, **kw)
        S.op('act', f, r, w)

    def tt(eng, out, in0, in1, op, r, w):
        S.op(eng, lambda e: e.tensor_tensor(out=out, in0=in0, in1=in1, op=op), r, w)

    def ts(eng, out, in0, s1, s2, op0, op1, r, w):
        if s2 is None:
            S.op(eng, lambda e: e.tensor_scalar(out=out, in0=in0, scalar1=s1, scalar2=None, op0=op0), r, w)
        else:
            S.op(eng, lambda e: e.tensor_scalar(out=out, in0=in0, scalar1=s1, scalar2=s2, op0=op0, op1=op1), r, w)

    def stt(eng, out, in0, sc, in1, op0, op1, r, w):
        S.op(eng, lambda e: e.scalar_tensor_tensor(out=out, in0=in0, scalar=sc, in1=in1, op0=op0, op1=op1), r, w)

    def cp(eng, out, in_, r, w):
        if eng == 'act':
            act(out, in_, AF.Identity, r, w)
        else:
            S.op(eng, lambda e: e.tensor_copy(out=out, in_=in_), r, w)

    def red(out, in_, op, r, w):
        S.op('dve', lambda e: e.tensor_reduce(out=out, in_=in_, axis=AX.X, op=op), r, w)

    def recip(out, in_, r, w):
        S.op('dve', lambda e: e.reciprocal(out=out, in_=in_), r, w)

    def memset(eng, out, val, w):
        S.op(eng, lambda e: e.memset(out, val), (), w)

    def bank(b, c0=0, c1=512):
        return PS[:, b * 512 + c0: b * 512 + c1]

    def bankb(b):
        return PS[:, b * 512:(b + 1) * 512].bitcast(BF16)

    w_in_v = I['w_in'].rearrange("(kc p) n -> p kc n", p=128)

    def dump_all(bufs):
        S.barrier()
        for name, ap in bufs.items():
            sh = list(ap.shape)
            dt = nc.dram_tensor('dbg_' + name, sh, ap.dtype, kind="ExternalOutput").ap()
            DUMPS.append('dbg_' + name)
            dma('sp', dt, ap, (), (), is_out=True)

    identf = A.alloc([128], F32)
    identb = A.alloc([128], BF16)
    onesf = A.alloc([128], F32)
    maskNM = A.alloc([2, 512], BF16)
    maskT = A.alloc([2, 128], BF16)
    bonesb = A.alloc([128], BF16)
    hindb = A.alloc([2], BF16)
    selb = A.alloc([8, 256], BF16)
    cst = A.alloc([4], F32)
    hTp = A.alloc([8, 1024], BF16)
    hTs = A.alloc([8, 1024], BF16)
    hTo = A.alloc([8, 256], BF16)
    mixT = A.alloc([16, 1280], BF16)
    modfm = A.alloc([24, 2], F32)
    scale1 = A.alloc([8, 2], F32)
    neglam = A.alloc([1], F32)
    sgl = A.alloc([128], F32)
    mu = A.alloc([32, 2], F32)
    c0v = A.alloc([32], F32)
    w0v = A.alloc([20], F32)
    a0v = A.alloc([20], F32)
    w0h = A.alloc([20], F32)
    a0h = A.alloc([20], F32)
    kkv = A.alloc([10], F32)
    kav = A.alloc([10], F32)
    omka = A.alloc([10], F32)
    rkv_ = A.alloc([10], F32)
    W2b = A.alloc([1280], BF16)
    A2b = A.alloc([1280], BF16)
    twT = A.alloc([2048], BF16)
    laT = A.alloc([2048], BF16)

    dma('sp', identf, I['ident'], (), ['identf'])
    dma('pool', identb, I['ident'], (), ['identb'])
    dma('pool', maskNM, I['maskNM'].rearrange("d p n -> p d n"), (), ['maskNM'])
    dma('pool', maskT, I['maskT'].rearrange("d p n -> p d n"), (), ['maskT'])
    dma('pool', bonesb, I['bones'], (), ['bonesb'])
    dma('pool', hindb, I['hind'], (), ['hindb'])
    dma('pool', selb, I['selT'].rearrange("(j p) n -> p j n", p=128), (), ['selb'])
    dma('sp', mu, I['mu_fm'].rearrange("p (c j) -> p c j", j=2), (), ['mu'])
    dma('sp', w0v, I['w0_fm'], (), ['w0v'])
    dma('sp', a0v, I['a0_fm'], (), ['a0v'])
    dma('sp', kkv, I['kk_fm'], (), ['kkv'])
    dma('sp', kav, I['ka_fm'], (), ['kav'])
    dma('sp', rkv_, I['rk_fm'], (), ['rkv_'])
    dma('pool', W2b, I['w2'], (), ['W2b'])
    dma('pool', A2b, I['a2'], (), ['A2b'])
    memset('dve', onesf, 1.0, ['onesf'])
    memset('dve', cst[:, 0:1], 1e-12, ['cst'])
    memset('dve', cst[:, 1:2], -0.5, ['cstb'])
    tt('dve', c0v, mu[:, :, 0], mu[:, :, 1], ALU.add, ['mu'], ['c0v'])
    ts('dve', c0v, c0v, -1.0, 1.0, ALU.mult, ALU.add, ['c0v'], ['c0v'])
    ts('dve', omka, kav, -1.0, 1.0, ALU.mult, ALU.add, ['kav'], ['omka'])
    ts('dve', w0h, w0v, 0.5, None, ALU.mult, None, ['w0v'], ['w0h'])
    ts('dve', a0h, a0v, 0.5, None, ALU.mult, None, ['a0v'], ['a0h'])

    scp = A.alloc([8, 2], BF16)
    m0 = A.off
    cT = A.alloc([16], F32)
    sc = A.alloc([8, 2], F32)
    wadaf = [A.alloc([8, 512], F32) for _ in range(4)]
    bada = A.alloc([24], F32)
    normg = A.alloc([8], F32)
    lamt = A.alloc([4, 64], F32)
    lamp = A.alloc([2, 64], F32)
    lams = A.alloc([4], F32)

    dma('sp', cT, I['cT'], (), ['cT'])
    dma('sp', bada, I['bada_fm'], (), ['bada'])
    dma('sp', normg, I['normg_fm'], (), ['normg'])
    dma('sp', lamt.rearrange("p a b -> p (a b)"), I['lamv'][0:1, :].partition_broadcast(128), (), ['lamt'])
    dma('sp', sgl, I['sublng'][0:1, :].partition_broadcast(128), (), ['sgl'])
    wada_v = I['w_ada'].rearrange("(kc p) n -> p kc n", p=128)
    for n in range(4):
        dma('sp', wadaf[n], wada_v[:, :, n * 512:(n + 1) * 512], (), [('wada', n)])
    act(sc, cT.rearrange("p (c v) -> p c v", v=2), AF.Silu, ['cT'], ['sc'])
    cp('dve', scp, sc, ['sc'], ['scp'])
    for fc in range(16):
        for kc in range(8):
            mm(bank(0, fc * 2, fc * 2 + 2), wadaf[fc // 4][:, kc, (fc % 4) * 128:(fc % 4 + 1) * 128], sc[:, kc, :],
               kc == 0, kc == 7, ['sc', ('wada', fc // 4)], [('ps', 0)])
    tt('dve', modfm[:, 0:16, :], bank(0, 0, 32).rearrange("p (a b) -> p a b", b=2),
       bada[:, 0:16].unsqueeze(2).to_broadcast([128, 16, 2]), ALU.add, [('ps', 0), 'bada'], ['modfm'])
    ts('dve', scale1, modfm[:, 8:16, :], 1.0, None, ALU.add, None, ['modfm'], ['scale1'])
    tt('dve', scale1, scale1, normg.unsqueeze(2).to_broadcast([128, 8, 2]), ALU.mult, ['scale1', 'normg'], ['scale1'])
    tt('dve', lamp[:, 0, :], lamt[:, 0, :], lamt[:, 1, :], ALU.mult, ['lamt'], ['lamp'])
    tt('dve', lamp[:, 1, :], lamt[:, 2, :], lamt[:, 3, :], ALU.mult, ['lamt', 'lamp'], ['lamp'])
    red(lams[:, 0:2], lamp, ALU.add, ['lamp'], ['lams'])
    act(lams[:, 2:4], lams[:, 0:2], AF.Exp, ['lams'], ['lams2'])
    lam_init = 0.8 - 0.6 * math.exp(-0.3 * 0)
    tt('dve', neglam, lams[:, 3:4], lams[:, 2:3], ALU.subtract, ['lams2'], ['neglam'])
    ts('dve', neglam, neglam, -lam_init, None, ALU.add, None, ['neglam'], ['neglam'])
    ts('dve', sgl, sgl, 0.5 * (1.0 - lam_init), None, ALU.mult, None, ['sgl'], ['sgl'])

    if STOP == '0':
        return
    xt = [A.alloc([1024], F32) for _ in range(2)]
    xn = [A.alloc([1024], BF16) for _ in range(2)]
    junk = A.alloc([1024], BF16)
    st1 = [A.alloc([4], F32) for _ in range(2)]
    tiles = [('xp', g, hTp, g, 0) for g in range(8)] + [('xs', g, hTs, g, 1) for g in range(8)] + \
            [('xo', g, hTo, g, 1) for g in range(2)]
    for ti, (src, g, hT, tg, v) in enumerate(tiles):
        b = ti % 2
        dma('sp', xt[b], I[src][g * 128:(g + 1) * 128, :], (), [('xt', b)])
        act(junk, xt[b], AF.Square, [('xt', b)], ['junk', ('st1', b)], accum=st1[b][:, 0:1])
        ts('dve', st1[b][:, 1:2], st1[b][:, 0:1], 1.0 / 1024, 1e-6, ALU.mult, ALU.add, [('st1', b)], [('st1b', b)])
        act(st1[b][:, 2:3], st1[b][:, 1:2], AF.Sqrt, [('st1b', b)], [('st1c', b)])
        recip(st1[b][:, 3:4], st1[b][:, 2:3], [('st1c', b)], [('st1d', b)])
        ts('dve', xn[b], xt[b], st1[b][:, 3:4], None, ALU.mult, None, [('xt', b), ('st1d', b)], [('xn', b)])
        pb_ = bankb(3 + b)
        for kc in range(8):
            tr(pb_[:, kc * 128:(kc + 1) * 128], xn[b][:, kc * 128:(kc + 1) * 128], identb,
               [('xn', b), 'identb'], [('ps', 3 + b)])
        for kc in range(8):
            if kc % 2 == 0:
                act(hT[:, kc, tg * 128:(tg + 1) * 128], pb_[:, kc * 128:(kc + 1) * 128], AF.Identity,
                    [('ps', 3 + b), 'scale1', 'modfm'], [(src + 'h', tg)],
                    bias=modfm[:, kc, v:v + 1], scale=scale1[:, kc, v:v + 1])
        for kc in range(8):
            if kc % 2 == 1:
                ts('dve', hT[:, kc, tg * 128:(tg + 1) * 128], pb_[:, kc * 128:(kc + 1) * 128], scale1[:, kc, v:v + 1],
                   modfm[:, kc, v:v + 1], ALU.mult, ALU.add, [('ps', 3 + b), 'scale1', 'modfm'], [(src + 'h', tg)])
    if STOP == '1':
        return
    S.barrier()
    A.off = m0
    if STOP == '1b':
        return

    mA = A.off
    wA = [A.alloc([8, 512], BF16) for _ in range(2)]
    qkb = [A.alloc([256], BF16) for _ in range(2)]
    kvf = [A.alloc([256], F32) for _ in range(2)]
    qT2 = [A.alloc([256], BF16) for _ in range(2)]
    sg2 = [A.alloc([2, 128], F32) for _ in range(2)]
    kTp = [A.alloc([256], BF16) for _ in range(2)]
    vbp = [A.alloc([2, 128], BF16) for _ in range(2)]
    kTs = A.alloc([1536], BF16)
    vbs = A.alloc([12, 128], BF16)
    ckb = A.alloc([4, 128], BF16)
    ropa = A.alloc([8, 256], F32)
    ropo = A.alloc([2, 256], F32)
    xf = [A.alloc([128], F32) for _ in range(2)]
    rt1 = [A.alloc([128], F32) for _ in range(2)]
    rt2 = [A.alloc([128], F32) for _ in range(2)]
    xb16 = [A.alloc([128], BF16) for _ in range(2)]
    pbuf = [A.alloc([1536], BF16) for _ in range(2)]
    pT = [A.alloc([1536], BF16) for _ in range(2)]
    pbufp = [[A.alloc([256], BF16) for _ in range(2)] for _ in range(2)]
    pTp = [[A.alloc([256], BF16) for _ in range(2)] for _ in range(2)]
    ast = [A.alloc([16], F32) for _ in range(4)]
    of = [A.alloc([128], F32) for _ in range(4)]
    o1 = [A.alloc([128], F32) for _ in range(4)]
    on = [A.alloc([128], F32) for _ in range(4)]
    ob = [A.alloc([128], BF16) for _ in range(4)]
    ajunk = A.alloc([128], BF16)

    dma('sp', ropa, I['ropeall'].rearrange("(j p) n -> p j n", p=128), (), ['ropa'])
    dma('sp', ropo, I['ropeown'].rearrange("(j p) n -> p j n", p=128), (), ['ropo'])

    ctr = {'x': 0}

    def rope(src_ps, tab, dst16, rkeys, wkey):
        i = ctr['x'] % 2
        ctr['x'] += 1
        cp('act', xf[i], src_ps, rkeys, [('xf', i)])
        tt('dve', rt1[i], xf[i], tab[:, 0:128], ALU.mult, [('xf', i), 'ropa', 'ropo'], [('rt1', i)])
        xv = xf[i].rearrange("p (g h f) -> p g h f", h=2, f=16)
        sv = tab[:, 128:256].rearrange("p (g h f) -> p g h f", h=2, f=16)
        r2 = rt2[i].rearrange("p (g h f) -> p g h f", h=2, f=16)
        tt('pool', r2[:, :, 0, :], xv[:, :, 1, :], sv[:, :, 0, :], ALU.mult, [('xf', i), 'ropa', 'ropo'], [('rt2', i, 0)])
        tt('pool', r2[:, :, 1, :], xv[:, :, 0, :], sv[:, :, 1, :], ALU.mult, [('xf', i), 'ropa', 'ropo'], [('rt2', i, 1)])
        tt('dve', dst16, rt1[i], rt2[i], ALU.add, [('rt1', i), ('rt2', i, 0), ('rt2', i, 1)], [wkey])

    def drive2(gens):
        gens = [g for g in gens if g is not None]
        while gens:
            for g in list(gens):
                try:
                    next(g)
                except StopIteration:
                    gens.remove(g)

    def attn_unit(par, j, m, kind):
        ai = par * 2 + j
        if kind == 'p':
            ntk, kT_, vb_, kkey, vkey = 2, kTp[par], vbp[par], ('kTp', par), ('vbp', par)
            sb0 = 2 + 2 * j + m
            ob_ = 6 + j
            pbs, pTs, pkey = pbufp[j], pTp[j], ('pp', j)
            tcol = 512
        else:
            ntk, kT_, vb_, kkey, vkey = 12, kTs, vbs, 'kTs', 'vbs'
            sb0 = 2 + 3 * m
            ob_ = sb0 + 2
            pbs, pTs, pkey = pbuf, pT, ('ps_', 0)
            tcol = 0
        Tk = ntk * 128
        qT_ = qT2[par]
        s0c = sb0 * 512
        nsb = (Tk + 511) // 512
        sck = [('ps', sb0 + b_) for b_ in range(nsb)]
        for n0 in range(0, Tk, 512):
            w_ = min(512, Tk - n0)
            mm(PS[:, s0c + n0:s0c + n0 + w_], qT_[64 * m:64 * m + 64, j * 128:(j + 1) * 128],
               kT_[64 * m:64 * m + 64, n0:n0 + w_], True, True, [('qT', par, j), kkey], [('ps', sb0 + n0 // 512)])
        red(ast[ai][:, m:m + 1], PS[:, s0c:s0c + Tk], ALU.max, sck, [('ast', ai, 'mx', m)])
        ts('dve', ast[ai][:, 2 + m:3 + m], ast[ai][:, m:m + 1], -0.125, None, ALU.mult, None,
           [('ast', ai, 'mx', m)], [('ast', ai, 'nb', m)])
        act(pbs[m][:, 0:Tk], PS[:, s0c:s0c + Tk], AF.Exp, sck + [('ast', ai, 'nb', m)],
            [('pbuf', pkey, m), ('ast', ai, 'sum', m)], bias=ast[ai][:, 2 + m:3 + m], scale=0.125,
            accum=ast[ai][:, 4 + m:5 + m])
        yield
        ptv = PS[:, s0c:s0c + 1024].bitcast(BF16)
        for t in range(ntk):
            c_ = tcol + t * 128
            tr(ptv[:, c_:c_ + 128], pbs[m][:, t * 128:(t + 1) * 128], identb,
               [('pbuf', pkey, m), 'identb'], [('ps', sb0 + (c_ // 1024))])
        if ntk <= 2:
            cp('dve', pTs[m][:, 0:Tk], ptv[:, tcol:tcol + Tk], [('ps', sb0)], [('pT', pkey, m, 0)])
            ptk = [('pT', pkey, m, 0)]
        else:
            cp('dve', pTs[m][:, 0:768], ptv[:, 0:768], [('ps', sb0)], [('pT', pkey, m, 0)])
            cp('act', pTs[m][:, 768:1536], ptv[:, 768:1536], [('ps', sb0), ('ps', sb0 + 1)], [('pT', pkey, m, 1)])
            ptk = [('pT', pkey, m, 0), ('pT', pkey, m, 1)]
        yield
        oc = m * 128 if kind == 'p' else 0
        for t in range(ntk):
            mm(bank(ob_, oc, oc + 128), pTs[m][:, t * 128:(t + 1) * 128], vb_[:, t, :],
               t == 0, t == ntk - 1, ptk + [vkey], [('ps', ob_)])
        yield

    def attn_comb(h, par, j, kind, mixcol0):
        ai = par * 2 + j
        if kind == 'p':
            o1src, o2src, k1, k2, ob_ = bank(6 + j, 0, 128), bank(6 + j, 128, 256), ('ps', 6 + j), ('ps', 6 + j), 6 + j
        else:
            o1src, o2src, k1, k2, ob_ = bank(4, 0, 128), bank(7, 0, 128), ('ps', 4), ('ps', 7), 4
        a_ = ast[ai]
        recip(a_[:, 6:8], a_[:, 4:6], [('ast', ai, 'sum', 0), ('ast', ai, 'sum', 1)], [('ast', ai, 'rs')])
        tt('dve', a_[:, 8:9], a_[:, 7:8], neglam, ALU.mult, [('ast', ai, 'rs'), 'neglam'], [('ast', ai, 'c2')])
        act(o1[ai], o1src, AF.Identity, [k1, ('ast', ai, 'rs')], [('o1', ai)], scale=a_[:, 6:7])
        stt('dve', of[ai], o2src, a_[:, 8:9], o1[ai], ALU.mult, ALU.add, [k2, ('ast', ai, 'c2'), ('o1', ai)], [('of', ai)])
        yield
        act(ajunk, of[ai], AF.Square, [('of', ai)], ['ajunk', ('ast', ai, 'ss')], accum=a_[:, 9:10])
        ts('dve', a_[:, 10:11], a_[:, 9:10], 1.0 / 128, 1e-5, ALU.mult, ALU.add, [('ast', ai, 'ss')], [('ast', ai, 'ms')])
        tt('pool', a_[:, 12:13], a_[:, 10:11], cst[:, 1:2], ALU.pow, [('ast', ai, 'ms'), 'cstb'], [('ast', ai, 'rstd')])
        yield
        stt('dve', on[ai], of[ai], a_[:, 12:13], sgl, ALU.mult, ALU.mult, [('of', ai), ('ast', ai, 'rstd'), 'sgl'], [('on', ai)])
        tt('pool', ob[ai], on[ai], sg2[par][:, j, :], ALU.mult, [('on', ai), ('sg', par, j)], [('ob', ai)])
        pso = bankb(ob_)
        tr(pso[:, 512:640], ob[ai], identb, [('ob', ai), 'identb'], [('ps', ob_)])
        cp('act', mixT[:, h, mixcol0 + j * 128: mixcol0 + (j + 1) * 128], pso[:, 512:640], [('ps', ob_)],
           [('mixT', h, (mixcol0 // 128) + j)])
        yield

    def rr(gens):
        gens = list(gens)
        while gens:
            for g in list(gens):
                try:
                    next(g)
                except StopIteration:
                    gens.remove(g)
            yield

    def attn_gen(h, par, kind, mixcol0):
        if kind == 'p':
            for _ in rr([attn_unit(par, j, m, kind) for j in range(2) for m in range(2)]):
                yield
            for _ in rr([attn_comb(h, par, j, kind, mixcol0) for j in range(2)]):
                yield
        else:
            for j in range(2):
                for _ in rr([attn_unit(par, j, m, kind) for m in range(2)]):
                    yield
                for _ in attn_comb(h, par, j, kind, mixcol0):
                    yield

    def proj_gen(h, par, kind, s_):
        wb = wA[h % 2]
        wk = [('wA', h % 2, j4) for j4 in range(4)]
        pst = bankb(1)
        if kind == 'p':
            for j in range(2):
                g = s_ * 2 + j
                bi = g % 2
                for kc in range(8):
                    mm(bank(0), hTp[:, kc, g * 128:(g + 1) * 128], wb[:, kc, :], kc == 0, kc == 7,
                       [('xph', g)] + wk, [('ps', 0)])
                cp('dve', qkb[bi], bank(0, 0, 256), [('ps', 0)], [('qkb', bi)])
                cp('act', kvf[bi], bank(0, 128, 384), [('ps', 0)], [('kvf', bi)])
                dma('sp', O['nk'][g * 128:(g + 1) * 128, h * 128:(h + 1) * 128], kvf[bi][:, 0:128], [('kvf', bi)], (), is_out=True)
                dma('sp', O['nv'][g * 128:(g + 1) * 128, h * 128:(h + 1) * 128], kvf[bi][:, 128:256], [('kvf', bi)], (), is_out=True)
                cp('dve', vbp[par][:, j, :], bank(0, 256, 384), [('ps', 0)], [('vbp', par)])
                act(sg2[par][:, j, :], bank(0, 384, 512), AF.Tanh, [('ps', 0)], [('sg', par, j)], scale=0.5)
                stt('dve', sg2[par][:, j, :], sg2[par][:, j, :], 1.0, bank(0, 384, 512), ALU.add, ALU.mult, [('ps', 0), ('sg', par, j)], [('sg', par, j)])
                yield
                tr(pst[:, 0:128], qkb[bi][:, 0:128], identb, [('qkb', bi), 'identb'], [('ps', 1)])
                tr(pst[:, 128:256], qkb[bi][:, 128:256], identb, [('qkb', bi), 'identb'], [('ps', 1)])
                cp('act', qT2[par][:, j * 128:(j + 1) * 128], pst[:, 0:128], [('ps', 1)], [('qT', par, j)])
                cp('act', kTp[par][:, j * 128:(j + 1) * 128], pst[:, 128:256], [('ps', 1)], [('kTp', par)])
                yield
        else:
            dma('pool', ckb, I['ck'].rearrange("(j p) c -> p j c", p=128)[:, :, h * 128:(h + 1) * 128], (), ['ckb'])
            dma('pool', vbs[:, 0:4, :], I['cv'].rearrange("(j p) c -> p j c", p=128)[:, :, h * 128:(h + 1) * 128], (), ['vbs'])
            for t in range(4):
                tr(pst[:, 512 + t * 128:512 + (t + 1) * 128], ckb[:, t, :], identb, ['ckb', 'identb'], [('ps', 1)])
            cp('dve', kTs[:, 0:512], pst[:, 512:1024], [('ps', 1)], ['kTs'])
            yield
            for j in range(8):
                for kc in range(8):
                    mm(bank(0, 0, 256), hTs[:, kc, j * 128:(j + 1) * 128], wb[:, kc, 128:384], kc == 0, kc == 7,
                       [('xsh', j)] + wk, [('ps', 0)])
                bi = j % 2
                cp('dve', vbs[:, 4 + j, :], bank(0, 128, 256), [('ps', 0)], ['vbs'])
                rope(bank(0, 0, 128), ropa[:, j, :], xb16[bi], [('ps', 0)], ('xb16', bi))
                yield
                tr(pst[:, 0:128], xb16[bi], identb, [('xb16', bi), 'identb'], [('ps', 1)])
                cp('act', kTs[:, 512 + j * 128:512 + (j + 1) * 128], pst[:, 0:128], [('ps', 1)], ['kTs'])
                yield
            for j in range(2):
                for kc in range(8):
                    mm(bank(0), hTo[:, kc, j * 128:(j + 1) * 128], wb[:, kc, :], kc == 0, kc == 7,
                       [('xoh', j)] + wk, [('ps', 0)])
                bi = j % 2
                act(sg2[par][:, j, :], bank(0, 384, 512), AF.Tanh, [('ps', 0)], [('sg', par, j)], scale=0.5)
                stt('dve', sg2[par][:, j, :], sg2[par][:, j, :], 1.0, bank(0, 384, 512), ALU.add, ALU.mult, [('ps', 0), ('sg', par, j)], [('sg', par, j)])
                rope(bank(0, 0, 128), ropo[:, j, :], xb16[bi], [('ps', 0)], ('xb16', bi))
                yield
                tr(pst[:, 128:256], xb16[bi], identb, [('xb16', bi), 'identb'], [('ps', 1)])
                cp('act', qT2[par][:, j * 128:(j + 1) * 128], pst[:, 128:256], [('ps', 1)], [('qT', par, j)])
                yield

    ajobs = []
    for h in range(NHEAD_A):
        for s_ in range(4):
            ajobs.append((h, 'p', s_))
        ajobs.append((h, 's', 0))
    loaded = set()

    def load_w(h):
        if h in loaded or h >= NHEAD_A:
            return
        loaded.add(h)
        for j4, base in enumerate([0, 1024, 2048, 3072]):
            dma('pool', wA[h % 2][:, :, j4 * 128:(j4 + 1) * 128], w_in_v[:, :, base + h * 128: base + (h + 1) * 128], (),
                [('wA', h % 2, j4)])
    if ajobs:
        load_w(0)
        drive2([proj_gen(ajobs[0][0], 0, ajobs[0][1], ajobs[0][2])])
        for n, (h, kind, s_) in enumerate(ajobs):
            par = n % 2
            ag = attn_gen(h, par, kind, s_ * 256 if kind == 'p' else 1024)
            pg = None
            if n + 1 < len(ajobs):
                h2, kind2, s2 = ajobs[n + 1]
                load_w(h2)
                pg = proj_gen(h2, (n + 1) % 2, kind2, s2)
            drive2([ag, pg])
    if STOP == 'A':
        return
    S.barrier()
    A.off = mA

    wR = [A.alloc([8, 512], BF16) for _ in range(2)]
    rkv = A.alloc([3, 1024], F32)
    kk = A.alloc([1024], F32)
    t1 = A.alloc([1024], F32)
    prodb = A.alloc([1024], BF16)
    sqb = prodb
    vb16 = A.alloc([1024], BF16)
    VT = A.alloc([8, 128], BF16)
    sgb = A.alloc([8, 128], F32)
    yacc = A.alloc([8, 128], F32)
    wL = yacc.rearrange("p a b -> p (a b)").bitcast(BF16).rearrange("p (a b) -> p a b", b=256)
    bsum = A.alloc([16], F32)
    lnxg = [A.alloc([128], F32) for _ in range(2)]
    lnxb = [A.alloc([128], F32) for _ in range(2)]
    sgw = A.alloc([256], F32)
    Pp = A.alloc([256], F32)
    csb = A.alloc([256], F32)
    Winv = A.alloc([256], F32)
    av = A.alloc([256], F32)
    tmpa = A.alloc([256], F32)
    tmpb = A.alloc([256], F32)
    BW = A.alloc([256], BF16)
    KW = A.alloc([256], BF16)
    Wb2 = [A.alloc([2, 130], F32) for _ in range(2)]
    Kt2 = [A.alloc([256], BF16) for _ in range(2)]
    Bt2 = [A.alloc([256], BF16) for _ in range(2)]
    ARb2 = [A.alloc([2, 256], BF16) for _ in range(2)]
    BKT2 = [A.alloc([2, 256], BF16) for _ in range(2)]
    ATb2 = [A.alloc([2, 128], BF16) for _ in range(2)]
    NM = [A.alloc([512], BF16) for _ in range(4)]
    lo = [0]

    def lalloc(n):
        a_ = lvl[:, lo[0]:lo[0] + n].bitcast(F32)
        lo[0] += n
        assert lo[0] <= LVL_WORDS
        return a_
    X0 = [lalloc(128) for _ in range(4)]
    X0T = [lalloc(128) for _ in range(4)]
    XX = [[lalloc(256) for _ in range(2)] for _ in range(4)]
    Zb = [[lalloc(128) for _ in range(2)] for _ in range(4)]
    Zh = [A.alloc([128], BF16) for _ in range(4)]
    GT = A.alloc([8, 64], BF16)
    Hs = A.alloc([8, 64], F32)
    Qb = A.alloc([8, 128], BF16)
    Sf = [A.alloc([64], F32) for _ in range(2)]
    Sb = [A.alloc([64], BF16) for _ in range(2)]
    s0raw = A.alloc([128], F32)
    stg = [A.alloc([128], F32) for _ in range(2)]
    gst = A.alloc([8, 16], F32)
    ysq = t1.rearrange("p (a b) -> p a b", b=128)
    ybon = kk.rearrange("p (a b) -> p a b", b=128)
    obR = A.alloc([8, 128], BF16)

    dma('pool', wL, w_in_v[:, :, 4096 + 3072:4096 + 3328], (), ['wL'])
    for q in range(2):
        memset('dve', Wb2[q][:, :, 0:1], 1.0, [('Wbpad0', q)])
        memset('dve', Wb2[q][:, :, 129:130], 1.0, [('Wbpad1', q)])

    T = 1024
    units = [(hTp, 'xph', 0, 'p'), (hTs, 'xsh', 1024, 's')]

    def proj_shift(w_ap, wkeys, hT, hkey, ci, dst, dkey, kind, b0):
        for n0 in range(0, T, 512):
            bnk = b0 + n0 // 512
            hk = [(hkey, n0 // 128 + q) for q in range(4)]
            for kc in range(8):
                mm(bank(bnk), w_ap[:, kc, :], hT[:, kc, n0:n0 + 512], kc == 0, kc == 7, hk + wkeys, [('ps', bnk)])
        for n0 in range(0, T, 512):
            bnk = b0 + n0 // 512
            act(dst[:, n0:n0 + 512], bank(bnk), AF.Identity, [('ps', bnk), 'c0v'], [dkey], scale=c0v[:, ci:ci + 1])
        psv = PS[:, b0 * 512:b0 * 512 + 1024]
        pk = [('ps', b0), ('ps', b0 + 1)]
        blocks = [(0, T)] if kind == 's' else [(q * 256, (q + 1) * 256) for q in range(4)]
        for (s_, e_) in blocks:
            kk_ = [('ps', b0 + s_ // 512)] if (s_ // 512 == (e_ - 1) // 512) else pk
            stt('dve', dst[:, s_ + 1:e_], psv[:, s_:e_ - 1], mu[:, ci, 0:1], dst[:, s_ + 1:e_], ALU.mult, ALU.add,
                kk_ + ['mu', dkey], [dkey])
            stt('dve', dst[:, s_:e_ - 1], psv[:, s_ + 1:e_], mu[:, ci, 1:2], dst[:, s_:e_ - 1], ALU.mult, ALU.add,
                kk_ + ['mu', dkey], [dkey])

    for (hT, hkey, lc0, kind) in units:
        for c in range(2):
            proj_shift(wL[:, :, c * 128:(c + 1) * 128], ['wL'], hT, hkey, 24 + c, t1, 't1', kind, 2 * c)
            if c == 0:
                act(twT[:, lc0:lc0 + T], t1[:, 0:T], AF.Tanh, ['t1'], [('twT', kind)])
            else:
                cp('act', laT[:, lc0:lc0 + T], t1[:, 0:T], ['t1'], [('laT', kind)])
    S.barrier()

    def prep_gen(hp, kind, lc0, d, seg, par):
        Wb, Kt, Bt, ARb, BKT, ATb = Wb2[par], Kt2[par], Bt2[par], ARb2[par], BKT2[par], ATb2[par]
        kT_, rT_ = rkv[:, 1, :], rkv[:, 0, :]
        c0_ = seg * 256
        lc = lc0 + c0_
        mm(bank(0, 0, 256), W2b[64 * d:64 * d + 64, hp * 128:(hp + 1) * 128], twT[64 * d:64 * d + 64, lc:lc + 256],
           True, True, ['W2b', ('twT', kind)], [('ps', 0)])
        act(sgw, bank(0, 0, 256), AF.Tanh, [('ps', 0), 'w0h'], ['sgw'], bias=w0h[:, hp * 2 + d:hp * 2 + d + 1], scale=0.5)
        ts('dve', sgw, sgw, 0.5, 0.5, ALU.mult, ALU.add, ['sgw'], ['sgw'])
        mm(bank(1, 0, 256), A2b[64 * d:64 * d + 64, hp * 128:(hp + 1) * 128], laT[64 * d:64 * d + 64, lc:lc + 256],
           True, True, ['A2b', ('laT', kind)], [('ps', 1)])
        act(av, bank(1, 0, 256), AF.Tanh, [('ps', 1), 'a0h'], ['av'], bias=a0h[:, hp * 2 + d:hp * 2 + d + 1], scale=0.5)
        ts('dve', av, av, 0.5, 0.5, ALU.mult, ALU.add, ['av'], ['av'])
        yield
        for t in range(2):
            S.op('dve', (lambda t=t: (lambda e: e.tensor_tensor_scan(
                out=Pp[:, t * 128:(t + 1) * 128], data0=onesf, data1=sgw[:, t * 128:(t + 1) * 128],
                initial=0.0, op0=ALU.mult, op1=ALU.add)))(), ['sgw', 'onesf'], [('Pp', t)])
        if d == 0:
            cs = Pp
            csk = [('Pp', 0), ('Pp', 1)]
        else:
            for t in range(2):
                stt('dve', csb[:, t * 128:(t + 1) * 128], sgw[:, t * 128:(t + 1) * 128],
                    Pp[:, t * 128 + 127:t * 128 + 128], Pp[:, t * 128:(t + 1) * 128], ALU.add, ALU.subtract,
                    ['sgw', ('Pp', t)], [('csb', t)])
            cs = csb
            csk = [('csb', 0), ('csb', 1)]
        yield
        act(Wb[:, :, 1:129], cs.rearrange("p (a b) -> p a b", b=128), AF.Exp, csk, [('Wb', par)], scale=-DC)
        act(Winv, cs, AF.Exp, csk, ['Winv'], scale=DC)
        ts('dve', tmpa, av, kav[:, hp:hp + 1], omka[:, hp:hp + 1], ALU.mult, ALU.add, ['av', 'kav', 'omka'], ['tmpa'])
        tt('pool', tmpb, kk[:, c0_:c0_ + 256], av, ALU.mult, ['kk', 'av'], ['tmpb'])
        yield
        tt('dve', tmpa, tmpa, kT_[:, c0_:c0_ + 256], ALU.mult, ['tmpa', ('rkv', 1)], ['tmpa'])
        tt('dve', Kt, tmpa, Winv, ALU.mult, ['tmpa', 'Winv'], [('Kt', par)])
        tt('dve', Bt, tmpb, Winv, ALU.mult, ['tmpb', 'Winv'], [('Bt', par)])
        yield
        Wprev = Wb[:, :, 0:128] if d == 0 else Wb[:, :, 2:130]
        stt('dve', ARb[:, :, 0:128], kk[:, c0_:c0_ + 256].rearrange("p (a b) -> p a b", b=128), -1.0, Wprev,
            ALU.mult, ALU.mult, ['kk', ('Wb', par), ('Wbpad0', par), ('Wbpad1', par)], [('ARb', par, 'a')])
        tt('dve', ARb[:, :, 128:256], rT_[:, c0_:c0_ + 256].rearrange("p (a b) -> p a b", b=128), Wb[:, :, 1:129],
           ALU.mult, [('rkv', 0), ('Wb', par)], [('ARb', par, 'r')])
        yield
        pst2 = bankb(2)
        for t in range(2):
            wc = Wb[:, t, 128:129] if d == 0 else Wb[:, t, 1:2]
            ts('dve', BW[:, t * 128:(t + 1) * 128], Bt[:, t * 128:(t + 1) * 128], wc, None, ALU.mult, None,
               [('Bt', par), ('Wb', par)], [('BW', t)])
            ts('dve', KW[:, t * 128:(t + 1) * 128], Kt[:, t * 128:(t + 1) * 128], wc, None, ALU.mult, None,
               [('Kt', par), ('Wb', par)], [('KW', t)])
            yield
        for t in range(2):
            tr(pst2[:, t * 384:t * 384 + 128], BW[:, t * 128:(t + 1) * 128], identb, [('BW', t), 'identb'], [('ps', 2)])
            tr(pst2[:, t * 384 + 128:t * 384 + 256], KW[:, t * 128:(t + 1) * 128], identb, [('KW', t), 'identb'], [('ps', 2)])
            tr(pst2[:, t * 384 + 256:t * 384 + 384], ARb[:, t, 0:128], identb, [('ARb', par, 'a'), 'identb'], [('ps', 2)])
        for t in range(2):
            cp('act', BKT[:, t, :], pst2[:, t * 384:t * 384 + 256], [('ps', 2)], [('BKT', par, t)])
            cp('act', ATb[:, t, :], pst2[:, t * 384 + 256:t * 384 + 384], [('ps', 2)], [('ATb', par, t)])
        yield

    def rest_gen(hp, kind, d, seg, par, state_in):
        Wb, Kt, Bt, ARb, BKT, ATb = Wb2[par], Kt2[par], Bt2[par], ARb2[par], BKT2[par], ATb2[par]
        gt0 = seg * 2
        P = []
        for t in range(2):
            for e in range(2):
                zi = t * 2 + e
                P.append(dict(t=t, e=e, zi=zi, si=zi, pb=64 * e, bM=4 + zi, gt=gt0 + t))
        for p in P:
            t, e, pb, bM, si = p['t'], p['e'], p['pb'], p['bM'], p['si']
            mm(bank(bM, 0, 256), Bt[pb:pb + 64, t * 128:(t + 1) * 128], ARb[pb:pb + 64, t, :], True, True,
               [('Bt', par), ('ARb', par, 'a'), ('ARb', par, 'r')], [('ps', bM)])
            mm(bank(bM, 256, 512), Kt[pb:pb + 64, t * 128:(t + 1) * 128], ARb[pb:pb + 64, t, :], True, True,
               [('Kt', par), ('ARb', par, 'a'), ('ARb', par, 'r')], [('ps', bM)])
        for p in P:
            bM, si = p['bM'], p['si']
            tt('dve', NM[si], bank(bM), maskNM[:, d, :], ALU.mult, [('ps', bM), 'maskNM'], [('NM', si)])
            tt('dve', X0[si].bitcast(LEVEL_DT), bank(bM, 0, 128), maskNM[:, d, 0:128], ALU.mult, [('ps', bM), 'maskNM'], [('X0', si)])
        yield
        for p in P:
            t, e, pb, bM, si, gt = p['t'], p['e'], p['pb'], p['bM'], p['si'], p['gt']
            mm(bank(bM, 0, 128), ARb[pb:pb + 64, t, 0:128], Bt[pb:pb + 64, t * 128:(t + 1) * 128], True, True,
               [('Bt', par), ('ARb', par, 'a')], [('ps', bM)])
            mm(bank(bM, 384, 448), NM[si][:, 256:384], VT[:, gt, e * 64:(e + 1) * 64], True, True,
               [('NM', si), 'VT'], [('ps', bM)])
        for p in P:
            t, e, bM, si, zi = p['t'], p['e'], p['bM'], p['si'], p['zi']
            tt('dve', X0T[si].bitcast(LEVEL_DT), bank(bM, 0, 128), maskT[:, d, :], ALU.mult, [('ps', bM), 'maskT'], [('X0T', si)])
            cp('act', Zb[zi][0].bitcast(LEVEL_DT)[:, 64:128], bank(bM, 384, 448), [('ps', bM)], [('Zb', zi, 0, 'u')])
            cp('pool', Zb[zi][0].bitcast(LEVEL_DT)[:, 0:64], ATb[:, t, e * 64:(e + 1) * 64], [('ATb', par, t)], [('Zb', zi, 0, 'a')])
        yield
        for j in range(7):
            for p in P:
                bM, si, zi = p['bM'], p['si'], p['zi']
                Xj = X0[si] if j == 0 else XX[si][j % 2][:, 0:128]
                XjT = X0T[si] if j == 0 else XX[si][j % 2][:, 128:256]
                xk = [('X0', si), ('X0T', si)] if j == 0 else [('XX', si, j % 2)]
                zc = Zb[zi][j % 2]
                zck = [('Zb', zi, j % 2, 'a'), ('Zb', zi, j % 2, 'u')]
                Xr, XTr, zr = Xj.bitcast(LEVEL_DT), XjT.bitcast(LEVEL_DT), zc.bitcast(LEVEL_DT)
                mm(bank(bM, 0, 128), Xr, zr, True, True, xk + zck, [('ps', bM)])
                if j < 6:
                    mm(bank(bM, 128, 256), XTr, Xr, True, True, xk, [('ps', bM)])
                    mm(bank(bM, 256, 384), Xr, XTr, True, True, xk, [('ps', bM)])
            for p in P:
                bM, si, zi = p['bM'], p['si'], p['zi']
                zc = Zb[zi][j % 2]
                zn = Zb[zi][(j + 1) % 2]
                zck = [('Zb', zi, j % 2, 'a'), ('Zb', zi, j % 2, 'u')]
                znk = [('Zb', zi, (j + 1) % 2, 'a'), ('Zb', zi, (j + 1) % 2, 'u')]
                tt('dve', zn.bitcast(LEVEL_DT), bank(bM, 0, 128), zc, ALU.add, [('ps', bM)] + zck, znk)
                if j < 6:
                    cp('act', XX[si][(j + 1) % 2].bitcast(LEVEL_DT), bank(bM, 128, 384), [('ps', bM)], [('XX', si, (j + 1) % 2)])
            yield
        for p in P:
            si, zi = p['si'], p['zi']
            cp('pool', Zh[si], Zb[zi][1], [('Zb', zi, 1, 'a'), ('Zb', zi, 1, 'u')], [('Zh', si)])
        for p in P:
            t, e, pb, bM, si, gt = p['t'], p['e'], p['pb'], p['bM'], p['si'], p['gt']
            Z = Zh[si]
            zk = [('Zh', si)]
            mm(bank(bM, 448, 512)[pb:pb + 64, :], Z[:, 0:64], BKT[:, t, e * 64:(e + 1) * 64], True, True,
               zk + [('BKT', par, t)], [('ps', bM)])
            mm(bank(bM, 384, 448)[pb:pb + 64, :], BKT[:, t, e * 64:(e + 1) * 64], Z[:, 64:128], True, False,
               zk + [('BKT', par, t)], [('ps', bM)])
            mm(bank(bM, 384, 448)[pb:pb + 64, :], BKT[:, t, 128 + e * 64:128 + (e + 1) * 64],
               VT[:, gt, e * 64:(e + 1) * 64], False, True, ['VT', ('BKT', par, t)], [('ps', bM)])
            mm(bank(bM, 0, 128)[pb:pb + 64, :], Z[:, 0:64], NM[si][:, 128:256], True, True,
               zk + [('NM', si)], [('ps', bM)])
            mm(bank(bM, 128, 192), NM[si][:, 128:256], Z[:, 64:128], True, False, zk + [('NM', si)], [('ps', bM)])
            mm(bank(bM, 128, 192), NM[si][:, 384:512], VT[:, gt, e * 64:(e + 1) * 64], False, True,
               ['VT', ('NM', si)], [('ps', bM)])
        yield
        for p in P:
            t, e, pb, bM, si, gt = p['t'], p['e'], p['pb'], p['bM'], p['si'], p['gt']
            wc = Wb[pb:pb + 64, t, 128:129] if d == 0 else Wb[pb:pb + 64, t, 1:2]
            stt('dve', GT[pb:pb + 64, gt, :], identf[pb:pb + 64, pb:pb + 64], wc, bank(bM, 448, 512)[pb:pb + 64, :],
                ALU.mult, ALU.add, [('ps', bM), 'identf', ('Wb', par)], [('GT', gt, e)])
            tt('dve', Qb[pb:pb + 64, gt, :], bank(bM, 0, 128)[pb:pb + 64, :], ARb[pb:pb + 64, t, 128:256], ALU.add,
               [('ps', bM), ('ARb', par, 'r')], [('Qb', gt, e)])
            cp('act', Hs[pb:pb + 64, gt, :], bank(bM, 384, 448)[pb:pb + 64, :], [('ps', bM)], [('Hs', gt, e)])
            if d == 0:
                cp('act', yacc[:, gt, e * 64:(e + 1) * 64], bank(bM, 128, 192), [('ps', bM)], [('yacc', gt, e)])
            else:
                tt('dve', yacc[:, gt, e * 64:(e + 1) * 64], bank(bM, 128, 192), yacc[:, gt, e * 64:(e + 1) * 64],
                   ALU.add, [('ps', bM), ('yacc', gt, e)], [('yacc', gt, e)])
        yield

    def chain_gen(hp, kind, d, seg, par, state_in):
        gt0 = seg * 2
        have_state = state_in
        order = [0, 1] if d == 0 else [1, 0]
        for t in order:
            gt = gt0 + t
            for e in range(2):
                pb = 64 * e
                if have_state:
                    mm(bank(3, e * 64, e * 64 + 64), Qb[pb:pb + 64, gt, :], Sb[d][pb:pb + 64, :], True, True,
                       [('Qb', gt, e), ('Sb', d, e)], [('ps', 3)])
                    mm(bank(3, 128 + e * 64, 192 + e * 64)[pb:pb + 64, :], GT[pb:pb + 64, gt, :], Sb[d][pb:pb + 64, :],
                       True, True, [('GT', gt, e), ('Sb', d, e)], [('ps', 3)])
            for e in range(2):
                pb = 64 * e
                if have_state:
                    tt('dve', yacc[:, gt, e * 64:(e + 1) * 64], bank(3, e * 64, e * 64 + 64),
                       yacc[:, gt, e * 64:(e + 1) * 64], ALU.add, [('ps', 3), ('yacc', gt, e)], [('yacc', gt, e)])
                    tt('dve', Sf[d][pb:pb + 64, :], bank(3, 128 + e * 64, 192 + e * 64)[pb:pb + 64, :], Hs[pb:pb + 64, gt, :],
                       ALU.add, [('ps', 3), ('Hs', gt, e)], [('Sf', d, e)])
                else:
                    cp('dve', Sf[d][pb:pb + 64, :], Hs[pb:pb + 64, gt, :], [('Hs', gt, e)], [('Sf', d, e)])
                cp('act', Sb[d][pb:pb + 64, :], Sf[d][pb:pb + 64, :], [('Sf', d, e)], [('Sb', d, e)])
            have_state = True
            yield
        if kind == 'p':
            q = (seg + d) % 2
            tr(bank(3, 256, 384)[0:64, :], Sf[d], identf, [('Sf', d, 0), ('Sf', d, 1), 'identf'], [('ps', 3)])
            cp('act', stg[q][0:64, :], bank(3, 256, 384)[0:64, :], [('ps', 3)], [('stg', q)])
            dma('sp', O['ns'][seg, d, 2 * hp:2 * hp + 2, :, :].rearrange("e v k -> v e k"),
                stg[q][0:64, :].rearrange("p (e k) -> p e k", e=2), [('stg', q)], (), is_out=True)
            yield

    def drive(a, b):
        gens = [g for g in (a, b) if g is not None]
        while gens:
            for g in list(gens):
                try:
                    next(g)
                except StopIteration:
                    gens.remove(g)

    jobno = [0]
    pending_chain = [None]

    def drive3(gens):
        gens = [g for g in gens if g is not None]
        while gens:
            for g in list(gens):
                try:
                    next(g)
                except StopIteration:
                    gens.remove(g)
    ag_in = nc.dram_tensor("ag_in", [1024, 256], BF16)
    ag_out = nc.dram_tensor("ag_out", [4096, 256], BF16)
    wrs_v = I['wrs'].rearrange("(kc p) n -> p kc n", p=128)
    unit_of = {'p': (hTp, 'xph', 0), 's': (hTs, 'xsh', 1024)}
    tasks = ([('s', 8), ('s', 9)] if NSEQ_R >= 2 else []) + [('p', h_) for h_ in range(NHP)]

    def ci_of(a_, hp):
        return a_ * 8 + hp if hp < 8 else 26 + a_ * 2 + (hp - 8)
    for ti_, (kind, hp) in enumerate(tasks):
        tp = ti_ % 2
        wr = wR[tp]
        for a_, base in enumerate([4096, 4096 + 1024, 4096 + 2048, 4096 + 3328]):
            if hp < 8:
                wsrc = w_in_v[:, :, base + hp * 128: base + (hp + 1) * 128]
            else:
                wsrc = wrs_v[:, :, (hp - 8) * 512 + a_ * 128:(hp - 8) * 512 + (a_ + 1) * 128]
            dma('pool', wr[:, :, a_ * 128:(a_ + 1) * 128], wsrc, (), [('wR', tp, a_)])
        dma('sp', lnxg[tp], I['lnxg'][0:1, hp * 128:(hp + 1) * 128].partition_broadcast(128), (), [('lnxg', tp)])
        dma('sp', lnxb[tp], I['lnxb'][0:1, hp * 128:(hp + 1) * 128].partition_broadcast(128), (), [('lnxb', tp)])
        if ti_ == 2 and tasks[0][0] == 's':
            S.op('pool', lambda e: e.collective_compute("AllGather", ALU.bypass, replica_groups=[[0, 1, 2, 3], [4, 5, 6, 7]],
                                                        ins=[ag_in.ap().opt()], outs=[ag_out.ap().opt()]),
                 [('ag_in', 0), ('ag_in', 1)], ['ag_out'], dma=True, cc=True)
        for (hT, hkey, lc0) in [unit_of[kind]]:
            nt = 8
            for a_ in range(3):
                proj_shift(wr[:, :, a_ * 128:(a_ + 1) * 128], [('wR', tp, a_)], hT, hkey, ci_of(a_, hp), rkv[:, a_, :],
                           ('rkv', a_), kind, 2 * a_)
            rT_, kT_, vT_ = rkv[:, 0, :], rkv[:, 1, :], rkv[:, 2, :]
            for t in range(nt):
                bnk = 6 + (t // 4) % 2
                for kc in range(8):
                    mm(bank(bnk, (t % 4) * 128, (t % 4 + 1) * 128), hT[:, kc, t * 128:(t + 1) * 128],
                       wr[:, kc, 384:512], kc == 0, kc == 7, [(hkey, t), ('wR', tp, 3)], [('ps', bnk)])
                if t % 4 == 3:
                    act(sgb[:, t - 3:t + 1, :], bank(bnk).rearrange("p (a b) -> p a b", b=128), AF.Silu, [('ps', bnk)],
                        [('sgb', q) for q in range(t - 3, t + 1)])
            ts('dve', t1[:, 0:T], kT_[:, 0:T], kkv[:, hp:hp + 1], None, ALU.mult, None, [('rkv', 1), 'kkv'], ['t1'])
            act(sqb[:, 0:T], t1[:, 0:T], AF.Square, ['t1'], ['prodb'])
            for n0 in range(0, T, 512):
                bnk = (n0 // 512) % 2
                mm(bank(bnk), bonesb, sqb[:, n0:n0 + 512], True, True, ['bonesb', 'prodb'], [('ps', bnk)])
                act(kk[:, n0:n0 + 512], bank(bnk), AF.Sqrt, [('ps', bnk), 'cst'], ['kk'], bias=cst[:, 0:1])
            recip(kk[:, 0:T], kk[:, 0:T], ['kk'], ['kk'])
            tt('dve', kk[:, 0:T], kk[:, 0:T], t1[:, 0:T], ALU.mult, ['kk', 't1'], ['kk'])
            stt('dve', prodb[:, 0:T], rT_[:, 0:T], rkv_[:, hp:hp + 1], kT_[:, 0:T], ALU.mult, ALU.mult,
                [('rkv', 0), ('rkv', 1), 'rkv_'], ['prodb'])
            for t in range(nt):
                mm(bank(1, t * 2, t * 2 + 2), prodb[:, t * 128:(t + 1) * 128], hindb, True, True, ['prodb', 'hindb'], [('ps', 1)])
            cp('act', bsum[:, 0:nt * 2], bank(1, 0, nt * 2), [('ps', 1)], ['bsum'])
            cp('act', vb16[:, 0:T], vT_[:, 0:T], [('rkv', 2)], ['vb16'])
            pst2 = bankb(2)
            for t in range(nt):
                tr(pst2[:, t * 128:(t + 1) * 128], vb16[:, t * 128:(t + 1) * 128], identb, ['vb16', 'identb'], [('ps', 2)])
            cp('dve', VT[:, 0:nt, :], pst2[:, 0:nt * 128].rearrange("p (a b) -> p a b", b=128), [('ps', 2)], ['VT'])
            jobs = []
            for d in range(ND):
                segs = list(range(4)) if d == 0 else list(range(3, -1, -1))
                for i_, seg in enumerate(segs):
                    st_in = (kind == 's')
                    jobs.append((d, seg, st_in, (kind == 's' and i_ == 0)))
            pars = []
            for _ in jobs:
                pars.append(jobno[0] % 2)
                jobno[0] += 1
            pg = prep_gen(hp, kind, lc0, jobs[0][0], jobs[0][1], pars[0])
            drive(pg, None)
            for n, (d, seg, st_in, load_s0) in enumerate(jobs):
                if load_s0:
                    dma('sp', s0raw[0:64, :].rearrange("p (e k) -> p e k", e=2),
                        I['s0'][d, 2 * (hp - 8):2 * (hp - 8) + 2, :, :].rearrange("e v k -> v e k"), (), ['s0raw'])
                    tr(bank(3, 0, 64), s0raw[0:64, :], identf[0:64, 0:64], ['s0raw', 'identf'], [('ps', 3)])
                    for e in range(2):
                        pb = 64 * e
                        cp('dve', Sf[d][pb:pb + 64, :], bank(3, 0, 64)[pb:pb + 64, :], [('ps', 3)], [('Sf', d, e)])
                        cp('act', Sb[d][pb:pb + 64, :], bank(3, 0, 64)[pb:pb + 64, :], [('ps', 3)], [('Sb', d, e)])
                rg = rest_gen(hp, kind, d, seg, pars[n], st_in)
                ng = None
                if n + 1 < len(jobs):
                    ng = prep_gen(hp, kind, lc0, jobs[n + 1][0], jobs[n + 1][1], pars[n + 1])
                drive3([rg, pending_chain[0], ng])
                pending_chain[0] = chain_gen(hp, kind, d, seg, pars[n], st_in)
            drive3([pending_chain[0]])
            pending_chain[0] = None
            n2 = nt * 2
            yk = [('yacc', t, e) for t in range(nt) for e in range(2)]
            yv = yacc[:, 0:nt, :].rearrange("p a (e f) -> p (a e) f", e=2)
            red(gst[:, 0, 0:n2], yv, ALU.add, yk, [('gst', 0)])
            act(ysq[:, 0:nt, :], yacc[:, 0:nt, :], AF.Square, yk, ['t1'])
            red(gst[:, 1, 0:n2], ysq[:, 0:nt, :].rearrange("p a (e f) -> p (a e) f", e=2), ALU.add, ['t1'], [('gst', 1)])
            ts('dve', gst[:, 2, 0:n2], gst[:, 0, 0:n2], 1.0 / 64, None, ALU.mult, None, [('gst', 0)], [('gst', 2)])
            tt('dve', gst[:, 3, 0:n2], gst[:, 2, 0:n2], gst[:, 2, 0:n2], ALU.mult, [('gst', 2)], [('gst', 3)])
            stt('dve', gst[:, 4, 0:n2], gst[:, 1, 0:n2], 1.0 / 64, gst[:, 3, 0:n2], ALU.mult, ALU.subtract,
                [('gst', 1), ('gst', 3)], [('gst', 4)])
            ts('dve', gst[:, 4, 0:n2], gst[:, 4, 0:n2], 64e-5, None, ALU.add, None, [('gst', 4)], [('gst', 4)])
            act(gst[:, 5, 0:n2], gst[:, 4, 0:n2], AF.Sqrt, [('gst', 4)], [('gst', 5)])
            recip(gst[:, 6, 0:n2], gst[:, 5, 0:n2], [('gst', 5)], [('gst', 6)])
            ysv = ysq[:, 0:nt, :].rearrange("p a (e f) -> p (a e) f", e=2)
            tt('dve', ysv, yv, gst[:, 2, 0:n2].unsqueeze(2).to_broadcast([128, n2, 64]), ALU.subtract, yk + [('gst', 2)], ['t1'])
            tt('dve', ysv, ysv, gst[:, 6, 0:n2].unsqueeze(2).to_broadcast([128, n2, 64]), ALU.mult, ['t1', ('gst', 6)], ['t1'])
            tt('dve', ysq[:, 0:nt, :], ysq[:, 0:nt, :], lnxg[tp].unsqueeze(1).to_broadcast([128, nt, 128]),
               ALU.mult, ['t1', ('lnxg', tp)], ['t1'])
            tt('dve', ysq[:, 0:nt, :], ysq[:, 0:nt, :], lnxb[tp].unsqueeze(1).to_broadcast([128, nt, 128]),
               ALU.add, ['t1', ('lnxb', tp)], ['t1'])
            tt('dve', ybon[:, 0:nt, :].rearrange("p a (e f) -> p (a e) f", e=2),
               VT[:, 0:nt, :].rearrange("p a (e f) -> p (a e) f", e=2),
               bsum[:, 0:n2].unsqueeze(2).to_broadcast([128, n2, 64]), ALU.mult, ['VT', 'bsum'], ['kk'])
            tt('dve', ysq[:, 0:nt, :], ysq[:, 0:nt, :], ybon[:, 0:nt, :], ALU.add, ['t1', 'kk'], ['t1'])
            tt('dve', obR[:, 0:nt, :], ysq[:, 0:nt, :], sgb[:, 0:nt, :], ALU.mult, ['t1'] + [('sgb', t) for t in range(nt)], ['obR'])
            if kind == 'p':
                pst2 = bankb(2)
                for t in range(nt):
                    tr(pst2[:, t * 128:(t + 1) * 128], obR[:, t, :], identb, ['obR', 'identb'], [('ps', 2)])
                cp('act', mixT[:, 8 + hp, 0:1024], pst2[:, 0:1024], [('ps', 2)], [('mixT', 8 + hp, g) for g in range(8)])
            else:
                i_ = hp - 8
                dma('sp', ag_in.ap().rearrange("(t p) (i f) -> p t i f", p=128, i=2)[:, :, i_, :], obR[:, 0:8, :], ['obR'],
                    [('ag_in', i_)])
    if DEBUG:
        dump_all(dict(yacc=yacc, Sf0=Sf[0], GT=GT, Hs=Hs))
    if STOP == 'R':
        return
    S.barrier()
    A.off = mA

    wout = A.alloc([16, 1024], BF16)
    fgbc = A.alloc([1024], F32)
    gatebc = A.alloc([2, 1024], F32)
    bgbc = A.alloc([1024], F32)
    scbc = A.alloc([8, 2, 128], BF16)
    wadg = A.alloc([8, 512], BF16)
    wout_v = I['w_out'].rearrange("(c p) n -> p c n", p=128)
    wada_v = I['w_ada'].rearrange("(kc p) n -> p kc n", p=128)
    dma('pool', wadg, wada_v[:, :, 2048:2560], (), [('wadg', 0)])
    for c4 in range(4):
        dma('pool', wout[:, c4 * 4:(c4 + 1) * 4, :], wout_v[:, c4 * 4:(c4 + 1) * 4, :], (), [('wout', c4)])
    dma('sp', fgbc, I['fg'][0:1, :].partition_broadcast(128), (), ['fgbc'])
    dma('sp', bgbc, I['bgate'][0:1, :].partition_broadcast(128), (), ['bgbc'])
    mO = A.off
    Gt = A.alloc([32, 256], BF16)
    dma('sp', Gt, ag_out.ap().rearrange("(rt p) f -> p rt f", p=128), ['ag_out'], ['Gt'])
    cp('dve', scbc, scp.unsqueeze(3).to_broadcast([128, 8, 2, 128]), ['scp'], ['scbc'])
    for n in range(2):
        if n == 1:
            dma('pool', wadg, wada_v[:, :, 2560:3072], [('wadg', 0)], [('wadg', 0)])
        for v in range(2):
            for kc in range(8):
                mm(bank(2 + v), scbc[:, kc, v, :], wadg[:, kc, :], kc == 0, kc == 7, ['scbc', ('wadg', 0)], [('ps', 2 + v)])
            tt('dve', gatebc[:, v, n * 512:(n + 1) * 512], bank(2 + v), bgbc[:, n * 512:(n + 1) * 512], ALU.add,
               [('ps', 2 + v), 'bgbc'], [('gatebc', v, n)])
    for r_ in range(4):
        for i_ in range(2):
            hq = 2 * r_ + i_
            bq = 4 + (hq % 4)
            for t in range(8):
                mm(bank(bq, 0, 256), Gt[:, r_ * 8 + t, i_ * 128:(i_ + 1) * 128], selb[:, t, :], t == 0, t == 7,
                   ['Gt', 'selb'], [('ps', bq)])
            cp('act', mixT[:, 8 + hq, 1024:1280], bank(bq, 0, 256), [('ps', bq)], [('mixT', 8 + hq, 8), ('mixT', 8 + hq, 9)])
    S.barrier()
    A.off = mO
    xr = [A.alloc([1024], F32) for _ in range(2)]
    yv_ = [A.alloc([1024], F32) for _ in range(2)]
    ojunk = A.alloc([1024], BF16)
    ost = [A.alloc([4], F32) for _ in range(2)]
    otiles = [('xp', g, 'yp', 0) for g in range(8)] + [('xo', g, 'ys', 1) for g in range(2)]
    for ti, (src, g, dst, v) in enumerate(otiles):
        b = ti % 2
        mg = g if src == 'xp' else 8 + g
        dma('sp', xr[b], I[src][g * 128:(g + 1) * 128, :], (), [('xr', b)])
        for n in range(2):
            for c in range(16):
                mm(bank(n), mixT[:, c, mg * 128:(mg + 1) * 128], wout[:, c, n * 512:(n + 1) * 512], c == 0, c == 15,
                   [('mixT', c, mg), ('wout', c // 4)], [('ps', n)])
            tt('dve', yv_[b][:, n * 512:(n + 1) * 512], bank(n), gatebc[:, v, n * 512:(n + 1) * 512], ALU.mult,
               [('ps', n), ('gatebc', v, n)], [('yv', b, n)])
            tt('dve', yv_[b][:, n * 512:(n + 1) * 512], yv_[b][:, n * 512:(n + 1) * 512], xr[b][:, n * 512:(n + 1) * 512], ALU.add,
               [('yv', b, n), ('xr', b)], [('yv', b, n)])
        act(ojunk, yv_[b], AF.Square, [('yv', b, 0), ('yv', b, 1)], ['ojunk', ('ost', b)], accum=ost[b][:, 0:1])
        ts('dve', ost[b][:, 1:2], ost[b][:, 0:1], 1.0 / 1024, 1e-6, ALU.mult, ALU.add, [('ost', b)], [('ostb', b)])
        act(ost[b][:, 2:3], ost[b][:, 1:2], AF.Sqrt, [('ostb', b)], [('ostc', b)])
        recip(ost[b][:, 3:4], ost[b][:, 2:3], [('ostc', b)], [('ostd', b)])
        stt('dve', xr[b], yv_[b], ost[b][:, 3:4], fgbc, ALU.mult, ALU.mult, [('yv', b, 0), ('yv', b, 1), ('ostd', b), 'fgbc', ('xr', b)], [('xr', b)])
        dma('sp', O[dst][g * 128:(g + 1) * 128, :], xr[b], [('xr', b)], (), is_out=True)


_NC = None


def _rope_tab(pos_rows, pos_cols):
    n_freq = 16
    inv = (10000.0 ** (-np.arange(n_freq, dtype=np.float32) / n_freq)).astype(np.float32)
    T = len(pos_rows)
    tab = np.zeros((T, 256), np.float32)
    for s_, pos in enumerate([pos_rows, pos_cols]):
        ang = pos.astype(np.float32)[:, None] * inv[None, :]
        c, sn = np.cos(ang).astype(np.float32), np.sin(ang).astype(np.float32)
        for m in range(2):
            for hf in range(2):
                o = m * 64 + s_ * 32 + hf * 16
                tab[:, o:o + 16] = c
                tab[:, 128 + o:128 + o + 16] = -sn if hf == 0 else sn
    return tab


def kernel(x_prompt, x_sample, cache_k, cache_v, state_rwkv, c, c_ctx, norm_g, w_ada, b_ada,
           w_in, lam_q1, lam_k1, lam_q2, lam_k2, subln_g, shift_mu, decay_w0, decay_w2,
           iclr_a0, iclr_a2, k_k, k_a, r_k, lnx_g, lnx_b, w_out, final_g):
    global _NC
    f = lambda a: np.ascontiguousarray(np.asarray(a, dtype=np.float32))
    x_prompt, x_sample, cache_k, cache_v, state_rwkv = map(f, (x_prompt, x_sample, cache_k, cache_v, state_rwkv))
    c, c_ctx = f(c), f(c_ctx)
    if _NC is None:
        _NC = build()
    nc = _NC

    def fm(v, nch):
        return np.ascontiguousarray(f(v).reshape(nch, 128).T)

    i = np.arange(128)
    su = (i[:, None] < i[None, :]).astype(np.float32)
    ui = (i[:, None] <= i[None, :]).astype(np.float32)
    sl = (i[:, None] > i[None, :]).astype(np.float32)
    li = (i[:, None] >= i[None, :]).astype(np.float32)
    maskNM = np.stack([np.concatenate([su, ui, su, ui], 1), np.concatenate([sl, li, sl, li], 1)])
    maskT = np.stack([sl, su])
    bones = np.kron(np.eye(2, dtype=np.float32), np.ones((64, 64), np.float32))
    hind = np.kron(np.eye(2, dtype=np.float32), np.ones((64, 1), np.float32))
    tok = np.arange(1024)
    ropeall = _rope_tab(tok // 64, tok % 64)
    W_in = f(w_in)[0]
    smu, sw0, sa0 = f(shift_mu)[0], f(decay_w0)[0], f(iclr_a0)[0]
    sw2, sa2 = f(decay_w2)[0].reshape(128, 1024), f(iclr_a2)[0].reshape(128, 1024)
    skk, ska, srk = f(k_k)[0], f(k_a)[0], f(r_k)[0].reshape(-1)
    slg, slb = f(lnx_g)[0], f(lnx_b)[0]
    shared = {
        "w_in": W_in, "w_ada": f(w_ada)[0], "w_out": f(w_out)[0],
        "bada_fm": fm(f(b_ada)[0], 24), "bgate": f(b_ada)[0:1, 2048:3072], "normg_fm": fm(f(norm_g)[0], 8),
        "fg": f(final_g)[None, :],
        "lamv": np.concatenate([f(lam_q1)[0], f(lam_k1)[0], f(lam_q2)[0], f(lam_k2)[0]])[None, :],
        "sublng": f(subln_g)[0:1],
        "ident": np.eye(128, dtype=np.float32), "maskNM": maskNM, "maskT": maskT, "bones": bones, "hind": hind,
        "ropeall": ropeall,
    }

    def cols(v, hp):
        return v[hp * 128:(hp + 1) * 128]
    in_maps = []
    for core in range(8):
        b, q = core // 4, core % 4
        hps = list(range(8)) + [2 * q, 2 * q + 1]
        sel = np.zeros((1024, 256), np.float32)
        sel[q * 256 + np.arange(256), np.arange(256)] = 1.0
        cT = np.stack([fm(c_ctx, 8), fm(c[b], 8)], -1).reshape(128, 16)
        chunks = [smu[:, ci * 128:(ci + 1) * 128] for ci in range(26)]
        for a_ in range(3):
            for i_ in range(2):
                ci = a_ * 8 + hps[8 + i_]
                chunks.append(smu[:, ci * 128:(ci + 1) * 128])
        mu_fm = np.stack([np.stack([ch[0], ch[1]], -1) for ch in chunks], 1).reshape(128, 64)
        w0_fm = np.stack([np.stack([cols(sw0[0], h_), cols(sw0[1], h_)], -1) for h_ in hps], 1).reshape(128, 20)
        a0_fm = np.stack([np.stack([cols(sa0[0], h_), cols(sa0[1], h_)], -1) for h_ in hps], 1).reshape(128, 20)
        ext = lambda v: np.stack([cols(v, h_) for h_ in hps], 1)
        extc = lambda m: np.concatenate([m[:, h_ * 128:(h_ + 1) * 128] for h_ in hps], 1)
        wrs = np.concatenate([W_in[:, base + h_ * 128: base + (h_ + 1) * 128]
                              for h_ in hps[8:] for base in (4096, 4096 + 1024, 4096 + 2048, 4096 + 3328)], 1)
        m = dict(shared)
        m.update({
            "xp": x_prompt[core * 4:(core + 1) * 4].reshape(1024, 1024),
            "xs": x_sample[b], "xo": x_sample[b, q * 256:(q + 1) * 256],
            "ck": cache_k[b, 0].reshape(512, 1024), "cv": cache_v[b, 0].reshape(512, 1024),
            "s0": state_rwkv[b, 0][:, 4 * q:4 * q + 4], "cT": np.ascontiguousarray(cT), "selT": sel,
            "ropeown": np.ascontiguousarray(ropeall[q * 256:(q + 1) * 256]),
            "mu_fm": mu_fm, "w0_fm": w0_fm, "a0_fm": a0_fm, "w2": extc(sw2), "a2": extc(sa2),
            "kk_fm": ext(skk), "ka_fm": ext(ska), "rk_fm": ext(srk),
            "lnxg": extc(slg[None, :]), "lnxb": extc(slb[None, :]), "wrs": wrs,
        })
        in_maps.append({k: np.ascontiguousarray(v, dtype=np.float32) for k, v in m.items()})
    res = run_bass_kernel_spmd(nc, in_maps, core_ids=list(range(8)))
    R = res.results
    y_prompt = np.concatenate([R[i]["yp"].reshape(4, 256, 1024) for i in range(8)], 0)
    y_sample = np.stack([np.concatenate([R[b * 4 + q]["ys"] for q in range(4)], 0) for b in range(2)], 0)
    new_k = np.concatenate([R[i]["nk"].reshape(4, 1, 256, 8, 2, 64) for i in range(8)], 0)
    new_v = np.concatenate([R[i]["nv"].reshape(4, 1, 256, 8, 128) for i in range(8)], 0)
    new_s = np.concatenate([R[i]["ns"].reshape(4, 1, 2, 16, 64, 64) for i in range(8)], 0)
    return (y_prompt.astype(np.float32), y_sample.astype(np.float32), new_k.astype(np.float32),
            new_v.astype(np.float32), new_s.astype(np.float32))
```

```python
import math
from contextlib import ExitStack
import numpy as np
import concourse.bass as bass
import concourse.mybir as mybir
from concourse.bass_utils import run_bass_kernel_spmd

F32 = mybir.dt.float32
F32R = mybir.dt.float32r
LEVEL_DT = F32
BF16 = mybir.dt.bfloat16
AF = mybir.ActivationFunctionType
ALU = mybir.AluOpType
AX = mybir.AxisListType

ENG = ['pe', 'dve', 'act', 'pool', 'sp']
NDMA = 40
DC = math.exp(-0.5)


class Sched:
    def __init__(s, nc, stack):
        s.nc = nc
        s.ops = {e: [] for e in ENG}
        s.cnt = {e: 0 for e in ENG}
        s.known = {e: {f: 0 for f in ENG} for e in ENG}
        s.snap = {e: [] for e in ENG}
        s.kdma = {e: {} for e in ENG}
        s.lastw = {}
        s.rd_e = {}
        s.rd_d = {}
        s.sems = {e: stack.enter_context(nc.semaphore('c_' + e)) for e in ENG}
        s.dsems = [stack.enter_context(nc.semaphore('d%d' % i)) for i in range(2 * NDMA)]
        s.dcnt = {'sp': 0, 'pool': 0, 'act': 0}
        s.dcount = 0
        s.dlast = {}
        s.out_events = []

    def op(s, eng, fn, r=(), w=(), dma=False, is_out=False, noinc=False, cc=False):
        r = [(k[0], k[1]) if (isinstance(k, tuple) and k[0] == 'ps') else k for k in r]
        w = [(k[0], k[1]) if (isinstance(k, tuple) and k[0] == 'ps') else k for k in w]
        psr = [k for k in r if isinstance(k, tuple) and k[0] == 'ps']
        if psr:
            r = [k for k in r if not (isinstance(k, tuple) and k[0] == 'ps')]
            w = list(w) + [k for k in psr if k not in w]
        deps = []
        for k in r:
            ev = s.lastw.get(k)
            if ev is not None:
                deps.append((ev, True))
        for k in w:
            ev = s.lastw.get(k)
            if ev is not None:
                deps.append((ev, False))
            for f, c in s.rd_e.get(k, {}).items():
                deps.append((('E', f, c), True))
            for ev in s.rd_d.get(k, ()):
                deps.append((ev, False))
        waits = {}
        for ev, raw in deps:
            if ev[0] == 'E':
                _, f, c = ev
                if f == eng and not dma:
                    if (not raw) or eng == 'pe':
                        continue
                if s.known[eng][f] >= c:
                    continue
                waits[('E', f)] = max(waits.get(('E', f), 0), c)
            else:
                _, si, v = ev
                if s.kdma[eng].get(si, 0) >= v:
                    continue
                waits[('D', si)] = max(waits.get(('D', si), 0), v)
        if cc:
            ev = ('D', 'cc', 1)
            s.dlast['cc'] = 1
        elif dma:
            qn = s.dcnt[eng]
            si = qn % NDMA + (NDMA if eng == 'pool' else 0)
            v = 16 * (qn // NDMA + 1)
            if qn >= NDMA and s.kdma[eng].get(si, 0) < v - 16:
                waits[('D', si)] = max(waits.get(('D', si), 0), v - 16)
            s.dcnt[eng] += 1
            s.dcount += 1
            s.dlast[si] = v
            ev = ('D', si, v)
        for (t, x), v in waits.items():
            if t == 'E':
                kn = s.known[eng]
                if kn[x] < v:
                    kn[x] = v
                sn = s.snap[x][v - 1]
                for f2, c2 in sn.items():
                    if kn[f2] < c2:
                        kn[f2] = c2
            else:
                s.kdma[eng][x] = v
        if not (dma or cc):
            if noinc:
                ev = ('E', eng, s.cnt[eng] + 1)
            else:
                s.cnt[eng] += 1
                ev = ('E', eng, s.cnt[eng])
                s.snap[eng].append(dict(s.known[eng]))
        s.ops[eng].append((list(waits.items()), fn, None if noinc else ev))
        for k in r:
            if ev[0] == 'E':
                s.rd_e.setdefault(k, {})[eng] = ev[2]
            else:
                s.rd_d.setdefault(k, []).append(ev)
        for k in w:
            s.lastw[k] = ev
            s.rd_e[k] = {}
            s.rd_d[k] = []
        if is_out:
            s.out_events.append(ev)
        return ev

    def barrier(s):
        waits = {}
        for f in ENG:
            if f != 'sp' and s.cnt[f] > 0:
                waits[('E', f)] = s.cnt[f]
        for si, v in s.dlast.items():
            waits[('D', si)] = v
        s.cnt['sp'] += 1
        c = s.cnt['sp']
        for f in ENG:
            if f != 'sp':
                s.known['sp'][f] = s.cnt[f]
        s.kdma['sp'] = dict(s.dlast)
        s.snap['sp'].append(dict(s.known['sp']))
        s.ops['sp'].append((list(waits.items()), (lambda e: e.nop()), ('E', 'sp', c)))
        for f in ENG:
            if f == 'sp':
                continue
            s.ops[f].append(([(('E', 'sp'), c)], None, None))
            for g in ENG:
                if g != f:
                    s.known[f][g] = max(s.known[f][g], s.cnt[g])
            s.kdma[f] = dict(s.dlast)
        s.lastw.clear()
        s.rd_e.clear()
        s.rd_d.clear()

    def finish(s):
        waits = {}
        for ev in s.out_events:
            waits[('D', ev[1])] = max(waits.get(('D', ev[1]), 0), ev[2])
        s.ops['sp'].append((list(waits.items()), None, None))

    def emit(s, block):
        def mk(engname):
            def body(e):
                for waits, fn, ev in s.ops[engname]:
                    for (t, x), v in waits:
                        sem = s.sems[x] if t == 'E' else (s.ccsem if x == 'cc' else s.dsems[x])
                        e.wait_ge(sem, v)
                    if fn is None:
                        continue
                    ins = fn(e)
                    if ev is None:
                        continue
                    if ev[0] == 'E':
                        ins.then_inc(s.sems[engname], 1)
                    elif ev[1] == 'cc':
                        ins.then_inc(s.ccsem, 1)
                    else:
                        ins.then_inc(s.dsems[ev[1]], 16)
            return body
        block.tensor(mk('pe'))
        block.vector(mk('dve'))
        block.scalar(mk('act'))
        block.gpsimd(mk('pool'))
        block.sync(mk('sp'))


class Arena:
    def __init__(s, ar, nwords):
        s.ar = ar
        s.off = 0
        s.n = nwords
        s.peak = 0

    def alloc(s, shape, dt):
        n = 1
        for x in shape:
            n *= x
        words = n if dt == F32 else (n + 1) // 2
        words = (words + 7) // 8 * 8
        a = s.ar[:, s.off:s.off + words]
        s.off += words
        s.peak = max(s.peak, s.off)
        assert s.off <= s.n, ("arena overflow", s.off, s.n)
        if dt == BF16:
            a = a.bitcast(BF16)
        a = a[:, 0:n]
        if len(shape) == 2:
            a = a.rearrange("p (a b) -> p a b", b=shape[1])
        elif len(shape) == 3:
            a = a.rearrange("p (a b c) -> p a b c", b=shape[1], c=shape[2])
        elif len(shape) == 4:
            a = a.rearrange("p (a b c d) -> p a b c d", b=shape[1], c=shape[2], d=shape[3])
        return a


IN_SPECS = [
    ("xp", [1024, 1024]), ("xs", [1024, 1024]), ("xo", [256, 1024]),
    ("ck", [512, 1024]), ("cv", [512, 1024]), ("s0", [2, 4, 64, 64]),
    ("cT", [128, 16]), ("w_in", [1024, 8448]), ("w_ada", [1024, 3072]), ("w_out", [2048, 1024]),
    ("bada_fm", [128, 24]), ("bgate", [1, 1024]), ("normg_fm", [128, 8]), ("fg", [1, 1024]),
    ("lamv", [1, 256]), ("sublng", [1, 128]), ("mu_fm", [128, 64]), ("w0_fm", [128, 20]),
    ("a0_fm", [128, 20]), ("w2", [128, 1280]), ("a2", [128, 1280]), ("kk_fm", [128, 10]),
    ("ka_fm", [128, 10]), ("rk_fm", [128, 10]), ("lnxg", [1, 1280]), ("lnxb", [1, 1280]), ("wrs", [1024, 1024]),
    ("ident", [128, 128]), ("maskNM", [2, 128, 512]), ("maskT", [2, 128, 128]),
    ("bones", [128, 128]), ("hind", [128, 2]), ("selT", [1024, 256]),
    ("ropeall", [1024, 256]), ("ropeown", [256, 256]),
]
OUT_SPECS = [
    ("yp", [1024, 1024]), ("ys", [256, 1024]), ("nk", [1024, 1024]), ("nv", [1024, 1024]),
    ("ns", [4, 2, 16, 64, 64]),
]

ARENA_WORDS = 52480 - 4096
LVL_WORDS = 4096
NHEAD_A = 8
NHP = 8
NSEQ_R = 5
ND = 2
DEBUG = False
A_MODE = 'all'
DUMPS = []
STOP = None


def build():
    nc = bass.Bass("TRN2", target_bir_lowering=False)
    I = {n: nc.dram_tensor(n, sh, F32, kind="ExternalInput").ap() for n, sh in IN_SPECS}
    O = {n: nc.dram_tensor(n, sh, F32, kind="ExternalOutput").ap() for n, sh in OUT_SPECS}
    with ExitStack() as stack:
        ar = stack.enter_context(nc.sbuf_tensor("arena", [128, ARENA_WORDS], F32))
        PS = stack.enter_context(nc.psum_tensor("ps", [128, 4096], F32))
        lvl = stack.enter_context(nc.sbuf_tensor("lvl", [128, LVL_WORDS], LEVEL_DT))
        S = Sched(nc, stack)
        S.ccsem = stack.enter_context(nc.semaphore('ccsem'))
        A = Arena(ar, ARENA_WORDS)
        block = stack.enter_context(nc.Block())
        _program(nc, S, A, PS, I, O, lvl)
        S.finish()
        S.emit(block)
    return nc


def _program(nc, S, A, PS, I, O, lvl):
    def dma(eng, out, in_, r, w, is_out=False):
        S.op(eng, lambda e: e.dma_start(out=out, in_=in_), r, w, dma=True, is_out=is_out)

    def mm(out, lhsT, rhs, start, stop, r, w):
        S.op('pe', lambda e: e.matmul(out, lhsT, rhs, start=start, stop=stop), r, w, noinc=(not stop))

    def tr(out, in_, ident, r, w):
        S.op('pe', lambda e: e.transpose(out, in_, ident), r, w)

    def act(out, in_, func, r, w, bias=None, scale=None, accum=None):
        def f(e):
            kw = {}
            if bias is not None:
                kw['bias'] = bias
            if scale is not None:
                kw['scale'] = scale
            if accum is not None:
                kw['accum_out'] = accum
            return e.activation(out=out, in_=in_, func=func, **kw)
        S.op('act', f, r, w)

    def tt(eng, out, in0, in1, op, r, w):
        S.op(eng, lambda e: e.tensor_tensor(out=out, in0=in0, in1=in1, op=op), r, w)

    def ts(eng, out, in0, s1, s2, op0, op1, r, w):
        if s2 is None:
            S.op(eng, lambda e: e.tensor_scalar(out=out, in0=in0, scalar1=s1, scalar2=None, op0=op0), r, w)
        else:
            S.op(eng, lambda e: e.tensor_scalar(out=out, in0=in0, scalar1=s1, scalar2=s2, op0=op0, op1=op1), r, w)

    def stt(eng, out, in0, sc, in1, op0, op1, r, w):
        S.op(eng, lambda e: e.scalar_tensor_tensor(out=out, in0=in0, scalar=sc, in1=in1, op0=op0, op1=op1), r, w)

    def cp(eng, out, in_, r, w):
        if eng == 'act':
            act(out, in_, AF.Identity, r, w)
        else:
            S.op(eng, lambda e: e.tensor_copy(out=out, in_=in_), r, w)

    def red(out, in_, op, r, w):
        S.op('dve', lambda e: e.tensor_reduce(out=out, in_=in_, axis=AX.X, op=op), r, w)

    def recip(out, in_, r, w):
        S.op('dve', lambda e: e.reciprocal(out=out, in_=in_), r, w)

    def memset(eng, out, val, w):
        S.op(eng, lambda e: e.memset(out, val), (), w)

    def bank(b, c0=0, c1=512):
        return PS[:, b * 512 + c0: b * 512 + c1]

    def bankb(b):
        return PS[:, b * 512:(b + 1) * 512].bitcast(BF16)

    w_in_v = I['w_in'].rearrange("(kc p) n -> p kc n", p=128)

    def dump_all(bufs):
        S.barrier()
        for name, ap in bufs.items():
            sh = list(ap.shape)
            dt = nc.dram_tensor('dbg_' + name, sh, ap.dtype, kind="ExternalOutput").ap()
            DUMPS.append('dbg_' + name)
            dma('sp', dt, ap, (), (), is_out=True)

    identf = A.alloc([128], F32)
    identb = A.alloc([128], BF16)
    onesf = A.alloc([128], F32)
    maskNM = A.alloc([2, 512], BF16)
    maskT = A.alloc([2, 128], BF16)
    bonesb = A.alloc([128], BF16)
    hindb = A.alloc([2], BF16)
    selb = A.alloc([8, 256], BF16)
    cst = A.alloc([4], F32)
    hTp = A.alloc([8, 1024], BF16)
    hTs = A.alloc([8, 1024], BF16)
    hTo = A.alloc([8, 256], BF16)
    mixT = A.alloc([16, 1280], BF16)
    modfm = A.alloc([24, 2], F32)
    scale1 = A.alloc([8, 2], F32)
    neglam = A.alloc([1], F32)
    sgl = A.alloc([128], F32)
    mu = A.alloc([32, 2], F32)
    c0v = A.alloc([32], F32)
    w0v = A.alloc([20], F32)
    a0v = A.alloc([20], F32)
    w0h = A.alloc([20], F32)
    a0h = A.alloc([20], F32)
    kkv = A.alloc([10], F32)
    kav = A.alloc([10], F32)
    omka = A.alloc([10], F32)
    rkv_ = A.alloc([10], F32)
    W2b = A.alloc([1280], BF16)
    A2b = A.alloc([1280], BF16)
    twT = A.alloc([2048], BF16)
    laT = A.alloc([2048], BF16)

    dma('sp', identf, I['ident'], (), ['identf'])
    dma('pool', identb, I['ident'], (), ['identb'])
    dma('pool', maskNM, I['maskNM'].rearrange("d p n -> p d n"), (), ['maskNM'])
    dma('pool', maskT, I['maskT'].rearrange("d p n -> p d n"), (), ['maskT'])
    dma('pool', bonesb, I['bones'], (), ['bonesb'])
    dma('pool', hindb, I['hind'], (), ['hindb'])
    dma('pool', selb, I['selT'].rearrange("(j p) n -> p j n", p=128), (), ['selb'])
    dma('sp', mu, I['mu_fm'].rearrange("p (c j) -> p c j", j=2), (), ['mu'])
    dma('sp', w0v, I['w0_fm'], (), ['w0v'])
    dma('sp', a0v, I['a0_fm'], (), ['a0v'])
    dma('sp', kkv, I['kk_fm'], (), ['kkv'])
    dma('sp', kav, I['ka_fm'], (), ['kav'])
    dma('sp', rkv_, I['rk_fm'], (), ['rkv_'])
    dma('pool', W2b, I['w2'], (), ['W2b'])
    dma('pool', A2b, I['a2'], (), ['A2b'])
    memset('dve', onesf, 1.0, ['onesf'])
    memset('dve', cst[:, 0:1], 1e-12, ['cst'])
    memset('dve', cst[:, 1:2], -0.5, ['cstb'])
    tt('dve', c0v, mu[:, :, 0], mu[:, :, 1], ALU.add, ['mu'], ['c0v'])
    ts('dve', c0v, c0v, -1.0, 1.0, ALU.mult, ALU.add, ['c0v'], ['c0v'])
    ts('dve', omka, kav, -1.0, 1.0, ALU.mult, ALU.add, ['kav'], ['omka'])
    ts('dve', w0h, w0v, 0.5, None, ALU.mult, None, ['w0v'], ['w0h'])
    ts('dve', a0h, a0v, 0.5, None, ALU.mult, None, ['a0v'], ['a0h'])

    scp = A.alloc([8, 2], BF16)
    m0 = A.off
    cT = A.alloc([16], F32)
    sc = A.alloc([8, 2], F32)
    wadaf = [A.alloc([8, 512], F32) for _ in range(4)]
    bada = A.alloc([24], F32)
    normg = A.alloc([8], F32)
    lamt = A.alloc([4, 64], F32)
    lamp = A.alloc([2, 64], F32)
    lams = A.alloc([4], F32)

    dma('sp', cT, I['cT'], (), ['cT'])
    dma('sp', bada, I['bada_fm'], (), ['bada'])
    dma('sp', normg, I['normg_fm'], (), ['normg'])
    dma('sp', lamt.rearrange("p a b -> p (a b)"), I['lamv'][0:1, :].partition_broadcast(128), (), ['lamt'])
    dma('sp', sgl, I['sublng'][0:1, :].partition_broadcast(128), (), ['sgl'])
    wada_v = I['w_ada'].rearrange("(kc p) n -> p kc n", p=128)
    for n in range(4):
        dma('sp', wadaf[n], wada_v[:, :, n * 512:(n + 1) * 512], (), [('wada', n)])
    act(sc, cT.rearrange("p (c v) -> p c v", v=2), AF.Silu, ['cT'], ['sc'])
    cp('dve', scp, sc, ['sc'], ['scp'])
    for fc in range(16):
        for kc in range(8):
            mm(bank(0, fc * 2, fc * 2 + 2), wadaf[fc // 4][:, kc, (fc % 4) * 128:(fc % 4 + 1) * 128], sc[:, kc, :],
               kc == 0, kc == 7, ['sc', ('wada', fc // 4)], [('ps', 0)])
    tt('dve', modfm[:, 0:16, :], bank(0, 0, 32).rearrange("p (a b) -> p a b", b=2),
       bada[:, 0:16].unsqueeze(2).to_broadcast([128, 16, 2]), ALU.add, [('ps', 0), 'bada'], ['modfm'])
    ts('dve', scale1, modfm[:, 8:16, :], 1.0, None, ALU.add, None, ['modfm'], ['scale1'])
    tt('dve', scale1, scale1, normg.unsqueeze(2).to_broadcast([128, 8, 2]), ALU.mult, ['scale1', 'normg'], ['scale1'])
    tt('dve', lamp[:, 0, :], lamt[:, 0, :], lamt[:, 1, :], ALU.mult, ['lamt'], ['lamp'])
    tt('dve', lamp[:, 1, :], lamt[:, 2, :], lamt[:, 3, :], ALU.mult, ['lamt', 'lamp'], ['lamp'])
    red(lams[:, 0:2], lamp, ALU.add, ['lamp'], ['lams'])
    act(lams[:, 2:4], lams[:, 0:2], AF.Exp, ['lams'], ['lams2'])
    lam_init = 0.8 - 0.6 * math.exp(-0.3 * 0)
    tt('dve', neglam, lams[:, 3:4], lams[:, 2:3], ALU.subtract, ['lams2'], ['neglam'])
    ts('dve', neglam, neglam, -lam_init, None, ALU.add, None, ['neglam'], ['neglam'])
    ts('dve', sgl, sgl, 0.5 * (1.0 - lam_init), None, ALU.mult, None, ['sgl'], ['sgl'])

    if STOP == '0':
        return
    xt = [A.alloc([1024], F32) for _ in range(2)]
    xn = [A.alloc([1024], BF16) for _ in range(2)]
    junk = A.alloc([1024], BF16)
    st1 = [A.alloc([4], F32) for _ in range(2)]
    tiles = [('xp', g, hTp, g, 0) for g in range(8)] + [('xs', g, hTs, g, 1) for g in range(8)] + \
            [('xo', g, hTo, g, 1) for g in range(2)]
    for ti, (src, g, hT, tg, v) in enumerate(tiles):
        b = ti % 2
        dma('sp', xt[b], I[src][g * 128:(g + 1) * 128, :], (), [('xt', b)])
        act(junk, xt[b], AF.Square, [('xt', b)], ['junk', ('st1', b)], accum=st1[b][:, 0:1])
        ts('dve', st1[b][:, 1:2], st1[b][:, 0:1], 1.0 / 1024, 1e-6, ALU.mult, ALU.add, [('st1', b)], [('st1b', b)])
        act(st1[b][:, 2:3], st1[b][:, 1:2], AF.Sqrt, [('st1b', b)], [('st1c', b)])
        recip(st1[b][:, 3:4], st1[b][:, 2:3], [('st1c', b)], [('st1d', b)])
        ts('dve', xn[b], xt[b], st1[b][:, 3:4], None, ALU.mult, None, [('xt', b), ('st1d', b)], [('xn', b)])
        pb_ = bankb(3 + b)
        for kc in range(8):
            tr(pb_[:, kc * 128:(kc + 1) * 128], xn[b][:, kc * 128:(kc + 1) * 128], identb,
               [('xn', b), 'identb'], [('ps', 3 + b)])
        for kc in range(8):
            if kc % 2 == 0:
                act(hT[:, kc, tg * 128:(tg + 1) * 128], pb_[:, kc * 128:(kc + 1) * 128], AF.Identity,
                    [('ps', 3 + b), 'scale1', 'modfm'], [(src + 'h', tg)],
                    bias=modfm[:, kc, v:v + 1], scale=scale1[:, kc, v:v + 1])
        for kc in range(8):
            if kc % 2 == 1:
                ts('dve', hT[:, kc, tg * 128:(tg + 1) * 128], pb_[:, kc * 128:(kc + 1) * 128], scale1[:, kc, v:v + 1],
                   modfm[:, kc, v:v + 1], ALU.mult, ALU.add, [('ps', 3 + b), 'scale1', 'modfm'], [(src + 'h', tg)])
    if STOP == '1':
        return
    S.barrier()
    A.off = m0
    if STOP == '1b':
        return

    mA = A.off
    wA = [A.alloc([8, 512], BF16) for _ in range(2)]
    qkb = [A.alloc([256], BF16) for _ in range(2)]
    kvf = [A.alloc([256], F32) for _ in range(2)]
    qT2 = [A.alloc([256], BF16) for _ in range(2)]
    sg2 = [A.alloc([2, 128], F32) for _ in range(3)]
    kTp = [A.alloc([256], BF16) for _ in range(2)]
    vbp = [A.alloc([2, 128], BF16) for _ in range(2)]
    kTs = A.alloc([1536], BF16)
    vbs = A.alloc([12, 128], BF16)
    ckb = A.alloc([4, 128], BF16)
    ropa = A.alloc([8, 256], F32)
    ropo = A.alloc([2, 256], F32)
    xf = [A.alloc([128], F32) for _ in range(2)]
    rt1 = [A.alloc([128], F32) for _ in range(2)]
    rt2 = [A.alloc([128], F32) for _ in range(2)]
    xb16 = [A.alloc([128], BF16) for _ in range(2)]
    pbuf = [A.alloc([1536], BF16) for _ in range(2)]
    pT = [A.alloc([1536], BF16) for _ in range(2)]
    pbufp = [[A.alloc([256], BF16) for _ in range(2)] for _ in range(2)]
    pTp = [[A.alloc([256], BF16) for _ in range(2)] for _ in range(2)]
    ast = [A.alloc([16], F32) for _ in range(4)]
    of = [A.alloc([128], F32) for _ in range(4)]
    o1 = [A.alloc([128], F32) for _ in range(4)]
    on = [A.alloc([128], F32) for _ in range(4)]
    ob = [A.alloc([128], BF16) for _ in range(4)]
    ajunk = [A.alloc([128], BF16) for _ in range(4)]

    dma('sp', ropa, I['ropeall'].rearrange("(j p) n -> p j n", p=128), (), ['ropa'])
    dma('sp', ropo, I['ropeown'].rearrange("(j p) n -> p j n", p=128), (), ['ropo'])

    ctr = {'x': 0}

    def rope(src_ps, tab, dst16, rkeys, wkey):
        i = ctr['x'] % 2
        ctr['x'] += 1
        cp('act', xf[i], src_ps, rkeys, [('xf', i)])
        tt('dve', rt1[i], xf[i], tab[:, 0:128], ALU.mult, [('xf', i), 'ropa', 'ropo'], [('rt1', i)])
        xv = xf[i].rearrange("p (g h f) -> p g h f", h=2, f=16)
        sv = tab[:, 128:256].rearrange("p (g h f) -> p g h f", h=2, f=16)
        r2 = rt2[i].rearrange("p (g h f) -> p g h f", h=2, f=16)
        tt('pool', r2[:, :, 0, :], xv[:, :, 1, :], sv[:, :, 0, :], ALU.mult, [('xf', i), 'ropa', 'ropo'], [('rt2', i, 0)])
        tt('pool', r2[:, :, 1, :], xv[:, :, 0, :], sv[:, :, 1, :], ALU.mult, [('xf', i), 'ropa', 'ropo'], [('rt2', i, 1)])
        tt('dve', dst16, rt1[i], rt2[i], ALU.add, [('rt1', i), ('rt2', i, 0), ('rt2', i, 1)], [wkey])

    def drive2(gens):
        gens = [g for g in gens if g is not None]
        while gens:
            for g in list(gens):
                try:
                    next(g)
                except StopIteration:
                    gens.remove(g)

    def attn_unit(par, j, m, kind):
        ai = par * 2 + j
        if kind == 'p':
            ntk, kT_, vb_, kkey, vkey = 2, kTp[par], vbp[par], ('kTp', par), ('vbp', par)
            sb0 = 2 + 2 * j + m
            ob_ = 6 + j
            pbs, pTs, pkey = pbufp[j], pTp[j], ('pp', j)
            tcol = 512
        else:
            ntk, kT_, vb_, kkey, vkey = 12, kTs, vbs, 'kTs', 'vbs'
            sb0 = 2 + 3 * m
            ob_ = sb0 + 2
            pbs, pTs, pkey = pbuf, pT, ('ps_', 0)
            tcol = 0
        Tk = ntk * 128
        qT_ = qT2[par]
        s0c = sb0 * 512
        nsb = (Tk + 511) // 512
        sck = [('ps', sb0 + b_) for b_ in range(nsb)]
        for n0 in range(0, Tk, 512):
            w_ = min(512, Tk - n0)
            mm(PS[:, s0c + n0:s0c + n0 + w_], qT_[64 * m:64 * m + 64, j * 128:(j + 1) * 128],
               kT_[64 * m:64 * m + 64, n0:n0 + w_], True, True, [('qT', par, j), kkey], [('ps', sb0 + n0 // 512)])
        red(ast[ai][:, m:m + 1], PS[:, s0c:s0c + Tk], ALU.max, sck, [('ast', ai, 'mx', m)])
        ts('dve', ast[ai][:, 2 + m:3 + m], ast[ai][:, m:m + 1], -0.125, None, ALU.mult, None,
           [('ast', ai, 'mx', m)], [('ast', ai, 'nb', m)])
        act(pbs[m][:, 0:Tk], PS[:, s0c:s0c + Tk], AF.Exp, sck + [('ast', ai, 'nb', m)],
            [('pbuf', pkey, m), ('ast', ai, 'sum', m)], bias=ast[ai][:, 2 + m:3 + m], scale=0.125,
            accum=ast[ai][:, 4 + m:5 + m])
        yield
        ptv = PS[:, s0c:s0c + 1024].bitcast(BF16)
        for t in range(ntk):
            c_ = tcol + t * 128
            tr(ptv[:, c_:c_ + 128], pbs[m][:, t * 128:(t + 1) * 128], identb,
               [('pbuf', pkey, m), 'identb'], [('ps', sb0 + (c_ // 1024))])
        if ntk <= 2:
            cp('dve', pTs[m][:, 0:Tk], ptv[:, tcol:tcol + Tk], [('ps', sb0)], [('pT', pkey, m, 0)])
            ptk = [('pT', pkey, m, 0)]
        else:
            cp('dve', pTs[m][:, 0:768], ptv[:, 0:768], [('ps', sb0)], [('pT', pkey, m, 0)])
            cp('act', pTs[m][:, 768:1536], ptv[:, 768:1536], [('ps', sb0), ('ps', sb0 + 1)], [('pT', pkey, m, 1)])
            ptk = [('pT', pkey, m, 0), ('pT', pkey, m, 1)]
        yield
        oc = m * 128 if kind == 'p' else 0
        for t in range(ntk):
            mm(bank(ob_, oc, oc + 128), pTs[m][:, t * 128:(t + 1) * 128], vb_[:, t, :],
               t == 0, t == ntk - 1, ptk + [vkey], [('ps', ob_)])
        yield

    def attn_comb(h, par, j, kind, mixcol0, sgi):
        ai = par * 2 + j
        if kind == 'p':
            o1src, o2src, k1, k2, ob_ = bank(6 + j, 0, 128), bank(6 + j, 128, 256), ('ps', 6 + j), ('ps', 6 + j), 6 + j
        else:
            o1src, o2src, k1, k2, ob_ = bank(4, 0, 128), bank(7, 0, 128), ('ps', 4), ('ps', 7), 7
        a_ = ast[ai]
        recip(a_[:, 6:8], a_[:, 4:6], [('ast', ai, 'sum', 0), ('ast', ai, 'sum', 1)], [('ast', ai, 'rs')])
        tt('dve', a_[:, 8:9], a_[:, 7:8], neglam, ALU.mult, [('ast', ai, 'rs'), 'neglam'], [('ast', ai, 'c2')])
        act(o1[ai], o1src, AF.Identity, [k1, ('ast', ai, 'rs')], [('o1', ai)], scale=a_[:, 6:7])
        stt('dve', of[ai], o2src, a_[:, 8:9], o1[ai], ALU.mult, ALU.add, [k2, ('ast', ai, 'c2'), ('o1', ai)], [('of', ai)])
        yield
        act(ajunk[ai], of[ai], AF.Square, [('of', ai)], [('ajunk', ai), ('ast', ai, 'ss')], accum=a_[:, 9:10])
        ts('dve', a_[:, 10:11], a_[:, 9:10], 1.0 / 128, 1e-5, ALU.mult, ALU.add, [('ast', ai, 'ss')], [('ast', ai, 'ms')])
        tt('pool', a_[:, 12:13], a_[:, 10:11], cst[:, 1:2], ALU.pow, [('ast', ai, 'ms'), 'cstb'], [('ast', ai, 'rstd')])
        yield
        stt('dve', on[ai], of[ai], a_[:, 12:13], sgl, ALU.mult, ALU.mult, [('of', ai), ('ast', ai, 'rstd'), 'sgl'], [('on', ai)])
        tt('pool', ob[ai], on[ai], sg2[sgi][:, j, :], ALU.mult, [('on', ai), ('sg', sgi, j)], [('ob', ai)])
        pso = bankb(ob_)
        tr(pso[:, 512:640], ob[ai], identb, [('ob', ai), 'identb'], [('ps', ob_)])
        cp('act', mixT[:, h, mixcol0 + j * 128: mixcol0 + (j + 1) * 128], pso[:, 512:640], [('ps', ob_)],
           [('mixT', h, (mixcol0 // 128) + j)])
        yield

    def rr(gens):
        gens = list(gens)
        while gens:
            for g in list(gens):
                try:
                    next(g)
                except StopIteration:
                    gens.remove(g)
            yield

    def attn_gen(h, par, kind, mixcol0, sgi):
        if kind == 'p':
            for _ in rr([attn_unit(par, j, m, kind) for j in range(2) for m in range(2)]):
                yield
            comb_q.append(rr([attn_comb(h, par, j, kind, mixcol0, sgi) for j in range(2)]))
        else:
            for j in range(2):
                for _ in rr([attn_unit(par, j, m, kind) for m in range(2)]):
                    yield
                if j == 0:
                    for _ in attn_comb(h, par, j, kind, mixcol0, sgi):
                        yield
                else:
                    comb_q.append(attn_comb(h, par, j, kind, mixcol0, sgi))

    def proj_gen(h, par, kind, s_, sgi):
        wb = wA[h % 2]
        wk = [('wA', h % 2, j4) for j4 in range(4)]
        pst = bankb(1)
        if kind == 'p':
            for j in range(2):
                g = s_ * 2 + j
                bi = g % 2
                for kc in range(8):
                    mm(bank(0), hTp[:, kc, g * 128:(g + 1) * 128], wb[:, kc, :], kc == 0, kc == 7,
                       [('xph', g)] + wk, [('ps', 0)])
                cp('dve', qkb[bi], bank(0, 0, 256), [('ps', 0)], [('qkb', bi)])
                cp('act', kvf[bi], bank(0, 128, 384), [('ps', 0)], [('kvf', bi)])
                dma('sp', O['nk'][g * 128:(g + 1) * 128, h * 128:(h + 1) * 128], kvf[bi][:, 0:128], [('kvf', bi)], (), is_out=True)
                dma('sp', O['nv'][g * 128:(g + 1) * 128, h * 128:(h + 1) * 128], kvf[bi][:, 128:256], [('kvf', bi)], (), is_out=True)
                cp('dve', vbp[par][:, j, :], bank(0, 256, 384), [('ps', 0)], [('vbp', par)])
                act(sg2[sgi][:, j, :], bank(0, 384, 512), AF.Tanh, [('ps', 0)], [('sg', sgi, j)], scale=0.5)
                stt('dve', sg2[sgi][:, j, :], sg2[sgi][:, j, :], 1.0, bank(0, 384, 512), ALU.add, ALU.mult, [('ps', 0), ('sg', sgi, j)], [('sg', sgi, j)])
                yield
                tr(pst[:, 0:128], qkb[bi][:, 0:128], identb, [('qkb', bi), 'identb'], [('ps', 1)])
                tr(pst[:, 128:256], qkb[bi][:, 128:256], identb, [('qkb', bi), 'identb'], [('ps', 1)])
                cp('act', qT2[par][:, j * 128:(j + 1) * 128], pst[:, 0:128], [('ps', 1)], [('qT', par, j)])
                cp('act', kTp[par][:, j * 128:(j + 1) * 128], pst[:, 128:256], [('ps', 1)], [('kTp', par)])
                yield
        else:
            dma('pool', ckb, I['ck'].rearrange("(j p) c -> p j c", p=128)[:, :, h * 128:(h + 1) * 128], (), ['ckb'])
            dma('pool', vbs[:, 0:4, :], I['cv'].rearrange("(j p) c -> p j c", p=128)[:, :, h * 128:(h + 1) * 128], (), ['vbs'])
            for t in range(4):
                tr(pst[:, 512 + t * 128:512 + (t + 1) * 128], ckb[:, t, :], identb, ['ckb', 'identb'], [('ps', 1)])
            cp('dve', kTs[:, 0:512], pst[:, 512:1024], [('ps', 1)], ['kTs'])
            yield
            for j in range(8):
                for kc in range(8):
                    mm(bank(0, 0, 256), hTs[:, kc, j * 128:(j + 1) * 128], wb[:, kc, 128:384], kc == 0, kc == 7,
                       [('xsh', j)] + wk, [('ps', 0)])
                bi = j % 2
                cp('dve', vbs[:, 4 + j, :], bank(0, 128, 256), [('ps', 0)], ['vbs'])
                rope(bank(0, 0, 128), ropa[:, j, :], xb16[bi], [('ps', 0)], ('xb16', bi))
                yield
                tr(pst[:, 0:128], xb16[bi], identb, [('xb16', bi), 'identb'], [('ps', 1)])
                cp('act', kTs[:, 512 + j * 128:512 + (j + 1) * 128], pst[:, 0:128], [('ps', 1)], ['kTs'])
                yield
            for j in range(2):
                for kc in range(8):
                    mm(bank(0), hTo[:, kc, j * 128:(j + 1) * 128], wb[:, kc, :], kc == 0, kc == 7,
                       [('xoh', j)] + wk, [('ps', 0)])
                bi = j % 2
                act(sg2[sgi][:, j, :], bank(0, 384, 512), AF.Tanh, [('ps', 0)], [('sg', sgi, j)], scale=0.5)
                stt('dve', sg2[sgi][:, j, :], sg2[sgi][:, j, :], 1.0, bank(0, 384, 512), ALU.add, ALU.mult, [('ps', 0), ('sg', sgi, j)], [('sg', sgi, j)])
                rope(bank(0, 0, 128), ropo[:, j, :], xb16[bi], [('ps', 0)], ('xb16', bi))
                yield
                tr(pst[:, 128:256], xb16[bi], identb, [('xb16', bi), 'identb'], [('ps', 1)])
                cp('act', qT2[par][:, j * 128:(j + 1) * 128], pst[:, 128:256], [('ps', 1)], [('qT', par, j)])
                yield

    comb_q = []
    ajobs = []
    for h in range(NHEAD_A):
        for s_ in range(4):
            ajobs.append((h, 'p', s_))
        ajobs.append((h, 's', 0))
    loaded = set()

    def load_w(h):
        if h in loaded or h >= NHEAD_A:
            return
        loaded.add(h)
        for j4, base in enumerate([0, 1024, 2048, 3072]):
            dma('pool', wA[h % 2][:, :, j4 * 128:(j4 + 1) * 128], w_in_v[:, :, base + h * 128: base + (h + 1) * 128], (),
                [('wA', h % 2, j4)])
    if ajobs:
        load_w(0)
        drive2([proj_gen(ajobs[0][0], 0, ajobs[0][1], ajobs[0][2], 0)])
        for n, (h, kind, s_) in enumerate(ajobs):
            par = n % 2
            ag = attn_gen(h, par, kind, s_ * 256 if kind == 'p' else 1024, n % 3)
            pg = None
            if n + 1 < len(ajobs):
                h2, kind2, s2 = ajobs[n + 1]
                load_w(h2)
                pg = proj_gen(h2, (n + 1) % 2, kind2, s2, (n + 1) % 3)
            cg = comb_q.pop(0) if comb_q else None
            drive2([cg, ag, pg])
        while comb_q:
            drive2([comb_q.pop(0)])
    if STOP == 'A':
        return
    S.barrier()
    A.off = mA

    wR = [A.alloc([8, 512], BF16) for _ in range(2)]
    rkv = A.alloc([3, 1024], F32)
    kk = A.alloc([1024], F32)
    t1 = A.alloc([1024], F32)
    prodb = A.alloc([1024], BF16)
    sqb = prodb
    vb16 = A.alloc([1024], BF16)
    VT = A.alloc([8, 128], BF16)
    sgb = A.alloc([8, 128], F32)
    yacc = A.alloc([8, 128], F32)
    wL = yacc.rearrange("p a b -> p (a b)").bitcast(BF16).rearrange("p (a b) -> p a b", b=256)
    bsum = A.alloc([16], F32)
    lnxg = [A.alloc([128], F32) for _ in range(2)]
    lnxb = [A.alloc([128], F32) for _ in range(2)]
    sgw = A.alloc([256], F32)
    Pp = A.alloc([256], F32)
    csb = A.alloc([256], F32)
    Winv = A.alloc([256], F32)
    av = A.alloc([256], F32)
    tmpa = A.alloc([256], F32)
    tmpb = A.alloc([256], F32)
    BW = A.alloc([256], BF16)
    KW = A.alloc([256], BF16)
    Wb2 = [A.alloc([2, 130], F32) for _ in range(2)]
    Kt2 = [A.alloc([256], BF16) for _ in range(2)]
    Bt2 = [A.alloc([256], BF16) for _ in range(2)]
    ARb2 = [A.alloc([2, 256], BF16) for _ in range(2)]
    BKT2 = [A.alloc([2, 256], BF16) for _ in range(2)]
    ATb2 = [A.alloc([2, 128], BF16) for _ in range(2)]
    NM = [A.alloc([512], BF16) for _ in range(4)]
    lo = [0]

    def lalloc(n):
        a_ = lvl[:, lo[0]:lo[0] + n].bitcast(F32)
        lo[0] += n
        assert lo[0] <= LVL_WORDS
        return a_
    X0 = [lalloc(128) for _ in range(4)]
    X0T = [lalloc(128) for _ in range(4)]
    XX = [[lalloc(256) for _ in range(2)] for _ in range(4)]
    Zb = [[lalloc(128) for _ in range(2)] for _ in range(4)]
    Zh = [A.alloc([128], BF16) for _ in range(4)]
    GT = A.alloc([8, 64], BF16)
    Hs = A.alloc([8, 64], F32)
    Qb = A.alloc([8, 128], BF16)
    Sf = [A.alloc([64], F32) for _ in range(2)]
    Sb = [A.alloc([64], BF16) for _ in range(2)]
    s0raw = A.alloc([128], F32)
    stg = [A.alloc([128], F32) for _ in range(2)]
    gst = A.alloc([8, 16], F32)
    ysq = t1.rearrange("p (a b) -> p a b", b=128)
    ybon = kk.rearrange("p (a b) -> p a b", b=128)
    obR = A.alloc([8, 128], BF16)

    dma('pool', wL, w_in_v[:, :, 4096 + 3072:4096 + 3328], (), ['wL'])
    for q in range(2):
        memset('dve', Wb2[q][:, :, 0:1], 1.0, [('Wbpad0', q)])
        memset('dve', Wb2[q][:, :, 129:130], 1.0, [('Wbpad1', q)])

    T = 1024
    units = [(hTp, 'xph', 0, 'p'), (hTs, 'xsh', 1024, 's')]

    def proj_shift(w_ap, wkeys, hT, hkey, ci, dst, dkey, kind, b0):
        for n0 in range(0, T, 512):
            bnk = b0 + n0 // 512
            hk = [(hkey, n0 // 128 + q) for q in range(4)]
            for kc in range(8):
                mm(bank(bnk), w_ap[:, kc, :], hT[:, kc, n0:n0 + 512], kc == 0, kc == 7, hk + wkeys, [('ps', bnk)])
        for n0 in range(0, T, 512):
            bnk = b0 + n0 // 512
            act(dst[:, n0:n0 + 512], bank(bnk), AF.Identity, [('ps', bnk), 'c0v'], [dkey], scale=c0v[:, ci:ci + 1])
        psv = PS[:, b0 * 512:b0 * 512 + 1024]
        pk = [('ps', b0), ('ps', b0 + 1)]
        blocks = [(0, T)] if kind == 's' else [(q * 256, (q + 1) * 256) for q in range(4)]
        for (s_, e_) in blocks:
            kk_ = [('ps', b0 + s_ // 512)] if (s_ // 512 == (e_ - 1) // 512) else pk
            stt('dve', dst[:, s_ + 1:e_], psv[:, s_:e_ - 1], mu[:, ci, 0:1], dst[:, s_ + 1:e_], ALU.mult, ALU.add,
                kk_ + ['mu', dkey], [dkey])
            stt('dve', dst[:, s_:e_ - 1], psv[:, s_ + 1:e_], mu[:, ci, 1:2], dst[:, s_:e_ - 1], ALU.mult, ALU.add,
                kk_ + ['mu', dkey], [dkey])

    for (hT, hkey, lc0, kind) in units:
        for c in range(2):
            proj_shift(wL[:, :, c * 128:(c + 1) * 128], ['wL'], hT, hkey, 24 + c, t1, 't1', kind, 2 * c)
            if c == 0:
                act(twT[:, lc0:lc0 + T], t1[:, 0:T], AF.Tanh, ['t1'], [('twT', kind)])
            else:
                cp('act', laT[:, lc0:lc0 + T], t1[:, 0:T], ['t1'], [('laT', kind)])
    S.barrier()

    def prep_gen(hp, kind, lc0, d, seg, par):
        Wb, Kt, Bt, ARb, BKT, ATb = Wb2[par], Kt2[par], Bt2[par], ARb2[par], BKT2[par], ATb2[par]
        kT_, rT_ = rkv[:, 1, :], rkv[:, 0, :]
        c0_ = seg * 256
        lc = lc0 + c0_
        mm(bank(0, 0, 256), W2b[64 * d:64 * d + 64, hp * 128:(hp + 1) * 128], twT[64 * d:64 * d + 64, lc:lc + 256],
           True, True, ['W2b', ('twT', kind)], [('ps', 0)])
        act(sgw, bank(0, 0, 256), AF.Tanh, [('ps', 0), 'w0h'], ['sgw'], bias=w0h[:, hp * 2 + d:hp * 2 + d + 1], scale=0.5)
        ts('dve', sgw, sgw, 0.5, 0.5, ALU.mult, ALU.add, ['sgw'], ['sgw'])
        mm(bank(1, 0, 256), A2b[64 * d:64 * d + 64, hp * 128:(hp + 1) * 128], laT[64 * d:64 * d + 64, lc:lc + 256],
           True, True, ['A2b', ('laT', kind)], [('ps', 1)])
        act(av, bank(1, 0, 256), AF.Tanh, [('ps', 1), 'a0h'], ['av'], bias=a0h[:, hp * 2 + d:hp * 2 + d + 1], scale=0.5)
        ts('dve', av, av, 0.5, 0.5, ALU.mult, ALU.add, ['av'], ['av'])
        yield
        for t in range(2):
            S.op('dve', (lambda t=t: (lambda e: e.tensor_tensor_scan(
                out=Pp[:, t * 128:(t + 1) * 128], data0=onesf, data1=sgw[:, t * 128:(t + 1) * 128],
                initial=0.0, op0=ALU.mult, op1=ALU.add)))(), ['sgw', 'onesf'], [('Pp', t)])
        if d == 0:
            cs = Pp
            csk = [('Pp', 0), ('Pp', 1)]
        else:
            for t in range(2):
                stt('dve', csb[:, t * 128:(t + 1) * 128], sgw[:, t * 128:(t + 1) * 128],
                    Pp[:, t * 128 + 127:t * 128 + 128], Pp[:, t * 128:(t + 1) * 128], ALU.add, ALU.subtract,
                    ['sgw', ('Pp', t)], [('csb', t)])
            cs = csb
            csk = [('csb', 0), ('csb', 1)]
        yield
        act(Wb[:, :, 1:129], cs.rearrange("p (a b) -> p a b", b=128), AF.Exp, csk, [('Wb', par)], scale=-DC)
        act(Winv, cs, AF.Exp, csk, ['Winv'], scale=DC)
        ts('dve', tmpa, av, kav[:, hp:hp + 1], omka[:, hp:hp + 1], ALU.mult, ALU.add, ['av', 'kav', 'omka'], ['tmpa'])
        tt('pool', tmpb, kk[:, c0_:c0_ + 256], av, ALU.mult, ['kk', 'av'], ['tmpb'])
        yield
        tt('dve', tmpa, tmpa, kT_[:, c0_:c0_ + 256], ALU.mult, ['tmpa', ('rkv', 1)], ['tmpa'])
        tt('dve', Kt, tmpa, Winv, ALU.mult, ['tmpa', 'Winv'], [('Kt', par)])
        tt('dve', Bt, tmpb, Winv, ALU.mult, ['tmpb', 'Winv'], [('Bt', par)])
        yield
        Wprev = Wb[:, :, 0:128] if d == 0 else Wb[:, :, 2:130]
        stt('dve', ARb[:, :, 0:128], kk[:, c0_:c0_ + 256].rearrange("p (a b) -> p a b", b=128), -1.0, Wprev,
            ALU.mult, ALU.mult, ['kk', ('Wb', par), ('Wbpad0', par), ('Wbpad1', par)], [('ARb', par, 'a')])
        tt('dve', ARb[:, :, 128:256], rT_[:, c0_:c0_ + 256].rearrange("p (a b) -> p a b", b=128), Wb[:, :, 1:129],
           ALU.mult, [('rkv', 0), ('Wb', par)], [('ARb', par, 'r')])
        yield
        pst2 = bankb(2)
        for t in range(2):
            wc = Wb[:, t, 128:129] if d == 0 else Wb[:, t, 1:2]
            ts('dve', BW[:, t * 128:(t + 1) * 128], Bt[:, t * 128:(t + 1) * 128], wc, None, ALU.mult, None,
               [('Bt', par), ('Wb', par)], [('BW', t)])
            ts('dve', KW[:, t * 128:(t + 1) * 128], Kt[:, t * 128:(t + 1) * 128], wc, None, ALU.mult, None,
               [('Kt', par), ('Wb', par)], [('KW', t)])
            yield
        for t in range(2):
            tr(pst2[:, t * 384:t * 384 + 128], BW[:, t * 128:(t + 1) * 128], identb, [('BW', t), 'identb'], [('ps', 2)])
            tr(pst2[:, t * 384 + 128:t * 384 + 256], KW[:, t * 128:(t + 1) * 128], identb, [('KW', t), 'identb'], [('ps', 2)])
            tr(pst2[:, t * 384 + 256:t * 384 + 384], ARb[:, t, 0:128], identb, [('ARb', par, 'a'), 'identb'], [('ps', 2)])
        for t in range(2):
            cp('act', BKT[:, t, :], pst2[:, t * 384:t * 384 + 256], [('ps', 2)], [('BKT', par, t)])
            cp('act', ATb[:, t, :], pst2[:, t * 384 + 256:t * 384 + 384], [('ps', 2)], [('ATb', par, t)])
        yield

    def rest_gen(hp, kind, d, seg, par, state_in):
        Wb, Kt, Bt, ARb, BKT, ATb = Wb2[par], Kt2[par], Bt2[par], ARb2[par], BKT2[par], ATb2[par]
        gt0 = seg * 2
        P = []
        for t in range(2):
            for e in range(2):
                zi = t * 2 + e
                P.append(dict(t=t, e=e, zi=zi, si=zi, pb=64 * e, bM=4 + zi, gt=gt0 + t))
        for p in P:
            t, e, pb, bM, si = p['t'], p['e'], p['pb'], p['bM'], p['si']
            mm(bank(bM, 0, 256), Bt[pb:pb + 64, t * 128:(t + 1) * 128], ARb[pb:pb + 64, t, :], True, True,
               [('Bt', par), ('ARb', par, 'a'), ('ARb', par, 'r')], [('ps', bM)])
            mm(bank(bM, 256, 512), Kt[pb:pb + 64, t * 128:(t + 1) * 128], ARb[pb:pb + 64, t, :], True, True,
               [('Kt', par), ('ARb', par, 'a'), ('ARb', par, 'r')], [('ps', bM)])
        for p in P:
            bM, si = p['bM'], p['si']
            tt('dve', NM[si], bank(bM), maskNM[:, d, :], ALU.mult, [('ps', bM), 'maskNM'], [('NM', si)])
            tt('dve', X0[si].bitcast(LEVEL_DT), bank(bM, 0, 128), maskNM[:, d, 0:128], ALU.mult, [('ps', bM), 'maskNM'], [('X0', si)])
        yield
        for p in P:
            t, e, pb, bM, si, gt = p['t'], p['e'], p['pb'], p['bM'], p['si'], p['gt']
            mm(bank(bM, 0, 128), ARb[pb:pb + 64, t, 0:128], Bt[pb:pb + 64, t * 128:(t + 1) * 128], True, True,
               [('Bt', par), ('ARb', par, 'a')], [('ps', bM)])
            mm(bank(bM, 384, 448), NM[si][:, 256:384], VT[:, gt, e * 64:(e + 1) * 64], True, True,
               [('NM', si), 'VT'], [('ps', bM)])
        for p in P:
            t, e, bM, si, zi = p['t'], p['e'], p['bM'], p['si'], p['zi']
            tt('dve', X0T[si].bitcast(LEVEL_DT), bank(bM, 0, 128), maskT[:, d, :], ALU.mult, [('ps', bM), 'maskT'], [('X0T', si)])
            cp('act', Zb[zi][0].bitcast(LEVEL_DT)[:, 64:128], bank(bM, 384, 448), [('ps', bM)], [('Zb', zi, 0, 'u')])
            cp('pool', Zb[zi][0].bitcast(LEVEL_DT)[:, 0:64], ATb[:, t, e * 64:(e + 1) * 64], [('ATb', par, t)], [('Zb', zi, 0, 'a')])
        yield
        for j in range(7):
            for p in P:
                bM, si, zi = p['bM'], p['si'], p['zi']
                Xj = X0[si] if j == 0 else XX[si][j % 2][:, 0:128]
                XjT = X0T[si] if j == 0 else XX[si][j % 2][:, 128:256]
                xk = [('X0', si), ('X0T', si)] if j == 0 else [('XX', si, j % 2)]
                zc = Zb[zi][j % 2]
                zck = [('Zb', zi, j % 2, 'a'), ('Zb', zi, j % 2, 'u')]
                Xr, XTr, zr = Xj.bitcast(LEVEL_DT), XjT.bitcast(LEVEL_DT), zc.bitcast(LEVEL_DT)
                mm(bank(bM, 0, 128), Xr, zr, True, True, xk + zck, [('ps', bM)])
                if j < 6:
                    mm(bank(bM, 128, 256), XTr, Xr, True, True, xk, [('ps', bM)])
                    mm(bank(bM, 256, 384), Xr, XTr, True, True, xk, [('ps', bM)])
            for p in P:
                bM, si, zi = p['bM'], p['si'], p['zi']
                zc = Zb[zi][j % 2]
                zn = Zb[zi][(j + 1) % 2]
                zck = [('Zb', zi, j % 2, 'a'), ('Zb', zi, j % 2, 'u')]
                znk = [('Zb', zi, (j + 1) % 2, 'a'), ('Zb', zi, (j + 1) % 2, 'u')]
                tt('dve', zn.bitcast(LEVEL_DT), bank(bM, 0, 128), zc, ALU.add, [('ps', bM)] + zck, znk)
                if j < 6:
                    cp('act', XX[si][(j + 1) % 2].bitcast(LEVEL_DT), bank(bM, 128, 384), [('ps', bM)], [('XX', si, (j + 1) % 2)])
            yield
        for p in P:
            si, zi = p['si'], p['zi']
            cp('pool', Zh[si], Zb[zi][1], [('Zb', zi, 1, 'a'), ('Zb', zi, 1, 'u')], [('Zh', si)])
        for p in P:
            t, e, pb, bM, si, gt = p['t'], p['e'], p['pb'], p['bM'], p['si'], p['gt']
            Z = Zh[si]
            zk = [('Zh', si)]
            mm(bank(bM, 448, 512)[pb:pb + 64, :], Z[:, 0:64], BKT[:, t, e * 64:(e + 1) * 64], True, True,
               zk + [('BKT', par, t)], [('ps', bM)])
            mm(bank(bM, 384, 448)[pb:pb + 64, :], BKT[:, t, e * 64:(e + 1) * 64], Z[:, 64:128], True, False,
               zk + [('BKT', par, t)], [('ps', bM)])
            mm(bank(bM, 384, 448)[pb:pb + 64, :], BKT[:, t, 128 + e * 64:128 + (e + 1) * 64],
               VT[:, gt, e * 64:(e + 1) * 64], False, True, ['VT', ('BKT', par, t)], [('ps', bM)])
            mm(bank(bM, 0, 128)[pb:pb + 64, :], Z[:, 0:64], NM[si][:, 128:256], True, True,
               zk + [('NM', si)], [('ps', bM)])
            mm(bank(bM, 128, 192), NM[si][:, 128:256], Z[:, 64:128], True, False, zk + [('NM', si)], [('ps', bM)])
            mm(bank(bM, 128, 192), NM[si][:, 384:512], VT[:, gt, e * 64:(e + 1) * 64], False, True,
               ['VT', ('NM', si)], [('ps', bM)])
        yield
        for p in P:
            t, e, pb, bM, si, gt = p['t'], p['e'], p['pb'], p['bM'], p['si'], p['gt']
            wc = Wb[pb:pb + 64, t, 128:129] if d == 0 else Wb[pb:pb + 64, t, 1:2]
            stt('dve', GT[pb:pb + 64, gt, :], identf[pb:pb + 64, pb:pb + 64], wc, bank(bM, 448, 512)[pb:pb + 64, :],
                ALU.mult, ALU.add, [('ps', bM), 'identf', ('Wb', par)], [('GT', gt, e)])
            tt('dve', Qb[pb:pb + 64, gt, :], bank(bM, 0, 128)[pb:pb + 64, :], ARb[pb:pb + 64, t, 128:256], ALU.add,
               [('ps', bM), ('ARb', par, 'r')], [('Qb', gt, e)])
            cp('act', Hs[pb:pb + 64, gt, :], bank(bM, 384, 448)[pb:pb + 64, :], [('ps', bM)], [('Hs', gt, e)])
            if d == 0:
                cp('act', yacc[:, gt, e * 64:(e + 1) * 64], bank(bM, 128, 192), [('ps', bM)], [('yacc', gt, e)])
            else:
                tt('dve', yacc[:, gt, e * 64:(e + 1) * 64], bank(bM, 128, 192), yacc[:, gt, e * 64:(e + 1) * 64],
                   ALU.add, [('ps', bM), ('yacc', gt, e)], [('yacc', gt, e)])
        yield

    def chain_gen(hp, kind, d, seg, par, state_in):
        gt0 = seg * 2
        have_state = state_in
        order = [0, 1] if d == 0 else [1, 0]
        for t in order:
            gt = gt0 + t
            for e in range(2):
                pb = 64 * e
                if have_state:
                    mm(bank(3, e * 64, e * 64 + 64), Qb[pb:pb + 64, gt, :], Sb[d][pb:pb + 64, :], True, True,
                       [('Qb', gt, e), ('Sb', d, e)], [('ps', 3)])
                    mm(bank(3, 128 + e * 64, 192 + e * 64)[pb:pb + 64, :], GT[pb:pb + 64, gt, :], Sb[d][pb:pb + 64, :],
                       True, True, [('GT', gt, e), ('Sb', d, e)], [('ps', 3)])
            for e in range(2):
                pb = 64 * e
                if have_state:
                    tt('dve', yacc[:, gt, e * 64:(e + 1) * 64], bank(3, e * 64, e * 64 + 64),
                       yacc[:, gt, e * 64:(e + 1) * 64], ALU.add, [('ps', 3), ('yacc', gt, e)], [('yacc', gt, e)])
                    tt('dve', Sf[d][pb:pb + 64, :], bank(3, 128 + e * 64, 192 + e * 64)[pb:pb + 64, :], Hs[pb:pb + 64, gt, :],
                       ALU.add, [('ps', 3), ('Hs', gt, e)], [('Sf', d, e)])
                else:
                    cp('dve', Sf[d][pb:pb + 64, :], Hs[pb:pb + 64, gt, :], [('Hs', gt, e)], [('Sf', d, e)])
                cp('act', Sb[d][pb:pb + 64, :], Sf[d][pb:pb + 64, :], [('Sf', d, e)], [('Sb', d, e)])
            have_state = True
            yield
        if kind == 'p':
            q = (seg + d) % 2
            tr(bank(3, 256, 384)[0:64, :], Sf[d], identf, [('Sf', d, 0), ('Sf', d, 1), 'identf'], [('ps', 3)])
            cp('act', stg[q][0:64, :], bank(3, 256, 384)[0:64, :], [('ps', 3)], [('stg', q)])
            dma('sp', O['ns'][seg, d, 2 * hp:2 * hp + 2, :, :].rearrange("e v k -> v e k"),
                stg[q][0:64, :].rearrange("p (e k) -> p e k", e=2), [('stg', q)], (), is_out=True)
            yield

    def drive(a, b):
        gens = [g for g in (a, b) if g is not None]
        while gens:
            for g in list(gens):
                try:
                    next(g)
                except StopIteration:
                    gens.remove(g)

    jobno = [0]
    pending_chain = [None]

    def drive3(gens):
        gens = [g for g in gens if g is not None]
        while gens:
            for g in list(gens):
                try:
                    next(g)
                except StopIteration:
                    gens.remove(g)
    ag_in = nc.dram_tensor("ag_in", [1024, 256], BF16)
    ag_out = nc.dram_tensor("ag_out", [4096, 256], BF16)
    wrs_v = I['wrs'].rearrange("(kc p) n -> p kc n", p=128)
    unit_of = {'p': (hTp, 'xph', 0), 's': (hTs, 'xsh', 1024)}
    tasks = ([('s', 8), ('s', 9)] if NSEQ_R >= 2 else []) + [('p', h_) for h_ in range(NHP)]

    def ci_of(a_, hp):
        return a_ * 8 + hp if hp < 8 else 26 + a_ * 2 + (hp - 8)
    for ti_, (kind, hp) in enumerate(tasks):
        tp = ti_ % 2
        wr = wR[tp]
        for a_, base in enumerate([4096, 4096 + 1024, 4096 + 2048, 4096 + 3328]):
            if hp < 8:
                wsrc = w_in_v[:, :, base + hp * 128: base + (hp + 1) * 128]
            else:
                wsrc = wrs_v[:, :, (hp - 8) * 512 + a_ * 128:(hp - 8) * 512 + (a_ + 1) * 128]
            dma('pool', wr[:, :, a_ * 128:(a_ + 1) * 128], wsrc, (), [('wR', tp, a_)])
        dma('sp', lnxg[tp], I['lnxg'][0:1, hp * 128:(hp + 1) * 128].partition_broadcast(128), (), [('lnxg', tp)])
        dma('sp', lnxb[tp], I['lnxb'][0:1, hp * 128:(hp + 1) * 128].partition_broadcast(128), (), [('lnxb', tp)])
        if ti_ == 2 and tasks[0][0] == 's':
            S.op('pool', lambda e: e.collective_compute("AllGather", ALU.bypass, replica_groups=[[0, 1, 2, 3], [4, 5, 6, 7]],
                                                        ins=[ag_in.ap().opt()], outs=[ag_out.ap().opt()]),
                 [('ag_in', 0), ('ag_in', 1)], ['ag_out'], dma=True, cc=True)
        for (hT, hkey, lc0) in [unit_of[kind]]:
            nt = 8
            for a_ in range(3):
                proj_shift(wr[:, :, a_ * 128:(a_ + 1) * 128], [('wR', tp, a_)], hT, hkey, ci_of(a_, hp), rkv[:, a_, :],
                           ('rkv', a_), kind, 2 * a_)
            rT_, kT_, vT_ = rkv[:, 0, :], rkv[:, 1, :], rkv[:, 2, :]
            for t in range(nt):
                bnk = 6 + (t // 4) % 2
                for kc in range(8):
                    mm(bank(bnk, (t % 4) * 128, (t % 4 + 1) * 128), hT[:, kc, t * 128:(t + 1) * 128],
                       wr[:, kc, 384:512], kc == 0, kc == 7, [(hkey, t), ('wR', tp, 3)], [('ps', bnk)])
                if t % 4 == 3:
                    act(sgb[:, t - 3:t + 1, :], bank(bnk).rearrange("p (a b) -> p a b", b=128), AF.Silu, [('ps', bnk)],
                        [('sgb', q) for q in range(t - 3, t + 1)])
            ts('dve', t1[:, 0:T], kT_[:, 0:T], kkv[:, hp:hp + 1], None, ALU.mult, None, [('rkv', 1), 'kkv'], ['t1'])
            act(sqb[:, 0:T], t1[:, 0:T], AF.Square, ['t1'], ['prodb'])
            for n0 in range(0, T, 512):
                bnk = (n0 // 512) % 2
                mm(bank(bnk), bonesb, sqb[:, n0:n0 + 512], True, True, ['bonesb', 'prodb'], [('ps', bnk)])
                act(kk[:, n0:n0 + 512], bank(bnk), AF.Sqrt, [('ps', bnk), 'cst'], ['kk'], bias=cst[:, 0:1])
            recip(kk[:, 0:T], kk[:, 0:T], ['kk'], ['kk'])
            tt('dve', kk[:, 0:T], kk[:, 0:T], t1[:, 0:T], ALU.mult, ['kk', 't1'], ['kk'])
            stt('dve', prodb[:, 0:T], rT_[:, 0:T], rkv_[:, hp:hp + 1], kT_[:, 0:T], ALU.mult, ALU.mult,
                [('rkv', 0), ('rkv', 1), 'rkv_'], ['prodb'])
            for t in range(nt):
                mm(bank(1, t * 2, t * 2 + 2), prodb[:, t * 128:(t + 1) * 128], hindb, True, True, ['prodb', 'hindb'], [('ps', 1)])
            cp('act', bsum[:, 0:nt * 2], bank(1, 0, nt * 2), [('ps', 1)], ['bsum'])
            cp('act', vb16[:, 0:T], vT_[:, 0:T], [('rkv', 2)], ['vb16'])
            pst2 = bankb(2)
            for t in range(nt):
                tr(pst2[:, t * 128:(t + 1) * 128], vb16[:, t * 128:(t + 1) * 128], identb, ['vb16', 'identb'], [('ps', 2)])
            cp('dve', VT[:, 0:nt, :], pst2[:, 0:nt * 128].rearrange("p (a b) -> p a b", b=128), [('ps', 2)], ['VT'])
            jobs = []
            for d in range(ND):
                segs = list(range(4)) if d == 0 else list(range(3, -1, -1))
                for i_, seg in enumerate(segs):
                    st_in = (kind == 's')
                    jobs.append((d, seg, st_in, (kind == 's' and i_ == 0)))
            pars = []
            for _ in jobs:
                pars.append(jobno[0] % 2)
                jobno[0] += 1
            pg = prep_gen(hp, kind, lc0, jobs[0][0], jobs[0][1], pars[0])
            drive(pg, None)
            for n, (d, seg, st_in, load_s0) in enumerate(jobs):
                if load_s0:
                    dma('sp', s0raw[0:64, :].rearrange("p (e k) -> p e k", e=2),
                        I['s0'][d, 2 * (hp - 8):2 * (hp - 8) + 2, :, :].rearrange("e v k -> v e k"), (), ['s0raw'])
                    tr(bank(3, 0, 64), s0raw[0:64, :], identf[0:64, 0:64], ['s0raw', 'identf'], [('ps', 3)])
                    for e in range(2):
                        pb = 64 * e
                        cp('dve', Sf[d][pb:pb + 64, :], bank(3, 0, 64)[pb:pb + 64, :], [('ps', 3)], [('Sf', d, e)])
                        cp('act', Sb[d][pb:pb + 64, :], bank(3, 0, 64)[pb:pb + 64, :], [('ps', 3)], [('Sb', d, e)])
                rg = rest_gen(hp, kind, d, seg, pars[n], st_in)
                ng = None
                if n + 1 < len(jobs):
                    ng = prep_gen(hp, kind, lc0, jobs[n + 1][0], jobs[n + 1][1], pars[n + 1])
                drive3([rg, pending_chain[0], ng])
                pending_chain[0] = chain_gen(hp, kind, d, seg, pars[n], st_in)
            drive3([pending_chain[0]])
            pending_chain[0] = None
            n2 = nt * 2
            yk = [('yacc', t, e) for t in range(nt) for e in range(2)]
            yv = yacc[:, 0:nt, :].rearrange("p a (e f) -> p (a e) f", e=2)
            red(gst[:, 0, 0:n2], yv, ALU.add, yk, [('gst', 0)])
            act(ysq[:, 0:nt, :], yacc[:, 0:nt, :], AF.Square, yk, ['t1'])
            red(gst[:, 1, 0:n2], ysq[:, 0:nt, :].rearrange("p a (e f) -> p (a e) f", e=2), ALU.add, ['t1'], [('gst', 1)])
            ts('dve', gst[:, 2, 0:n2], gst[:, 0, 0:n2], 1.0 / 64, None, ALU.mult, None, [('gst', 0)], [('gst', 2)])
            tt('dve', gst[:, 3, 0:n2], gst[:, 2, 0:n2], gst[:, 2, 0:n2], ALU.mult, [('gst', 2)], [('gst', 3)])
            stt('dve', gst[:, 4, 0:n2], gst[:, 1, 0:n2], 1.0 / 64, gst[:, 3, 0:n2], ALU.mult, ALU.subtract,
                [('gst', 1), ('gst', 3)], [('gst', 4)])
            ts('dve', gst[:, 4, 0:n2], gst[:, 4, 0:n2], 64e-5, None, ALU.add, None, [('gst', 4)], [('gst', 4)])
            act(gst[:, 5, 0:n2], gst[:, 4, 0:n2], AF.Sqrt, [('gst', 4)], [('gst', 5)])
            recip(gst[:, 6, 0:n2], gst[:, 5, 0:n2], [('gst', 5)], [('gst', 6)])
            ysv = ysq[:, 0:nt, :].rearrange("p a (e f) -> p (a e) f", e=2)
            tt('dve', ysv, yv, gst[:, 2, 0:n2].unsqueeze(2).to_broadcast([128, n2, 64]), ALU.subtract, yk + [('gst', 2)], ['t1'])
            tt('dve', ysv, ysv, gst[:, 6, 0:n2].unsqueeze(2).to_broadcast([128, n2, 64]), ALU.mult, ['t1', ('gst', 6)], ['t1'])
            tt('dve', ysq[:, 0:nt, :], ysq[:, 0:nt, :], lnxg[tp].unsqueeze(1).to_broadcast([128, nt, 128]),
               ALU.mult, ['t1', ('lnxg', tp)], ['t1'])
            tt('dve', ysq[:, 0:nt, :], ysq[:, 0:nt, :], lnxb[tp].unsqueeze(1).to_broadcast([128, nt, 128]),
               ALU.add, ['t1', ('lnxb', tp)], ['t1'])
            tt('dve', ybon[:, 0:nt, :].rearrange("p a (e f) -> p (a e) f", e=2),
               VT[:, 0:nt, :].rearrange("p a (e f) -> p (a e) f", e=2),
               bsum[:, 0:n2].unsqueeze(2).to_broadcast([128, n2, 64]), ALU.mult, ['VT', 'bsum'], ['kk'])
            tt('dve', ysq[:, 0:nt, :], ysq[:, 0:nt, :], ybon[:, 0:nt, :], ALU.add, ['t1', 'kk'], ['t1'])
            tt('dve', obR[:, 0:nt, :], ysq[:, 0:nt, :], sgb[:, 0:nt, :], ALU.mult, ['t1'] + [('sgb', t) for t in range(nt)], ['obR'])
            if kind == 'p':
                pst2 = bankb(2)
                for t in range(nt):
                    tr(pst2[:, t * 128:(t + 1) * 128], obR[:, t, :], identb, ['obR', 'identb'], [('ps', 2)])
                cp('act', mixT[:, 8 + hp, 0:1024], pst2[:, 0:1024], [('ps', 2)], [('mixT', 8 + hp, g) for g in range(8)])
            else:
                i_ = hp - 8
                dma('sp', ag_in.ap().rearrange("(t p) (i f) -> p t i f", p=128, i=2)[:, :, i_, :], obR[:, 0:8, :], ['obR'],
                    [('ag_in', i_)])
    if DEBUG:
        dump_all(dict(yacc=yacc, Sf0=Sf[0], GT=GT, Hs=Hs))
    if STOP == 'R':
        return
    S.barrier()
    A.off = mA

    wout = A.alloc([16, 1024], BF16)
    fgbc = A.alloc([1024], F32)
    gatebc = A.alloc([2, 1024], F32)
    bgbc = A.alloc([1024], F32)
    scbc = A.alloc([8, 2, 128], BF16)
    wadg = A.alloc([8, 512], BF16)
    wout_v = I['w_out'].rearrange("(c p) n -> p c n", p=128)
    wada_v = I['w_ada'].rearrange("(kc p) n -> p kc n", p=128)
    dma('pool', wadg, wada_v[:, :, 2048:2560], (), [('wadg', 0)])
    for c4 in range(4):
        dma('pool', wout[:, c4 * 4:(c4 + 1) * 4, :], wout_v[:, c4 * 4:(c4 + 1) * 4, :], (), [('wout', c4)])
    dma('sp', fgbc, I['fg'][0:1, :].partition_broadcast(128), (), ['fgbc'])
    dma('sp', bgbc, I['bgate'][0:1, :].partition_broadcast(128), (), ['bgbc'])
    mO = A.off
    Gt = A.alloc([32, 256], BF16)
    dma('sp', Gt, ag_out.ap().rearrange("(rt p) f -> p rt f", p=128), ['ag_out'], ['Gt'])
    cp('dve', scbc, scp.unsqueeze(3).to_broadcast([128, 8, 2, 128]), ['scp'], ['scbc'])
    for n in range(2):
        if n == 1:
            dma('pool', wadg, wada_v[:, :, 2560:3072], [('wadg', 0)], [('wadg', 0)])
        for v in range(2):
            for kc in range(8):
                mm(bank(2 + v), scbc[:, kc, v, :], wadg[:, kc, :], kc == 0, kc == 7, ['scbc', ('wadg', 0)], [('ps', 2 + v)])
            tt('dve', gatebc[:, v, n * 512:(n + 1) * 512], bank(2 + v), bgbc[:, n * 512:(n + 1) * 512], ALU.add,
               [('ps', 2 + v), 'bgbc'], [('gatebc', v, n)])
    for r_ in range(4):
        for i_ in range(2):
            hq = 2 * r_ + i_
            bq = 4 + (hq % 4)
            for t in range(8):
                mm(bank(bq, 0, 256), Gt[:, r_ * 8 + t, i_ * 128:(i_ + 1) * 128], selb[:, t, :], t == 0, t == 7,
                   ['Gt', 'selb'], [('ps', bq)])
            cp('act', mixT[:, 8 + hq, 1024:1280], bank(bq, 0, 256), [('ps', bq)], [('mixT', 8 + hq, 8), ('mixT', 8 + hq, 9)])
    S.barrier()
    A.off = mO
    xr = [A.alloc([1024], F32) for _ in range(2)]
    yv_ = [A.alloc([1024], F32) for _ in range(2)]
    ojunk = A.alloc([1024], BF16)
    ost = [A.alloc([4], F32) for _ in range(2)]
    otiles = [('xp', g, 'yp', 0) for g in range(8)] + [('xo', g, 'ys', 1) for g in range(2)]
    for ti, (src, g, dst, v) in enumerate(otiles):
        b = ti % 2
        mg = g if src == 'xp' else 8 + g
        dma('sp', xr[b], I[src][g * 128:(g + 1) * 128, :], (), [('xr', b)])
        for n in range(2):
            for c in range(16):
                mm(bank(n), mixT[:, c, mg * 128:(mg + 1) * 128], wout[:, c, n * 512:(n + 1) * 512], c == 0, c == 15,
                   [('mixT', c, mg), ('wout', c // 4)], [('ps', n)])
            tt('dve', yv_[b][:, n * 512:(n + 1) * 512], bank(n), gatebc[:, v, n * 512:(n + 1) * 512], ALU.mult,
               [('ps', n), ('gatebc', v, n)], [('yv', b, n)])
            tt('dve', yv_[b][:, n * 512:(n + 1) * 512], yv_[b][:, n * 512:(n + 1) * 512], xr[b][:, n * 512:(n + 1) * 512], ALU.add,
               [('yv', b, n), ('xr', b)], [('yv', b, n)])
        act(ojunk, yv_[b], AF.Square, [('yv', b, 0), ('yv', b, 1)], ['ojunk', ('ost', b)], accum=ost[b][:, 0:1])
        ts('dve', ost[b][:, 1:2], ost[b][:, 0:1], 1.0 / 1024, 1e-6, ALU.mult, ALU.add, [('ost', b)], [('ostb', b)])
        act(ost[b][:, 2:3], ost[b][:, 1:2], AF.Sqrt, [('ostb', b)], [('ostc', b)])
        recip(ost[b][:, 3:4], ost[b][:, 2:3], [('ostc', b)], [('ostd', b)])
        stt('dve', xr[b], yv_[b], ost[b][:, 3:4], fgbc, ALU.mult, ALU.mult, [('yv', b, 0), ('yv', b, 1), ('ostd', b), 'fgbc', ('xr', b)], [('xr', b)])
        dma('sp', O[dst][g * 128:(g + 1) * 128, :], xr[b], [('xr', b)], (), is_out=True)


_NC = None


def _rope_tab(pos_rows, pos_cols):
    n_freq = 16
    inv = (10000.0 ** (-np.arange(n_freq, dtype=np.float32) / n_freq)).astype(np.float32)
    T = len(pos_rows)
    tab = np.zeros((T, 256), np.float32)
    for s_, pos in enumerate([pos_rows, pos_cols]):
        ang = pos.astype(np.float32)[:, None] * inv[None, :]
        c, sn = np.cos(ang).astype(np.float32), np.sin(ang).astype(np.float32)
        for m in range(2):
            for hf in range(2):
                o = m * 64 + s_ * 32 + hf * 16
                tab[:, o:o + 16] = c
                tab[:, 128 + o:128 + o + 16] = -sn if hf == 0 else sn
    return tab


def kernel(x_prompt, x_sample, cache_k, cache_v, state_rwkv, c, c_ctx, norm_g, w_ada, b_ada,
           w_in, lam_q1, lam_k1, lam_q2, lam_k2, subln_g, shift_mu, decay_w0, decay_w2,
           iclr_a0, iclr_a2, k_k, k_a, r_k, lnx_g, lnx_b, w_out, final_g):
    global _NC
    f = lambda a: np.ascontiguousarray(np.asarray(a, dtype=np.float32))
    x_prompt, x_sample, cache_k, cache_v, state_rwkv = map(f, (x_prompt, x_sample, cache_k, cache_v, state_rwkv))
    c, c_ctx = f(c), f(c_ctx)
    if _NC is None:
        _NC = build()
    nc = _NC

    def fm(v, nch):
        return np.ascontiguousarray(f(v).reshape(nch, 128).T)

    i = np.arange(128)
    su = (i[:, None] < i[None, :]).astype(np.float32)
    ui = (i[:, None] <= i[None, :]).astype(np.float32)
    sl = (i[:, None] > i[None, :]).astype(np.float32)
    li = (i[:, None] >= i[None, :]).astype(np.float32)
    maskNM = np.stack([np.concatenate([su, ui, su, ui], 1), np.concatenate([sl, li, sl, li], 1)])
    maskT = np.stack([sl, su])
    bones = np.kron(np.eye(2, dtype=np.float32), np.ones((64, 64), np.float32))
    hind = np.kron(np.eye(2, dtype=np.float32), np.ones((64, 1), np.float32))
    tok = np.arange(1024)
    ropeall = _rope_tab(tok // 64, tok % 64)
    W_in = f(w_in)[0]
    smu, sw0, sa0 = f(shift_mu)[0], f(decay_w0)[0], f(iclr_a0)[0]
    sw2, sa2 = f(decay_w2)[0].reshape(128, 1024), f(iclr_a2)[0].reshape(128, 1024)
    skk, ska, srk = f(k_k)[0], f(k_a)[0], f(r_k)[0].reshape(-1)
    slg, slb = f(lnx_g)[0], f(lnx_b)[0]
    shared = {
        "w_in": W_in, "w_ada": f(w_ada)[0], "w_out": f(w_out)[0],
        "bada_fm": fm(f(b_ada)[0], 24), "bgate": f(b_ada)[0:1, 2048:3072], "normg_fm": fm(f(norm_g)[0], 8),
        "fg": f(final_g)[None, :],
        "lamv": np.concatenate([f(lam_q1)[0], f(lam_k1)[0], f(lam_q2)[0], f(lam_k2)[0]])[None, :],
        "sublng": f(subln_g)[0:1],
        "ident": np.eye(128, dtype=np.float32), "maskNM": maskNM, "maskT": maskT, "bones": bones, "hind": hind,
        "ropeall": ropeall,
    }

    def cols(v, hp):
        return v[hp * 128:(hp + 1) * 128]
    in_maps = []
    for core in range(8):
        b, q = core // 4, core % 4
        hps = list(range(8)) + [2 * q, 2 * q + 1]
        sel = np.zeros((1024, 256), np.float32)
        sel[q * 256 + np.arange(256), np.arange(256)] = 1.0
        cT = np.stack([fm(c_ctx, 8), fm(c[b], 8)], -1).reshape(128, 16)
        chunks = [smu[:, ci * 128:(ci + 1) * 128] for ci in range(26)]
        for a_ in range(3):
            for i_ in range(2):
                ci = a_ * 8 + hps[8 + i_]
                chunks.append(smu[:, ci * 128:(ci + 1) * 128])
        mu_fm = np.stack([np.stack([ch[0], ch[1]], -1) for ch in chunks], 1).reshape(128, 64)
        w0_fm = np.stack([np.stack([cols(sw0[0], h_), cols(sw0[1], h_)], -1) for h_ in hps], 1).reshape(128, 20)
        a0_fm = np.stack([np.stack([cols(sa0[0], h_), cols(sa0[1], h_)], -1) for h_ in hps], 1).reshape(128, 20)
        ext = lambda v: np.stack([cols(v, h_) for h_ in hps], 1)
        extc = lambda m: np.concatenate([m[:, h_ * 128:(h_ + 1) * 128] for h_ in hps], 1)
        wrs = np.concatenate([W_in[:, base + h_ * 128: base + (h_ + 1) * 128]
                              for h_ in hps[8:] for base in (4096, 4096 + 1024, 4096 + 2048, 4096 + 3328)], 1)
        m = dict(shared)
        m.update({
            "xp": x_prompt[core * 4:(core + 1) * 4].reshape(1024, 1024),
            "xs": x_sample[b], "xo": x_sample[b, q * 256:(q + 1) * 256],
            "ck": cache_k[b, 0].reshape(512, 1024), "cv": cache_v[b, 0].reshape(512, 1024),
            "s0": state_rwkv[b, 0][:, 4 * q:4 * q + 4], "cT": np.ascontiguousarray(cT), "selT": sel,
            "ropeown": np.ascontiguousarray(ropeall[q * 256:(q + 1) * 256]),
            "mu_fm": mu_fm, "w0_fm": w0_fm, "a0_fm": a0_fm, "w2": extc(sw2), "a2": extc(sa2),
            "kk_fm": ext(skk), "ka_fm": ext(ska), "rk_fm": ext(srk),
            "lnxg": extc(slg[None, :]), "lnxb": extc(slb[None, :]), "wrs": wrs,
        })
        in_maps.append({k: np.ascontiguousarray(v, dtype=np.float32) for k, v in m.items()})
    res = run_bass_kernel_spmd(nc, in_maps, core_ids=list(range(8)))
    R = res.results
    y_prompt = np.concatenate([R[i]["yp"].reshape(4, 256, 1024) for i in range(8)], 0)
    y_sample = np.stack([np.concatenate([R[b * 4 + q]["ys"] for q in range(4)], 0) for b in range(2)], 0)
    new_k = np.concatenate([R[i]["nk"].reshape(4, 1, 256, 8, 2, 64) for i in range(8)], 0)
    new_v = np.concatenate([R[i]["nv"].reshape(4, 1, 256, 8, 128) for i in range(8)], 0)
    new_s = np.concatenate([R[i]["ns"].reshape(4, 1, 2, 16, 64, 64) for i in range(8)], 0)
    return (y_prompt.astype(np.float32), y_sample.astype(np.float32), new_k.astype(np.float32),
            new_v.astype(np.float32), new_s.astype(np.float32))
```
